# Optimizing a Trainium2 kernel written in Bass

```python
import math
import numpy as np
import jax
import jax.numpy as jnp
from jax import lax

D_MODEL = 1024
BATCH = 2
SEQ = 8192
DEPTH = 4
DEC_BATCH = 32
DEC_SEQ = 16
PAST_LEN = 2048

CHUNK = 64
WINDOW = 128
N_PREV = WINDOW // CHUNK
A_HEADS = 8
A_KV_HEADS = 2
A_GROUP = A_HEADS // A_KV_HEADS
A_HEAD_DIM = 64
A_SCALE = A_HEAD_DIM ** -0.5
B_HEADS = 4
B_KEY_DIM = 128
B_VAL_DIM = 128
C_HEADS = 4
C_KEY_DIM = 128
C_VAL_DIM = 128
C_CONV = 4
D_FF = 2816
F_CONV = 3
N_BRANCH = 3
A_WIDTH = A_HEADS * A_HEAD_DIM
B_WIDTH = B_HEADS * B_VAL_DIM
C_WIDTH = C_HEADS * C_VAL_DIM
D_MIX = A_WIDTH + B_WIDTH + C_WIDTH
C_QKV = C_HEADS * (2 * C_KEY_DIM + C_VAL_DIM)
IN_SPLITS = (A_WIDTH, A_KV_HEADS * A_HEAD_DIM, A_KV_HEADS * A_HEAD_DIM,
             B_HEADS * B_KEY_DIM, B_HEADS * B_KEY_DIM, B_WIDTH, B_WIDTH,
             C_QKV, C_WIDTH, C_HEADS, C_HEADS, N_BRANCH * D_MODEL)
D_IN = sum(IN_SPLITS)
ALPHA = (2 * DEPTH) ** 0.25
BETA = (8 * DEPTH) ** -0.25
LN_EPS = 1e-5
RMS_EPS = 1e-6
NEG_BIG = -1e30
F_MIN = 1e-30
F32 = jnp.float32

kernel_name = 'hybrid_streaming_encoder_step'


def layer_norm(x, g, b):
    xf = x.astype(F32)
    mu = xf.mean(-1, keepdims=True)
    var = jnp.square(xf - mu).mean(-1, keepdims=True)
    return ((xf - mu) * lax.rsqrt(var + LN_EPS) * g.astype(F32) + b.astype(F32)).astype(x.dtype)


def rms_norm_gated(o, w, z):
    n = o * lax.rsqrt(jnp.mean(o * o, -1, keepdims=True) + RMS_EPS)
    return n * w.astype(F32) * jax.nn.silu(z)


def l2norm(a):
    return a * lax.rsqrt(jnp.sum(a * a, -1, keepdims=True) + 1e-6)


def masked_exp(d, mask):
    return jnp.where(mask, jnp.exp(jnp.where(mask, d, 0.0)), 0.0)


def causal_dwconv(u, buf, w):
    width = w.shape[0]
    t = u.shape[1]
    full = jnp.concatenate([buf.astype(u.dtype), u], axis=1)
    out = full[:, 0:t] * w[0]
    for j in range(1, width):
        out = out + full[:, j:j + t] * w[j]
    return out, full[:, t:]


def sink_softmax(s, sink):
    m = jnp.maximum(s.max(-1, keepdims=True), sink)
    e = jnp.exp(s - m)
    return e / (e.sum(-1, keepdims=True) + jnp.exp(sink - m))


def swa_prompt(q, k, v, sink):
    bsz, t = q.shape[:2]
    nc = t // CHUNK
    band_len = (N_PREV + 1) * CHUNK
    pad = ((0, 0), (N_PREV * CHUNK, 0), (0, 0), (0, 0))

    def band(a):
        ap = jnp.pad(a, pad).reshape(bsz, nc + N_PREV, CHUNK, A_KV_HEADS, A_HEAD_DIM)
        return jnp.concatenate([ap[:, j:j + nc] for j in range(N_PREV + 1)], axis=2)

    kb, vb = band(k), band(v)
    qc = q.reshape(bsz, nc, CHUNK, A_KV_HEADS, A_GROUP, A_HEAD_DIM)
    s = jnp.einsum('bnqgrd,bnkgd->bngrqk', qc, kb, preferred_element_type=F32) * A_SCALE
    key_pos = (jnp.arange(nc)[:, None] - N_PREV) * CHUNK + jnp.arange(band_len)[None, :]
    s = jnp.where((key_pos >= 0)[None, :, None, None, None, :], s, NEG_BIG)
    p = sink_softmax(s, sink[:, :, None, None])
    o = jnp.einsum('bngrqk,bnkgd->bnqgrd', p.astype(v.dtype), vb)
    return o.reshape(bsz, t, A_WIDTH)


def swa_sample(q, k, v, k_cache, v_cache, sink):
    bsz, t = q.shape[:2]
    rows = k_cache.shape[1]
    kk = jnp.concatenate([k_cache.astype(k.dtype), k], axis=1)
    vv = jnp.concatenate([v_cache.astype(v.dtype), v], axis=1)
    s = jnp.einsum('bqgrd,bkgd->bgrqk', q, kk, preferred_element_type=F32) * A_SCALE
    p = sink_softmax(s, sink[:, :, None, None])
    o = jnp.einsum('bgrqk,bkgd->bqgrd', p.astype(vv.dtype), vv)
    return o.reshape(bsz, t, A_WIDTH), kk[:, -rows:], vv[:, -rows:]


def gla_chunk(state, inp):
    q, k, v, lf = inp
    c = q.shape[2]
    cum = jnp.cumsum(lf, axis=2)
    incl = jnp.arange(c)[:, None] >= jnp.arange(c)[None, :]
    diff = cum[:, :, :, None, :] - cum[:, :, None, :, :]
    dec = masked_exp(diff, incl[None, None, :, :, None])
    att = jnp.einsum('bhtsk,bhsk->bhts', dec * q[:, :, :, None, :], k)
    o = jnp.einsum('bhtk,bhkv->bhtv', q * jnp.exp(cum), state) + jnp.einsum('bhts,bhsv->bhtv', att, v)
    last = cum[:, :, -1:, :]
    new_state = jnp.exp(last[:, :, 0, :, None]) * state + jnp.einsum('bhsk,bhsv->bhkv', k * jnp.exp(last - cum), v)
    return new_state, o


def gdn_chunk(state, inp):
    q, k, v, beta, g = inp
    c = q.shape[2]
    dv = v.shape[-1]
    cum = jnp.cumsum(g, axis=-1)
    diff = cum[..., :, None] - cum[..., None, :]
    idx = jnp.arange(c)
    incl = idx[:, None] >= idx[None, :]
    strict = idx[:, None] > idx[None, :]
    dec_incl = masked_exp(diff, incl)
    dec_strict = jnp.where(strict, dec_incl, 0.0)
    a = beta[..., :, None] * jnp.einsum('bhtk,bhsk->bhts', k, k) * dec_strict + jnp.eye(c, dtype=F32)
    rhs = jnp.concatenate([v * beta[..., None], k * (beta * jnp.exp(cum))[..., None]], axis=-1)
    sol = lax.linalg.triangular_solve(a, rhs, left_side=True, lower=True, unit_diagonal=True)
    w = sol[..., :dv] - jnp.einsum('bhtk,bhkv->bhtv', sol[..., dv:], state)
    qk = jnp.einsum('bhtk,bhsk->bhts', q, k) * dec_incl
    o = jnp.einsum('bhtk,bhkv->bhtv', q * jnp.exp(cum)[..., None], state) + jnp.einsum('bhts,bhsv->bhtv', qk, w)
    last = cum[..., -1:]
    new_state = jnp.exp(last)[..., None] * state + jnp.einsum('bhsk,bhsv->bhkv', k * jnp.exp(last - cum)[..., None], w)
    return new_state, o


def run_chunks(step, state, seqs):
    t = seqs[0].shape[2]
    c = min(CHUNK, t)
    n = t // c

    def split(a):
        return jnp.moveaxis(a.reshape(a.shape[:2] + (n, c) + a.shape[3:]), 2, 0)

    final, outs = lax.scan(step, state, tuple(split(a) for a in seqs))
    outs = jnp.moveaxis(outs, 0, 2)
    return outs.reshape(outs.shape[:2] + (t,) + outs.shape[4:]), final


def trunk_layer(x, state, params, prompt):
    k_cache, v_cache, s_hgrn, s_gdn, buf_gdn, buf_ffn = state
    (w_in, sink, lb, hgrn_w, gdn_conv_w, a_log, dt_bias, gdn_w, w_branch, w_out,
     ln1_g, ln1_b, w_up, f_conv_w, f_conv_b, w_down, ln2_g, ln2_b) = params
    bsz, t, _ = x.shape

    def heads(a, d):
        return a.reshape(bsz, t, -1, d).transpose(0, 2, 1, 3).astype(F32)

    def merge_heads(o):
        return o.transpose(0, 2, 1, 3).reshape(bsz, t, -1).astype(x.dtype)

    offsets = [int(o) for o in np.cumsum(IN_SPLITS)[:-1]]
    aq, ak, av, bq, bf, bi, bg, cqkv, cz, cb, ca, gt = jnp.split(x @ w_in, offsets, axis=-1)

    q_a = aq.reshape(bsz, t, A_KV_HEADS, A_GROUP, A_HEAD_DIM)
    k_a = ak.reshape(bsz, t, A_KV_HEADS, A_HEAD_DIM)
    v_a = av.reshape(bsz, t, A_KV_HEADS, A_HEAD_DIM)
    sink_a = sink.reshape(A_KV_HEADS, A_GROUP).astype(F32)
    if prompt:
        o_a = swa_prompt(q_a, k_a, v_a, sink_a)
        new_k, new_v = k_a[:, -WINDOW:], v_a[:, -WINDOW:]
    else:
        o_a, new_k, new_v = swa_sample(q_a, k_a, v_a, k_cache, v_cache, sink_a)

    f_pre = heads(bf, B_KEY_DIM)
    lbh = lb.reshape(B_HEADS, 1, B_KEY_DIM).astype(F32)
    sig_f = jax.nn.sigmoid(f_pre)
    log_f = jnp.log(jnp.maximum(lbh + (1.0 - lbh) * sig_f, F_MIN))
    k_b = (1.0 - lbh) * (1.0 - sig_f)
    q_b = jax.nn.silu(heads(bq, B_KEY_DIM))
    v_b = heads(bi, B_VAL_DIM)
    o_b, new_hgrn = run_chunks(gla_chunk, s_hgrn.astype(F32), (q_b, k_b, v_b, log_f))
    o_b = merge_heads(rms_norm_gated(o_b, hgrn_w, heads(bg, B_VAL_DIM)))

    u, new_gdn_conv = causal_dwconv(cqkv, buf_gdn, gdn_conv_w)
    u = jax.nn.silu(u)
    cq, ck, cv = jnp.split(u, [C_HEADS * C_KEY_DIM, 2 * C_HEADS * C_KEY_DIM], axis=-1)
    q_c = l2norm(heads(cq, C_KEY_DIM)) * (C_KEY_DIM ** -0.5)
    k_c = l2norm(heads(ck, C_KEY_DIM))
    v_c = heads(cv, C_VAL_DIM)
    beta = jax.nn.sigmoid(cb.astype(F32)).transpose(0, 2, 1)
    g = (-jnp.exp(a_log.astype(F32)) * jax.nn.softplus(ca.astype(F32) + dt_bias.astype(F32))).transpose(0, 2, 1)
    o_c, new_gdn = run_chunks(gdn_chunk, s_gdn.astype(F32), (q_c, k_c, v_c, beta, g))
    o_c = merge_heads(rms_norm_gated(o_c, gdn_w, heads(cz, C_VAL_DIM)))

    gates = jax.nn.sigmoid(gt.reshape(bsz, t, N_BRANCH, D_MODEL))
    w_ba, w_bb, w_bc = jnp.split(w_branch, [A_WIDTH, A_WIDTH + B_WIDTH], axis=0)
    mix = gates[:, :, 0] * (o_a @ w_ba) + gates[:, :, 1] * (o_b @ w_bb) + gates[:, :, 2] * (o_c @ w_bc)
    x = layer_norm(ALPHA * x + mix @ w_out, ln1_g, ln1_b)

    up, new_ffn_conv = causal_dwconv(x @ w_up, buf_ffn, f_conv_w)
    gate, val = jnp.split(up + f_conv_b, 2, axis=-1)
    x = layer_norm(ALPHA * x + (jax.nn.silu(gate) * val) @ w_down, ln2_g, ln2_b)
    return x, (new_k, new_v, new_hgrn, new_gdn, new_gdn_conv, new_ffn_conv)


def setup_inputs(seed: int = 0) -> dict:
    key = jax.random.key(seed)
    ks = jax.random.split(key, 32)

    def nrm(k, shape, s):
        return jax.random.normal(k, shape, F32) * s

    rows = min(WINDOW, PAST_LEN)
    dt = jnp.exp(jax.random.uniform(ks[14], (DEPTH, C_HEADS), F32, math.log(1e-3), math.log(1e-1)))
    return {
        'x_prompt': nrm(ks[0], (BATCH, SEQ, D_MODEL), 1.0),
        'x_sample': nrm(ks[1], (DEC_BATCH, DEC_SEQ, D_MODEL), 1.0),
        'cache_swa_k': nrm(ks[2], (DEPTH, DEC_BATCH, rows, A_KV_HEADS, A_HEAD_DIM), 1.0),
        'cache_swa_v': nrm(ks[3], (DEPTH, DEC_BATCH, rows, A_KV_HEADS, A_HEAD_DIM), 1.0),
        'state_hgrn': nrm(ks[4], (DEPTH, DEC_BATCH, B_HEADS, B_KEY_DIM, B_VAL_DIM), 0.5),
        'state_gdn': nrm(ks[5], (DEPTH, DEC_BATCH, C_HEADS, C_KEY_DIM, C_VAL_DIM), C_KEY_DIM ** -0.5),
        'state_gdn_conv': nrm(ks[6], (DEPTH, DEC_BATCH, C_CONV - 1, C_QKV), 1.0),
        'state_ffn_conv': nrm(ks[7], (DEPTH, DEC_BATCH, F_CONV - 1, 2 * D_FF), 1.0),
        'ln_in_g': 1.0 + nrm(ks[8], (D_MODEL,), 0.02),
        'ln_in_b': nrm(ks[9], (D_MODEL,), 0.02),
        'w_in': nrm(ks[10], (DEPTH, D_MODEL, D_IN), D_MODEL ** -0.5),
        'attn_sinks': nrm(ks[11], (DEPTH, A_HEADS), 0.5),
        'hgrn_lb_logits': nrm(ks[12], (DEPTH, B_HEADS * B_KEY_DIM), 0.5),
        'hgrn_norm_w': 1.0 + nrm(ks[13], (DEPTH, B_VAL_DIM), 0.02),
        'gdn_conv_w': nrm(ks[15], (DEPTH, C_CONV, C_QKV), C_CONV ** -0.5),
        'gdn_a_log': jnp.log(jax.random.uniform(ks[16], (DEPTH, C_HEADS), F32, 1.0, 16.0)),
        'gdn_dt_bias': dt + jnp.log(-jnp.expm1(-dt)),
        'gdn_norm_w': 1.0 + nrm(ks[17], (DEPTH, C_VAL_DIM), 0.02),
        'w_branch': nrm(ks[18], (DEPTH, D_MIX, D_MODEL), BETA * A_WIDTH ** -0.5),
        'w_out': nrm(ks[19], (DEPTH, D_MODEL, D_MODEL), BETA * D_MODEL ** -0.5),
        'ln1_g': 1.0 + nrm(ks[20], (DEPTH, D_MODEL), 0.02),
        'ln1_b': nrm(ks[21], (DEPTH, D_MODEL), 0.02),
        'w_up': nrm(ks[22], (DEPTH, D_MODEL, 2 * D_FF), D_MODEL ** -0.5),
        'ffn_conv_w': nrm(ks[23], (DEPTH, F_CONV, 2 * D_FF), F_CONV ** -0.5),
        'ffn_conv_b': nrm(ks[24], (DEPTH, 2 * D_FF), 0.02),
        'w_down': nrm(ks[25], (DEPTH, D_FF, D_MODEL), BETA * D_FF ** -0.5),
        'ln2_g': 1.0 + nrm(ks[26], (DEPTH, D_MODEL), 0.02),
        'ln2_b': nrm(ks[27], (DEPTH, D_MODEL), 0.02),
    }


def reference(x_prompt, x_sample, cache_swa_k, cache_swa_v, state_hgrn, state_gdn,
              state_gdn_conv, state_ffn_conv, ln_in_g, ln_in_b, w_in, attn_sinks,
              hgrn_lb_logits, hgrn_norm_w, gdn_conv_w, gdn_a_log, gdn_dt_bias, gdn_norm_w,
              w_branch, w_out, ln1_g, ln1_b, w_up, ffn_conv_w, ffn_conv_b, w_down, ln2_g, ln2_b):
    lb_sm = jax.nn.softmax(hgrn_lb_logits.astype(F32), axis=0)
    lower_bounds = jnp.cumsum(lb_sm, axis=0) - lb_sm[0]

    hp = layer_norm(x_prompt, ln_in_g, ln_in_b)
    hs = layer_norm(x_sample, ln_in_g, ln_in_b)
    bp = x_prompt.shape[0]
    prompt_state = (None, None,
                    jnp.zeros((bp, B_HEADS, B_KEY_DIM, B_VAL_DIM), F32),
                    jnp.zeros((bp, C_HEADS, C_KEY_DIM, C_VAL_DIM), F32),
                    jnp.zeros((bp, C_CONV - 1, C_QKV), x_prompt.dtype),
                    jnp.zeros((bp, F_CONV - 1, 2 * D_FF), x_prompt.dtype))
    p_new = []
    s_new = []
    for l in range(DEPTH):
        params = (w_in[l], attn_sinks[l], lower_bounds[l], hgrn_norm_w[l], gdn_conv_w[l],
                  gdn_a_log[l], gdn_dt_bias[l], gdn_norm_w[l], w_branch[l], w_out[l],
                  ln1_g[l], ln1_b[l], w_up[l], ffn_conv_w[l], ffn_conv_b[l], w_down[l],
                  ln2_g[l], ln2_b[l])
        hp, st_p = trunk_layer(hp, prompt_state, params, True)
        sample_state = (cache_swa_k[l], cache_swa_v[l], state_hgrn[l], state_gdn[l],
                        state_gdn_conv[l], state_ffn_conv[l])
        hs, st_s = trunk_layer(hs, sample_state, params, False)
        p_new.append(st_p)
        s_new.append(st_s)

    def stack(states, i):
        return jnp.stack([st[i] for st in states])

    return (hp, hs,
            stack(p_new, 0), stack(p_new, 1), stack(p_new, 2), stack(p_new, 3), stack(p_new, 4), stack(p_new, 5),
            stack(s_new, 0), stack(s_new, 1), stack(s_new, 2), stack(s_new, 3), stack(s_new, 4), stack(s_new, 5))
```

```python
import numpy as np
from contextlib import ExitStack
import concourse.bass as bass
import concourse.mybir as mybir
from concourse.bass_utils import run_bass_kernel_spmd

F32 = mybir.dt.float32
BF16 = mybir.dt.bfloat16
AF = mybir.ActivationFunctionType
ALU = mybir.AluOpType

D = 1024
DEPTH = 4
SEG = 2048
NS = 64
NTOK = SEG + NS
TB = 512
NBLK = SEG // TB
A_HEADS, A_KV, A_HD = 8, 2, 64
D_FF = 2816
C_QKV = 1536
D_IN = 7944
OFF = {}
_o = 0
for _n, _w in (("aq", 512), ("ak", 128), ("av", 128), ("bq", 512), ("bf", 512), ("bi", 512), ("bg", 512),
               ("cqkv", 1536), ("cz", 512), ("cb", 4), ("ca", 4), ("gt", 3072)):
    OFF[_n] = _o
    _o += _w
ALPHA = (2 * DEPTH) ** 0.25
LN_EPS = 1e-5
RMS_EPS = 1e-6
NEGM = -30000.0


class V:
    __slots__ = ("ap", "key")

    def __init__(self, ap, key):
        self.ap = ap
        self.key = key

    def __getitem__(self, idx):
        return V(self.ap[idx], self.key)

    def k(self, key):
        return V(self.ap, key)

    def re(self, pat, **kw):
        return V(self.ap.rearrange(pat, **kw), self.key)

    def bc(self, shape):
        return V(self.ap.to_broadcast(shape), self.key)


class Op:
    __slots__ = ("eng", "kind", "fn", "deps", "idx", "inc", "sem", "val", "gid")


class Sched:
    ENGS = ("pe", "act", "dve", "pool", "sp")
    NDMA = 14

    def __init__(self):
        self.ops = {e: [] for e in self.ENGS}
        self.res = {}
        self.dma_slots = {e: [None] * self.NDMA for e in ("sp", "act", "pool")}
        self.dma_cnt = {e: [0] * self.NDMA for e in ("sp", "act", "pool")}
        self.dma_rr = {e: 0 for e in ("sp", "act", "pool")}
        self.all_dma = []
        self.gid = 0
        self.pending_dma = []
        self.ccs = []

    def _rec(self, eng, kind, fn, reads, writes, extra_deps=()):
        op = Op()
        op.eng, op.kind, op.fn = eng, kind, fn
        op.inc = False
        op.sem = None
        op.val = 0
        op.gid = self.gid
        self.gid += 1
        deps = []
        for r in reads:
            st = self.res.get(r)
            if st and st[0] is not None:
                deps.append(st[0])
        for w in writes:
            st = self.res.get(w)
            if st:
                if st[0] is not None:
                    deps.append(st[0])
                deps.extend(st[1])
        deps.extend(extra_deps)
        for r in reads:
            st = self.res.setdefault(r, [None, []])
            st[1].append(op)
        for w in writes:
            self.res[w] = [op, []]
        seen = set()
        dd = []
        for d in deps:
            if d is op or id(d) in seen:
                continue
            seen.add(id(d))
            if d.kind == "c" and kind == "c" and d.eng == "pe" and eng == "pe":
                continue
            dd.append(d)
        op.deps = dd
        op.idx = len(self.ops[eng])
        self.ops[eng].append(op)
        return op

    def c(self, eng, fn, reads, writes):
        return self._rec(eng, "c", fn, reads, writes)

    def cc(self, fn, reads, writes):
        op = self._rec("pool", "x", fn, reads, writes)
        op.sem = ("cc", len(self.ccs))
        op.val = 1
        self.ccs.append(op)
        self.pending_dma.append(op)
        return op

    def dma(self, q, out, in_, **kw):
        slot = self.dma_rr[q]
        self.dma_rr[q] = (slot + 1) % self.NDMA
        prev = self.dma_slots[q][slot]
        extra = [prev] if prev is not None else []
        o, i = out.ap, in_.ap
        op = self._rec(q, "d", lambda e: e.dma_start(out=o, in_=i, **kw), [in_.key], [out.key], extra)
        self.dma_cnt[q][slot] += 1
        op.sem = (q, slot)
        op.val = 16 * self.dma_cnt[q][slot]
        self.dma_slots[q][slot] = op
        self.all_dma.append(op)
        self.pending_dma.append(op)
        return op

    def fence(self):
        lasts = [self.ops[e][-1] for e in self.ENGS if self.ops[e] and self.ops[e][-1].kind != "d"]
        for e in self.ENGS:
            pass
        lastc = []
        for e in self.ENGS:
            for o in reversed(self.ops[e]):
                if o.kind == "c" and o.fn is not None:
                    lastc.append(o)
                    break
                if o.fn is None:
                    break
        pend = list(self.pending_dma)
        self.pending_dma = []
        for e in self.ENGS:
            deps = list(lastc) + pend
            op = Op()
            op.eng, op.kind, op.fn = e, "c", None
            op.inc, op.sem, op.val, op.gid = False, None, 0, self.gid
            self.gid += 1
            op.deps = deps
            op.idx = len(self.ops[e])
            self.ops[e].append(op)
        self.res = {}

    def emit(self, nc, es):
        CH = 20000
        for e in self.ENGS:
            for op in self.ops[e]:
                for d in op.deps:
                    d.inc = True
        self.csems = {}
        for e in self.ENGS:
            n = 0
            for op in self.ops[e]:
                if op.kind == "c" and op.inc:
                    op.sem = (e, n // CH)
                    op.val = n % CH + 1
                    n += 1
            nsem = max(1, (n + CH - 1) // CH)
            self.csems[e] = [es.enter_context(nc.semaphore(f"c_{e}_{j}")) for j in range(nsem)]
        self.dsems = {q: [es.enter_context(nc.semaphore(f"d_{q}_{j}")) for j in range(self.NDMA)]
                      for q in ("sp", "act", "pool")}
        block = es.enter_context(nc.Block())
        engmap = {"pe": block.tensor, "act": block.scalar, "dve": block.vector, "pool": block.gpsimd, "sp": block.sync}

        self.xsems = [es.enter_context(nc.semaphore(f"x_{j}")) for j in range(len(self.ccs))]

        def semof(op):
            if op.kind == "d":
                return self.dsems[op.sem[0]][op.sem[1]]
            if op.kind == "x":
                return self.xsems[op.sem[1]]
            return self.csems[op.sem[0]][op.sem[1]]

        def make(e):
            oplist = self.ops[e]

            def body(eng):
                waited = {}
                for op in oplist:
                    for d in op.deps:
                        key = (d.kind, d.sem)
                        if waited.get(key, 0) >= d.val:
                            continue
                        eng.wait_ge(semof(d), d.val)
                        waited[key] = d.val
                    if op.fn is None:
                        continue
                    ins = op.fn(eng)
                    if op.kind == "d":
                        ins.then_inc(semof(op), 16)
                    elif op.kind == "x":
                        ins.then_inc(semof(op), 1)
                    elif op.inc:
                        ins.then_inc(semof(op), 1)
            return body

        for e in self.ENGS:
            engmap[e](make(e))


S = None


def kof(*vs):
    return [v.key for v in vs if isinstance(v, V)]


def apof(x):
    return x.ap if isinstance(x, V) else x


def mm(out, lhsT, rhs, start=True, stop=True):
    o, l, r = out.ap, lhsT.ap, rhs.ap
    return S.c("pe", lambda e: e.matmul(o, l, r, start=start, stop=stop), kof(lhsT, rhs), kof(out))


def tr(out, in_, ident):
    o, i, d = out.ap, in_.ap, ident.ap
    return S.c("pe", lambda e: e.transpose(o, i, d), kof(in_, ident), kof(out))


def act(out, in_, func, bias=None, scale=None):
    kw = {}
    if bias is not None:
        kw["bias"] = apof(bias)
    if scale is not None:
        kw["scale"] = apof(scale)
    o, i = out.ap, in_.ap
    return S.c("act", lambda e: e.activation(out=o, in_=i, func=func, **kw), kof(in_, bias, scale), kof(out))


def ts(eng, out, in0, s1, s2, op0, op1=None):
    o, i = out.ap, in0.ap
    a1, a2 = apof(s1), apof(s2)
    if op1 is None:
        return S.c(eng, lambda e: e.tensor_scalar(o, i, a1, None, op0), kof(in0, s1), kof(out))
    return S.c(eng, lambda e: e.tensor_scalar(o, i, a1, a2, op0, op1), kof(in0, s1, s2), kof(out))


def tt(eng, out, in0, in1, op):
    o, a, b = out.ap, in0.ap, in1.ap
    return S.c(eng, lambda e: e.tensor_tensor(out=o, in0=a, in1=b, op=op), kof(in0, in1), kof(out))


def stt(out, in0, scalar, in1, op0, op1):
    o, a, b = out.ap, in0.ap, in1.ap
    sc = apof(scalar)
    return S.c("dve", lambda e: e.scalar_tensor_tensor(out=o, in0=a, scalar=sc, in1=b, op0=op0, op1=op1),
               kof(in0, scalar, in1), kof(out))


def cp(eng, out, in_):
    o, i = out.ap, in_.ap
    if eng == "act":
        return S.c("act", lambda e: e.activation(out=o, in_=i, func=AF.Copy), kof(in_), kof(out))
    return S.c(eng, lambda e: e.tensor_copy(o, i), kof(in_), kof(out))


def recip(out, in_):
    o, i = out.ap, in_.ap
    return S.c("dve", lambda e: e.reciprocal(o, i), kof(in_), kof(out))


def memset(eng, out, val):
    o = out.ap
    return S.c(eng, lambda e: e.memset(o, val), [], kof(out))


def make_consts():
    cols = {}
    parts = []
    pos = [0]

    def add(name, a):
        a = np.asarray(a, np.float32)
        if a.shape[0] < 128:
            a = np.concatenate([a, np.zeros((128 - a.shape[0], a.shape[1]), np.float32)], 0)
        cols[name] = (pos[0], a.shape[1])
        parts.append(a)
        pos[0] += a.shape[1]

    i = np.arange(128)
    add("ident", np.eye(128))
    add("ones", np.ones((128, 128)))
    for tag, T, blk in (("p", 128, 128), ("s", 64, 16)):
        s = np.arange(T)[:, None]
        t = np.arange(T)[None, :]
        same = (s // blk) == (t // blk)
        add(f"triLE_{tag}", (same & (s <= t)))
        add(f"triGT_{tag}", (same & (s > t)))
        add(f"mbias_{tag}", np.where(same & (s <= t), 0.0, NEGM))
        add(f"offd_{tag}", (same & (s < t)))
    for tag, T, blk in (("p", 128, 64), ("s", 64, 16)):
        nb = T // blk
        sidx = np.arange(T)[:, None]
        tidx = np.arange(T)[None, :]
        same = (sidx // blk) == (tidx // blk)
        mid = (tidx // blk) * blk + blk // 2 - 1
        drel = (same & (sidx <= tidx)).astype(np.float32) - (same & (sidx <= mid)).astype(np.float32)
        tot = np.stack([(np.arange(T) // blk == b_) for b_ in range(nb)], 1).astype(np.float32)
        midc = np.stack([((np.arange(T) // blk == b_) & (np.arange(T) <= b_ * blk + blk // 2 - 1)) for b_ in range(nb)], 1).astype(np.float32)
        add(f"hdext_{tag}", np.concatenate([drel, tot, midc], 1))
        add(f"hle_{tag}", (same & (sidx <= tidx)))
        add(f"hgt_{tag}", (same & (sidx > tidx)))
        add(f"blk_{tag}", tot)
    sm = np.full((64, 4), NEGM, np.float32)
    for q in range(4):
        sm[16 * q:16 * q + 16, q] = 0.0
    add("seqmask", sm)
    return np.concatenate(parts, 1), cols


CONSTS, CCOL = make_consts()
NCON = CONSTS.shape[1]


class Arena:
    def __init__(self, ap, ncols32):
        self.ap = ap
        self.n = ncols32
        self.off = 0
        self.gen = 0

    def reset(self):
        self.off = 0
        self.gen += 1

    def alloc(self, name, shape, dtype=F32, parts=128):
        free = int(np.prod(shape))
        n32 = free if dtype == F32 else (free + 1) // 2
        self.off = (self.off + 31) // 32 * 32
        assert self.off + n32 <= self.n, (name, self.off, n32, self.n)
        a = self.ap[0:parts, self.off:self.off + n32]
        self.off += n32
        if dtype != F32:
            a = a.bitcast(dtype)[:, 0:free]
        if len(shape) == 2:
            a = a.rearrange("p (a b) -> p a b", a=shape[0])
        elif len(shape) == 3:
            a = a.rearrange("p (a b c) -> p a b c", a=shape[0], b=shape[1])
        return V(a, f"{name}#{self.gen}")


DRAM_IN = [
    ("xp", [SEG, D]), ("xs", [NS, D]),
    ("cache_k", [DEPTH, 4, 128, 128]), ("cache_v", [DEPTH, 4, 128, 128]),
    ("st_hgrn", [DEPTH, 4, 4, 128, 128]), ("st_gdn", [DEPTH, 4, 4, 128, 128]),
    ("st_gconv", [DEPTH, 4, 3, C_QKV]), ("st_fconv", [DEPTH, 4, 2, 2 * D_FF]),
    ("ln_in_g", [D]), ("ln_in_b", [D]), ("w_in", [DEPTH, D, D_IN]), ("attn_sinks", [DEPTH, 8]),
    ("hgrn_lb_logits", [DEPTH, 512]), ("hgrn_norm_w", [DEPTH, 128]), ("gdn_conv_w", [DEPTH, 4, C_QKV]),
    ("gdn_a_log", [DEPTH, 4]), ("gdn_dt_bias", [DEPTH, 4]), ("gdn_norm_w", [DEPTH, 128]),
    ("w_branch", [DEPTH, 1536, D]), ("w_out", [DEPTH, D, D]), ("ln1_g", [DEPTH, D]), ("ln1_b", [DEPTH, D]),
    ("w_up", [DEPTH, D, 2 * D_FF]), ("ffn_conv_w", [DEPTH, 3, 2 * D_FF]), ("ffn_conv_b", [DEPTH, 2 * D_FF]),
    ("w_down", [DEPTH, D_FF, D]), ("ln2_g", [DEPTH, D]), ("ln2_b", [DEPTH, D]),
    ("consts", [128, NCON]), ("corev", [128, 16]),
]
DRAM_OUT = [
    ("y_p", [SEG, D]), ("y_s", [NS, D]),
    ("p_swa_k", [DEPTH, 128, 128]), ("p_swa_v", [DEPTH, 128, 128]),
    ("p_hgrn", [DEPTH, 4, 128, 128]), ("p_gdn", [DEPTH, 4, 128, 128]),
    ("p_gconv", [DEPTH, 3, C_QKV]), ("p_fconv", [DEPTH, 2, 2 * D_FF]),
    ("s_swa_k", [DEPTH, 4, 128, 128]), ("s_swa_v", [DEPTH, 4, 128, 128]),
    ("s_hgrn", [DEPTH, 4, 4, 128, 128]), ("s_gdn", [DEPTH, 4, 4, 128, 128]),
    ("s_gconv", [DEPTH, 4, 3, C_QKV]), ("s_fconv", [DEPTH, 4, 2, 2 * D_FF]),
]
GROUPS = [[0, 1, 2, 3], [4, 5, 6, 7]]
NW = 3
WSLOT = 4096
ARENA32 = 13900


def build_nc(stage=99, nlayers=DEPTH):
    global S
    S = Sched()
    nc = bass.Bass("TRN2", target_bir_lowering=False)
    es = ExitStack()
    dr = {}
    for n, shp in DRAM_IN:
        if shp[0] == DEPTH and n not in ('hgrn_lb_logits', 'gdn_dt_bias', 'gdn_a_log'):
            shp = [nlayers] + list(shp[1:])
        dr[n] = V(nc.dram_tensor(n, shp, F32, kind="ExternalInput").ap(), "RO")
    for n, shp in DRAM_OUT:
        dr[n] = V(nc.dram_tensor(n, shp, F32, kind="ExternalOutput").ap(), "out_" + n)
    AG3W = D * 2
    AG1W = 256 + 36
    dr["ag1_in"] = V(nc.dram_tensor("ag1_in", [128, AG1W], F32).ap(), "ag1_in")
    dr["ag1_out"] = V(nc.dram_tensor("ag1_out", [512, AG1W], F32).ap(), "ag1_out")
    AG2W = 516 + 1024
    dr["ag2_in"] = V(nc.dram_tensor("ag2_in", [128, AG2W], F32).ap(), "ag2_in")
    dr["ag2_out"] = V(nc.dram_tensor("ag2_out", [512, AG2W], F32).ap(), "ag2_out")
    dr["ob_loc"] = V(nc.dram_tensor("ob_loc", [128, 4, NTOK], BF16).ap(), "ob_loc")
    dr["qh"] = V(nc.dram_tensor("qh", [128, 4, SEG], BF16).ap(), "qh")
    dr["oc_loc"] = V(nc.dram_tensor("oc_loc", [128, 4, NTOK], BF16).ap(), "oc_loc")
    dr["rh"] = V(nc.dram_tensor("rh", [128, 4, SEG], BF16).ap(), "rh")
    dr["ag3_in"] = V(nc.dram_tensor("ag3_in", [128, 16], F32).ap(), "ag3_in")
    dr["ag3_out"] = V(nc.dram_tensor("ag3_out", [512, 16], F32).ap(), "ag3_out")

    def sb(name, shape, dt=F32):
        return V(es.enter_context(nc.sbuf_tensor(name, shape, dt))[:], name)

    xres = sb("xres", [128, 8, NTOK])
    xbf = sb("xbf", [128, 8, NTOK], BF16)
    con = sb("con", [128, NCON])
    conb = sb("conb", [128, 256], BF16)
    corev = sb("corev_sb", [128, 16])
    wsl = [sb(f"w{i}", [128, WSLOT], BF16) for i in range(NW)]
    par = sb("par", [128, 512])
    arena_t = sb("arena", [128, ARENA32])
    AR = Arena(arena_t.ap, ARENA32)
    ln_st = sb("ln_st", [128, 2, 512])
    ln_mean = sb("ln_mean", [128, 512])
    ln_rstd = sb("ln_rstd", [128, 512])
    ln_tmp = sb("ln_tmp", [128, 2, 512])
    ffn_tail = sb("ffn_tail", [128, 44, 2])
    kprev = sb("kprev", [64, 2, 128], BF16)
    vprev = sb("vprev", [128, 128], BF16)
    gc_prev = sb("gc_prev", [128, 12, 3])
    esinkB = sb("esinkB", [64, 4 * 64 * 2])
    zero1 = sb("zero1", [128, 1])
    eps1 = sb("eps1", [128, 1])
    hS = sb("hS", [128, 4, 128])
    hpre = sb("hpre", [128, 4])
    lbF = sb("lbF", [128, 4, 4])
    sinB = sb("sinB", [128, 4, 128], BF16)
    sinC = sb("sinC", [128, 4, 128], BF16)
    gpar = sb("gpar", [128, 64])
    nrmw = sb("nrmw", [128, 2])
    psA = [V(es.enter_context(nc.psum_tensor(f"psA{i}", [128, 512], F32))[:], f"psA{i}") for i in range(4)]
    psB_t = [es.enter_context(nc.psum_tensor(f"psB{i}", [128, 512], F32)) for i in range(4)]
    psB = [V(psB_t[i][:], f"psB{i}") for i in range(4)]
    rr = {"w": 0, "pa": 0, "pb": 0, "stg": 0}

    def nxt(kind, lst):
        v = lst[rr[kind] % len(lst)]
        rr[kind] += 1
        return v

    def C(name, parts=128):
        c0, n = CCOL[name]
        return con[0:parts, c0:c0 + n]

    ident = C("ident")
    ones32 = C("ones")
    identb = conb[:, 0:128]
    onesb = conb[:, 128:256]

    S.dma("sp", con, dr["consts"])
    S.dma("sp", corev, dr["corev"])
    cp("dve", identb, ident)
    cp("dve", onesb, ones32)
    memset("pool", zero1, 0.0)
    memset("pool", eps1, RMS_EPS)
    seqmask = C("seqmask", 64)

    def load_w(src2d, kp, kt, ncols, q="pool"):
        slot = nxt("w", wsl)
        assert kt * ncols <= WSLOT
        v = V(slot.ap[0:kp, 0:kt * ncols].rearrange("p (k m) -> p k m", k=kt), slot.key)
        S.dma(q, v, V(src2d.ap.rearrange("(k p) m -> p k m", p=kp), "RO"))
        return v

    def layer_norm(c0, n, gcol, bcol):
        st, mean, rstd, tmp = ln_st, ln_mean, ln_rstd, ln_tmp
        for s0 in range(0, n, 512):
            sn = min(512, n - s0)
            a, b = c0 + s0, c0 + s0 + sn
            p1 = nxt("pa", psA)
            p2 = nxt("pa", psA)
            for dt_ in range(8):
                mm(p1[:, 0:sn], ones32, xres[:, dt_, a:b], start=(dt_ == 0), stop=(dt_ == 7))
            for dt_ in range(8):
                sq = st[:, dt_ % 2, 0:sn]
                act(sq, xres[:, dt_, a:b], AF.Square)
                mm(p2[:, 0:sn], ones32, sq, start=(dt_ == 0), stop=(dt_ == 7))
            act(mean[:, 0:sn], p1[:, 0:sn], AF.Copy, scale=1.0 / D)
            tt("pool", rstd[:, 0:sn], mean[:, 0:sn], mean[:, 0:sn], ALU.mult)
            stt(rstd[:, 0:sn], p2[:, 0:sn], 1.0 / D, rstd[:, 0:sn], ALU.mult, ALU.subtract)
            ts("dve", rstd[:, 0:sn], rstd[:, 0:sn], LN_EPS, None, ALU.add)
            act(rstd[:, 0:sn], rstd[:, 0:sn], AF.Sqrt)
            recip(rstd[:, 0:sn], rstd[:, 0:sn])
            for dt_ in range(8):
                t1 = tmp[:, dt_ % 2, 0:sn]
                tt("dve", t1, xres[:, dt_, a:b], mean[:, 0:sn], ALU.subtract)
                tt("dve", t1, t1, rstd[:, 0:sn], ALU.mult)
                act(xres[:, dt_, a:b], t1, AF.Identity, scale=par[:, gcol + dt_:gcol + dt_ + 1],
                    bias=par[:, bcol + dt_:bcol + dt_ + 1])
                cp("pool", xbf[:, dt_, a:b], xres[:, dt_, a:b])

    stg = [sb(f"stg{i}", [128, 128]) for i in range(2)]

    def load_cols(col, src1d, t):
        st_ = nxt("stg", stg)
        S.dma("sp", st_[0:t, :], V(src1d.ap.rearrange("(t p) -> t p", p=128), "RO"))
        pb = nxt("pb", psB)
        tr(pb[:, 0:t], st_[0:t, :], ident[0:t, 0:t])
        cp("dve", par[:, col:col + t], pb[:, 0:t])

    def load_vec8(col, src1d):
        load_cols(col, src1d, 8)

    if stage == -1:
        AR.reset()
        t0 = AR.alloc("t0", [D])
        S.dma("sp", t0[0:NS, :], dr["xs"])
        S.dma("sp", dr["y_s"], t0[0:NS, :])
        S.fence()
        S.emit(nc, es)
        es.close()
        return nc
    load_vec8(0, dr["ln_in_g"])
    load_vec8(8, dr["ln_in_b"])
    AR.reset()
    xin = [AR.alloc(f"xin{i}", [D]) for i in range(2)]
    tiles = [("xp", i * 128, 128, i * 128) for i in range(SEG // 128)] + [("xs", 0, NS, SEG)]
    for ti, (src, r0, nr, c0) in enumerate(tiles):
        xi = xin[ti % 2]
        S.dma("sp", xi[0:nr, :], dr[src][r0:r0 + nr, :])
        for dq in range(2):
            pb = nxt("pb", psB)
            for q in range(4):
                dt_ = dq * 4 + q
                tr(pb[:, q * 128:q * 128 + nr], xi[0:nr, dt_ * 128:(dt_ + 1) * 128], ident[0:nr, 0:nr])
            cp("act" if dq % 2 else "dve", xres[:, dq * 4:dq * 4 + 4, c0:c0 + nr],
               pb.re("p (q t) -> p q t", q=4)[:, :, 0:nr])
    S.fence()
    AR.reset()
    if stage >= 1:
        layer_norm(0, NTOK, 0, 8)
    S.fence()

    PC = {"ln1g": 0, "ln1b": 8, "ln2g": 16, "ln2b": 24, "fcw": 32, "fcb": 32 + 132}

    for l in range(nlayers if stage >= 2 else 0):
        load_vec8(PC["ln1g"], dr["ln1_g"][l])
        load_vec8(PC["ln1b"], dr["ln1_b"][l])
        load_vec8(PC["ln2g"], dr["ln2_g"][l])
        load_vec8(PC["ln2b"], dr["ln2_b"][l])
        for tap in range(3):
            load_cols(PC["fcw"] + tap * 44, dr["ffn_conv_w"][l][tap], 44)
        load_cols(PC["fcb"], dr["ffn_conv_b"][l], 44)
        S.fence()

        wi = dr["w_in"][l]
        if stage >= 5:
            AR.reset()
            wkv = load_w(wi[:, OFF["ak"]:OFF["ak"] + 256], 128, 8, 256)
            kvl = AR.alloc("kvl", [AG1W])
            kvs = AR.alloc("kvs", [256])
            pa = nxt("pa", psA)
            for kt in range(8):
                mm(pa[:, 0:256], xbf[:, kt, SEG - 128:SEG], wkv[:, kt, :], start=(kt == 0), stop=(kt == 7))
            cp("act", kvl[:, 0:256], pa[:, 0:256])
            pa = nxt("pa", psA)
            for kt in range(8):
                mm(pa[0:NS, 0:256], xbf[:, kt, SEG:SEG + NS], wkv[:, kt, :], start=(kt == 0), stop=(kt == 7))
            cp("act", kvs[0:NS, :], pa[0:NS, 0:256])
            S.dma("sp", dr["p_swa_k"][l], kvl[:, 0:128])
            S.dma("sp", dr["p_swa_v"][l], kvl[:, 128:256])
            for sq in range(4):
                S.dma("sp", dr["s_swa_k"][l][sq, 112:128, :], kvs[16 * sq:16 * sq + 16, 0:128])
                S.dma("sp", dr["s_swa_v"][l][sq, 112:128, :], kvs[16 * sq:16 * sq + 16, 128:256])
            S.dma("sp", dr["s_swa_k"][l][:, 0:112, :], dr["cache_k"][l][:, 16:128, :])
            S.dma("sp", dr["s_swa_v"][l][:, 0:112, :], dr["cache_v"][l][:, 16:128, :])
            gt3 = kvl[:, 256:AG1W].re("p (j t) -> p j t", t=3)
            for jg in range(3):
                wc_ = load_w(wi[:, OFF["cqkv"] + jg * 512:OFF["cqkv"] + (jg + 1) * 512], 128, 8, 512)
                pb = nxt("pb", psB)
                for jj in range(4):
                    for kt in range(8):
                        mm(pb[:, jj * 4:jj * 4 + 3], wc_[:, kt, jj * 128:(jj + 1) * 128], xbf[:, kt, SEG - 3:SEG],
                           start=(kt == 0), stop=(kt == 7))
                cp("dve", gt3[:, jg * 4:jg * 4 + 4, :], pb[:, 0:16].re("p (j t) -> p j t", t=4)[:, :, 0:3])
            gct = AR.alloc("gct", [C_QKV])
            for jq in range(3):
                pb = nxt("pb", psB)
                for jj in range(4):
                    tr(pb[0:3, jj * 128:(jj + 1) * 128], gt3[:, jq * 4 + jj, :], ident)
                cp("act", gct[0:3, jq * 512:(jq + 1) * 512], pb[0:3, :])
            S.dma("sp", dr["p_gconv"][l], gct[0:3, :])
            S.dma("sp", dr["ag1_in"], kvl)
            S.fence()
            a1i, a1o = dr["ag1_in"].ap, dr["ag1_out"].ap
            S.cc(lambda e: e.collective_compute("AllGather", ALU.bypass, replica_groups=GROUPS,
                                                       ins=[a1i], outs=[a1o]), ["ag1_in"], ["ag1_out"])
            g1 = AR.alloc("g1", [4, AG1W])
            S.dma("sp", g1, dr["ag1_out"].re("(r p) c -> p r c", p=128))
            hsel = AR.alloc("hsel", [AG1W])
            ts("dve", hsel, g1[:, 0, :], corev[:, 1:2], None, ALU.mult)
            for j in range(1, 4):
                stt(hsel, g1[:, j, :], corev[:, 1 + j:2 + j], hsel, ALU.mult, ALU.add)
            cp("pool", gc_prev, hsel[:, 256:AG1W].re("p (j t) -> p j t", t=3))
            cp("pool", vprev, hsel[:, 128:256])
            pb = nxt("pb", psB)
            for g in range(2):
                tr(pb[0:64, g * 128:(g + 1) * 128], hsel[:, g * 64:(g + 1) * 64], ident)
            cp("act", kprev, pb[0:64, 0:256].re("p (g t) -> p g t", g=2))
            es8 = AR.alloc("es8", [8])
            S.dma("sp", es8[0:64, :], V(dr["attn_sinks"].ap[l].partition_broadcast(64), "RO"))
            act(es8[0:64, :], es8[0:64, :], AF.Exp)
            cp("dve", esinkB.re("p (h t) -> p h t", h=8), es8[0:64, :].re("p (h o) -> p h o", o=1).bc([64, 8, 64]))
            S.fence()

        if stage >= 6:
            if l == 0:
                st_ = nxt("stg", stg)
                S.dma("sp", st_[0:16, :], V(dr["hgrn_lb_logits"].ap.rearrange("l (h p) -> (l h) p", p=128), "RO"))
                pb = nxt("pb", psB)
                tr(pb[:, 0:16], st_[0:16, :], ident[0:16, 0:16])
                AR.reset()
                e16 = AR.alloc("e16", [4, 4])
                act(e16.re("p l h -> p (l h)"), pb[:, 0:16], AF.Exp)
                ssum = AR.alloc("ssum", [4])
                tt("dve", ssum, e16[:, 0, :], e16[:, 1, :], ALU.add)
                tt("dve", ssum, ssum, e16[:, 2, :], ALU.add)
                tt("dve", ssum, ssum, e16[:, 3, :], ALU.add)
                recip(ssum, ssum)
                for ll in range(1, 4):
                    tt("dve", e16[:, ll, :], e16[:, ll, :], ssum, ALU.mult)
                memset("pool", lbF[:, 0, :], 0.0)
                cp("dve", lbF[:, 1, :], e16[:, 1, :])
                tt("dve", lbF[:, 2, :], lbF[:, 1, :], e16[:, 2, :], ALU.add)
                tt("dve", lbF[:, 3, :], lbF[:, 2, :], e16[:, 3, :], ALU.add)
                S.fence()
            AR.reset()
            lbT = AR.alloc("lbT", [512])
            omlT = AR.alloc("omlT", [512])
            omlF = AR.alloc("omlF", [4])
            nomlF = AR.alloc("nomlF", [4])
            hSs = AR.alloc("hSs", [4, 4, 128])
            keep1 = AR.off
            lg = AR.alloc("lg", [4, 512])
            S.dma("sp", lg, V(dr["hgrn_lb_logits"].ap.partition_broadcast(128), "RO"))
            act(lg, lg, AF.Exp)
            tt("dve", omlT, lg[:, 0, :], lg[:, 1, :], ALU.add)
            tt("dve", omlT, omlT, lg[:, 2, :], ALU.add)
            tt("dve", omlT, omlT, lg[:, 3, :], ALU.add)
            recip(omlT, omlT)
            memset("pool", lbT, 0.0)
            for ll in range(1, l + 1):
                tt("dve", lbT, lbT, lg[:, ll, :], ALU.add)
            tt("dve", lbT, lbT, omlT, ALU.mult)
            ts("dve", omlT, lbT, -1.0, 1.0, ALU.mult, ALU.add)
            ts("dve", omlF, lbF[:, l, :], -1.0, 1.0, ALU.mult, ALU.add)
            ts("dve", nomlF, omlF, -1.0, None, ALU.mult)
            S.dma("sp", nrmw[:, 0:1], V(dr["hgrn_norm_w"].ap[l].rearrange("(p o) -> p o", o=1), "RO"))
            S.dma("sp", nrmw[:, 1:2], V(dr["gdn_norm_w"].ap[l].rearrange("(p o) -> p o", o=1), "RO"))
            memset("pool", hS, 0.0)
            memset("pool", hpre, 1.0)
            for sq in range(4):
                S.dma("sp", hSs[:, sq, :, :], V(dr["st_hgrn"].ap[l][sq].rearrange("h k v -> k h v"), "RO"))
            S.fence()
            for b in range(NBLK):
                AR.off = keep1
                AR.gen += 1
                c0 = b * TB
                last = (b == NBLK - 1)
                n = TB + (NS if last else 0)
                segs = [(c0, 0, TB)] + ([(SEG, TB, NS)] if last else [])
                wbq = load_w(wi[:, OFF["bq"]:OFF["bq"] + 512], 128, 8, 512)
                wbf = load_w(wi[:, OFF["bf"]:OFF["bf"] + 512], 128, 8, 512)
                wbi = load_w(wi[:, OFF["bi"]:OFF["bi"] + 512], 128, 8, 512)
                qT = AR.alloc("qT", [4, n], BF16)
                kT = AR.alloc("kT", [4, n], BF16)
                obuf = AR.alloc("obuf", [4, n], BF16)
                qhat = AR.alloc("qhat", [4, TB], BF16)
                sgF = [AR.alloc(f"sgF{i}", [512]) for i in range(2)]
                for h in range(4):
                    for (a, ha, sn) in segs:
                        pa = nxt("pa", psA)
                        for kt in range(8):
                            mm(pa[:, 0:sn], wbq[:, kt, h * 128:(h + 1) * 128], xbf[:, kt, a:a + sn], start=(kt == 0), stop=(kt == 7))
                        act(qT[:, h, ha:ha + sn], pa[:, 0:sn], AF.Silu)
                        pa = nxt("pa", psA)
                        for kt in range(8):
                            mm(pa[:, 0:sn], wbf[:, kt, h * 128:(h + 1) * 128], xbf[:, kt, a:a + sn], start=(kt == 0), stop=(kt == 7))
                        sg = sgF[h % 2]
                        act(sg[:, 0:sn], pa[:, 0:sn], AF.Sigmoid)
                        ts("dve", kT[:, h, ha:ha + sn], sg[:, 0:sn], nomlF[:, h:h + 1], omlF[:, h:h + 1], ALU.mult, ALU.add)
                sgT = AR.alloc("sgT", [512])
                tT = AR.alloc("tT", [512])
                lfT = AR.alloc("lfT", [512])
                kTM = AR.alloc("kTM", [512])
                vTM = AR.alloc("vTM", [512], BF16)
                kend = AR.alloc("kend", [512], BF16)
                kendm = AR.alloc("kendm", [4, 512], BF16) if last else None
                eq = AR.alloc("eq", [128])
                ek = AR.alloc("ek", [128])
                dm = AR.alloc("dm", [8])
                qt = AR.alloc("qt", [128], BF16)
                kt_ = AR.alloc("kt_", [128], BF16)
                attm = AR.alloc("attm", [128], BF16)
                sbf = AR.alloc("sbf", [128], BF16)
                fac1 = AR.alloc("fac1", [1])
                tl = [(c0 + t * 128, t * 128, 128, "p") for t in range(4)] + ([(SEG, TB, NS, "s")] if last else [])
                for (a, ha, T, tag) in tl:
                    nb = 2 if tag == "p" else 4
                    blk = T // nb
                    hdext = C(f"hdext_{tag}", T)
                    hle = C(f"hle_{tag}", T)
                    hgt = C(f"hgt_{tag}", T)
                    pf = nxt("pa", psA)
                    pi = nxt("pa", psA)
                    for kt in range(8):
                        mm(pf[0:T, :], xbf[:, kt, a:a + T], wbf[:, kt, :], start=(kt == 0), stop=(kt == 7))
                    for kt in range(8):
                        mm(pi[0:T, :], xbf[:, kt, a:a + T], wbi[:, kt, :], start=(kt == 0), stop=(kt == 7))
                    act(sgT[0:T, :], pf[0:T, :], AF.Sigmoid)
                    tt("dve", tT[0:T, :], sgT[0:T, :], omlT[0:T, :], ALU.mult)
                    stt(sgT[0:T, :], tT[0:T, :], 1e-30, lbT[0:T, :], ALU.max, ALU.add)
                    act(lfT[0:T, :], sgT[0:T, :], AF.Ln)
                    tt("pool", kTM[0:T, :], omlT[0:T, :], tT[0:T, :], ALU.subtract)
                    cp("act", vTM[0:T, :], pi[0:T, :])
                    pr = nxt("pa", psA)
                    mm(pr[0:T, :], hgt, lfT[0:T, :])
                    act(tT[0:T, :], pr[0:T, :], AF.Exp)
                    tt("dve", kend[0:T, :], kTM[0:T, :], tT[0:T, :], ALU.mult)
                    if tag == "s":
                        bls = C("blk_s", T)
                        for bb in range(4):
                            ts("pool", kendm[0:T, bb, :], kend[0:T, :], bls[:, bb:bb + 1], None, ALU.mult)
                    for h in range(4):
                        hs = slice(h * 128, (h + 1) * 128)
                        pc = nxt("pb", psB)
                        mm(pc[:, 0:T + 2 * nb], lfT[0:T, hs], hdext)
                        act(eq[:, 0:T], pc[:, 0:T], AF.Exp)
                        tt("dve", qt[:, 0:T], qT[:, h, ha:ha + T], eq[:, 0:T], ALU.mult)
                        act(ek[:, 0:T], pc[:, 0:T], AF.Exp, scale=-1.0)
                        tt("pool", kt_[:, 0:T], kT[:, h, ha:ha + T], ek[:, 0:T], ALU.mult)
                        act(dm[:, 0:2 * nb], pc[:, T:T + 2 * nb], AF.Exp)
                        pat = nxt("pb", psB)
                        mm(pat[0:T, 0:T], kt_[:, 0:T], qt[:, 0:T])
                        stt(attm[0:T, 0:T], pat[0:T, 0:T], 1e30, hle, ALU.min, ALU.mult)
                        po = nxt("pb", psB)
                        mm(po[:, 0:T], vTM[0:T, hs], attm[0:T, 0:T], start=True, stop=False)
                        for bb in range(nb):
                            Sb = hS[:, h, :] if tag == "p" else hSs[:, bb, h, :]
                            bs = slice(bb * blk, (bb + 1) * blk)
                            act(sbf, Sb, AF.Copy, scale=dm[:, nb + bb:nb + bb + 1])
                            mm(po[:, bs], sbf, qt[:, bs], start=False, stop=(bb == nb - 1))
                            pu = nxt("pa", psA)
                            if tag == "p":
                                ts("dve", fac1, dm[:, nb + bb:nb + bb + 1], hpre[:, h:h + 1], None, ALU.mult)
                                ts("dve", qhat[:, h, ha + bb * blk:ha + (bb + 1) * blk], qt[:, bs], fac1, None, ALU.mult)
                                mm(pu[:, 0:128], kend[bs, hs], vTM[bs, hs])
                            else:
                                mm(pu[:, 0:128], kendm[0:T, bb, hs], vTM[0:T, hs])
                            stt(Sb, Sb, dm[:, bb:bb + 1], pu[:, 0:128], ALU.mult, ALU.add)
                            if tag == "p":
                                tt("dve", hpre[:, h:h + 1], hpre[:, h:h + 1], dm[:, bb:bb + 1], ALU.mult)
                        cp("act", obuf[:, h, ha:ha + T], po[:, 0:T])
                S.dma("sp", dr["ob_loc"][:, :, c0:c0 + TB], obuf[:, :, 0:TB])
                S.dma("sp", dr["qh"][:, :, c0:c0 + TB], qhat)
                if last:
                    S.dma("sp", dr["ob_loc"][:, :, SEG:SEG + NS], obuf[:, :, TB:TB + NS])
                S.fence()
            for sq in range(4):
                S.dma("sp", V(dr["s_hgrn"].ap[l][sq].rearrange("h k v -> k h v"), "out_s_hgrn"), hSs[:, sq, :, :])
            if stage >= 7:
                S.fence()
                AR.off = 0
                AR.gen += 1
                for tap in range(4):
                    st_ = nxt("stg", stg)
                    S.dma("sp", st_[0:12, :], V(dr["gdn_conv_w"].ap[l][tap].rearrange("(t p) -> t p", p=128), "RO"))
                    pb = nxt("pb", psB)
                    tr(pb[:, 0:12], st_[0:12, :], ident[0:12, 0:12])
                    cp("dve", gpar[:, tap * 12:(tap + 1) * 12], pb[:, 0:12])
                gtmp = AR.alloc("gtmp", [2, 4 * DEPTH])
                S.dma("sp", gtmp[:, 0, :], V(dr["gdn_dt_bias"].ap.rearrange("l h -> (l h)").partition_broadcast(128), "RO"))
                S.dma("sp", gtmp[:, 1, :], V(dr["gdn_a_log"].ap.rearrange("l h -> (l h)").partition_broadcast(128), "RO"))
                cp("dve", gpar[:, 48:52], gtmp[:, 0, 4 * l:4 * l + 4])
                act(gpar[:, 52:56], gtmp[:, 1, 4 * l:4 * l + 4], AF.Exp)
                ts("dve", gpar[:, 52:56], gpar[:, 52:56], -1.0, None, ALU.mult)
                gS = AR.alloc("gS", [4, 256])
                gSb = AR.alloc("gSb", [4, 256], BF16)
                memset("pool", gS, 0.0)
                for h in range(4):
                    cp("dve", gS[:, h, 128:256], ident)
                cp("act", gSb, gS)
                keepg = AR.off
                triLE = C("triLE_p")
                triGT = C("triGT_p")
                mbias = C("mbias_p")
                offd = C("offd_p")
                S.fence()
                import os
                for u_ in range(NBLK + (0 if os.environ.get('GDN_NOSAMPLE') else 1)):
                    AR.off = keepg
                    AR.gen += 1
                    smp = (u_ == NBLK)
                    n = NS if smp else TB
                    a0 = SEG if smp else u_ * TB
                    wcq = [load_w(wi[:, OFF["cqkv"] + jg * 512:OFF["cqkv"] + (jg + 1) * 512], 128, 8, 512) for jg in range(1)]
                    qn = AR.alloc("qn", [4, n], BF16)
                    kn = AR.alloc("kn", [4, n], BF16)
                    ntile = 4
                    TT = 16 if smp else 128
                    kTMa = AR.alloc("kTMa", [ntile, 4, 128], BF16)
                    vTMa = AR.alloc("vTMa", [ntile, 4, 128], BF16)
                    ocb = AR.alloc("ocb", [4, n], BF16)
                    rhb = AR.alloc("rhb", [4, n], BF16)
                    cpre = [AR.alloc(f"cpre{i}", [n + 12]) for i in range(2)]
                    u32 = [AR.alloc(f"u32{i}", [n]) for i in range(2)]
                    sqg = AR.alloc("sqg", [n])
                    rsg = AR.alloc("rsg", [n])
                    if smp:
                        gSs = AR.alloc("gSs", [4, 4, 128])
                        gSsb = AR.alloc("gSsb", [4, 4, 128], BF16)
                        for sq in range(4):
                            S.dma("sp", gSs[:, sq, :, :], V(dr["st_gdn"].ap[l][sq].rearrange("h k v -> k h v"), "RO"))
                        cp("act", gSsb, gSs)
                        sg12 = AR.alloc("sg12", [C_QKV])
                        so12 = AR.alloc("so12", [C_QKV])
                        S.dma("sp", sg12[0:12, :], V(dr["st_gconv"].ap[l].rearrange("s r c -> (s r) c"), "RO"))
                    for j in range(12):
                        if j % 4 == 0 and j > 0:
                            wcq = [load_w(wi[:, OFF["cqkv"] + (j // 4) * 512:OFF["cqkv"] + (j // 4 + 1) * 512], 128, 8, 512)]
                        pa = nxt("pa", psA)
                        for kt in range(8):
                            mm(pa[:, 0:n], wcq[0][:, kt, (j % 4) * 128:(j % 4 + 1) * 128], xbf[:, kt, a0:a0 + n], start=(kt == 0), stop=(kt == 7))
                        cpj = cpre[j % 2]
                        uj = u32[j % 2]
                        gw = lambda t, j=j: gpar[:, t * 12 + j:t * 12 + j + 1]
                        if not smp:
                            cp("pool", cpj[:, 0:3], gc_prev[:, j, :])
                            cp("act", cpj[:, 3:3 + n], pa[:, 0:n])
                            cp("pool", gc_prev[:, j, :], cpj[:, n:n + 3])
                            ts("pool", uj, cpj[:, 0:n], gw(0), None, ALU.mult)
                            for t in range(1, 4):
                                stt(uj, cpj[:, t:t + n], gw(t), uj, ALU.mult, ALU.add)
                        else:
                            c3 = cpj[:, 0:76].re("p (s t) -> p s t", s=4)
                            pb = nxt("pb", psB)
                            tr(pb[:, 0:12], sg12[0:12, j * 128:(j + 1) * 128], ident[0:12, 0:12])
                            cp("dve", c3[:, :, 0:3], pb[:, 0:12].re("p (s r) -> p s r", s=4))
                            cp("act", c3[:, :, 3:19], pa[:, 0:n].re("p (s t) -> p s t", s=4))
                            u3 = uj.re("p (s t) -> p s t", s=4)
                            ts("pool", u3, c3[:, :, 0:16], gw(0), None, ALU.mult)
                            for t in range(1, 4):
                                stt(u3, c3[:, :, t:t + 16], gw(t), u3, ALU.mult, ALU.add)
                            uc12 = sqg[:, 0:12]
                            cp("pool", uc12.re("p (s r) -> p s r", s=4), c3[:, :, 16:19])
                            pb = nxt("pb", psB)
                            tr(pb[0:12, 0:128], uc12, ident)
                            cp("dve", so12[0:12, j * 128:(j + 1) * 128], pb[0:12, 0:128])
                        act(uj, uj, AF.Silu)
                        h = j % 4
                        if j < 8:
                            act(sqg, uj, AF.Square)
                            ps_ = nxt("pb", psB)
                            mm(ps_[:, 0:n], ones32, sqg)
                            act(rsg, ps_[:, 0:n], AF.Sqrt, bias=eps1)
                            recip(rsg, rsg)
                            if j < 4:
                                stt(qn[:, h, :], uj, 128.0 ** -0.5, rsg, ALU.mult, ALU.mult)
                            else:
                                tt("dve", uj, uj, rsg, ALU.mult)
                                cp("pool", kn[:, h, :], uj)
                        if j >= 4:
                            dstT = kTMa if j < 8 else vTMa
                            for t in range(ntile):
                                pb = nxt("pb", psB)
                                tr(pb[0:TT, 0:128], uj[:, t * TT:(t + 1) * TT], ident)
                                cp("act" if t % 2 else "dve", dstT[0:TT, t, h, :], pb[0:TT, 0:128])
                    if smp:
                        S.dma("sp", V(dr["s_gconv"].ap[l].rearrange("s r c -> (s r) c"), "out_s_gconv"), so12[0:12, :])
                    wcb = load_w(wi[:, OFF["cb"] - 248:OFF["cb"] + 8], 128, 8, 256)[:, :, 248:256]
                    bg8 = AR.alloc("bg8", [8])
                    beta = AR.alloc("beta", [4])
                    gg = AR.alloc("gg", [4])
                    sm8 = AR.alloc("sm8", [16])
                    erow = AR.alloc("erow", [128])
                    decT = AR.alloc("decT", [128])
                    decS = AR.alloc("decS", [128])
                    U32 = AR.alloc("U32", [128])
                    QKd = AR.alloc("QKd", [128], BF16)
                    keT = AR.alloc("keT", [128], BF16)
                    qeT = AR.alloc("qeT", [128], BF16)
                    ktil = AR.alloc("ktil", [128], BF16)
                    vaug = AR.alloc("vaug", [256], BF16)
                    memset("pool", vaug, 0.0)
                    Aa = [AR.alloc(f"Aa{i}", [128]) for i in range(2)]
                    At = [AR.alloc(f"At{i}", [128]) for i in range(2)]
                    Wm = [AR.alloc(f"Wm{i}", [128]) for i in range(2)]
                    Wmb = AR.alloc("Wmb", [128], BF16)
                    P1s = AR.alloc("P1s", [256], BF16)
                    wbf_ = AR.alloc("wbf_", [256], BF16)
                    GC = int(os.environ.get('GDN_CUT', '9'))
                    for t in range(ntile if GC >= 2 else 0):
                        T = TT
                        a = a0 + t * T
                        cs = slice(t * T, (t + 1) * T)
                        NV = 128 if smp else 256
                        pg = nxt("pb", psB)
                        for kt in range(8):
                            mm(pg[0:T, 0:8], xbf[:, kt, a:a + T], wcb[:, kt, :], start=(kt == 0), stop=(kt == 7))
                        cp("dve", bg8[0:T, :], pg[0:T, 0:8])
                        act(beta[0:T, :], bg8[0:T, 0:4], AF.Sigmoid)
                        tt("dve", gg[0:T, :], bg8[0:T, 4:8], gpar[0:T, 48:52], ALU.add)
                        act(gg[0:T, :], gg[0:T, :], AF.Exp)
                        act(gg[0:T, :], gg[0:T, :], AF.Ln, bias=1.0)
                        tt("dve", gg[0:T, :], gg[0:T, :], gpar[0:T, 52:56], ALU.mult)
                        pcm = nxt("pb", psB)
                        mm(pcm[0:T, 0:4], triLE[0:T, 0:T], gg[0:T, :])
                        mm(pcm[0:T, 4:8], triGT[0:T, 0:T], gg[0:T, :])
                        mm(pcm[:, 8:12], ones32[0:T, :], gg[0:T, :])
                        act(sm8[0:T, 0:4], pcm[0:T, 0:4], AF.Exp)
                        act(sm8[0:T, 4:8], pcm[0:T, 0:4], AF.Copy, scale=-1.0)
                        act(sm8[0:T, 8:12], pcm[0:T, 4:8], AF.Exp)
                        act(sm8[:, 12:16], pcm[:, 8:12], AF.Exp)
                        for h in range(4):
                            St = gSs[:, t, h, :] if smp else gS[:, h, :]
                            Sb = gSsb[:, t, h, :] if smp else gSb[:, h, :]
                            pcr = nxt("pb", psB)
                            gbc = gg[0:T, h:h + 1].bc([T, 128])
                            mm(pcr[:, 0:T], gbc, triLE[0:T, 0:T])
                            mm(pcr[0:T, 128:128 + T], gg[0:T, h:h + 1].bc([T, T]), triLE[0:T, 0:T], start=True, stop=False)
                            mm(pcr[0:T, 128:128 + T], ident[0:T, 0:T], mbias[0:T, 0:T], start=False, stop=True)
                            act(erow[:, 0:T], pcr[:, 0:T], AF.Exp)
                            act(decT[0:T, 0:T], pcr[0:T, 128:128 + T], AF.Exp, bias=sm8[0:T, 4 + h:5 + h])
                            tt("pool", decS[0:T, 0:T], decT[0:T, 0:T], offd[0:T, 0:T], ALU.mult)
                            pk = nxt("pb", psB)
                            mm(pk[0:T, 0:T], kn[:, h, cs], kn[:, h, cs])
                            mm(pk[0:T, 128:128 + T], kn[:, h, cs], qn[:, h, cs])
                            stt(U32[0:T, 0:T], pk[0:T, 0:T], beta[0:T, h:h + 1], decS[0:T, 0:T], ALU.mult, ALU.mult)
                            tt("dve", QKd[0:T, 0:T], pk[0:T, 128:128 + T], decT[0:T, 0:T], ALU.mult)
                            tt("pool", keT[:, 0:T], kn[:, h, cs], erow[:, 0:T], ALU.mult)
                            tt("pool", qeT[:, 0:T], qn[:, h, cs], erow[:, 0:T], ALU.mult)
                            ts("pool", ktil[0:T, :], kTMa[0:T, t, h, :], sm8[0:T, 8 + h:9 + h], None, ALU.mult)
                            cp("pool", vaug[0:T, 0:128], vTMa[0:T, t, h, :])
                            if GC < 3:
                                continue
                            pl_ = nxt("pb", psB)
                            tr(pl_[0:T, 0:T], U32[0:T, 0:T], ident[0:T, 0:T])
                            A, ATr = Aa[0], At[0]
                            cp("dve", A[0:T, 0:T], U32[0:T, 0:T])
                            cp("act", ATr[0:T, 0:T], pl_[0:T, 0:T])
                            W = Wm[0]
                            tt("dve", W[0:T, 0:T], ident[0:T, 0:T], U32[0:T, 0:T], ALU.subtract)
                            nsq = 3 if smp else 6
                            nsq = min(nsq, int(os.environ.get('GDN_NSQ', '9')))
                            for js in range(nsq):
                                A2, AT2, W2 = Aa[(js + 1) % 2], At[(js + 1) % 2], Wm[(js + 1) % 2]
                                p1_ = nxt("pb", psB)
                                p1b_ = nxt("pb", psB)
                                mm(p1_[0:T, 0:T], ATr[0:T, 0:T], A[0:T, 0:T])
                                mm(p1b_[0:T, 0:T], A[0:T, 0:T], ATr[0:T, 0:T])
                                cp("dve", A2[0:T, 0:T], p1_[0:T, 0:T])
                                cp("act", AT2[0:T, 0:T], p1b_[0:T, 0:T])
                                if os.environ.get('GDN_NOW'):
                                    A, ATr = A2, AT2
                                    continue
                                p2_ = nxt("pb", psB)
                                mm(p2_[0:T, 0:T], AT2[0:T, 0:T], W[0:T, 0:T])
                                tt("dve", W2[0:T, 0:T], W[0:T, 0:T], p2_[0:T, 0:T], ALU.add)
                                A, ATr, W = A2, AT2, W2
                            cp("act", Wmb[0:T, 0:T], W[0:T, 0:T])
                            if GC < 4:
                                continue
                            pp = nxt("pa", psA)
                            mm(pp[0:T, 0:NV], keT[:, 0:T], Sb[:, 0:NV] if not smp else Sb)
                            tt("dve", P1s[0:T, 0:NV], vaug[0:T, 0:NV], pp[0:T, 0:NV], ALU.subtract)
                            pw = nxt("pa", psA)
                            mm(pw[0:T, 0:NV], Wmb[0:T, 0:T], P1s[0:T, 0:NV])
                            act(wbf_[0:T, 0:NV], pw[0:T, 0:NV], AF.Copy, scale=beta[0:T, h:h + 1])
                            pq = nxt("pa", psA)
                            mm(pq[:, 0:T], wbf_[0:T, 0:128], QKd[0:T, 0:T], start=True, stop=False)
                            mm(pq[:, 0:T], Sb[:, 0:128] if not smp else Sb, qeT[:, 0:T], start=False, stop=True)
                            cp("act", ocb[:, h, cs], pq[:, 0:T])
                            if not smp:
                                mm(pq[:, 128:128 + T], wbf_[0:T, 128:256], QKd[0:T, 0:T], start=True, stop=False)
                                mm(pq[:, 128:128 + T], Sb[:, 128:256], qeT[:, 0:T], start=False, stop=True)
                                cp("dve", rhb[:, h, cs], pq[:, 128:128 + T])
                            pu = nxt("pa", psA)
                            mm(pu[:, 0:NV], ktil[0:T, :], wbf_[0:T, 0:NV])
                            if smp:
                                stt(St, St, sm8[:, 12 + h:13 + h], pu[:, 0:NV], ALU.mult, ALU.add)
                            else:
                                stt(St, St, sm8[:, 12 + h:13 + h], pu[:, 0:NV], ALU.mult, ALU.add)
                            cp("act", Sb, St)
                    if smp:
                        S.dma("sp", dr["oc_loc"][:, :, SEG:SEG + NS], ocb)
                        for sq in range(4):
                            S.dma("sp", V(dr["s_gdn"].ap[l][sq].rearrange("h k v -> k h v"), "out_s_gdn"), gSs[:, sq, :, :])
                    else:
                        S.dma("sp", dr["oc_loc"][:, :, a0:a0 + TB], ocb)
                        S.dma("sp", dr["rh"][:, :, a0:a0 + TB], rhb)
                    S.fence()
                S.dma("sp", dr["ag2_in"][:, 516:516 + 1024].re("p (h c) -> p h c", h=4), gS)
                AR.off = keepg
                AR.gen += 1
            AR.off = keep1
            AR.gen += 1
            S.dma("sp", dr["ag2_in"][:, 0:512], hS.re("p h v -> p (h v)"))
            S.dma("sp", dr["ag2_in"][:, 512:516], hpre)
            S.fence()
            a2i, a2o = dr["ag2_in"].ap, dr["ag2_out"].ap
            S.cc(lambda e: e.collective_compute("AllGather", ALU.bypass, replica_groups=GROUPS,
                                                       ins=[a2i], outs=[a2o]), ["ag2_in"], ["ag2_out"])
            g2 = AR.alloc("g2", [4, AG2W])
            S.dma("sp", g2, dr["ag2_out"].re("(r p) c -> p r c", p=128))
            X = AR.alloc("X", [4, 128])
            Sin = AR.alloc("Sin", [4, 128])
            memset("pool", X, 0.0)
            memset("pool", Sin, 0.0)
            for j in range(3):
                for h in range(4):
                    stt(X[:, h, :], X[:, h, :], g2[:, j, 512 + h:513 + h], g2[:, j, h * 128:(h + 1) * 128], ALU.mult, ALU.add)
                stt(Sin.re("p h v -> p (h v)"), X.re("p h v -> p (h v)"), corev[:, 6 + j:7 + j], Sin.re("p h v -> p (h v)"), ALU.mult, ALU.add)
            cp("act", sinB, Sin)
            for h in range(4):
                stt(X[:, h, :], Sin[:, h, :], hpre[:, h:h + 1], hS[:, h, :], ALU.mult, ALU.add)
            S.dma("sp", V(dr["p_hgrn"].ap[l].rearrange("h k v -> k h v"), "out_p_hgrn"), X)
            if stage >= 7:
                Xc = AR.alloc("Xc", [4, 128])
                SinC32 = AR.alloc("SinC32", [4, 128])
                Xcb = AR.alloc("Xcb", [4, 128], BF16)
                PmT = AR.alloc("PmT", [128], BF16)
                memset("pool", Xc, 0.0)
                memset("pool", SinC32, 0.0)
                for j in range(3):
                    for h in range(4):
                        base = 516 + h * 256
                        pt = nxt("pb", psB)
                        tr(pt[:, 0:128], g2[:, j, base + 128:base + 256], ident)
                        cp("act", PmT, pt[:, 0:128])
                        cp("dve", Xcb[:, h, :], Xc[:, h, :])
                        px = nxt("pb", psB)
                        mm(px[:, 0:128], PmT, Xcb[:, h, :])
                        tt("dve", Xc[:, h, :], px[:, 0:128], g2[:, j, base:base + 128], ALU.add)
                    stt(SinC32.re("p h v -> p (h v)"), Xc.re("p h v -> p (h v)"), corev[:, 6 + j:7 + j], SinC32.re("p h v -> p (h v)"), ALU.mult, ALU.add)
                cp("act", sinC, SinC32)
                for h in range(4):
                    pt = nxt("pb", psB)
                    tr(pt[:, 0:128], gS[:, h, 128:256], ident)
                    cp("act", PmT, pt[:, 0:128])
                    px = nxt("pb", psB)
                    mm(px[:, 0:128], PmT, sinC[:, h, :])
                    tt("dve", Xc[:, h, :], px[:, 0:128], gS[:, h, 0:128], ALU.add)
                S.dma("sp", V(dr["p_gdn"].ap[l].rearrange("h k v -> k h v"), "out_p_gdn"), Xc)
            S.fence()

        for b in range(NBLK):
            AR.reset()
            c0 = b * TB
            last = (b == NBLK - 1)
            n = TB + (NS if last else 0)
            segs = [(c0, 0, TB)] + ([(SEG, TB, NS)] if last else [])
            o_a = AR.alloc("o_a", [8, n], BF16)
            o_bn = AR.alloc("o_bn", [4, n], BF16)
            o_cn = AR.alloc("o_cn", [4, n], BF16)
            mark = AR.off
            for (tagB, onrm, loc, qsrc, sinX, zoff, nwc, stg_min) in (("B", o_bn, "ob_loc", "qh", sinB, OFF["bg"], 0, 6), ("C", o_cn, "oc_loc", "rh", sinC, OFF["cz"], 1, 7)):
                if stage < stg_min:
                    continue
                obl = AR.alloc("obl", [4, n], BF16)
                qhl = AR.alloc("qhl", [4, TB], BF16)
                o32 = AR.alloc("o32", [512])
                sq32 = AR.alloc("sq32", [512])
                rs_ = AR.alloc("rs_", [512])
                sz = AR.alloc("sz", [512])
                S.dma("sp", obl[:, :, 0:TB], dr[loc][:, :, c0:c0 + TB])
                if last:
                    S.dma("sp", obl[:, :, TB:TB + NS], dr[loc][:, :, SEG:SEG + NS])
                S.dma("sp", qhl, dr[qsrc][:, :, c0:c0 + TB])
                wz = load_w(wi[:, zoff:zoff + 512], 128, 8, 512)
                for h in range(4):
                    for (a, ha, sn) in segs:
                        if ha == 0:
                            pd = nxt("pb", psB)
                            mm(pd[:, 0:TB], sinX[:, h, :], qhl[:, h, :])
                            tt("dve", o32[:, 0:sn], pd[:, 0:TB], obl[:, h, 0:TB], ALU.add)
                        else:
                            cp("dve", o32[:, 0:sn], obl[:, h, TB:TB + NS])
                        act(sq32[:, 0:sn], o32[:, 0:sn], AF.Square)
                        ps_ = nxt("pb", psB)
                        mm(ps_[:, 0:sn], ones32, sq32[:, 0:sn])
                        act(rs_[:, 0:sn], ps_[:, 0:sn], AF.Sqrt, scale=1.0 / 128, bias=eps1)
                        recip(rs_[:, 0:sn], rs_[:, 0:sn])
                        pz = nxt("pa", psA)
                        for kt in range(8):
                            mm(pz[:, 0:sn], wz[:, kt, h * 128:(h + 1) * 128], xbf[:, kt, a:a + sn], start=(kt == 0), stop=(kt == 7))
                        act(sz[:, 0:sn], pz[:, 0:sn], AF.Silu)
                        tt("dve", o32[:, 0:sn], o32[:, 0:sn], rs_[:, 0:sn], ALU.mult)
                        stt(onrm[:, h, ha:ha + sn], o32[:, 0:sn], nrmw[:, nwc:nwc + 1], sz[:, 0:sn], ALU.mult, ALU.mult)
                S.fence()
                AR.off = mark
                AR.gen += 1
            if stage >= 5:
                qF = AR.alloc("qF", [8, n], BF16)
                kF = AR.alloc("kF", [2, 128 + TB], BF16)
                vT = AR.alloc("vT", [5, 128], BF16)
                wq = load_w(wi[:, OFF["aq"]:OFF["aq"] + 512], 128, 8, 512)
                wkv = load_w(wi[:, OFF["ak"]:OFF["ak"] + 256], 128, 8, 256)
                cp("pool", kF[0:64, :, 0:128], kprev)
                cp("pool", vT[:, 0, :], vprev)
                if last:
                    kFs = AR.alloc("kFs", [2, NS], BF16)
                    vTs = AR.alloc("vTs", [128], BF16)
                for h in range(8):
                    for (a, ha, sn) in segs:
                        pa = nxt("pa", psA)
                        for kt in range(8):
                            mm(pa[0:64, 0:sn], wq[:, kt, h * 64:(h + 1) * 64], xbf[:, kt, a:a + sn], start=(kt == 0), stop=(kt == 7))
                        act(qF[0:64, h, ha:ha + sn], pa[0:64, 0:sn], AF.Copy, scale=0.125)
                for g in range(2):
                    for (a, ha, sn) in segs:
                        pa = nxt("pa", psA)
                        for kt in range(8):
                            mm(pa[0:64, 0:sn], wkv[:, kt, g * 64:(g + 1) * 64], xbf[:, kt, a:a + sn], start=(kt == 0), stop=(kt == 7))
                        if ha == 0:
                            cp("act", kF[0:64, g, 128:128 + TB], pa[0:64, 0:TB])
                        else:
                            cp("act", kFs[0:64, g, :], pa[0:64, 0:NS])
                for t in range(4):
                    pa = nxt("pa", psA)
                    for kt in range(8):
                        mm(pa[:, 0:128], xbf[:, kt, c0 + t * 128:c0 + (t + 1) * 128], wkv[:, kt, 128:256], start=(kt == 0), stop=(kt == 7))
                    cp("dve", vT[:, 1 + t, :], pa[:, 0:128])
                if last:
                    pa = nxt("pa", psA)
                    for kt in range(8):
                        mm(pa[0:NS, 0:128], xbf[:, kt, SEG:SEG + NS], wkv[:, kt, 128:256], start=(kt == 0), stop=(kt == 7))
                    cp("dve", vTs[0:NS, :], pa[0:NS, 0:128])
                E = [AR.alloc(f"E{i}", [512], BF16) for i in range(2)]
                dsm = [AR.alloc(f"dsm{i}", [256]) for i in range(2)]
                it = 0
                for cq in range(8):
                    i = cq // 2
                    for g in range(2):
                        rX = slice(0, 128) if cq % 2 == 0 else slice(64, 128)
                        rY = slice(0, 64) if cq % 2 == 0 else slice(0, 128)
                        ps_ = nxt("pa", psA)
                        q3 = qF[0:64, 4 * g:4 * g + 4, cq * 64:(cq + 1) * 64]
                        mm(ps_[:, 0:256].re("p (h t) -> p h t", h=4), kF[0:64, g, i * 128:(i + 1) * 128], q3)
                        mm(ps_[:, 256:512].re("p (h t) -> p h t", h=4), kF[0:64, g, (i + 1) * 128:(i + 2) * 128], q3)
                        e_ = E[it % 2]
                        bX = corev[:, 9:10] if (b == 0 and i == 0) else zero1
                        act(e_[:, 0:256], ps_[:, 0:256], AF.Exp, bias=bX)
                        act(e_[:, 256:512], ps_[:, 256:512], AF.Exp, bias=zero1)
                        po = nxt("pb", psB)
                        mm(po[0:64, 0:256], vT[rX, i, g * 64:(g + 1) * 64], e_[rX, 0:256], start=True, stop=False)
                        mm(po[0:64, 0:256], vT[rY, i + 1, g * 64:(g + 1) * 64], e_[rY, 256:512], start=False, stop=True)
                        mm(po[0:64, 256:512], onesb[rX, 0:64], e_[rX, 0:256], start=True, stop=False)
                        mm(po[0:64, 256:512], onesb[rY, 0:64], e_[rY, 256:512], start=False, stop=True)
                        d_ = dsm[it % 2]
                        tt("dve", d_[0:64, :], po[0:64, 256:512], esinkB[:, g * 256:(g + 1) * 256], ALU.add)
                        recip(d_[0:64, :], d_[0:64, :])
                        tt("dve", o_a[0:64, 4 * g:4 * g + 4, cq * 64:(cq + 1) * 64],
                           po[0:64, 0:256].re("p (h t) -> p h t", h=4), d_[0:64, :].re("p (h t) -> p h t", h=4), ALU.mult)
                        it += 1
                if last:
                    kct = [AR.alloc(f"kct{i}", [128]) for i in range(2)]
                    vct = [AR.alloc(f"vct{i}", [128]) for i in range(2)]
                    kc = [AR.alloc(f"kc{i}", [2, 128], BF16) for i in range(2)]
                    vc = [AR.alloc(f"vc{i}", [128], BF16) for i in range(2)]
                    for sq in range(4):
                        S.dma("sp", kct[sq % 2], dr["cache_k"][l][sq])
                        S.dma("sp", vct[sq % 2], dr["cache_v"][l][sq])
                        pb = nxt("pb", psB)
                        for g in range(2):
                            tr(pb[0:64, g * 128:(g + 1) * 128], kct[sq % 2][:, g * 64:(g + 1) * 64], ident)
                        cp("act", kc[sq % 2][0:64, :, :], pb[0:64, 0:256].re("p (g t) -> p g t", g=2))
                        cp("pool", vc[sq % 2], vct[sq % 2])
                        for g in range(2):
                            ps_ = nxt("pa", psA)
                            q3 = qF[0:64, 4 * g:4 * g + 4, TB + sq * 16:TB + (sq + 1) * 16]
                            mm(ps_[:, 0:64].re("p (h t) -> p h t", h=4), kc[sq % 2][0:64, g, :], q3)
                            mm(ps_[0:64, 64:128].re("p (h t) -> p h t", h=4), kFs[0:64, g, :], q3)
                            e_ = E[it % 2]
                            act(e_[:, 0:64], ps_[:, 0:64], AF.Exp, bias=zero1)
                            act(e_[0:64, 64:128], ps_[0:64, 64:128], AF.Exp, bias=seqmask[:, sq:sq + 1])
                            po = nxt("pb", psB)
                            mm(po[0:64, 0:64], vc[sq % 2][:, g * 64:(g + 1) * 64], e_[:, 0:64], start=True, stop=False)
                            mm(po[0:64, 0:64], vTs[0:64, g * 64:(g + 1) * 64], e_[0:64, 64:128], start=False, stop=True)
                            mm(po[0:64, 64:128], onesb[:, 0:64], e_[:, 0:64], start=True, stop=False)
                            mm(po[0:64, 64:128], onesb[0:64, 0:64], e_[0:64, 64:128], start=False, stop=True)
                            d_ = dsm[it % 2]
                            tt("dve", d_[0:64, 0:64].re("p (h t) -> p h t", h=4), po[0:64, 64:128].re("p (h t) -> p h t", h=4),
                               esinkB[:, g * 256:(g + 1) * 256].re("p (h t) -> p h t", h=4)[:, :, 0:16], ALU.add)
                            recip(d_[0:64, 0:64], d_[0:64, 0:64])
                            tt("dve", o_a[0:64, 4 * g:4 * g + 4, TB + sq * 16:TB + (sq + 1) * 16],
                               po[0:64, 0:64].re("p (h t) -> p h t", h=4), d_[0:64, 0:64].re("p (h t) -> p h t", h=4), ALU.mult)
                            it += 1
                cp("pool", kprev, kF[0:64, :, TB:TB + 128])
                cp("pool", vprev, vT[:, 4, :])
                S.fence()
            AR.off = mark
            AR.gen += 1
            mixacc = AR.alloc("mixacc", [8, n])
            mixbf = AR.alloc("mixbf", [8, n], BF16)
            gsb = [AR.alloc(f"gsb{i}", [512]) for i in range(2)]
            branches = (["A"] if stage >= 5 else []) + (["B"] if stage >= 6 else []) + (["C"] if stage >= 7 else [])
            if not branches:
                memset("pool", mixbf, 0.0)
            for bi_, br in enumerate(branches):
                brow = {"A": 0, "B": 512, "C": 1024}[br]
                for hf in range(2):
                    if br == "A":
                        wb = load_w(dr["w_branch"][l][brow:brow + 512, hf * 512:(hf + 1) * 512], 64, 8, 512)
                    else:
                        wb = load_w(dr["w_branch"][l][brow:brow + 512, hf * 512:(hf + 1) * 512], 128, 4, 512)
                    gcol = OFF["gt"] + bi_ * 0 + {"A": 0, "B": 1024, "C": 2048}[br] + hf * 512
                    wg = load_w(wi[:, gcol:gcol + 512], 128, 8, 512)
                    for q in range(4):
                        i = hf * 4 + q
                        for (a, ha, sn) in segs:
                            p1 = nxt("pa", psA)
                            p2 = nxt("pa", psA)
                            if br == "A":
                                for kt in range(8):
                                    mm(p1[:, 0:sn], wb[0:64, kt, q * 128:(q + 1) * 128], o_a[0:64, kt, ha:ha + sn], start=(kt == 0), stop=(kt == 7))
                            else:
                                osrc = o_bn if br == "B" else o_cn
                                for kt in range(4):
                                    mm(p1[:, 0:sn], wb[:, kt, q * 128:(q + 1) * 128], osrc[:, kt, ha:ha + sn], start=(kt == 0), stop=(kt == 3))
                            for kt in range(8):
                                mm(p2[:, 0:sn], wg[:, kt, q * 128:(q + 1) * 128], xbf[:, kt, a:a + sn], start=(kt == 0), stop=(kt == 7))
                            gs_ = gsb[(i + ha) % 2]
                            act(gs_[:, 0:sn], p2[:, 0:sn], AF.Sigmoid)
                            if bi_ == 0:
                                tt("dve", mixacc[:, i, ha:ha + sn], p1[:, 0:sn], gs_[:, 0:sn], ALU.mult)
                            else:
                                tt("dve", gs_[:, 0:sn], p1[:, 0:sn], gs_[:, 0:sn], ALU.mult)
                                tt("pool", mixacc[:, i, ha:ha + sn], mixacc[:, i, ha:ha + sn], gs_[:, 0:sn], ALU.add)
            if branches:
                cp("pool", mixbf, mixacc)
            for hf in range(2):
                wo = load_w(dr["w_out"][l][:, hf * 512:(hf + 1) * 512], 128, 8, 512)
                for q in range(4):
                    i = hf * 4 + q
                    for (a, ha, sn) in segs:
                        pa = nxt("pa", psA)
                        for kt in range(8):
                            mm(pa[:, 0:sn], wo[:, kt, q * 128:(q + 1) * 128], mixbf[:, kt, ha:ha + sn], start=(kt == 0), stop=(kt == 7))
                        stt(xres[:, i, a:a + sn], xres[:, i, a:a + sn], ALPHA, pa[:, 0:sn], ALU.mult, ALU.add)
            for (a, ha, sn) in segs:
                layer_norm(a, sn, PC["ln1g"], PC["ln1b"])
            S.fence()

        if stage < 3:
            continue
        AR.reset()
        tailx = AR.alloc("tailx", [8, 2])
        S.dma("sp", dr["ag3_in"].re("p (a b) -> p a b", b=2), xres[:, :, SEG - 2:SEG])
        S.fence()
        agi, ago = dr["ag3_in"].ap, dr["ag3_out"].ap
        ccop = S.cc(lambda e: e.collective_compute("AllGather", ALU.bypass, replica_groups=GROUPS,
                                                          ins=[agi], outs=[ago]), ["ag3_in"], ["ag3_out"])
        g4 = AR.alloc("g4", [4, 16])
        S.dma("sp", g4, dr["ag3_out"].re("(r p) c -> p r c", p=128))
        tx = tailx.re("p a b -> p (a b)")
        ts("dve", tx, g4[:, 0, :], corev[:, 1:2], None, ALU.mult)
        for j in range(1, 4):
            stt(tx, g4[:, j, :], corev[:, 1 + j:2 + j], tx, ALU.mult, ALU.add)
        tailb = AR.alloc("tailb", [8, 2], BF16)
        cp("dve", tailb, tailx)
        S.fence()
        keep = AR.off

        for b in range(NBLK if stage >= 4 else 0):
            AR.off = keep
            AR.gen += 1
            c0 = b * TB
            last = (b == NBLK - 1)
            h = AR.alloc("h", [22, TB + NS], BF16)
            ug = [AR.alloc(f"ug{i}", [2 + TB]) for i in range(2)]
            uv = [AR.alloc(f"uv{i}", [2 + TB]) for i in range(2)]
            cg = [AR.alloc(f"cg{i}", [TB]) for i in range(2)]
            cv = [AR.alloc(f"cv{i}", [TB]) for i in range(2)]
            usg = [AR.alloc(f"usg{i}", [4, 18]) for i in range(2)]
            usv = [AR.alloc(f"usv{i}", [4, 18]) for i in range(2)]
            csg = [AR.alloc(f"csg{i}", [4, 16]) for i in range(2)]
            csv = [AR.alloc(f"csv{i}", [4, 16]) for i in range(2)]
            sio = [(AR.alloc(f"sin{i}", [512]), AR.alloc(f"sout{i}", [512]), None) for i in range(2)] if last else None
            ucb = [AR.alloc(f"uc{i}", [16]) for i in range(2)]
            for jg in range(6):
                nj = 4 if jg < 5 else 2
                wg = load_w(dr["w_up"][l][:, jg * 512:jg * 512 + nj * 128], 128, 8, nj * 128)
                wv = load_w(dr["w_up"][l][:, D_FF + jg * 512:D_FF + jg * 512 + nj * 128], 128, 8, nj * 128)
                for jj in range(nj):
                    j = jg * 4 + jj
                    fw = PC["fcw"]
                    for half, (wt, ubuf, cbuf, usb, csb) in enumerate(((wg, ug, cg, usg, csg), (wv, uv, cv, usv, csv))):
                        jc = j + 22 * half
                        u = ubuf[j % 2]
                        cc_ = cbuf[j % 2]
                        pa = nxt("pa", psA)
                        for kt in range(8):
                            mm(pa[:, 0:TB], wt[:, kt, jj * 128:(jj + 1) * 128], xbf[:, kt, c0:c0 + TB],
                               start=(kt == 0), stop=(kt == 7))
                        if b == 0:
                            pb = nxt("pb", psB)
                            for kt in range(8):
                                mm(pb[:, 0:2], wt[:, kt, jj * 128:(jj + 1) * 128], tailb[:, kt, :],
                                   start=(kt == 0), stop=(kt == 7))
                            cp("dve", u[:, 0:2], pb[:, 0:2])
                        else:
                            cp("pool", u[:, 0:2], ffn_tail[:, jc, :])
                        cp("act", u[:, 2:2 + TB], pa[:, 0:TB])
                        cp("pool", ffn_tail[:, jc, :], u[:, TB:TB + 2])
                        wc = lambda t, jc=jc: par[:, fw + t * 44 + jc:fw + t * 44 + jc + 1]
                        ts("pool", cc_, u[:, 0:TB], wc(0), par[:, PC["fcb"] + jc:PC["fcb"] + jc + 1], ALU.mult, ALU.add)
                        stt(cc_, u[:, 1:TB + 1], wc(1), cc_, ALU.mult, ALU.add)
                        stt(cc_, u[:, 2:TB + 2], wc(2), cc_, ALU.mult, ALU.add)
                        if last:
                            us = usb[j % 2]
                            cs = csb[j % 2]
                            pb = nxt("pb", psB)
                            for kt in range(8):
                                mm(pb[:, 0:NS], wt[:, kt, jj * 128:(jj + 1) * 128], xbf[:, kt, SEG:SEG + NS],
                                   start=(kt == 0), stop=(kt == 7))
                            sin, sout, pout = sio[half]
                            if jj == 0:
                                S.dma("sp", sin[0:8, 0:nj * 128],
                                      V(dr["st_fconv"].ap[l].rearrange("s r c -> (s r) c")[:, jc * 128:(jc + nj) * 128], "RO"))
                            pb2 = nxt("pb", psB)
                            tr(pb2[:, 0:8], sin[0:8, jj * 128:(jj + 1) * 128], ident[0:8, 0:8])
                            cp("dve", us[:, :, 0:2], pb2[:, 0:8].re("p (s r) -> p s r", s=4))
                            cp("act", us[:, :, 2:18], pb[:, 0:NS].re("p (s t) -> p s t", s=4))
                            ts("pool", cs, us[:, :, 0:16], wc(0), par[:, PC["fcb"] + jc:PC["fcb"] + jc + 1], ALU.mult, ALU.add)
                            stt(cs, us[:, :, 1:17], wc(1), cs, ALU.mult, ALU.add)
                            stt(cs, us[:, :, 2:18], wc(2), cs, ALU.mult, ALU.add)
                            uc = ucb[(2 * j + half) % 2]
                            cp("pool", uc[:, 0:8].re("p (s r) -> p s r", s=4), us[:, :, 16:18])
                            cp("pool", uc[:, 8:10], u[:, TB:TB + 2])
                            pb3 = nxt("pb", psB)
                            tr(pb3[0:10, 0:128], uc[:, 0:10], ident)
                            cp("dve", sout[0:10, jj * 128:(jj + 1) * 128], pb3[0:10, 0:128])
                            if jj == nj - 1:
                                S.dma("sp", V(dr["s_fconv"].ap[l].rearrange("s r c -> (s r) c")[:, (jc - nj + 1) * 128:(jc + 1) * 128], "out_s_fconv"),
                                      sout[0:8, 0:nj * 128])
                                S.dma("sp", V(dr["p_fconv"].ap[l][:, (jc - nj + 1) * 128:(jc + 1) * 128], "out_p_fconv"),
                                      sout[8:10, 0:nj * 128])
                    act(cg[j % 2], cg[j % 2], AF.Silu)
                    tt("dve", h[:, j, 0:TB], cg[j % 2], cv[j % 2], ALU.mult)
                    if last:
                        act(csg[j % 2], csg[j % 2], AF.Silu)
                        tt("dve", h[:, j, TB:TB + NS].re("p (s t) -> p s t", s=4), csg[j % 2], csv[j % 2], ALU.mult)
            segs = [(c0, 0, TB)] + ([(SEG, TB, NS)] if last else [])
            for i2 in range(2):
                wd = [load_w(dr["w_down"][l][:, (i2 * 4 + q) * 128:(i2 * 4 + q + 1) * 128], 128, 22, 128) for q in range(1)]
                for q in range(4):
                    i = i2 * 4 + q
                    if q > 0:
                        wd = [load_w(dr["w_down"][l][:, i * 128:(i + 1) * 128], 128, 22, 128)]
                    for (a, ha, sn) in segs:
                        pa = nxt("pa", psA)
                        for kt in range(22):
                            mm(pa[:, 0:sn], wd[0][:, kt, :], h[:, kt, ha:ha + sn], start=(kt == 0), stop=(kt == 21))
                        stt(xres[:, i, a:a + sn], xres[:, i, a:a + sn], ALPHA, pa[:, 0:sn], ALU.mult, ALU.add)
            for (a, ha, sn) in segs:
                layer_norm(a, sn, PC["ln2g"], PC["ln2b"])
            S.fence()

    AR.reset()
    yo = [AR.alloc(f"yo{i}", [D]) for i in range(2)]
    for ti, (src, r0, nr, c0) in enumerate(tiles):
        y = yo[ti % 2]
        for dq in range(2):
            pb = nxt("pb", psB)
            for q in range(4):
                dt_ = dq * 4 + q
                tr(pb[0:nr, q * 128:(q + 1) * 128], xres[:, dt_, c0:c0 + nr], ident)
            cp("act" if dq % 2 else "dve", y[0:nr, dq * 512:(dq + 1) * 512], pb[0:nr, :])
        dst = dr["y_p"][r0:r0 + nr, :] if src == "xp" else dr["y_s"][0:nr, :]
        S.dma("sp", dst, y[0:nr, :])
    S.fence()
    S.emit(nc, es)
    es.close()
    return nc


_NC_CACHE = {}


def _core_inputs(inp, c):
    b, p = c // 4, c % 4
    m = {}
    m["xp"] = np.ascontiguousarray(inp["x_prompt"][b, p * SEG:(p + 1) * SEG])
    m["xs"] = np.ascontiguousarray(inp["x_sample"][4 * c:4 * c + 4]).reshape(NS, D)
    m["cache_k"] = np.ascontiguousarray(inp["cache_swa_k"][:, 4 * c:4 * c + 4]).reshape(DEPTH, 4, 128, 128)
    m["cache_v"] = np.ascontiguousarray(inp["cache_swa_v"][:, 4 * c:4 * c + 4]).reshape(DEPTH, 4, 128, 128)
    m["st_hgrn"] = np.ascontiguousarray(inp["state_hgrn"][:, 4 * c:4 * c + 4])
    m["st_gdn"] = np.ascontiguousarray(inp["state_gdn"][:, 4 * c:4 * c + 4])
    m["st_gconv"] = np.ascontiguousarray(inp["state_gdn_conv"][:, 4 * c:4 * c + 4])
    m["st_fconv"] = np.ascontiguousarray(inp["state_ffn_conv"][:, 4 * c:4 * c + 4])
    for k in ("ln_in_g", "ln_in_b", "w_in", "attn_sinks", "hgrn_lb_logits", "hgrn_norm_w", "gdn_conv_w",
              "gdn_a_log", "gdn_dt_bias", "gdn_norm_w", "w_branch", "w_out", "ln1_g", "ln1_b", "w_up",
              "ffn_conv_w", "ffn_conv_b", "w_down", "ln2_g", "ln2_b"):
        m[k] = np.ascontiguousarray(inp[k], dtype=np.float32)
    m["consts"] = CONSTS
    cv = np.zeros((128, 16), np.float32)
    cv[:, 0] = 1.0 if p > 0 else 0.0
    if p > 0:
        cv[:, 1 + (p - 1)] = 1.0
    cv[:, 5 + p] = 1.0
    cv[:, 9] = 0.0 if p > 0 else NEGM
    m["corev"] = cv
    return m


def run_cores(inp, nlayers=DEPTH, stage=99):
    key = (nlayers, stage)
    if key not in _NC_CACHE:
        _NC_CACHE[key] = build_nc(stage=stage, nlayers=nlayers)
    nc = _NC_CACHE[key]
    in_maps = [_core_inputs(inp, c) for c in range(8)]
    if nlayers < DEPTH:
        for m in in_maps:
            for n, shp in DRAM_IN:
                if shp[0] == DEPTH and n not in ('hgrn_lb_logits', 'gdn_dt_bias', 'gdn_a_log'):
                    m[n] = np.ascontiguousarray(m[n][:nlayers])
    res = run_bass_kernel_spmd(nc, in_maps, core_ids=list(range(8)))
    return res.results


def kernel(**inp):
    inp = {k: np.asarray(v) for k, v in inp.items()}
    r = run_cores(inp)
    y_p = np.stack([np.concatenate([r[b * 4 + p]["y_p"] for p in range(4)], 0) for b in range(2)], 0)
    y_s = np.concatenate([r[c]["y_s"].reshape(4, 16, D) for c in range(8)], 0)

    def pl(name, shape):
        return np.stack([r[3][name], r[7][name]], 1).reshape(shape)

    def sl(name, shape):
        return np.concatenate([r[c][name] for c in range(8)], 1).reshape(shape)

    outs = (y_p, y_s,
            pl("p_swa_k", (DEPTH, 2, 128, 2, 64)), pl("p_swa_v", (DEPTH, 2, 128, 2, 64)),
            pl("p_hgrn", (DEPTH, 2, 4, 128, 128)), pl("p_gdn", (DEPTH, 2, 4, 128, 128)),
            pl("p_gconv", (DEPTH, 2, 3, C_QKV)), pl("p_fconv", (DEPTH, 2, 2, 2 * D_FF)),
            sl("s_swa_k", (DEPTH, 32, 128, 2, 64)), sl("s_swa_v", (DEPTH, 32, 128, 2, 64)),
            sl("s_hgrn", (DEPTH, 32, 4, 128, 128)), sl("s_gdn", (DEPTH, 32, 4, 128, 128)),
            sl("s_gconv", (DEPTH, 32, 3, C_QKV)), sl("s_fconv", (DEPTH, 32, 2, 2 * D_FF)))
    return tuple(np.ascontiguousarray(o, dtype=np.float32) for o in outs)
```

```python
import numpy as np
from contextlib import ExitStack
import concourse.bass as bass
import concourse.mybir as mybir
from concourse.bass_utils import run_bass_kernel_spmd

F32 = mybir.dt.float32
BF16 = mybir.dt.bfloat16
AF = mybir.ActivationFunctionType
ALU = mybir.AluOpType

D = 1024
DEPTH = 4
SEG = 2048
NS = 64
NTOK = SEG + NS
TB = 512
NBLK = SEG // TB
A_HEADS, A_KV, A_HD = 8, 2, 64
D_FF = 2816
C_QKV = 1536
D_IN = 7944
OFF = {}
_o = 0
for _n, _w in (("aq", 512), ("ak", 128), ("av", 128), ("bq", 512), ("bf", 512), ("bi", 512), ("bg", 512),
               ("cqkv", 1536), ("cz", 512), ("cb", 4), ("ca", 4), ("gt", 3072)):
    OFF[_n] = _o
    _o += _w
ALPHA = (2 * DEPTH) ** 0.25
LN_EPS = 1e-5
RMS_EPS = 1e-6
NEGM = -30000.0


class V:
    __slots__ = ("ap", "key")

    def __init__(self, ap, key):
        self.ap = ap
        self.key = key

    def __getitem__(self, idx):
        return V(self.ap[idx], self.key)

    def k(self, key):
        return V(self.ap, key)

    def re(self, pat, **kw):
        return V(self.ap.rearrange(pat, **kw), self.key)

    def bc(self, shape):
        return V(self.ap.to_broadcast(shape), self.key)


class Op:
    __slots__ = ("eng", "kind", "fn", "deps", "idx", "inc", "sem", "val", "gid")


class Sched:
    ENGS = ("pe", "act", "dve", "pool", "sp")
    NDMA = 20

    def __init__(self):
        self.ops = {e: [] for e in self.ENGS}
        self.res = {}
        self.dma_slots = {e: [None] * self.NDMA for e in ("sp", "act", "pool")}
        self.dma_cnt = {e: [0] * self.NDMA for e in ("sp", "act", "pool")}
        self.dma_rr = {e: 0 for e in ("sp", "act", "pool")}
        self.all_dma = []
        self.gid = 0
        self.pending_dma = []
        self.ccs = []
        self.bg = {}

    def _rec(self, eng, kind, fn, reads, writes, extra_deps=()):
        op = Op()
        op.eng, op.kind, op.fn = eng, kind, fn
        op.inc = False
        op.sem = None
        op.val = 0
        op.gid = self.gid
        self.gid += 1
        deps = []
        for r in reads:
            if r in self.bg:
                deps.append(self.bg[r])
            st = self.res.get(r)
            if st and st[0] is not None:
                deps.append(st[0])
        for w in writes:
            st = self.res.get(w)
            if st:
                if st[0] is not None:
                    deps.append(st[0])
                deps.extend(st[1])
        deps.extend(extra_deps)
        for r in reads:
            st = self.res.setdefault(r, [None, []])
            st[1].append(op)
        for w in writes:
            self.res[w] = [op, []]
        seen = set()
        dd = []
        for d in deps:
            if d is op or id(d) in seen:
                continue
            seen.add(id(d))
            if d.kind == "c" and kind == "c" and d.eng == "pe" and eng == "pe":
                continue
            dd.append(d)
        op.deps = dd
        op.idx = len(self.ops[eng])
        self.ops[eng].append(op)
        return op

    def c(self, eng, fn, reads, writes):
        return self._rec(eng, "c", fn, reads, writes)

    def cc(self, fn, reads, writes):
        op = self._rec("pool", "x", fn, reads, writes)
        op.sem = ("cc", len(self.ccs))
        op.val = 1
        self.ccs.append(op)
        self.pending_dma.append(op)
        return op

    def dma(self, q, out, in_, bg=False, **kw):
        slot = self.dma_rr[q]
        self.dma_rr[q] = (slot + 1) % self.NDMA
        prev = self.dma_slots[q][slot]
        extra = [prev] if prev is not None else []
        o, i = out.ap, in_.ap
        op = self._rec(q, "d", lambda e: e.dma_start(out=o, in_=i, **kw), [in_.key], [out.key], extra)
        self.dma_cnt[q][slot] += 1
        op.sem = (q, slot)
        op.val = 16 * self.dma_cnt[q][slot]
        self.dma_slots[q][slot] = op
        self.all_dma.append(op)
        if bg:
            self.bg[out.key] = op
        else:
            self.pending_dma.append(op)
        return op

    def fence(self):
        lasts = [self.ops[e][-1] for e in self.ENGS if self.ops[e] and self.ops[e][-1].kind != "d"]
        for e in self.ENGS:
            pass
        lastc = []
        for e in self.ENGS:
            for o in reversed(self.ops[e]):
                if o.kind == "c" and o.fn is not None:
                    lastc.append(o)
                    break
                if o.fn is None:
                    break
        pend = list(self.pending_dma)
        self.pending_dma = []
        for e in self.ENGS:
            deps = list(lastc) + pend
            op = Op()
            op.eng, op.kind, op.fn = e, "c", None
            op.inc, op.sem, op.val, op.gid = False, None, 0, self.gid
            self.gid += 1
            op.deps = deps
            op.idx = len(self.ops[e])
            self.ops[e].append(op)
        self.res = {}

    def emit(self, nc, es):
        CH = 20000
        for e in self.ENGS:
            for op in self.ops[e]:
                for d in op.deps:
                    d.inc = True
        self.csems = {}
        for e in self.ENGS:
            n = 0
            for op in self.ops[e]:
                if op.kind == "c" and op.inc:
                    op.sem = (e, n // CH)
                    op.val = n % CH + 1
                    n += 1
            nsem = max(1, (n + CH - 1) // CH)
            self.csems[e] = [es.enter_context(nc.semaphore(f"c_{e}_{j}")) for j in range(nsem)]
        self.dsems = {q: [es.enter_context(nc.semaphore(f"d_{q}_{j}")) for j in range(self.NDMA)]
                      for q in ("sp", "act", "pool")}
        block = es.enter_context(nc.Block())
        engmap = {"pe": block.tensor, "act": block.scalar, "dve": block.vector, "pool": block.gpsimd, "sp": block.sync}

        self.xsems = [es.enter_context(nc.semaphore(f"x_{j}")) for j in range(len(self.ccs))]

        def semof(op):
            if op.kind == "d":
                return self.dsems[op.sem[0]][op.sem[1]]
            if op.kind == "x":
                return self.xsems[op.sem[1]]
            return self.csems[op.sem[0]][op.sem[1]]

        def make(e):
            oplist = self.ops[e]

            def body(eng):
                waited = {}
                for op in oplist:
                    for d in op.deps:
                        key = (d.kind, d.sem)
                        if waited.get(key, 0) >= d.val:
                            continue
                        eng.wait_ge(semof(d), d.val)
                        waited[key] = d.val
                    if op.fn is None:
                        continue
                    ins = op.fn(eng)
                    if op.kind == "d":
                        ins.then_inc(semof(op), 16)
                    elif op.kind == "x":
                        ins.then_inc(semof(op), 1)
                    elif op.inc:
                        ins.then_inc(semof(op), 1)
            return body

        for e in self.ENGS:
            engmap[e](make(e))


S = None


def kof(*vs):
    return [v.key for v in vs if isinstance(v, V)]


def apof(x):
    return x.ap if isinstance(x, V) else x


def mm(out, lhsT, rhs, start=True, stop=True):
    o, l, r = out.ap, lhsT.ap, rhs.ap
    return S.c("pe", lambda e: e.matmul(o, l, r, start=start, stop=stop), kof(lhsT, rhs), kof(out))


def tr(out, in_, ident):
    o, i, d = out.ap, in_.ap, ident.ap
    return S.c("pe", lambda e: e.transpose(o, i, d), kof(in_, ident), kof(out))


def act(out, in_, func, bias=None, scale=None):
    kw = {}
    if bias is not None:
        kw["bias"] = apof(bias)
    if scale is not None:
        kw["scale"] = apof(scale)
    o, i = out.ap, in_.ap
    return S.c("act", lambda e: e.activation(out=o, in_=i, func=func, **kw), kof(in_, bias, scale), kof(out))


def ts(eng, out, in0, s1, s2, op0, op1=None):
    o, i = out.ap, in0.ap
    a1, a2 = apof(s1), apof(s2)
    if op1 is None:
        return S.c(eng, lambda e: e.tensor_scalar(o, i, a1, None, op0), kof(in0, s1), kof(out))
    return S.c(eng, lambda e: e.tensor_scalar(o, i, a1, a2, op0, op1), kof(in0, s1, s2), kof(out))


def tt(eng, out, in0, in1, op):
    o, a, b = out.ap, in0.ap, in1.ap
    return S.c(eng, lambda e: e.tensor_tensor(out=o, in0=a, in1=b, op=op), kof(in0, in1), kof(out))


def stt(out, in0, scalar, in1, op0, op1):
    o, a, b = out.ap, in0.ap, in1.ap
    sc = apof(scalar)
    return S.c("dve", lambda e: e.scalar_tensor_tensor(out=o, in0=a, scalar=sc, in1=b, op0=op0, op1=op1),
               kof(in0, scalar, in1), kof(out))


def cp(eng, out, in_):
    o, i = out.ap, in_.ap
    if eng == "act":
        return S.c("act", lambda e: e.activation(out=o, in_=i, func=AF.Copy), kof(in_), kof(out))
    return S.c(eng, lambda e: e.tensor_copy(o, i), kof(in_), kof(out))


def recip(out, in_):
    o, i = out.ap, in_.ap
    return S.c("dve", lambda e: e.reciprocal(o, i), kof(in_), kof(out))


def memset(eng, out, val):
    o = out.ap
    return S.c(eng, lambda e: e.memset(o, val), [], kof(out))


def make_consts():
    cols = {}
    parts = []
    pos = [0]

    def add(name, a):
        a = np.asarray(a, np.float32)
        if a.shape[0] < 128:
            a = np.concatenate([a, np.zeros((128 - a.shape[0], a.shape[1]), np.float32)], 0)
        cols[name] = (pos[0], a.shape[1])
        parts.append(a)
        pos[0] += a.shape[1]

    i = np.arange(128)
    add("ident", np.eye(128))
    add("ones", np.ones((128, 128)))
    for tag, T, blk in (("p", 128, 128), ("s", 64, 16)):
        s = np.arange(T)[:, None]
        t = np.arange(T)[None, :]
        same = (s // blk) == (t // blk)
        add(f"triLE_{tag}", (same & (s <= t)))
        add(f"triGT_{tag}", (same & (s > t)))
        add(f"mbias_{tag}", np.where(same & (s <= t), 0.0, NEGM))
        add(f"offd_{tag}", (same & (s < t)))
    for tag, T, blk in (("p", 128, 64), ("s", 64, 16)):
        nb = T // blk
        sidx = np.arange(T)[:, None]
        tidx = np.arange(T)[None, :]
        same = (sidx // blk) == (tidx // blk)
        mid = (tidx // blk) * blk + blk // 2 - 1
        drel = (same & (sidx <= tidx)).astype(np.float32) - (same & (sidx <= mid)).astype(np.float32)
        tot = np.stack([(np.arange(T) // blk == b_) for b_ in range(nb)], 1).astype(np.float32)
        midc = np.stack([((np.arange(T) // blk == b_) & (np.arange(T) <= b_ * blk + blk // 2 - 1)) for b_ in range(nb)], 1).astype(np.float32)
        add(f"hdext_{tag}", np.concatenate([drel, tot, midc], 1))
        add(f"hle_{tag}", (same & (sidx <= tidx)))
        add(f"hgt_{tag}", (same & (sidx > tidx)))
        add(f"blk_{tag}", tot)
    sm = np.full((64, 4), NEGM, np.float32)
    for q in range(4):
        sm[16 * q:16 * q + 16, q] = 0.0
    add("seqmask", sm)
    return np.concatenate(parts, 1), cols


CONSTS, CCOL = make_consts()
NCON = CONSTS.shape[1]


class Arena:
    def __init__(self, ap, ncols32):
        self.ap = ap
        self.n = ncols32
        self.off = 0
        self.gen = 0

    def reset(self):
        self.off = 0
        self.gen += 1

    def alloc(self, name, shape, dtype=F32, parts=128):
        free = int(np.prod(shape))
        n32 = free if dtype == F32 else (free + 1) // 2
        self.off = (self.off + 31) // 32 * 32
        assert self.off + n32 <= self.n, (name, self.off, n32, self.n)
        a = self.ap[0:parts, self.off:self.off + n32]
        self.off += n32
        if dtype != F32:
            a = a.bitcast(dtype)[:, 0:free]
        if len(shape) == 2:
            a = a.rearrange("p (a b) -> p a b", a=shape[0])
        elif len(shape) == 3:
            a = a.rearrange("p (a b c) -> p a b c", a=shape[0], b=shape[1])
        return V(a, f"{name}#{self.gen}")


DRAM_IN = [
    ("xp", [SEG, D]), ("xs", [NS, D]),
    ("cache_k", [DEPTH, 4, 128, 128]), ("cache_v", [DEPTH, 4, 128, 128]),
    ("st_hgrn", [DEPTH, 4, 4, 128, 128]), ("st_gdn", [DEPTH, 4, 4, 128, 128]),
    ("st_gconv", [DEPTH, 4, 3, C_QKV]), ("st_fconv", [DEPTH, 4, 2, 2 * D_FF]),
    ("ln_in_g", [D]), ("ln_in_b", [D]), ("w_in", [DEPTH, D, D_IN]), ("attn_sinks", [DEPTH, 8]),
    ("hgrn_lb_logits", [DEPTH, 512]), ("hgrn_norm_w", [DEPTH, 128]), ("gdn_conv_w", [DEPTH, 4, C_QKV]),
    ("gdn_a_log", [DEPTH, 4]), ("gdn_dt_bias", [DEPTH, 4]), ("gdn_norm_w", [DEPTH, 128]),
    ("w_branch", [DEPTH, 1536, D]), ("w_out", [DEPTH, D, D]), ("ln1_g", [DEPTH, D]), ("ln1_b", [DEPTH, D]),
    ("w_up", [DEPTH, D, 2 * D_FF]), ("ffn_conv_w", [DEPTH, 3, 2 * D_FF]), ("ffn_conv_b", [DEPTH, 2 * D_FF]),
    ("w_down", [DEPTH, D_FF, D]), ("ln2_g", [DEPTH, D]), ("ln2_b", [DEPTH, D]),
    ("consts", [128, NCON]), ("corev", [128, 16]),
]
DRAM_OUT = [
    ("y_p", [SEG, D]), ("y_s", [NS, D]),
    ("p_swa_k", [DEPTH, 128, 128]), ("p_swa_v", [DEPTH, 128, 128]),
    ("p_hgrn", [DEPTH, 4, 128, 128]), ("p_gdn", [DEPTH, 4, 128, 128]),
    ("p_gconv", [DEPTH, 3, C_QKV]), ("p_fconv", [DEPTH, 2, 2 * D_FF]),
    ("s_swa_k", [DEPTH, 4, 128, 128]), ("s_swa_v", [DEPTH, 4, 128, 128]),
    ("s_hgrn", [DEPTH, 4, 4, 128, 128]), ("s_gdn", [DEPTH, 4, 4, 128, 128]),
    ("s_gconv", [DEPTH, 4, 3, C_QKV]), ("s_fconv", [DEPTH, 4, 2, 2 * D_FF]),
]
GROUPS = [[0, 1, 2, 3], [4, 5, 6, 7]]
NW = 3
WSLOT = 4096
ARENA32 = 13900


def build_nc(stage=99, nlayers=DEPTH):
    global S
    S = Sched()
    nc = bass.Bass("TRN2", target_bir_lowering=False)
    es = ExitStack()
    dr = {}
    for n, shp in DRAM_IN:
        if shp[0] == DEPTH and n not in ('hgrn_lb_logits', 'gdn_dt_bias', 'gdn_a_log'):
            shp = [nlayers] + list(shp[1:])
        dr[n] = V(nc.dram_tensor(n, shp, F32, kind="ExternalInput").ap(), "RO")
    for n, shp in DRAM_OUT:
        dr[n] = V(nc.dram_tensor(n, shp, F32, kind="ExternalOutput").ap(), "out_" + n)
    AG3W = D * 2
    AG1W = 256 + 36
    dr["ag1_in"] = V(nc.dram_tensor("ag1_in", [128, AG1W], F32).ap(), "ag1_in")
    dr["ag1_out"] = V(nc.dram_tensor("ag1_out", [512, AG1W], F32).ap(), "ag1_out")
    AG2W = 516 + 1024
    dr["ag2_in"] = V(nc.dram_tensor("ag2_in", [128, AG2W], F32).ap(), "ag2_in")
    dr["ag2_out"] = V(nc.dram_tensor("ag2_out", [512, AG2W], F32).ap(), "ag2_out")
    dr["ob_loc"] = V(nc.dram_tensor("ob_loc", [128, 4, NTOK], BF16).ap(), "ob_loc")
    dr["qh"] = V(nc.dram_tensor("qh", [128, 4, SEG], BF16).ap(), "qh")
    dr["oc_loc"] = V(nc.dram_tensor("oc_loc", [128, 4, NTOK], BF16).ap(), "oc_loc")
    dr["rh"] = V(nc.dram_tensor("rh", [128, 4, SEG], BF16).ap(), "rh")
    dr["ag3_in"] = V(nc.dram_tensor("ag3_in", [128, 16], F32).ap(), "ag3_in")
    dr["ag3_out"] = V(nc.dram_tensor("ag3_out", [512, 16], F32).ap(), "ag3_out")

    def sb(name, shape, dt=F32):
        return V(es.enter_context(nc.sbuf_tensor(name, shape, dt))[:], name)

    WSH = {"w_in": (D, D_IN), "w_branch": (1536, D), "w_out": (D, D), "w_up": (D, 2 * D_FF), "w_down": (D_FF, D)}
    WB = {}
    for wn, (r_, c_) in WSH.items():
        t_ = nc.dram_tensor("wb_" + wn, [nlayers, r_, c_], BF16).ap()
        WB[wn] = [V(t_[l_], f"wb_{wn}_{l_}") for l_ in range(nlayers)]

    def convert_weights(l_):
        for wn, (r_, c_) in WSH.items():
            step = 256 if r_ % 256 == 0 else 128
            step = r_ // max(1, min(4, r_ // step))
            for r0 in range(0, r_, step):
                r1 = min(r_, r0 + step)
                S.dma("pool", V(WB[wn][l_].ap[r0:r1, :], f"wb_{wn}_{l_}_{r0}"), V(dr[wn].ap[l_][r0:r1, :], "RO"), bg=True)

    xres = sb("xres", [128, 8, NTOK])
    xbf = sb("xbf", [128, 8, NTOK], BF16)
    con = sb("con", [128, NCON])
    conb = sb("conb", [128, 256], BF16)
    corev = sb("corev_sb", [128, 16])
    wsl = [sb(f"w{i}", [128, WSLOT], BF16) for i in range(NW)]
    par = sb("par", [128, 512])
    arena_t = sb("arena", [128, ARENA32])
    AR = Arena(arena_t.ap, ARENA32)
    ln_st = sb("ln_st", [128, 2, 512])
    ln_mean = sb("ln_mean", [128, 512])
    ln_rstd = sb("ln_rstd", [128, 512])
    ln_tmp = sb("ln_tmp", [128, 2, 512])
    ffn_tail = sb("ffn_tail", [128, 44, 2])
    kprev = sb("kprev", [64, 2, 128], BF16)
    vprev = sb("vprev", [128, 128], BF16)
    gc_prev = sb("gc_prev", [128, 12, 3])
    esinkB = sb("esinkB", [64, 4 * 64 * 2])
    zero1 = sb("zero1", [128, 1])
    eps1 = sb("eps1", [128, 1])
    hS = sb("hS", [128, 4, 128])
    hpre = sb("hpre", [128, 4])
    lbF = sb("lbF", [128, 4, 4])
    sinB = sb("sinB", [128, 4, 128], BF16)
    sinC = sb("sinC", [128, 4, 128], BF16)
    gpar = sb("gpar", [128, 64])
    nrmw = sb("nrmw", [128, 2])
    psA = [V(es.enter_context(nc.psum_tensor(f"psA{i}", [128, 512], F32))[:], f"psA{i}") for i in range(4)]
    psB_t = [es.enter_context(nc.psum_tensor(f"psB{i}", [128, 512], F32)) for i in range(4)]
    psB = [V(psB_t[i][:], f"psB{i}") for i in range(4)]
    rr = {"w": 0, "pa": 0, "pb": 0, "stg": 0}

    def nxt(kind, lst):
        v = lst[rr[kind] % len(lst)]
        rr[kind] += 1
        return v

    def C(name, parts=128):
        c0, n = CCOL[name]
        return con[0:parts, c0:c0 + n]

    ident = C("ident")
    ones32 = C("ones")
    identb = conb[:, 0:128]
    onesb = conb[:, 128:256]

    S.dma("sp", con, dr["consts"])
    S.dma("sp", corev, dr["corev"])
    convert_weights(0)
    cp("dve", identb, ident)
    cp("dve", onesb, ones32)
    memset("pool", zero1, 0.0)
    memset("pool", eps1, RMS_EPS)
    seqmask = C("seqmask", 64)

    def load_w(src2d, kp, kt, ncols, q="pool"):
        slot = nxt("w", wsl)
        assert kt * ncols <= WSLOT
        v = V(slot.ap[0:kp, 0:kt * ncols].rearrange("p (k m) -> p k m", k=kt), slot.key)
        extra = [o for k_, o in S.bg.items() if k_.startswith(src2d.key + "_")]
        op = S.dma("sp", v, V(src2d.ap.rearrange("(k p) m -> p k m", p=kp), "RO"))
        have = set(id(d) for d in op.deps)
        op.deps.extend(o for o in extra if id(o) not in have)
        return v

    def layer_norm(c0, n, gcol, bcol):
        st, mean, rstd, tmp = ln_st, ln_mean, ln_rstd, ln_tmp
        for s0 in range(0, n, 512):
            sn = min(512, n - s0)
            a, b = c0 + s0, c0 + s0 + sn
            p1 = nxt("pa", psA)
            p2 = nxt("pa", psA)
            for dt_ in range(8):
                mm(p1[:, 0:sn], ones32, xres[:, dt_, a:b], start=(dt_ == 0), stop=(dt_ == 7))
            for dt_ in range(8):
                sq = st[:, dt_ % 2, 0:sn]
                act(sq, xres[:, dt_, a:b], AF.Square)
                mm(p2[:, 0:sn], ones32, sq, start=(dt_ == 0), stop=(dt_ == 7))
            act(mean[:, 0:sn], p1[:, 0:sn], AF.Copy, scale=1.0 / D)
            tt("pool", rstd[:, 0:sn], mean[:, 0:sn], mean[:, 0:sn], ALU.mult)
            stt(rstd[:, 0:sn], p2[:, 0:sn], 1.0 / D, rstd[:, 0:sn], ALU.mult, ALU.subtract)
            ts("dve", rstd[:, 0:sn], rstd[:, 0:sn], LN_EPS, None, ALU.add)
            act(rstd[:, 0:sn], rstd[:, 0:sn], AF.Sqrt)
            recip(rstd[:, 0:sn], rstd[:, 0:sn])
            for dt_ in range(8):
                t1 = tmp[:, dt_ % 2, 0:sn]
                tt("dve", t1, xres[:, dt_, a:b], mean[:, 0:sn], ALU.subtract)
                tt("dve", t1, t1, rstd[:, 0:sn], ALU.mult)
                act(xres[:, dt_, a:b], t1, AF.Identity, scale=par[:, gcol + dt_:gcol + dt_ + 1],
                    bias=par[:, bcol + dt_:bcol + dt_ + 1])
                cp("pool", xbf[:, dt_, a:b], xres[:, dt_, a:b])

    stg = [sb(f"stg{i}", [128, 128]) for i in range(2)]

    def load_cols(col, src1d, t):
        st_ = nxt("stg", stg)
        S.dma("sp", st_[0:t, :], V(src1d.ap.rearrange("(t p) -> t p", p=128), "RO"))
        pb = nxt("pb", psB)
        tr(pb[:, 0:t], st_[0:t, :], ident[0:t, 0:t])
        cp("dve", par[:, col:col + t], pb[:, 0:t])

    def load_vec8(col, src1d):
        load_cols(col, src1d, 8)

    if stage == -1:
        AR.reset()
        t0 = AR.alloc("t0", [D])
        S.dma("sp", t0[0:NS, :], dr["xs"])
        S.dma("sp", dr["y_s"], t0[0:NS, :])
        S.fence()
        S.emit(nc, es)
        es.close()
        return nc
    load_vec8(0, dr["ln_in_g"])
    load_vec8(8, dr["ln_in_b"])
    AR.reset()
    xin = [AR.alloc(f"xin{i}", [D]) for i in range(2)]
    tiles = [("xp", i * 128, 128, i * 128) for i in range(SEG // 128)] + [("xs", 0, NS, SEG)]
    for ti, (src, r0, nr, c0) in enumerate(tiles):
        xi = xin[ti % 2]
        S.dma("sp", xi[0:nr, :], dr[src][r0:r0 + nr, :])
        for dq in range(2):
            pb = nxt("pb", psB)
            for q in range(4):
                dt_ = dq * 4 + q
                tr(pb[:, q * 128:q * 128 + nr], xi[0:nr, dt_ * 128:(dt_ + 1) * 128], ident[0:nr, 0:nr])
            cp("act" if dq % 2 else "dve", xres[:, dq * 4:dq * 4 + 4, c0:c0 + nr],
               pb.re("p (q t) -> p q t", q=4)[:, :, 0:nr])
    S.fence()
    AR.reset()
    if stage >= 1:
        layer_norm(0, NTOK, 0, 8)
    S.fence()

    PC = {"ln1g": 0, "ln1b": 8, "ln2g": 16, "ln2b": 24, "fcw": 32, "fcb": 32 + 132}

    for l in range(nlayers if stage >= 2 else 0):
        load_vec8(PC["ln1g"], dr["ln1_g"][l])
        load_vec8(PC["ln1b"], dr["ln1_b"][l])
        load_vec8(PC["ln2g"], dr["ln2_g"][l])
        load_vec8(PC["ln2b"], dr["ln2_b"][l])
        for tap in range(3):
            load_cols(PC["fcw"] + tap * 44, dr["ffn_conv_w"][l][tap], 44)
        load_cols(PC["fcb"], dr["ffn_conv_b"][l], 44)
        S.fence()

        wi = WB["w_in"][l]
        if l + 1 < nlayers:
            convert_weights(l + 1)
        if stage >= 5:
            AR.reset()
            wkv = load_w(wi[:, OFF["ak"]:OFF["ak"] + 256], 128, 8, 256)
            kvl = AR.alloc("kvl", [AG1W])
            kvs = AR.alloc("kvs", [256])
            pa = nxt("pa", psA)
            for kt in range(8):
                mm(pa[:, 0:256], xbf[:, kt, SEG - 128:SEG], wkv[:, kt, :], start=(kt == 0), stop=(kt == 7))
            cp("act", kvl[:, 0:256], pa[:, 0:256])
            pa = nxt("pa", psA)
            for kt in range(8):
                mm(pa[0:NS, 0:256], xbf[:, kt, SEG:SEG + NS], wkv[:, kt, :], start=(kt == 0), stop=(kt == 7))
            cp("act", kvs[0:NS, :], pa[0:NS, 0:256])
            S.dma("sp", dr["p_swa_k"][l], kvl[:, 0:128])
            S.dma("sp", dr["p_swa_v"][l], kvl[:, 128:256])
            for sq in range(4):
                S.dma("sp", dr["s_swa_k"][l][sq, 112:128, :], kvs[16 * sq:16 * sq + 16, 0:128])
                S.dma("sp", dr["s_swa_v"][l][sq, 112:128, :], kvs[16 * sq:16 * sq + 16, 128:256])
            S.dma("sp", dr["s_swa_k"][l][:, 0:112, :], dr["cache_k"][l][:, 16:128, :])
            S.dma("sp", dr["s_swa_v"][l][:, 0:112, :], dr["cache_v"][l][:, 16:128, :])
            gt3 = kvl[:, 256:AG1W].re("p (j t) -> p j t", t=3)
            for jg in range(3):
                wc_ = load_w(wi[:, OFF["cqkv"] + jg * 512:OFF["cqkv"] + (jg + 1) * 512], 128, 8, 512)
                pb = nxt("pb", psB)
                for jj in range(4):
                    for kt in range(8):
                        mm(pb[:, jj * 4:jj * 4 + 3], wc_[:, kt, jj * 128:(jj + 1) * 128], xbf[:, kt, SEG - 3:SEG],
                           start=(kt == 0), stop=(kt == 7))
                cp("dve", gt3[:, jg * 4:jg * 4 + 4, :], pb[:, 0:16].re("p (j t) -> p j t", t=4)[:, :, 0:3])
            gct = AR.alloc("gct", [C_QKV])
            for jq in range(3):
                pb = nxt("pb", psB)
                for jj in range(4):
                    tr(pb[0:3, jj * 128:(jj + 1) * 128], gt3[:, jq * 4 + jj, :], ident)
                cp("act", gct[0:3, jq * 512:(jq + 1) * 512], pb[0:3, :])
            S.dma("sp", dr["p_gconv"][l], gct[0:3, :])
            S.dma("sp", dr["ag1_in"], kvl)
            S.fence()
            a1i, a1o = dr["ag1_in"].ap, dr["ag1_out"].ap
            S.cc(lambda e: e.collective_compute("AllGather", ALU.bypass, replica_groups=GROUPS,
                                                       ins=[a1i], outs=[a1o]), ["ag1_in"], ["ag1_out"])
            g1 = AR.alloc("g1", [4, AG1W])
            S.dma("sp", g1, dr["ag1_out"].re("(r p) c -> p r c", p=128))
            hsel = AR.alloc("hsel", [AG1W])
            ts("dve", hsel, g1[:, 0, :], corev[:, 1:2], None, ALU.mult)
            for j in range(1, 4):
                stt(hsel, g1[:, j, :], corev[:, 1 + j:2 + j], hsel, ALU.mult, ALU.add)
            cp("pool", gc_prev, hsel[:, 256:AG1W].re("p (j t) -> p j t", t=3))
            cp("pool", vprev, hsel[:, 128:256])
            pb = nxt("pb", psB)
            for g in range(2):
                tr(pb[0:64, g * 128:(g + 1) * 128], hsel[:, g * 64:(g + 1) * 64], ident)
            cp("act", kprev, pb[0:64, 0:256].re("p (g t) -> p g t", g=2))
            es8 = AR.alloc("es8", [8])
            S.dma("sp", es8[0:64, :], V(dr["attn_sinks"].ap[l].partition_broadcast(64), "RO"))
            act(es8[0:64, :], es8[0:64, :], AF.Exp)
            cp("dve", esinkB.re("p (h t) -> p h t", h=8), es8[0:64, :].re("p (h o) -> p h o", o=1).bc([64, 8, 64]))
            S.fence()

        if stage >= 6:
            if l == 0:
                st_ = nxt("stg", stg)
                S.dma("sp", st_[0:16, :], V(dr["hgrn_lb_logits"].ap.rearrange("l (h p) -> (l h) p", p=128), "RO"))
                pb = nxt("pb", psB)
                tr(pb[:, 0:16], st_[0:16, :], ident[0:16, 0:16])
                AR.reset()
                e16 = AR.alloc("e16", [4, 4])
                act(e16.re("p l h -> p (l h)"), pb[:, 0:16], AF.Exp)
                ssum = AR.alloc("ssum", [4])
                tt("dve", ssum, e16[:, 0, :], e16[:, 1, :], ALU.add)
                tt("dve", ssum, ssum, e16[:, 2, :], ALU.add)
                tt("dve", ssum, ssum, e16[:, 3, :], ALU.add)
                recip(ssum, ssum)
                for ll in range(1, 4):
                    tt("dve", e16[:, ll, :], e16[:, ll, :], ssum, ALU.mult)
                memset("pool", lbF[:, 0, :], 0.0)
                cp("dve", lbF[:, 1, :], e16[:, 1, :])
                tt("dve", lbF[:, 2, :], lbF[:, 1, :], e16[:, 2, :], ALU.add)
                tt("dve", lbF[:, 3, :], lbF[:, 2, :], e16[:, 3, :], ALU.add)
                S.fence()
            AR.reset()
            lbT = AR.alloc("lbT", [512])
            omlT = AR.alloc("omlT", [512])
            omlF = AR.alloc("omlF", [4])
            nomlF = AR.alloc("nomlF", [4])
            hSs = AR.alloc("hSs", [4, 4, 128])
            keep1 = AR.off
            lg = AR.alloc("lg", [4, 512])
            S.dma("sp", lg, V(dr["hgrn_lb_logits"].ap.partition_broadcast(128), "RO"))
            act(lg, lg, AF.Exp)
            tt("dve", omlT, lg[:, 0, :], lg[:, 1, :], ALU.add)
            tt("dve", omlT, omlT, lg[:, 2, :], ALU.add)
            tt("dve", omlT, omlT, lg[:, 3, :], ALU.add)
            recip(omlT, omlT)
            memset("pool", lbT, 0.0)
            for ll in range(1, l + 1):
                tt("dve", lbT, lbT, lg[:, ll, :], ALU.add)
            tt("dve", lbT, lbT, omlT, ALU.mult)
            ts("dve", omlT, lbT, -1.0, 1.0, ALU.mult, ALU.add)
            ts("dve", omlF, lbF[:, l, :], -1.0, 1.0, ALU.mult, ALU.add)
            ts("dve", nomlF, omlF, -1.0, None, ALU.mult)
            S.dma("sp", nrmw[:, 0:1], V(dr["hgrn_norm_w"].ap[l].rearrange("(p o) -> p o", o=1), "RO"))
            S.dma("sp", nrmw[:, 1:2], V(dr["gdn_norm_w"].ap[l].rearrange("(p o) -> p o", o=1), "RO"))
            memset("pool", hS, 0.0)
            memset("pool", hpre, 1.0)
            for sq in range(4):
                S.dma("sp", hSs[:, sq, :, :], V(dr["st_hgrn"].ap[l][sq].rearrange("h k v -> k h v"), "RO"))
            S.fence()
            for b in range(NBLK):
                AR.off = keep1
                AR.gen += 1
                c0 = b * TB
                last = (b == NBLK - 1)
                n = TB + (NS if last else 0)
                segs = [(c0, 0, TB)] + ([(SEG, TB, NS)] if last else [])
                wbq = load_w(wi[:, OFF["bq"]:OFF["bq"] + 512], 128, 8, 512)
                wbf = load_w(wi[:, OFF["bf"]:OFF["bf"] + 512], 128, 8, 512)
                wbi = load_w(wi[:, OFF["bi"]:OFF["bi"] + 512], 128, 8, 512)
                qT = AR.alloc("qT", [4, n], BF16)
                kT = AR.alloc("kT", [4, n], BF16)
                obuf = AR.alloc("obuf", [4, n], BF16)
                qhat = AR.alloc("qhat", [4, TB], BF16)
                sgF = [AR.alloc(f"sgF{i}", [512]) for i in range(2)]
                for h in range(4):
                    for (a, ha, sn) in segs:
                        pa = nxt("pa", psA)
                        for kt in range(8):
                            mm(pa[:, 0:sn], wbq[:, kt, h * 128:(h + 1) * 128], xbf[:, kt, a:a + sn], start=(kt == 0), stop=(kt == 7))
                        act(qT[:, h, ha:ha + sn], pa[:, 0:sn], AF.Silu)
                        pa = nxt("pa", psA)
                        for kt in range(8):
                            mm(pa[:, 0:sn], wbf[:, kt, h * 128:(h + 1) * 128], xbf[:, kt, a:a + sn], start=(kt == 0), stop=(kt == 7))
                        sg = sgF[h % 2]
                        act(sg[:, 0:sn], pa[:, 0:sn], AF.Sigmoid)
                        ts("dve", kT[:, h, ha:ha + sn], sg[:, 0:sn], nomlF[:, h:h + 1], omlF[:, h:h + 1], ALU.mult, ALU.add)
                sgT = AR.alloc("sgT", [512])
                tT = AR.alloc("tT", [512])
                lfT = AR.alloc("lfT", [512])
                kTM = AR.alloc("kTM", [512])
                vTM = AR.alloc("vTM", [512], BF16)
                kend = AR.alloc("kend", [512], BF16)
                kendm = AR.alloc("kendm", [4, 512], BF16) if last else None
                eq = AR.alloc("eq", [128])
                ek = AR.alloc("ek", [128])
                dm = AR.alloc("dm", [8])
                qt = AR.alloc("qt", [128], BF16)
                kt_ = AR.alloc("kt_", [128], BF16)
                attm = AR.alloc("attm", [128], BF16)
                sbf = AR.alloc("sbf", [128], BF16)
                fac1 = AR.alloc("fac1", [1])
                tl = [(c0 + t * 128, t * 128, 128, "p") for t in range(4)] + ([(SEG, TB, NS, "s")] if last else [])
                for (a, ha, T, tag) in tl:
                    nb = 2 if tag == "p" else 4
                    blk = T // nb
                    hdext = C(f"hdext_{tag}", T)
                    hle = C(f"hle_{tag}", T)
                    hgt = C(f"hgt_{tag}", T)
                    pf = nxt("pa", psA)
                    pi = nxt("pa", psA)
                    for kt in range(8):
                        mm(pf[0:T, :], xbf[:, kt, a:a + T], wbf[:, kt, :], start=(kt == 0), stop=(kt == 7))
                    for kt in range(8):
                        mm(pi[0:T, :], xbf[:, kt, a:a + T], wbi[:, kt, :], start=(kt == 0), stop=(kt == 7))
                    act(sgT[0:T, :], pf[0:T, :], AF.Sigmoid)
                    tt("dve", tT[0:T, :], sgT[0:T, :], omlT[0:T, :], ALU.mult)
                    stt(sgT[0:T, :], tT[0:T, :], 1e-30, lbT[0:T, :], ALU.max, ALU.add)
                    act(lfT[0:T, :], sgT[0:T, :], AF.Ln)
                    tt("pool", kTM[0:T, :], omlT[0:T, :], tT[0:T, :], ALU.subtract)
                    cp("act", vTM[0:T, :], pi[0:T, :])
                    pr = nxt("pa", psA)
                    mm(pr[0:T, :], hgt, lfT[0:T, :])
                    act(tT[0:T, :], pr[0:T, :], AF.Exp)
                    tt("dve", kend[0:T, :], kTM[0:T, :], tT[0:T, :], ALU.mult)
                    if tag == "s":
                        bls = C("blk_s", T)
                        for bb in range(4):
                            ts("pool", kendm[0:T, bb, :], kend[0:T, :], bls[:, bb:bb + 1], None, ALU.mult)
                    for h in range(4):
                        hs = slice(h * 128, (h + 1) * 128)
                        pc = nxt("pb", psB)
                        mm(pc[:, 0:T + 2 * nb], lfT[0:T, hs], hdext)
                        act(eq[:, 0:T], pc[:, 0:T], AF.Exp)
                        tt("dve", qt[:, 0:T], qT[:, h, ha:ha + T], eq[:, 0:T], ALU.mult)
                        act(ek[:, 0:T], pc[:, 0:T], AF.Exp, scale=-1.0)
                        tt("pool", kt_[:, 0:T], kT[:, h, ha:ha + T], ek[:, 0:T], ALU.mult)
                        act(dm[:, 0:2 * nb], pc[:, T:T + 2 * nb], AF.Exp)
                        pat = nxt("pb", psB)
                        mm(pat[0:T, 0:T], kt_[:, 0:T], qt[:, 0:T])
                        stt(attm[0:T, 0:T], pat[0:T, 0:T], 1e30, hle, ALU.min, ALU.mult)
                        po = nxt("pb", psB)
                        mm(po[:, 0:T], vTM[0:T, hs], attm[0:T, 0:T], start=True, stop=False)
                        for bb in range(nb):
                            Sb = hS[:, h, :] if tag == "p" else hSs[:, bb, h, :]
                            bs = slice(bb * blk, (bb + 1) * blk)
                            act(sbf, Sb, AF.Copy, scale=dm[:, nb + bb:nb + bb + 1])
                            mm(po[:, bs], sbf, qt[:, bs], start=False, stop=(bb == nb - 1))
                            pu = nxt("pa", psA)
                            if tag == "p":
                                ts("dve", fac1, dm[:, nb + bb:nb + bb + 1], hpre[:, h:h + 1], None, ALU.mult)
                                ts("dve", qhat[:, h, ha + bb * blk:ha + (bb + 1) * blk], qt[:, bs], fac1, None, ALU.mult)
                                mm(pu[:, 0:128], kend[bs, hs], vTM[bs, hs])
                            else:
                                mm(pu[:, 0:128], kendm[0:T, bb, hs], vTM[0:T, hs])
                            stt(Sb, Sb, dm[:, bb:bb + 1], pu[:, 0:128], ALU.mult, ALU.add)
                            if tag == "p":
                                tt("dve", hpre[:, h:h + 1], hpre[:, h:h + 1], dm[:, bb:bb + 1], ALU.mult)
                        cp("act", obuf[:, h, ha:ha + T], po[:, 0:T])
                S.dma("sp", dr["ob_loc"][:, :, c0:c0 + TB], obuf[:, :, 0:TB])
                S.dma("sp", dr["qh"][:, :, c0:c0 + TB], qhat)
                if last:
                    S.dma("sp", dr["ob_loc"][:, :, SEG:SEG + NS], obuf[:, :, TB:TB + NS])
                S.fence()
            for sq in range(4):
                S.dma("sp", V(dr["s_hgrn"].ap[l][sq].rearrange("h k v -> k h v"), "out_s_hgrn"), hSs[:, sq, :, :])
            if stage >= 7:
                S.fence()
                AR.off = 0
                AR.gen += 1
                for tap in range(4):
                    st_ = nxt("stg", stg)
                    S.dma("sp", st_[0:12, :], V(dr["gdn_conv_w"].ap[l][tap].rearrange("(t p) -> t p", p=128), "RO"))
                    pb = nxt("pb", psB)
                    tr(pb[:, 0:12], st_[0:12, :], ident[0:12, 0:12])
                    cp("dve", gpar[:, tap * 12:(tap + 1) * 12], pb[:, 0:12])
                gtmp = AR.alloc("gtmp", [2, 4 * DEPTH])
                S.dma("sp", gtmp[:, 0, :], V(dr["gdn_dt_bias"].ap.rearrange("l h -> (l h)").partition_broadcast(128), "RO"))
                S.dma("sp", gtmp[:, 1, :], V(dr["gdn_a_log"].ap.rearrange("l h -> (l h)").partition_broadcast(128), "RO"))
                cp("dve", gpar[:, 48:52], gtmp[:, 0, 4 * l:4 * l + 4])
                act(gpar[:, 52:56], gtmp[:, 1, 4 * l:4 * l + 4], AF.Exp)
                ts("dve", gpar[:, 52:56], gpar[:, 52:56], -1.0, None, ALU.mult)
                gS = AR.alloc("gS", [4, 256])
                gSb = AR.alloc("gSb", [4, 256], BF16)
                memset("pool", gS, 0.0)
                for h in range(4):
                    cp("dve", gS[:, h, 128:256], ident)
                cp("act", gSb, gS)
                keepg = AR.off
                triLE = C("triLE_p")
                triGT = C("triGT_p")
                mbias = C("mbias_p")
                offd = C("offd_p")
                S.fence()
                import os
                for u_ in range(NBLK + (0 if os.environ.get('GDN_NOSAMPLE') else 1)):
                    AR.off = keepg
                    AR.gen += 1
                    smp = (u_ == NBLK)
                    n = NS if smp else TB
                    a0 = SEG if smp else u_ * TB
                    wcq = [load_w(wi[:, OFF["cqkv"] + jg * 512:OFF["cqkv"] + (jg + 1) * 512], 128, 8, 512) for jg in range(1)]
                    qn = AR.alloc("qn", [4, n], BF16)
                    kn = AR.alloc("kn", [4, n], BF16)
                    ntile = 4
                    TT = 16 if smp else 128
                    kTMa = AR.alloc("kTMa", [ntile, 4, 128], BF16)
                    vTMa = AR.alloc("vTMa", [ntile, 4, 128], BF16)
                    ocb = AR.alloc("ocb", [4, n], BF16)
                    rhb = AR.alloc("rhb", [4, n], BF16)
                    cpre = [AR.alloc(f"cpre{i}", [n + 12]) for i in range(2)]
                    u32 = [AR.alloc(f"u32{i}", [n]) for i in range(2)]
                    sqg = AR.alloc("sqg", [n])
                    rsg = AR.alloc("rsg", [n])
                    if smp:
                        gSs = AR.alloc("gSs", [4, 4, 128])
                        gSsb = AR.alloc("gSsb", [4, 4, 128], BF16)
                        for sq in range(4):
                            S.dma("sp", gSs[:, sq, :, :], V(dr["st_gdn"].ap[l][sq].rearrange("h k v -> k h v"), "RO"))
                        cp("act", gSsb, gSs)
                        sg12 = AR.alloc("sg12", [C_QKV])
                        so12 = AR.alloc("so12", [C_QKV])
                        S.dma("sp", sg12[0:12, :], V(dr["st_gconv"].ap[l].rearrange("s r c -> (s r) c"), "RO"))
                    for j in range(12):
                        if j % 4 == 0 and j > 0:
                            wcq = [load_w(wi[:, OFF["cqkv"] + (j // 4) * 512:OFF["cqkv"] + (j // 4 + 1) * 512], 128, 8, 512)]
                        pa = nxt("pa", psA)
                        for kt in range(8):
                            mm(pa[:, 0:n], wcq[0][:, kt, (j % 4) * 128:(j % 4 + 1) * 128], xbf[:, kt, a0:a0 + n], start=(kt == 0), stop=(kt == 7))
                        cpj = cpre[j % 2]
                        uj = u32[j % 2]
                        gw = lambda t, j=j: gpar[:, t * 12 + j:t * 12 + j + 1]
                        if not smp:
                            cp("pool", cpj[:, 0:3], gc_prev[:, j, :])
                            cp("act", cpj[:, 3:3 + n], pa[:, 0:n])
                            cp("pool", gc_prev[:, j, :], cpj[:, n:n + 3])
                            ts("pool", uj, cpj[:, 0:n], gw(0), None, ALU.mult)
                            for t in range(1, 4):
                                stt(uj, cpj[:, t:t + n], gw(t), uj, ALU.mult, ALU.add)
                        else:
                            c3 = cpj[:, 0:76].re("p (s t) -> p s t", s=4)
                            pb = nxt("pb", psB)
                            tr(pb[:, 0:12], sg12[0:12, j * 128:(j + 1) * 128], ident[0:12, 0:12])
                            cp("dve", c3[:, :, 0:3], pb[:, 0:12].re("p (s r) -> p s r", s=4))
                            cp("act", c3[:, :, 3:19], pa[:, 0:n].re("p (s t) -> p s t", s=4))
                            u3 = uj.re("p (s t) -> p s t", s=4)
                            ts("pool", u3, c3[:, :, 0:16], gw(0), None, ALU.mult)
                            for t in range(1, 4):
                                stt(u3, c3[:, :, t:t + 16], gw(t), u3, ALU.mult, ALU.add)
                            uc12 = sqg[:, 0:12]
                            cp("pool", uc12.re("p (s r) -> p s r", s=4), c3[:, :, 16:19])
                            pb = nxt("pb", psB)
                            tr(pb[0:12, 0:128], uc12, ident)
                            cp("dve", so12[0:12, j * 128:(j + 1) * 128], pb[0:12, 0:128])
                        act(uj, uj, AF.Silu)
                        h = j % 4
                        if j < 8:
                            act(sqg, uj, AF.Square)
                            ps_ = nxt("pb", psB)
                            mm(ps_[:, 0:n], ones32, sqg)
                            act(rsg, ps_[:, 0:n], AF.Sqrt, bias=eps1)
                            recip(rsg, rsg)
                            if j < 4:
                                stt(qn[:, h, :], uj, 128.0 ** -0.5, rsg, ALU.mult, ALU.mult)
                            else:
                                tt("dve", uj, uj, rsg, ALU.mult)
                                cp("pool", kn[:, h, :], uj)
                        if j >= 4:
                            dstT = kTMa if j < 8 else vTMa
                            for t in range(ntile):
                                pb = nxt("pb", psB)
                                tr(pb[0:TT, 0:128], uj[:, t * TT:(t + 1) * TT], ident)
                                cp("act" if t % 2 else "dve", dstT[0:TT, t, h, :], pb[0:TT, 0:128])
                    if smp:
                        S.dma("sp", V(dr["s_gconv"].ap[l].rearrange("s r c -> (s r) c"), "out_s_gconv"), so12[0:12, :])
                    wcb = load_w(wi[:, OFF["cb"] - 248:OFF["cb"] + 8], 128, 8, 256)[:, :, 248:256]
                    bg8 = AR.alloc("bg8", [8])
                    beta = AR.alloc("beta", [4])
                    gg = AR.alloc("gg", [4])
                    sm8 = AR.alloc("sm8", [16])
                    erow = AR.alloc("erow", [128])
                    decT = AR.alloc("decT", [128])
                    decS = AR.alloc("decS", [128])
                    U32 = AR.alloc("U32", [128])
                    QKd = AR.alloc("QKd", [128], BF16)
                    keT = AR.alloc("keT", [128], BF16)
                    qeT = AR.alloc("qeT", [128], BF16)
                    ktil = AR.alloc("ktil", [128], BF16)
                    vaug = AR.alloc("vaug", [256], BF16)
                    memset("pool", vaug, 0.0)
                    Aa = [AR.alloc(f"Aa{i}", [128]) for i in range(2)]
                    At = [AR.alloc(f"At{i}", [128]) for i in range(2)]
                    Wm = [AR.alloc(f"Wm{i}", [128]) for i in range(2)]
                    Wmb = AR.alloc("Wmb", [128], BF16)
                    P1s = AR.alloc("P1s", [256], BF16)
                    wbf_ = AR.alloc("wbf_", [256], BF16)
                    GC = int(os.environ.get('GDN_CUT', '9'))
                    for t in range(ntile if GC >= 2 else 0):
                        T = TT
                        a = a0 + t * T
                        cs = slice(t * T, (t + 1) * T)
                        NV = 128 if smp else 256
                        pg = nxt("pb", psB)
                        for kt in range(8):
                            mm(pg[0:T, 0:8], xbf[:, kt, a:a + T], wcb[:, kt, :], start=(kt == 0), stop=(kt == 7))
                        cp("dve", bg8[0:T, :], pg[0:T, 0:8])
                        act(beta[0:T, :], bg8[0:T, 0:4], AF.Sigmoid)
                        tt("dve", gg[0:T, :], bg8[0:T, 4:8], gpar[0:T, 48:52], ALU.add)
                        act(gg[0:T, :], gg[0:T, :], AF.Exp)
                        act(gg[0:T, :], gg[0:T, :], AF.Ln, bias=1.0)
                        tt("dve", gg[0:T, :], gg[0:T, :], gpar[0:T, 52:56], ALU.mult)
                        pcm = nxt("pb", psB)
                        mm(pcm[0:T, 0:4], triLE[0:T, 0:T], gg[0:T, :])
                        mm(pcm[0:T, 4:8], triGT[0:T, 0:T], gg[0:T, :])
                        mm(pcm[:, 8:12], ones32[0:T, :], gg[0:T, :])
                        act(sm8[0:T, 0:4], pcm[0:T, 0:4], AF.Exp)
                        act(sm8[0:T, 4:8], pcm[0:T, 0:4], AF.Copy, scale=-1.0)
                        act(sm8[0:T, 8:12], pcm[0:T, 4:8], AF.Exp)
                        act(sm8[:, 12:16], pcm[:, 8:12], AF.Exp)
                        for h in range(4):
                            St = gSs[:, t, h, :] if smp else gS[:, h, :]
                            Sb = gSsb[:, t, h, :] if smp else gSb[:, h, :]
                            pcr = nxt("pb", psB)
                            gbc = gg[0:T, h:h + 1].bc([T, 128])
                            mm(pcr[:, 0:T], gbc, triLE[0:T, 0:T])
                            mm(pcr[0:T, 128:128 + T], gg[0:T, h:h + 1].bc([T, T]), triLE[0:T, 0:T], start=True, stop=False)
                            mm(pcr[0:T, 128:128 + T], ident[0:T, 0:T], mbias[0:T, 0:T], start=False, stop=True)
                            act(erow[:, 0:T], pcr[:, 0:T], AF.Exp)
                            act(decT[0:T, 0:T], pcr[0:T, 128:128 + T], AF.Exp, bias=sm8[0:T, 4 + h:5 + h])
                            tt("pool", decS[0:T, 0:T], decT[0:T, 0:T], offd[0:T, 0:T], ALU.mult)
                            pk = nxt("pb", psB)
                            mm(pk[0:T, 0:T], kn[:, h, cs], kn[:, h, cs])
                            mm(pk[0:T, 128:128 + T], kn[:, h, cs], qn[:, h, cs])
                            stt(U32[0:T, 0:T], pk[0:T, 0:T], beta[0:T, h:h + 1], decS[0:T, 0:T], ALU.mult, ALU.mult)
                            tt("dve", QKd[0:T, 0:T], pk[0:T, 128:128 + T], decT[0:T, 0:T], ALU.mult)
                            tt("pool", keT[:, 0:T], kn[:, h, cs], erow[:, 0:T], ALU.mult)
                            tt("pool", qeT[:, 0:T], qn[:, h, cs], erow[:, 0:T], ALU.mult)
                            ts("pool", ktil[0:T, :], kTMa[0:T, t, h, :], sm8[0:T, 8 + h:9 + h], None, ALU.mult)
                            cp("pool", vaug[0:T, 0:128], vTMa[0:T, t, h, :])
                            if GC < 3:
                                continue
                            pl_ = nxt("pb", psB)
                            tr(pl_[0:T, 0:T], U32[0:T, 0:T], ident[0:T, 0:T])
                            A, ATr = Aa[0], At[0]
                            cp("dve", A[0:T, 0:T], U32[0:T, 0:T])
                            cp("act", ATr[0:T, 0:T], pl_[0:T, 0:T])
                            W = Wm[0]
                            tt("dve", W[0:T, 0:T], ident[0:T, 0:T], U32[0:T, 0:T], ALU.subtract)
                            nsq = 3 if smp else 6
                            nsq = min(nsq, int(os.environ.get('GDN_NSQ', '9')))
                            for js in range(nsq):
                                A2, AT2, W2 = Aa[(js + 1) % 2], At[(js + 1) % 2], Wm[(js + 1) % 2]
                                p1_ = nxt("pb", psB)
                                p1b_ = nxt("pb", psB)
                                mm(p1_[0:T, 0:T], ATr[0:T, 0:T], A[0:T, 0:T])
                                mm(p1b_[0:T, 0:T], A[0:T, 0:T], ATr[0:T, 0:T])
                                cp("dve", A2[0:T, 0:T], p1_[0:T, 0:T])
                                cp("act", AT2[0:T, 0:T], p1b_[0:T, 0:T])
                                if os.environ.get('GDN_NOW'):
                                    A, ATr = A2, AT2
                                    continue
                                p2_ = nxt("pb", psB)
                                mm(p2_[0:T, 0:T], AT2[0:T, 0:T], W[0:T, 0:T])
                                tt("dve", W2[0:T, 0:T], W[0:T, 0:T], p2_[0:T, 0:T], ALU.add)
                                A, ATr, W = A2, AT2, W2
                            cp("act", Wmb[0:T, 0:T], W[0:T, 0:T])
                            if GC < 4:
                                continue
                            pp = nxt("pa", psA)
                            mm(pp[0:T, 0:NV], keT[:, 0:T], Sb[:, 0:NV] if not smp else Sb)
                            tt("dve", P1s[0:T, 0:NV], vaug[0:T, 0:NV], pp[0:T, 0:NV], ALU.subtract)
                            pw = nxt("pa", psA)
                            mm(pw[0:T, 0:NV], Wmb[0:T, 0:T], P1s[0:T, 0:NV])
                            act(wbf_[0:T, 0:NV], pw[0:T, 0:NV], AF.Copy, scale=beta[0:T, h:h + 1])
                            pq = nxt("pa", psA)
                            mm(pq[:, 0:T], wbf_[0:T, 0:128], QKd[0:T, 0:T], start=True, stop=False)
                            mm(pq[:, 0:T], Sb[:, 0:128] if not smp else Sb, qeT[:, 0:T], start=False, stop=True)
                            cp("act", ocb[:, h, cs], pq[:, 0:T])
                            if not smp:
                                mm(pq[:, 128:128 + T], wbf_[0:T, 128:256], QKd[0:T, 0:T], start=True, stop=False)
                                mm(pq[:, 128:128 + T], Sb[:, 128:256], qeT[:, 0:T], start=False, stop=True)
                                cp("dve", rhb[:, h, cs], pq[:, 128:128 + T])
                            pu = nxt("pa", psA)
                            mm(pu[:, 0:NV], ktil[0:T, :], wbf_[0:T, 0:NV])
                            if smp:
                                stt(St, St, sm8[:, 12 + h:13 + h], pu[:, 0:NV], ALU.mult, ALU.add)
                            else:
                                stt(St, St, sm8[:, 12 + h:13 + h], pu[:, 0:NV], ALU.mult, ALU.add)
                            cp("act", Sb, St)
                    if smp:
                        S.dma("sp", dr["oc_loc"][:, :, SEG:SEG + NS], ocb)
                        for sq in range(4):
                            S.dma("sp", V(dr["s_gdn"].ap[l][sq].rearrange("h k v -> k h v"), "out_s_gdn"), gSs[:, sq, :, :])
                    else:
                        S.dma("sp", dr["oc_loc"][:, :, a0:a0 + TB], ocb)
                        S.dma("sp", dr["rh"][:, :, a0:a0 + TB], rhb)
                    S.fence()
                S.dma("sp", dr["ag2_in"][:, 516:516 + 1024].re("p (h c) -> p h c", h=4), gS)
                AR.off = keepg
                AR.gen += 1
            AR.off = keep1
            AR.gen += 1
            S.dma("sp", dr["ag2_in"][:, 0:512], hS.re("p h v -> p (h v)"))
            S.dma("sp", dr["ag2_in"][:, 512:516], hpre)
            S.fence()
            a2i, a2o = dr["ag2_in"].ap, dr["ag2_out"].ap
            S.cc(lambda e: e.collective_compute("AllGather", ALU.bypass, replica_groups=GROUPS,
                                                       ins=[a2i], outs=[a2o]), ["ag2_in"], ["ag2_out"])
            g2 = AR.alloc("g2", [4, AG2W])
            S.dma("sp", g2, dr["ag2_out"].re("(r p) c -> p r c", p=128))
            X = AR.alloc("X", [4, 128])
            Sin = AR.alloc("Sin", [4, 128])
            memset("pool", X, 0.0)
            memset("pool", Sin, 0.0)
            for j in range(3):
                for h in range(4):
                    stt(X[:, h, :], X[:, h, :], g2[:, j, 512 + h:513 + h], g2[:, j, h * 128:(h + 1) * 128], ALU.mult, ALU.add)
                stt(Sin.re("p h v -> p (h v)"), X.re("p h v -> p (h v)"), corev[:, 6 + j:7 + j], Sin.re("p h v -> p (h v)"), ALU.mult, ALU.add)
            cp("act", sinB, Sin)
            for h in range(4):
                stt(X[:, h, :], Sin[:, h, :], hpre[:, h:h + 1], hS[:, h, :], ALU.mult, ALU.add)
            S.dma("sp", V(dr["p_hgrn"].ap[l].rearrange("h k v -> k h v"), "out_p_hgrn"), X)
            if stage >= 7:
                Xc = AR.alloc("Xc", [4, 128])
                SinC32 = AR.alloc("SinC32", [4, 128])
                Xcb = AR.alloc("Xcb", [4, 128], BF16)
                PmT = AR.alloc("PmT", [128], BF16)
                memset("pool", Xc, 0.0)
                memset("pool", SinC32, 0.0)
                for j in range(3):
                    for h in range(4):
                        base = 516 + h * 256
                        pt = nxt("pb", psB)
                        tr(pt[:, 0:128], g2[:, j, base + 128:base + 256], ident)
                        cp("act", PmT, pt[:, 0:128])
                        cp("dve", Xcb[:, h, :], Xc[:, h, :])
                        px = nxt("pb", psB)
                        mm(px[:, 0:128], PmT, Xcb[:, h, :])
                        tt("dve", Xc[:, h, :], px[:, 0:128], g2[:, j, base:base + 128], ALU.add)
                    stt(SinC32.re("p h v -> p (h v)"), Xc.re("p h v -> p (h v)"), corev[:, 6 + j:7 + j], SinC32.re("p h v -> p (h v)"), ALU.mult, ALU.add)
                cp("act", sinC, SinC32)
                for h in range(4):
                    pt = nxt("pb", psB)
                    tr(pt[:, 0:128], gS[:, h, 128:256], ident)
                    cp("act", PmT, pt[:, 0:128])
                    px = nxt("pb", psB)
                    mm(px[:, 0:128], PmT, sinC[:, h, :])
                    tt("dve", Xc[:, h, :], px[:, 0:128], gS[:, h, 0:128], ALU.add)
                S.dma("sp", V(dr["p_gdn"].ap[l].rearrange("h k v -> k h v"), "out_p_gdn"), Xc)
            S.fence()

        for b in range(NBLK):
            AR.reset()
            c0 = b * TB
            last = (b == NBLK - 1)
            n = TB + (NS if last else 0)
            segs = [(c0, 0, TB)] + ([(SEG, TB, NS)] if last else [])
            o_a = AR.alloc("o_a", [8, n], BF16)
            o_bn = AR.alloc("o_bn", [4, n], BF16)
            o_cn = AR.alloc("o_cn", [4, n], BF16)
            mark = AR.off
            for (tagB, onrm, loc, qsrc, sinX, zoff, nwc, stg_min) in (("B", o_bn, "ob_loc", "qh", sinB, OFF["bg"], 0, 6), ("C", o_cn, "oc_loc", "rh", sinC, OFF["cz"], 1, 7)):
                if stage < stg_min:
                    continue
                obl = AR.alloc("obl", [4, n], BF16)
                qhl = AR.alloc("qhl", [4, TB], BF16)
                o32 = AR.alloc("o32", [512])
                sq32 = AR.alloc("sq32", [512])
                rs_ = AR.alloc("rs_", [512])
                sz = AR.alloc("sz", [512])
                S.dma("sp", obl[:, :, 0:TB], dr[loc][:, :, c0:c0 + TB])
                if last:
                    S.dma("sp", obl[:, :, TB:TB + NS], dr[loc][:, :, SEG:SEG + NS])
                S.dma("sp", qhl, dr[qsrc][:, :, c0:c0 + TB])
                wz = load_w(wi[:, zoff:zoff + 512], 128, 8, 512)
                for h in range(4):
                    for (a, ha, sn) in segs:
                        if ha == 0:
                            pd = nxt("pb", psB)
                            mm(pd[:, 0:TB], sinX[:, h, :], qhl[:, h, :])
                            tt("dve", o32[:, 0:sn], pd[:, 0:TB], obl[:, h, 0:TB], ALU.add)
                        else:
                            cp("dve", o32[:, 0:sn], obl[:, h, TB:TB + NS])
                        act(sq32[:, 0:sn], o32[:, 0:sn], AF.Square)
                        ps_ = nxt("pb", psB)
                        mm(ps_[:, 0:sn], ones32, sq32[:, 0:sn])
                        act(rs_[:, 0:sn], ps_[:, 0:sn], AF.Sqrt, scale=1.0 / 128, bias=eps1)
                        recip(rs_[:, 0:sn], rs_[:, 0:sn])
                        pz = nxt("pa", psA)
                        for kt in range(8):
                            mm(pz[:, 0:sn], wz[:, kt, h * 128:(h + 1) * 128], xbf[:, kt, a:a + sn], start=(kt == 0), stop=(kt == 7))
                        act(sz[:, 0:sn], pz[:, 0:sn], AF.Silu)
                        tt("dve", o32[:, 0:sn], o32[:, 0:sn], rs_[:, 0:sn], ALU.mult)
                        stt(onrm[:, h, ha:ha + sn], o32[:, 0:sn], nrmw[:, nwc:nwc + 1], sz[:, 0:sn], ALU.mult, ALU.mult)
                S.fence()
                AR.off = mark
                AR.gen += 1
            if stage >= 5:
                qF = AR.alloc("qF", [8, n], BF16)
                kF = AR.alloc("kF", [2, 128 + TB], BF16)
                vT = AR.alloc("vT", [5, 128], BF16)
                wq = load_w(wi[:, OFF["aq"]:OFF["aq"] + 512], 128, 8, 512)
                wkv = load_w(wi[:, OFF["ak"]:OFF["ak"] + 256], 128, 8, 256)
                cp("pool", kF[0:64, :, 0:128], kprev)
                cp("pool", vT[:, 0, :], vprev)
                if last:
                    kFs = AR.alloc("kFs", [2, NS], BF16)
                    vTs = AR.alloc("vTs", [128], BF16)
                for h in range(8):
                    for (a, ha, sn) in segs:
                        pa = nxt("pa", psA)
                        for kt in range(8):
                            mm(pa[0:64, 0:sn], wq[:, kt, h * 64:(h + 1) * 64], xbf[:, kt, a:a + sn], start=(kt == 0), stop=(kt == 7))
                        act(qF[0:64, h, ha:ha + sn], pa[0:64, 0:sn], AF.Copy, scale=0.125)
                for g in range(2):
                    for (a, ha, sn) in segs:
                        pa = nxt("pa", psA)
                        for kt in range(8):
                            mm(pa[0:64, 0:sn], wkv[:, kt, g * 64:(g + 1) * 64], xbf[:, kt, a:a + sn], start=(kt == 0), stop=(kt == 7))
                        if ha == 0:
                            cp("act", kF[0:64, g, 128:128 + TB], pa[0:64, 0:TB])
                        else:
                            cp("act", kFs[0:64, g, :], pa[0:64, 0:NS])
                for t in range(4):
                    pa = nxt("pa", psA)
                    for kt in range(8):
                        mm(pa[:, 0:128], xbf[:, kt, c0 + t * 128:c0 + (t + 1) * 128], wkv[:, kt, 128:256], start=(kt == 0), stop=(kt == 7))
                    cp("dve", vT[:, 1 + t, :], pa[:, 0:128])
                if last:
                    pa = nxt("pa", psA)
                    for kt in range(8):
                        mm(pa[0:NS, 0:128], xbf[:, kt, SEG:SEG + NS], wkv[:, kt, 128:256], start=(kt == 0), stop=(kt == 7))
                    cp("dve", vTs[0:NS, :], pa[0:NS, 0:128])
                E = [AR.alloc(f"E{i}", [512], BF16) for i in range(2)]
                dsm = [AR.alloc(f"dsm{i}", [256]) for i in range(2)]
                it = 0
                for cq in range(8):
                    i = cq // 2
                    for g in range(2):
                        rX = slice(0, 128) if cq % 2 == 0 else slice(64, 128)
                        rY = slice(0, 64) if cq % 2 == 0 else slice(0, 128)
                        ps_ = nxt("pa", psA)
                        q3 = qF[0:64, 4 * g:4 * g + 4, cq * 64:(cq + 1) * 64]
                        mm(ps_[:, 0:256].re("p (h t) -> p h t", h=4), kF[0:64, g, i * 128:(i + 1) * 128], q3)
                        mm(ps_[:, 256:512].re("p (h t) -> p h t", h=4), kF[0:64, g, (i + 1) * 128:(i + 2) * 128], q3)
                        e_ = E[it % 2]
                        bX = corev[:, 9:10] if (b == 0 and i == 0) else zero1
                        act(e_[:, 0:256], ps_[:, 0:256], AF.Exp, bias=bX)
                        act(e_[:, 256:512], ps_[:, 256:512], AF.Exp, bias=zero1)
                        po = nxt("pb", psB)
                        mm(po[0:64, 0:256], vT[rX, i, g * 64:(g + 1) * 64], e_[rX, 0:256], start=True, stop=False)
                        mm(po[0:64, 0:256], vT[rY, i + 1, g * 64:(g + 1) * 64], e_[rY, 256:512], start=False, stop=True)
                        mm(po[0:64, 256:512], onesb[rX, 0:64], e_[rX, 0:256], start=True, stop=False)
                        mm(po[0:64, 256:512], onesb[rY, 0:64], e_[rY, 256:512], start=False, stop=True)
                        d_ = dsm[it % 2]
                        tt("dve", d_[0:64, :], po[0:64, 256:512], esinkB[:, g * 256:(g + 1) * 256], ALU.add)
                        recip(d_[0:64, :], d_[0:64, :])
                        tt("dve", o_a[0:64, 4 * g:4 * g + 4, cq * 64:(cq + 1) * 64],
                           po[0:64, 0:256].re("p (h t) -> p h t", h=4), d_[0:64, :].re("p (h t) -> p h t", h=4), ALU.mult)
                        it += 1
                if last:
                    kct = [AR.alloc(f"kct{i}", [128]) for i in range(2)]
                    vct = [AR.alloc(f"vct{i}", [128]) for i in range(2)]
                    kc = [AR.alloc(f"kc{i}", [2, 128], BF16) for i in range(2)]
                    vc = [AR.alloc(f"vc{i}", [128], BF16) for i in range(2)]
                    for sq in range(4):
                        S.dma("sp", kct[sq % 2], dr["cache_k"][l][sq])
                        S.dma("sp", vct[sq % 2], dr["cache_v"][l][sq])
                        pb = nxt("pb", psB)
                        for g in range(2):
                            tr(pb[0:64, g * 128:(g + 1) * 128], kct[sq % 2][:, g * 64:(g + 1) * 64], ident)
                        cp("act", kc[sq % 2][0:64, :, :], pb[0:64, 0:256].re("p (g t) -> p g t", g=2))
                        cp("pool", vc[sq % 2], vct[sq % 2])
                        for g in range(2):
                            ps_ = nxt("pa", psA)
                            q3 = qF[0:64, 4 * g:4 * g + 4, TB + sq * 16:TB + (sq + 1) * 16]
                            mm(ps_[:, 0:64].re("p (h t) -> p h t", h=4), kc[sq % 2][0:64, g, :], q3)
                            mm(ps_[0:64, 64:128].re("p (h t) -> p h t", h=4), kFs[0:64, g, :], q3)
                            e_ = E[it % 2]
                            act(e_[:, 0:64], ps_[:, 0:64], AF.Exp, bias=zero1)
                            act(e_[0:64, 64:128], ps_[0:64, 64:128], AF.Exp, bias=seqmask[:, sq:sq + 1])
                            po = nxt("pb", psB)
                            mm(po[0:64, 0:64], vc[sq % 2][:, g * 64:(g + 1) * 64], e_[:, 0:64], start=True, stop=False)
                            mm(po[0:64, 0:64], vTs[0:64, g * 64:(g + 1) * 64], e_[0:64, 64:128], start=False, stop=True)
                            mm(po[0:64, 64:128], onesb[:, 0:64], e_[:, 0:64], start=True, stop=False)
                            mm(po[0:64, 64:128], onesb[0:64, 0:64], e_[0:64, 64:128], start=False, stop=True)
                            d_ = dsm[it % 2]
                            tt("dve", d_[0:64, 0:64].re("p (h t) -> p h t", h=4), po[0:64, 64:128].re("p (h t) -> p h t", h=4),
                               esinkB[:, g * 256:(g + 1) * 256].re("p (h t) -> p h t", h=4)[:, :, 0:16], ALU.add)
                            recip(d_[0:64, 0:64], d_[0:64, 0:64])
                            tt("dve", o_a[0:64, 4 * g:4 * g + 4, TB + sq * 16:TB + (sq + 1) * 16],
                               po[0:64, 0:64].re("p (h t) -> p h t", h=4), d_[0:64, 0:64].re("p (h t) -> p h t", h=4), ALU.mult)
                            it += 1
                cp("pool", kprev, kF[0:64, :, TB:TB + 128])
                cp("pool", vprev, vT[:, 4, :])
                S.fence()
            AR.off = mark
            AR.gen += 1
            mixacc = AR.alloc("mixacc", [8, n])
            mixbf = AR.alloc("mixbf", [8, n], BF16)
            gsb = [AR.alloc(f"gsb{i}", [512]) for i in range(2)]
            branches = (["A"] if stage >= 5 else []) + (["B"] if stage >= 6 else []) + (["C"] if stage >= 7 else [])
            if not branches:
                memset("pool", mixbf, 0.0)
            for bi_, br in enumerate(branches):
                brow = {"A": 0, "B": 512, "C": 1024}[br]
                for hf in range(2):
                    if br == "A":
                        wb = load_w(WB["w_branch"][l][brow:brow + 512, hf * 512:(hf + 1) * 512], 64, 8, 512)
                    else:
                        wb = load_w(WB["w_branch"][l][brow:brow + 512, hf * 512:(hf + 1) * 512], 128, 4, 512)
                    gcol = OFF["gt"] + bi_ * 0 + {"A": 0, "B": 1024, "C": 2048}[br] + hf * 512
                    wg = load_w(wi[:, gcol:gcol + 512], 128, 8, 512)
                    for q in range(4):
                        i = hf * 4 + q
                        for (a, ha, sn) in segs:
                            p1 = nxt("pa", psA)
                            p2 = nxt("pa", psA)
                            if br == "A":
                                for kt in range(8):
                                    mm(p1[:, 0:sn], wb[0:64, kt, q * 128:(q + 1) * 128], o_a[0:64, kt, ha:ha + sn], start=(kt == 0), stop=(kt == 7))
                            else:
                                osrc = o_bn if br == "B" else o_cn
                                for kt in range(4):
                                    mm(p1[:, 0:sn], wb[:, kt, q * 128:(q + 1) * 128], osrc[:, kt, ha:ha + sn], start=(kt == 0), stop=(kt == 3))
                            for kt in range(8):
                                mm(p2[:, 0:sn], wg[:, kt, q * 128:(q + 1) * 128], xbf[:, kt, a:a + sn], start=(kt == 0), stop=(kt == 7))
                            gs_ = gsb[(i + ha) % 2]
                            act(gs_[:, 0:sn], p2[:, 0:sn], AF.Sigmoid)
                            if bi_ == 0:
                                tt("dve", mixacc[:, i, ha:ha + sn], p1[:, 0:sn], gs_[:, 0:sn], ALU.mult)
                            else:
                                tt("dve", gs_[:, 0:sn], p1[:, 0:sn], gs_[:, 0:sn], ALU.mult)
                                tt("pool", mixacc[:, i, ha:ha + sn], mixacc[:, i, ha:ha + sn], gs_[:, 0:sn], ALU.add)
            if branches:
                cp("pool", mixbf, mixacc)
            for hf in range(2):
                wo = load_w(WB["w_out"][l][:, hf * 512:(hf + 1) * 512], 128, 8, 512)
                for q in range(4):
                    i = hf * 4 + q
                    for (a, ha, sn) in segs:
                        pa = nxt("pa", psA)
                        for kt in range(8):
                            mm(pa[:, 0:sn], wo[:, kt, q * 128:(q + 1) * 128], mixbf[:, kt, ha:ha + sn], start=(kt == 0), stop=(kt == 7))
                        stt(xres[:, i, a:a + sn], xres[:, i, a:a + sn], ALPHA, pa[:, 0:sn], ALU.mult, ALU.add)
            for (a, ha, sn) in segs:
                layer_norm(a, sn, PC["ln1g"], PC["ln1b"])
            S.fence()

        if stage < 3:
            continue
        AR.reset()
        tailx = AR.alloc("tailx", [8, 2])
        S.dma("sp", dr["ag3_in"].re("p (a b) -> p a b", b=2), xres[:, :, SEG - 2:SEG])
        S.fence()
        agi, ago = dr["ag3_in"].ap, dr["ag3_out"].ap
        ccop = S.cc(lambda e: e.collective_compute("AllGather", ALU.bypass, replica_groups=GROUPS,
                                                          ins=[agi], outs=[ago]), ["ag3_in"], ["ag3_out"])
        g4 = AR.alloc("g4", [4, 16])
        S.dma("sp", g4, dr["ag3_out"].re("(r p) c -> p r c", p=128))
        tx = tailx.re("p a b -> p (a b)")
        ts("dve", tx, g4[:, 0, :], corev[:, 1:2], None, ALU.mult)
        for j in range(1, 4):
            stt(tx, g4[:, j, :], corev[:, 1 + j:2 + j], tx, ALU.mult, ALU.add)
        tailb = AR.alloc("tailb", [8, 2], BF16)
        cp("dve", tailb, tailx)
        S.fence()
        keep = AR.off

        for b in range(NBLK if stage >= 4 else 0):
            AR.off = keep
            AR.gen += 1
            c0 = b * TB
            last = (b == NBLK - 1)
            h = AR.alloc("h", [22, TB + NS], BF16)
            ug = [AR.alloc(f"ug{i}", [2 + TB]) for i in range(2)]
            uv = [AR.alloc(f"uv{i}", [2 + TB]) for i in range(2)]
            cg = [AR.alloc(f"cg{i}", [TB]) for i in range(2)]
            cv = [AR.alloc(f"cv{i}", [TB]) for i in range(2)]
            usg = [AR.alloc(f"usg{i}", [4, 18]) for i in range(2)]
            usv = [AR.alloc(f"usv{i}", [4, 18]) for i in range(2)]
            csg = [AR.alloc(f"csg{i}", [4, 16]) for i in range(2)]
            csv = [AR.alloc(f"csv{i}", [4, 16]) for i in range(2)]
            sio = [(AR.alloc(f"sin{i}", [512]), AR.alloc(f"sout{i}", [512]), None) for i in range(2)] if last else None
            ucb = [AR.alloc(f"uc{i}", [16]) for i in range(2)]
            for jg in range(6):
                nj = 4 if jg < 5 else 2
                wg = load_w(WB["w_up"][l][:, jg * 512:jg * 512 + nj * 128], 128, 8, nj * 128)
                wv = load_w(WB["w_up"][l][:, D_FF + jg * 512:D_FF + jg * 512 + nj * 128], 128, 8, nj * 128)
                for jj in range(nj):
                    j = jg * 4 + jj
                    fw = PC["fcw"]
                    for half, (wt, ubuf, cbuf, usb, csb) in enumerate(((wg, ug, cg, usg, csg), (wv, uv, cv, usv, csv))):
                        jc = j + 22 * half
                        u = ubuf[j % 2]
                        cc_ = cbuf[j % 2]
                        pa = nxt("pa", psA)
                        for kt in range(8):
                            mm(pa[:, 0:TB], wt[:, kt, jj * 128:(jj + 1) * 128], xbf[:, kt, c0:c0 + TB],
                               start=(kt == 0), stop=(kt == 7))
                        if b == 0:
                            pb = nxt("pb", psB)
                            for kt in range(8):
                                mm(pb[:, 0:2], wt[:, kt, jj * 128:(jj + 1) * 128], tailb[:, kt, :],
                                   start=(kt == 0), stop=(kt == 7))
                            cp("dve", u[:, 0:2], pb[:, 0:2])
                        else:
                            cp("pool", u[:, 0:2], ffn_tail[:, jc, :])
                        cp("act", u[:, 2:2 + TB], pa[:, 0:TB])
                        cp("pool", ffn_tail[:, jc, :], u[:, TB:TB + 2])
                        wc = lambda t, jc=jc: par[:, fw + t * 44 + jc:fw + t * 44 + jc + 1]
                        ts("pool", cc_, u[:, 0:TB], wc(0), par[:, PC["fcb"] + jc:PC["fcb"] + jc + 1], ALU.mult, ALU.add)
                        stt(cc_, u[:, 1:TB + 1], wc(1), cc_, ALU.mult, ALU.add)
                        stt(cc_, u[:, 2:TB + 2], wc(2), cc_, ALU.mult, ALU.add)
                        if last:
                            us = usb[j % 2]
                            cs = csb[j % 2]
                            pb = nxt("pb", psB)
                            for kt in range(8):
                                mm(pb[:, 0:NS], wt[:, kt, jj * 128:(jj + 1) * 128], xbf[:, kt, SEG:SEG + NS],
                                   start=(kt == 0), stop=(kt == 7))
                            sin, sout, pout = sio[half]
                            if jj == 0:
                                S.dma("sp", sin[0:8, 0:nj * 128],
                                      V(dr["st_fconv"].ap[l].rearrange("s r c -> (s r) c")[:, jc * 128:(jc + nj) * 128], "RO"))
                            pb2 = nxt("pb", psB)
                            tr(pb2[:, 0:8], sin[0:8, jj * 128:(jj + 1) * 128], ident[0:8, 0:8])
                            cp("dve", us[:, :, 0:2], pb2[:, 0:8].re("p (s r) -> p s r", s=4))
                            cp("act", us[:, :, 2:18], pb[:, 0:NS].re("p (s t) -> p s t", s=4))
                            ts("pool", cs, us[:, :, 0:16], wc(0), par[:, PC["fcb"] + jc:PC["fcb"] + jc + 1], ALU.mult, ALU.add)
                            stt(cs, us[:, :, 1:17], wc(1), cs, ALU.mult, ALU.add)
                            stt(cs, us[:, :, 2:18], wc(2), cs, ALU.mult, ALU.add)
                            uc = ucb[(2 * j + half) % 2]
                            cp("pool", uc[:, 0:8].re("p (s r) -> p s r", s=4), us[:, :, 16:18])
                            cp("pool", uc[:, 8:10], u[:, TB:TB + 2])
                            pb3 = nxt("pb", psB)
                            tr(pb3[0:10, 0:128], uc[:, 0:10], ident)
                            cp("dve", sout[0:10, jj * 128:(jj + 1) * 128], pb3[0:10, 0:128])
                            if jj == nj - 1:
                                S.dma("sp", V(dr["s_fconv"].ap[l].rearrange("s r c -> (s r) c")[:, (jc - nj + 1) * 128:(jc + 1) * 128], "out_s_fconv"),
                                      sout[0:8, 0:nj * 128])
                                S.dma("sp", V(dr["p_fconv"].ap[l][:, (jc - nj + 1) * 128:(jc + 1) * 128], "out_p_fconv"),
                                      sout[8:10, 0:nj * 128])
                    act(cg[j % 2], cg[j % 2], AF.Silu)
                    tt("dve", h[:, j, 0:TB], cg[j % 2], cv[j % 2], ALU.mult)
                    if last:
                        act(csg[j % 2], csg[j % 2], AF.Silu)
                        tt("dve", h[:, j, TB:TB + NS].re("p (s t) -> p s t", s=4), csg[j % 2], csv[j % 2], ALU.mult)
            segs = [(c0, 0, TB)] + ([(SEG, TB, NS)] if last else [])
            for i2 in range(2):
                wd = [load_w(WB["w_down"][l][:, (i2 * 4 + q) * 128:(i2 * 4 + q + 1) * 128], 128, 22, 128) for q in range(1)]
                for q in range(4):
                    i = i2 * 4 + q
                    if q > 0:
                        wd = [load_w(WB["w_down"][l][:, i * 128:(i + 1) * 128], 128, 22, 128)]
                    for (a, ha, sn) in segs:
                        pa = nxt("pa", psA)
                        for kt in range(22):
                            mm(pa[:, 0:sn], wd[0][:, kt, :], h[:, kt, ha:ha + sn], start=(kt == 0), stop=(kt == 21))
                        stt(xres[:, i, a:a + sn], xres[:, i, a:a + sn], ALPHA, pa[:, 0:sn], ALU.mult, ALU.add)
            for (a, ha, sn) in segs:
                layer_norm(a, sn, PC["ln2g"], PC["ln2b"])
            S.fence()

    AR.reset()
    yo = [AR.alloc(f"yo{i}", [D]) for i in range(2)]
    for ti, (src, r0, nr, c0) in enumerate(tiles):
        y = yo[ti % 2]
        for dq in range(2):
            pb = nxt("pb", psB)
            for q in range(4):
                dt_ = dq * 4 + q
                tr(pb[0:nr, q * 128:(q + 1) * 128], xres[:, dt_, c0:c0 + nr], ident)
            cp("act" if dq % 2 else "dve", y[0:nr, dq * 512:(dq + 1) * 512], pb[0:nr, :])
        dst = dr["y_p"][r0:r0 + nr, :] if src == "xp" else dr["y_s"][0:nr, :]
        S.dma("sp", dst, y[0:nr, :])
    S.fence()
    S.emit(nc, es)
    es.close()
    return nc


_NC_CACHE = {}


def _core_inputs(inp, c):
    b, p = c // 4, c % 4
    m = {}
    m["xp"] = np.ascontiguousarray(inp["x_prompt"][b, p * SEG:(p + 1) * SEG])
    m["xs"] = np.ascontiguousarray(inp["x_sample"][4 * c:4 * c + 4]).reshape(NS, D)
    m["cache_k"] = np.ascontiguousarray(inp["cache_swa_k"][:, 4 * c:4 * c + 4]).reshape(DEPTH, 4, 128, 128)
    m["cache_v"] = np.ascontiguousarray(inp["cache_swa_v"][:, 4 * c:4 * c + 4]).reshape(DEPTH, 4, 128, 128)
    m["st_hgrn"] = np.ascontiguousarray(inp["state_hgrn"][:, 4 * c:4 * c + 4])
    m["st_gdn"] = np.ascontiguousarray(inp["state_gdn"][:, 4 * c:4 * c + 4])
    m["st_gconv"] = np.ascontiguousarray(inp["state_gdn_conv"][:, 4 * c:4 * c + 4])
    m["st_fconv"] = np.ascontiguousarray(inp["state_ffn_conv"][:, 4 * c:4 * c + 4])
    for k in ("ln_in_g", "ln_in_b", "w_in", "attn_sinks", "hgrn_lb_logits", "hgrn_norm_w", "gdn_conv_w",
              "gdn_a_log", "gdn_dt_bias", "gdn_norm_w", "w_branch", "w_out", "ln1_g", "ln1_b", "w_up",
              "ffn_conv_w", "ffn_conv_b", "w_down", "ln2_g", "ln2_b"):
        m[k] = np.ascontiguousarray(inp[k], dtype=np.float32)
    m["consts"] = CONSTS
    cv = np.zeros((128, 16), np.float32)
    cv[:, 0] = 1.0 if p > 0 else 0.0
    if p > 0:
        cv[:, 1 + (p - 1)] = 1.0
    cv[:, 5 + p] = 1.0
    cv[:, 9] = 0.0 if p > 0 else NEGM
    m["corev"] = cv
    return m


def run_cores(inp, nlayers=DEPTH, stage=99):
    key = (nlayers, stage)
    if key not in _NC_CACHE:
        _NC_CACHE[key] = build_nc(stage=stage, nlayers=nlayers)
    nc = _NC_CACHE[key]
    in_maps = [_core_inputs(inp, c) for c in range(8)]
    if nlayers < DEPTH:
        for m in in_maps:
            for n, shp in DRAM_IN:
                if shp[0] == DEPTH and n not in ('hgrn_lb_logits', 'gdn_dt_bias', 'gdn_a_log'):
                    m[n] = np.ascontiguousarray(m[n][:nlayers])
    res = run_bass_kernel_spmd(nc, in_maps, core_ids=list(range(8)))
    return res.results


def kernel(**inp):
    inp = {k: np.asarray(v) for k, v in inp.items()}
    r = run_cores(inp)
    y_p = np.stack([np.concatenate([r[b * 4 + p]["y_p"] for p in range(4)], 0) for b in range(2)], 0)
    y_s = np.concatenate([r[c]["y_s"].reshape(4, 16, D) for c in range(8)], 0)

    def pl(name, shape):
        return np.stack([r[3][name], r[7][name]], 1).reshape(shape)

    def sl(name, shape):
        return np.concatenate([r[c][name] for c in range(8)], 1).reshape(shape)

    outs = (y_p, y_s,
            pl("p_swa_k", (DEPTH, 2, 128, 2, 64)), pl("p_swa_v", (DEPTH, 2, 128, 2, 64)),
            pl("p_hgrn", (DEPTH, 2, 4, 128, 128)), pl("p_gdn", (DEPTH, 2, 4, 128, 128)),
            pl("p_gconv", (DEPTH, 2, 3, C_QKV)), pl("p_fconv", (DEPTH, 2, 2, 2 * D_FF)),
            sl("s_swa_k", (DEPTH, 32, 128, 2, 64)), sl("s_swa_v", (DEPTH, 32, 128, 2, 64)),
            sl("s_hgrn", (DEPTH, 32, 4, 128, 128)), sl("s_gdn", (DEPTH, 32, 4, 128, 128)),
            sl("s_gconv", (DEPTH, 32, 3, C_QKV)), sl("s_fconv", (DEPTH, 32, 2, 2 * D_FF)))
    return tuple(np.ascontiguousarray(o, dtype=np.float32) for o in outs)
```

```python
import numpy as np
from contextlib import ExitStack
import concourse.bass as bass
import concourse.mybir as mybir
from concourse.bass_utils import run_bass_kernel_spmd

F32 = mybir.dt.float32
BF16 = mybir.dt.bfloat16
AF = mybir.ActivationFunctionType
ALU = mybir.AluOpType

D = 1024
DEPTH = 4
SEG = 2048
NS = 64
NTOK = SEG + NS
TB = 512
NBLK = SEG // TB
A_HEADS, A_KV, A_HD = 8, 2, 64
D_FF = 2816
C_QKV = 1536
D_IN = 7944
OFF = {}
_o = 0
for _n, _w in (("aq", 512), ("ak", 128), ("av", 128), ("bq", 512), ("bf", 512), ("bi", 512), ("bg", 512),
               ("cqkv", 1536), ("cz", 512), ("cb", 4), ("ca", 4), ("gt", 3072)):
    OFF[_n] = _o
    _o += _w
ALPHA = (2 * DEPTH) ** 0.25
LN_EPS = 1e-5
RMS_EPS = 1e-6
NEGM = -30000.0


class V:
    __slots__ = ("ap", "key")

    def __init__(self, ap, key):
        self.ap = ap
        self.key = key

    def __getitem__(self, idx):
        return V(self.ap[idx], self.key)

    def k(self, key):
        return V(self.ap, key)

    def re(self, pat, **kw):
        return V(self.ap.rearrange(pat, **kw), self.key)

    def bc(self, shape):
        return V(self.ap.to_broadcast(shape), self.key)


class Op:
    __slots__ = ("eng", "kind", "fn", "deps", "idx", "inc", "sem", "val", "gid")


class Sched:
    ENGS = ("pe", "act", "dve", "pool", "sp")
    NDMA = 20

    def __init__(self):
        self.ops = {e: [] for e in self.ENGS}
        self.res = {}
        self.dma_slots = {e: [None] * self.NDMA for e in ("sp", "act", "pool")}
        self.dma_cnt = {e: [0] * self.NDMA for e in ("sp", "act", "pool")}
        self.dma_rr = {e: 0 for e in ("sp", "act", "pool")}
        self.all_dma = []
        self.gid = 0
        self.pending_dma = []
        self.ccs = []
        self.bg = {}

    def _rec(self, eng, kind, fn, reads, writes, extra_deps=()):
        op = Op()
        op.eng, op.kind, op.fn = eng, kind, fn
        op.inc = False
        op.sem = None
        op.val = 0
        op.gid = self.gid
        self.gid += 1
        deps = []
        for r in reads:
            if r in self.bg:
                deps.append(self.bg[r])
            st = self.res.get(r)
            if st and st[0] is not None:
                deps.append(st[0])
        for w in writes:
            st = self.res.get(w)
            if st:
                if st[0] is not None:
                    deps.append(st[0])
                deps.extend(st[1])
        deps.extend(extra_deps)
        for r in reads:
            st = self.res.setdefault(r, [None, []])
            st[1].append(op)
        for w in writes:
            self.res[w] = [op, []]
        seen = set()
        dd = []
        for d in deps:
            if d is op or id(d) in seen:
                continue
            seen.add(id(d))
            if d.kind == "c" and kind == "c" and d.eng == "pe" and eng == "pe":
                continue
            dd.append(d)
        op.deps = dd
        op.idx = len(self.ops[eng])
        self.ops[eng].append(op)
        return op

    def c(self, eng, fn, reads, writes):
        return self._rec(eng, "c", fn, reads, writes)

    def cc(self, fn, reads, writes):
        op = self._rec("pool", "x", fn, reads, writes)
        op.sem = ("cc", len(self.ccs))
        op.val = 1
        self.ccs.append(op)
        self.pending_dma.append(op)
        return op

    def dma(self, q, out, in_, bg=False, **kw):
        slot = self.dma_rr[q]
        self.dma_rr[q] = (slot + 1) % self.NDMA
        prev = self.dma_slots[q][slot]
        extra = [prev] if prev is not None else []
        o, i = out.ap, in_.ap
        op = self._rec(q, "d", lambda e: e.dma_start(out=o, in_=i, **kw), [in_.key], [out.key], extra)
        self.dma_cnt[q][slot] += 1
        op.sem = (q, slot)
        op.val = 16 * self.dma_cnt[q][slot]
        self.dma_slots[q][slot] = op
        self.all_dma.append(op)
        if bg:
            self.bg[out.key] = op
        else:
            self.pending_dma.append(op)
        return op

    def fence(self):
        lasts = [self.ops[e][-1] for e in self.ENGS if self.ops[e] and self.ops[e][-1].kind != "d"]
        for e in self.ENGS:
            pass
        lastc = []
        for e in self.ENGS:
            for o in reversed(self.ops[e]):
                if o.kind == "c" and o.fn is not None:
                    lastc.append(o)
                    break
                if o.fn is None:
                    break
        pend = list(self.pending_dma)
        self.pending_dma = []
        for e in self.ENGS:
            deps = list(lastc) + pend
            op = Op()
            op.eng, op.kind, op.fn = e, "c", None
            op.inc, op.sem, op.val, op.gid = False, None, 0, self.gid
            self.gid += 1
            op.deps = deps
            op.idx = len(self.ops[e])
            self.ops[e].append(op)
        self.res = {}

    def emit(self, nc, es):
        CH = 20000
        for e in self.ENGS:
            for op in self.ops[e]:
                for d in op.deps:
                    d.inc = True
        self.csems = {}
        for e in self.ENGS:
            n = 0
            for op in self.ops[e]:
                if op.kind == "c" and op.inc:
                    op.sem = (e, n // CH)
                    op.val = n % CH + 1
                    n += 1
            nsem = max(1, (n + CH - 1) // CH)
            self.csems[e] = [es.enter_context(nc.semaphore(f"c_{e}_{j}")) for j in range(nsem)]
        self.dsems = {q: [es.enter_context(nc.semaphore(f"d_{q}_{j}")) for j in range(self.NDMA)]
                      for q in ("sp", "act", "pool")}
        block = es.enter_context(nc.Block())
        engmap = {"pe": block.tensor, "act": block.scalar, "dve": block.vector, "pool": block.gpsimd, "sp": block.sync}

        self.xsems = [es.enter_context(nc.semaphore(f"x_{j}")) for j in range(len(self.ccs))]

        def semof(op):
            if op.kind == "d":
                return self.dsems[op.sem[0]][op.sem[1]]
            if op.kind == "x":
                return self.xsems[op.sem[1]]
            return self.csems[op.sem[0]][op.sem[1]]

        def make(e):
            oplist = self.ops[e]

            def body(eng):
                waited = {}
                for op in oplist:
                    for d in op.deps:
                        key = (d.kind, d.sem)
                        if waited.get(key, 0) >= d.val:
                            continue
                        eng.wait_ge(semof(d), d.val)
                        waited[key] = d.val
                    if op.fn is None:
                        continue
                    ins = op.fn(eng)
                    if op.kind == "d":
                        ins.then_inc(semof(op), 16)
                    elif op.kind == "x":
                        ins.then_inc(semof(op), 1)
                    elif op.inc:
                        ins.then_inc(semof(op), 1)
            return body

        for e in self.ENGS:
            engmap[e](make(e))


S = None


def kof(*vs):
    return [v.key for v in vs if isinstance(v, V)]


def apof(x):
    return x.ap if isinstance(x, V) else x


def mm(out, lhsT, rhs, start=True, stop=True):
    o, l, r = out.ap, lhsT.ap, rhs.ap
    return S.c("pe", lambda e: e.matmul(o, l, r, start=start, stop=stop), kof(lhsT, rhs), kof(out))


def tr(out, in_, ident):
    o, i, d = out.ap, in_.ap, ident.ap
    return S.c("pe", lambda e: e.transpose(o, i, d), kof(in_, ident), kof(out))


def act(out, in_, func, bias=None, scale=None):
    kw = {}
    if bias is not None:
        kw["bias"] = apof(bias)
    if scale is not None:
        kw["scale"] = apof(scale)
    o, i = out.ap, in_.ap
    return S.c("act", lambda e: e.activation(out=o, in_=i, func=func, **kw), kof(in_, bias, scale), kof(out))


def ts(eng, out, in0, s1, s2, op0, op1=None):
    o, i = out.ap, in0.ap
    a1, a2 = apof(s1), apof(s2)
    if op1 is None:
        return S.c(eng, lambda e: e.tensor_scalar(o, i, a1, None, op0), kof(in0, s1), kof(out))
    return S.c(eng, lambda e: e.tensor_scalar(o, i, a1, a2, op0, op1), kof(in0, s1, s2), kof(out))


def tt(eng, out, in0, in1, op):
    o, a, b = out.ap, in0.ap, in1.ap
    return S.c(eng, lambda e: e.tensor_tensor(out=o, in0=a, in1=b, op=op), kof(in0, in1), kof(out))


def stt(out, in0, scalar, in1, op0, op1):
    o, a, b = out.ap, in0.ap, in1.ap
    sc = apof(scalar)
    return S.c("dve", lambda e: e.scalar_tensor_tensor(out=o, in0=a, scalar=sc, in1=b, op0=op0, op1=op1),
               kof(in0, scalar, in1), kof(out))


def cp(eng, out, in_):
    o, i = out.ap, in_.ap
    if eng == "act":
        return S.c("act", lambda e: e.activation(out=o, in_=i, func=AF.Copy), kof(in_), kof(out))
    return S.c(eng, lambda e: e.tensor_copy(o, i), kof(in_), kof(out))


def recip(out, in_):
    o, i = out.ap, in_.ap
    return S.c("dve", lambda e: e.reciprocal(o, i), kof(in_), kof(out))


def memset(eng, out, val):
    o = out.ap
    return S.c(eng, lambda e: e.memset(o, val), [], kof(out))


def make_consts():
    cols = {}
    parts = []
    pos = [0]

    def add(name, a):
        a = np.asarray(a, np.float32)
        if a.shape[0] < 128:
            a = np.concatenate([a, np.zeros((128 - a.shape[0], a.shape[1]), np.float32)], 0)
        cols[name] = (pos[0], a.shape[1])
        parts.append(a)
        pos[0] += a.shape[1]

    i = np.arange(128)
    add("ident", np.eye(128))
    add("ones", np.ones((128, 128)))
    for tag, T, blk in (("p", 128, 128), ("s", 64, 16)):
        s = np.arange(T)[:, None]
        t = np.arange(T)[None, :]
        same = (s // blk) == (t // blk)
        add(f"triLE_{tag}", (same & (s <= t)))
        add(f"triGT_{tag}", (same & (s > t)))
        add(f"mbias_{tag}", np.where(same & (s <= t), 0.0, NEGM))
        add(f"offd_{tag}", (same & (s < t)))
    for tag, T, blk in (("p", 128, 64), ("s", 64, 16)):
        nb = T // blk
        sidx = np.arange(T)[:, None]
        tidx = np.arange(T)[None, :]
        same = (sidx // blk) == (tidx // blk)
        mid = (tidx // blk) * blk + blk // 2 - 1
        drel = (same & (sidx <= tidx)).astype(np.float32) - (same & (sidx <= mid)).astype(np.float32)
        tot = np.stack([(np.arange(T) // blk == b_) for b_ in range(nb)], 1).astype(np.float32)
        midc = np.stack([((np.arange(T) // blk == b_) & (np.arange(T) <= b_ * blk + blk // 2 - 1)) for b_ in range(nb)], 1).astype(np.float32)
        add(f"hdext_{tag}", np.concatenate([drel, tot, midc], 1))
        add(f"hle_{tag}", (same & (sidx <= tidx)))
        add(f"hgt_{tag}", (same & (sidx > tidx)))
        add(f"blk_{tag}", tot)
    sm = np.full((64, 4), NEGM, np.float32)
    for q in range(4):
        sm[16 * q:16 * q + 16, q] = 0.0
    add("seqmask", sm)
    return np.concatenate(parts, 1), cols


CONSTS, CCOL = make_consts()
NCON = CONSTS.shape[1]


class Arena:
    def __init__(self, ap, ncols32):
        self.ap = ap
        self.n = ncols32
        self.off = 0
        self.gen = 0

    def reset(self):
        self.off = 0
        self.gen += 1

    def alloc(self, name, shape, dtype=F32, parts=128):
        free = int(np.prod(shape))
        n32 = free if dtype == F32 else (free + 1) // 2
        self.off = (self.off + 31) // 32 * 32
        assert self.off + n32 <= self.n, (name, self.off, n32, self.n)
        a = self.ap[0:parts, self.off:self.off + n32]
        self.off += n32
        if dtype != F32:
            a = a.bitcast(dtype)[:, 0:free]
        if len(shape) == 2:
            a = a.rearrange("p (a b) -> p a b", a=shape[0])
        elif len(shape) == 3:
            a = a.rearrange("p (a b c) -> p a b c", a=shape[0], b=shape[1])
        return V(a, f"{name}#{self.gen}")


DRAM_IN = [
    ("xp", [SEG, D]), ("xs", [NS, D]),
    ("cache_k", [DEPTH, 4, 128, 128]), ("cache_v", [DEPTH, 4, 128, 128]),
    ("st_hgrn", [DEPTH, 4, 4, 128, 128]), ("st_gdn", [DEPTH, 4, 4, 128, 128]),
    ("st_gconv", [DEPTH, 4, 3, C_QKV]), ("st_fconv", [DEPTH, 4, 2, 2 * D_FF]),
    ("ln_in_g", [D]), ("ln_in_b", [D]), ("w_in", [DEPTH, D, D_IN]), ("attn_sinks", [DEPTH, 8]),
    ("hgrn_lb_logits", [DEPTH, 512]), ("hgrn_norm_w", [DEPTH, 128]), ("gdn_conv_w", [DEPTH, 4, C_QKV]),
    ("gdn_a_log", [DEPTH, 4]), ("gdn_dt_bias", [DEPTH, 4]), ("gdn_norm_w", [DEPTH, 128]),
    ("w_branch", [DEPTH, 1536, D]), ("w_out", [DEPTH, D, D]), ("ln1_g", [DEPTH, D]), ("ln1_b", [DEPTH, D]),
    ("w_up", [DEPTH, D, 2 * D_FF]), ("ffn_conv_w", [DEPTH, 3, 2 * D_FF]), ("ffn_conv_b", [DEPTH, 2 * D_FF]),
    ("w_down", [DEPTH, D_FF, D]), ("ln2_g", [DEPTH, D]), ("ln2_b", [DEPTH, D]),
    ("consts", [128, NCON]), ("corev", [128, 16]),
]
DRAM_OUT = [
    ("y_p", [SEG, D]), ("y_s", [NS, D]),
    ("p_swa_k", [DEPTH, 128, 128]), ("p_swa_v", [DEPTH, 128, 128]),
    ("p_hgrn", [DEPTH, 4, 128, 128]), ("p_gdn", [DEPTH, 4, 128, 128]),
    ("p_gconv", [DEPTH, 3, C_QKV]), ("p_fconv", [DEPTH, 2, 2 * D_FF]),
    ("s_swa_k", [DEPTH, 4, 128, 128]), ("s_swa_v", [DEPTH, 4, 128, 128]),
    ("s_hgrn", [DEPTH, 4, 4, 128, 128]), ("s_gdn", [DEPTH, 4, 4, 128, 128]),
    ("s_gconv", [DEPTH, 4, 3, C_QKV]), ("s_fconv", [DEPTH, 4, 2, 2 * D_FF]),
]
GROUPS = [[0, 1, 2, 3], [4, 5, 6, 7]]
NW = 3
WSLOT = 4096
ARENA32 = 13900


def build_nc(stage=99, nlayers=DEPTH):
    global S
    S = Sched()
    nc = bass.Bass("TRN2", target_bir_lowering=False)
    es = ExitStack()
    dr = {}
    for n, shp in DRAM_IN:
        if shp[0] == DEPTH and n not in ('hgrn_lb_logits', 'gdn_dt_bias', 'gdn_a_log'):
            shp = [nlayers] + list(shp[1:])
        dr[n] = V(nc.dram_tensor(n, shp, F32, kind="ExternalInput").ap(), "RO")
    for n, shp in DRAM_OUT:
        dr[n] = V(nc.dram_tensor(n, shp, F32, kind="ExternalOutput").ap(), "out_" + n)
    AG3W = D * 2
    AG1W = 256 + 36
    dr["ag1_in"] = V(nc.dram_tensor("ag1_in", [128, AG1W], F32).ap(), "ag1_in")
    dr["ag1_out"] = V(nc.dram_tensor("ag1_out", [512, AG1W], F32).ap(), "ag1_out")
    AG2W = 516 + 1024
    dr["ag2_in"] = V(nc.dram_tensor("ag2_in", [128, AG2W], F32).ap(), "ag2_in")
    dr["ag2_out"] = V(nc.dram_tensor("ag2_out", [512, AG2W], F32).ap(), "ag2_out")
    dr["ob_loc"] = V(nc.dram_tensor("ob_loc", [128, 4, NTOK], BF16).ap(), "ob_loc")
    dr["qh"] = V(nc.dram_tensor("qh", [128, 4, SEG], BF16).ap(), "qh")
    dr["oc_loc"] = V(nc.dram_tensor("oc_loc", [128, 4, NTOK], BF16).ap(), "oc_loc")
    dr["rh"] = V(nc.dram_tensor("rh", [128, 4, SEG], BF16).ap(), "rh")
    dr["ag3_in"] = V(nc.dram_tensor("ag3_in", [128, 16], F32).ap(), "ag3_in")
    dr["ag3_out"] = V(nc.dram_tensor("ag3_out", [512, 16], F32).ap(), "ag3_out")

    def sb(name, shape, dt=F32):
        return V(es.enter_context(nc.sbuf_tensor(name, shape, dt))[:], name)

    WSH = {"w_in": (D, D_IN), "w_branch": (1536, D), "w_out": (D, D), "w_up": (D, 2 * D_FF), "w_down": (D_FF, D)}
    WB = {}
    for wn, (r_, c_) in WSH.items():
        t_ = nc.dram_tensor("wb_" + wn, [nlayers, r_, c_], BF16).ap()
        WB[wn] = [V(t_[l_], f"wb_{wn}_{l_}") for l_ in range(nlayers)]

    def convert_weights(l_):
        for wn, (r_, c_) in WSH.items():
            step = 256 if r_ % 256 == 0 else 128
            step = r_ // max(1, min(4, r_ // step))
            for r0 in range(0, r_, step):
                r1 = min(r_, r0 + step)
                S.dma("pool", V(WB[wn][l_].ap[r0:r1, :], f"wb_{wn}_{l_}_{r0}"), V(dr[wn].ap[l_][r0:r1, :], "RO"), bg=True)

    xres = sb("xres", [128, 8, NTOK])
    xbf = sb("xbf", [128, 8, NTOK], BF16)
    con = sb("con", [128, NCON])
    conb = sb("conb", [128, 256], BF16)
    corev = sb("corev_sb", [128, 16])
    wsl = [sb(f"w{i}", [128, WSLOT], BF16) for i in range(NW)]
    par = sb("par", [128, 512])
    arena_t = sb("arena", [128, ARENA32])
    AR = Arena(arena_t.ap, ARENA32)
    ln_sq4 = sb("ln_sq4", [128, 4, 512])
    ln_mean = sb("ln_mean", [128, 512])
    ln_rstd = sb("ln_rstd", [128, 512])
    ffn_tail = sb("ffn_tail", [128, 44, 2])
    kprev = sb("kprev", [64, 2, 128], BF16)
    vprev = sb("vprev", [128, 128], BF16)
    gc_prev = sb("gc_prev", [128, 12, 3])
    esinkB = sb("esinkB", [64, 4 * 64 * 2])
    zero1 = sb("zero1", [128, 1])
    eps1 = sb("eps1", [128, 1])
    hS = sb("hS", [128, 4, 128])
    hpre = sb("hpre", [128, 4])
    lbF = sb("lbF", [128, 4, 4])
    sinB = sb("sinB", [128, 4, 128], BF16)
    sinC = sb("sinC", [128, 4, 128], BF16)
    gpar = sb("gpar", [128, 64])
    nrmw = sb("nrmw", [128, 2])
    psA = [V(es.enter_context(nc.psum_tensor(f"psA{i}", [128, 512], F32))[:], f"psA{i}") for i in range(4)]
    psB_t = [es.enter_context(nc.psum_tensor(f"psB{i}", [128, 512], F32)) for i in range(4)]
    psB = [V(psB_t[i][:], f"psB{i}") for i in range(4)]
    rr = {"w": 0, "pa": 0, "pb": 0, "stg": 0}

    def nxt(kind, lst):
        v = lst[rr[kind] % len(lst)]
        rr[kind] += 1
        return v

    def C(name, parts=128):
        c0, n = CCOL[name]
        return con[0:parts, c0:c0 + n]

    ident = C("ident")
    ones32 = C("ones")
    identb = conb[:, 0:128]
    onesb = conb[:, 128:256]

    S.dma("sp", con, dr["consts"])
    S.dma("sp", corev, dr["corev"])
    convert_weights(0)
    cp("dve", identb, ident)
    cp("dve", onesb, ones32)
    memset("pool", zero1, 0.0)
    memset("pool", eps1, RMS_EPS)
    seqmask = C("seqmask", 64)

    def load_w(src2d, kp, kt, ncols, q="pool"):
        slot = nxt("w", wsl)
        assert kt * ncols <= WSLOT
        v = V(slot.ap[0:kp, 0:kt * ncols].rearrange("p (k m) -> p k m", k=kt), slot.key)
        extra = [o for k_, o in S.bg.items() if k_.startswith(src2d.key + "_")]
        op = S.dma("sp", v, V(src2d.ap.rearrange("(k p) m -> p k m", p=kp), "RO"))
        have = set(id(d) for d in op.deps)
        op.deps.extend(o for o in extra if id(o) not in have)
        return v

    def layer_norm(c0, n, gcol, bcol):
        mean, rstd = ln_mean, ln_rstd
        sq4 = ln_sq4
        for s0 in range(0, n, 512):
            sn = min(512, n - s0)
            a, b = c0 + s0, c0 + s0 + sn
            p1 = nxt("pa", psA)
            p2 = nxt("pa", psA)
            for dt_ in range(8):
                mm(p1[:, 0:sn], ones32, xres[:, dt_, a:b], start=(dt_ == 0), stop=(dt_ == 7))
            for hf in range(2):
                act(sq4[:, :, 0:sn], xres[:, hf * 4:hf * 4 + 4, a:b], AF.Square)
                for q in range(4):
                    mm(p2[:, 0:sn], ones32, sq4[:, q, 0:sn], start=(hf == 0 and q == 0), stop=(hf == 1 and q == 3))
            act(mean[:, 0:sn], p1[:, 0:sn], AF.Copy, scale=1.0 / D)
            tt("pool", rstd[:, 0:sn], mean[:, 0:sn], mean[:, 0:sn], ALU.mult)
            stt(rstd[:, 0:sn], p2[:, 0:sn], 1.0 / D, rstd[:, 0:sn], ALU.mult, ALU.subtract)
            ts("dve", rstd[:, 0:sn], rstd[:, 0:sn], LN_EPS, None, ALU.add)
            act(rstd[:, 0:sn], rstd[:, 0:sn], AF.Sqrt)
            recip(rstd[:, 0:sn], rstd[:, 0:sn])
            xv = xres[:, :, a:b]
            mb = mean[:, 0:sn].re("p (o t) -> p o t", o=1).bc([128, 8, sn])
            rb = rstd[:, 0:sn].re("p (o t) -> p o t", o=1).bc([128, 8, sn])
            gb = par[:, gcol:gcol + 8].re("p (d o) -> p d o", o=1).bc([128, 8, sn])
            bb_ = par[:, bcol:bcol + 8].re("p (d o) -> p d o", o=1).bc([128, 8, sn])
            tt("dve", xv, xv, mb, ALU.subtract)
            tt("pool", xv, xv, rb, ALU.mult)
            tt("dve", xv, xv, gb, ALU.mult)
            tt("pool", xv, xv, bb_, ALU.add)
            cp("act", xbf[:, :, a:b], xv)

    stg = [sb(f"stg{i}", [128, 128]) for i in range(2)]

    def load_cols(col, src1d, t):
        st_ = nxt("stg", stg)
        S.dma("sp", st_[0:t, :], V(src1d.ap.rearrange("(t p) -> t p", p=128), "RO"))
        pb = nxt("pb", psB)
        tr(pb[:, 0:t], st_[0:t, :], ident[0:t, 0:t])
        cp("dve", par[:, col:col + t], pb[:, 0:t])

    def load_vec8(col, src1d):
        load_cols(col, src1d, 8)

    if stage == -1:
        AR.reset()
        t0 = AR.alloc("t0", [D])
        S.dma("sp", t0[0:NS, :], dr["xs"])
        S.dma("sp", dr["y_s"], t0[0:NS, :])
        S.fence()
        S.emit(nc, es)
        es.close()
        return nc
    load_vec8(0, dr["ln_in_g"])
    load_vec8(8, dr["ln_in_b"])
    AR.reset()
    xin = [AR.alloc(f"xin{i}", [D]) for i in range(2)]
    tiles = [("xp", i * 128, 128, i * 128) for i in range(SEG // 128)] + [("xs", 0, NS, SEG)]
    for ti, (src, r0, nr, c0) in enumerate(tiles):
        xi = xin[ti % 2]
        S.dma("sp", xi[0:nr, :], dr[src][r0:r0 + nr, :])
        for dq in range(2):
            pb = nxt("pb", psB)
            for q in range(4):
                dt_ = dq * 4 + q
                tr(pb[:, q * 128:q * 128 + nr], xi[0:nr, dt_ * 128:(dt_ + 1) * 128], ident[0:nr, 0:nr])
            cp("act" if dq % 2 else "dve", xres[:, dq * 4:dq * 4 + 4, c0:c0 + nr],
               pb.re("p (q t) -> p q t", q=4)[:, :, 0:nr])
    S.fence()
    AR.reset()
    if stage >= 1:
        layer_norm(0, NTOK, 0, 8)
    S.fence()

    PC = {"ln1g": 0, "ln1b": 8, "ln2g": 16, "ln2b": 24, "fcw": 32, "fcb": 32 + 132}

    for l in range(nlayers if stage >= 2 else 0):
        load_vec8(PC["ln1g"], dr["ln1_g"][l])
        load_vec8(PC["ln1b"], dr["ln1_b"][l])
        load_vec8(PC["ln2g"], dr["ln2_g"][l])
        load_vec8(PC["ln2b"], dr["ln2_b"][l])
        for tap in range(3):
            load_cols(PC["fcw"] + tap * 44, dr["ffn_conv_w"][l][tap], 44)
        load_cols(PC["fcb"], dr["ffn_conv_b"][l], 44)
        S.fence()

        wi = WB["w_in"][l]
        if l + 1 < nlayers:
            convert_weights(l + 1)
        if stage >= 5:
            AR.reset()
            wkv = load_w(wi[:, OFF["ak"]:OFF["ak"] + 256], 128, 8, 256)
            kvl = AR.alloc("kvl", [AG1W])
            kvs = AR.alloc("kvs", [256])
            pa = nxt("pa", psA)
            for kt in range(8):
                mm(pa[:, 0:256], xbf[:, kt, SEG - 128:SEG], wkv[:, kt, :], start=(kt == 0), stop=(kt == 7))
            cp("act", kvl[:, 0:256], pa[:, 0:256])
            pa = nxt("pa", psA)
            for kt in range(8):
                mm(pa[0:NS, 0:256], xbf[:, kt, SEG:SEG + NS], wkv[:, kt, :], start=(kt == 0), stop=(kt == 7))
            cp("act", kvs[0:NS, :], pa[0:NS, 0:256])
            S.dma("sp", dr["p_swa_k"][l], kvl[:, 0:128])
            S.dma("sp", dr["p_swa_v"][l], kvl[:, 128:256])
            for sq in range(4):
                S.dma("sp", dr["s_swa_k"][l][sq, 112:128, :], kvs[16 * sq:16 * sq + 16, 0:128])
                S.dma("sp", dr["s_swa_v"][l][sq, 112:128, :], kvs[16 * sq:16 * sq + 16, 128:256])
            S.dma("sp", dr["s_swa_k"][l][:, 0:112, :], dr["cache_k"][l][:, 16:128, :])
            S.dma("sp", dr["s_swa_v"][l][:, 0:112, :], dr["cache_v"][l][:, 16:128, :])
            gt3 = kvl[:, 256:AG1W].re("p (j t) -> p j t", t=3)
            for jg in range(3):
                wc_ = load_w(wi[:, OFF["cqkv"] + jg * 512:OFF["cqkv"] + (jg + 1) * 512], 128, 8, 512)
                pb = nxt("pb", psB)
                for jj in range(4):
                    for kt in range(8):
                        mm(pb[:, jj * 4:jj * 4 + 3], wc_[:, kt, jj * 128:(jj + 1) * 128], xbf[:, kt, SEG - 3:SEG],
                           start=(kt == 0), stop=(kt == 7))
                cp("dve", gt3[:, jg * 4:jg * 4 + 4, :], pb[:, 0:16].re("p (j t) -> p j t", t=4)[:, :, 0:3])
            gct = AR.alloc("gct", [C_QKV])
            for jq in range(3):
                pb = nxt("pb", psB)
                for jj in range(4):
                    tr(pb[0:3, jj * 128:(jj + 1) * 128], gt3[:, jq * 4 + jj, :], ident)
                cp("act", gct[0:3, jq * 512:(jq + 1) * 512], pb[0:3, :])
            S.dma("sp", dr["p_gconv"][l], gct[0:3, :])
            S.dma("sp", dr["ag1_in"], kvl)
            S.fence()
            a1i, a1o = dr["ag1_in"].ap, dr["ag1_out"].ap
            S.cc(lambda e: e.collective_compute("AllGather", ALU.bypass, replica_groups=GROUPS,
                                                       ins=[a1i], outs=[a1o]), ["ag1_in"], ["ag1_out"])
            g1 = AR.alloc("g1", [4, AG1W])
            S.dma("sp", g1, dr["ag1_out"].re("(r p) c -> p r c", p=128))
            hsel = AR.alloc("hsel", [AG1W])
            ts("dve", hsel, g1[:, 0, :], corev[:, 1:2], None, ALU.mult)
            for j in range(1, 4):
                stt(hsel, g1[:, j, :], corev[:, 1 + j:2 + j], hsel, ALU.mult, ALU.add)
            cp("pool", gc_prev, hsel[:, 256:AG1W].re("p (j t) -> p j t", t=3))
            cp("pool", vprev, hsel[:, 128:256])
            pb = nxt("pb", psB)
            for g in range(2):
                tr(pb[0:64, g * 128:(g + 1) * 128], hsel[:, g * 64:(g + 1) * 64], ident)
            cp("act", kprev, pb[0:64, 0:256].re("p (g t) -> p g t", g=2))
            es8 = AR.alloc("es8", [8])
            S.dma("sp", es8[0:64, :], V(dr["attn_sinks"].ap[l].partition_broadcast(64), "RO"))
            act(es8[0:64, :], es8[0:64, :], AF.Exp)
            cp("dve", esinkB.re("p (h t) -> p h t", h=8), es8[0:64, :].re("p (h o) -> p h o", o=1).bc([64, 8, 64]))
            S.fence()

        if stage >= 6:
            if l == 0:
                st_ = nxt("stg", stg)
                S.dma("sp", st_[0:16, :], V(dr["hgrn_lb_logits"].ap.rearrange("l (h p) -> (l h) p", p=128), "RO"))
                pb = nxt("pb", psB)
                tr(pb[:, 0:16], st_[0:16, :], ident[0:16, 0:16])
                AR.reset()
                e16 = AR.alloc("e16", [4, 4])
                act(e16.re("p l h -> p (l h)"), pb[:, 0:16], AF.Exp)
                ssum = AR.alloc("ssum", [4])
                tt("dve", ssum, e16[:, 0, :], e16[:, 1, :], ALU.add)
                tt("dve", ssum, ssum, e16[:, 2, :], ALU.add)
                tt("dve", ssum, ssum, e16[:, 3, :], ALU.add)
                recip(ssum, ssum)
                for ll in range(1, 4):
                    tt("dve", e16[:, ll, :], e16[:, ll, :], ssum, ALU.mult)
                memset("pool", lbF[:, 0, :], 0.0)
                cp("dve", lbF[:, 1, :], e16[:, 1, :])
                tt("dve", lbF[:, 2, :], lbF[:, 1, :], e16[:, 2, :], ALU.add)
                tt("dve", lbF[:, 3, :], lbF[:, 2, :], e16[:, 3, :], ALU.add)
                S.fence()
            AR.reset()
            lbT = AR.alloc("lbT", [512])
            omlT = AR.alloc("omlT", [512])
            omlF = AR.alloc("omlF", [4])
            nomlF = AR.alloc("nomlF", [4])
            hSs = AR.alloc("hSs", [4, 4, 128])
            keep1 = AR.off
            lg = AR.alloc("lg", [4, 512])
            S.dma("sp", lg, V(dr["hgrn_lb_logits"].ap.partition_broadcast(128), "RO"))
            act(lg, lg, AF.Exp)
            tt("dve", omlT, lg[:, 0, :], lg[:, 1, :], ALU.add)
            tt("dve", omlT, omlT, lg[:, 2, :], ALU.add)
            tt("dve", omlT, omlT, lg[:, 3, :], ALU.add)
            recip(omlT, omlT)
            memset("pool", lbT, 0.0)
            for ll in range(1, l + 1):
                tt("dve", lbT, lbT, lg[:, ll, :], ALU.add)
            tt("dve", lbT, lbT, omlT, ALU.mult)
            ts("dve", omlT, lbT, -1.0, 1.0, ALU.mult, ALU.add)
            ts("dve", omlF, lbF[:, l, :], -1.0, 1.0, ALU.mult, ALU.add)
            ts("dve", nomlF, omlF, -1.0, None, ALU.mult)
            S.dma("sp", nrmw[:, 0:1], V(dr["hgrn_norm_w"].ap[l].rearrange("(p o) -> p o", o=1), "RO"))
            S.dma("sp", nrmw[:, 1:2], V(dr["gdn_norm_w"].ap[l].rearrange("(p o) -> p o", o=1), "RO"))
            memset("pool", hS, 0.0)
            memset("pool", hpre, 1.0)
            for sq in range(4):
                S.dma("sp", hSs[:, sq, :, :], V(dr["st_hgrn"].ap[l][sq].rearrange("h k v -> k h v"), "RO"))
            S.fence()
            for b in range(NBLK):
                AR.off = keep1
                AR.gen += 1
                c0 = b * TB
                last = (b == NBLK - 1)
                n = TB + (NS if last else 0)
                segs = [(c0, 0, TB)] + ([(SEG, TB, NS)] if last else [])
                wbq = load_w(wi[:, OFF["bq"]:OFF["bq"] + 512], 128, 8, 512)
                wbf = load_w(wi[:, OFF["bf"]:OFF["bf"] + 512], 128, 8, 512)
                wbi = load_w(wi[:, OFF["bi"]:OFF["bi"] + 512], 128, 8, 512)
                qT = AR.alloc("qT", [4, n], BF16)
                kT = AR.alloc("kT", [4, n], BF16)
                obuf = AR.alloc("obuf", [4, n], BF16)
                qhat = AR.alloc("qhat", [4, TB], BF16)
                sgF = [AR.alloc(f"sgF{i}", [512]) for i in range(2)]
                for h in range(4):
                    for (a, ha, sn) in segs:
                        pa = nxt("pa", psA)
                        for kt in range(8):
                            mm(pa[:, 0:sn], wbq[:, kt, h * 128:(h + 1) * 128], xbf[:, kt, a:a + sn], start=(kt == 0), stop=(kt == 7))
                        act(qT[:, h, ha:ha + sn], pa[:, 0:sn], AF.Silu)
                        pa = nxt("pa", psA)
                        for kt in range(8):
                            mm(pa[:, 0:sn], wbf[:, kt, h * 128:(h + 1) * 128], xbf[:, kt, a:a + sn], start=(kt == 0), stop=(kt == 7))
                        sg = sgF[h % 2]
                        act(sg[:, 0:sn], pa[:, 0:sn], AF.Sigmoid)
                        ts("dve", kT[:, h, ha:ha + sn], sg[:, 0:sn], nomlF[:, h:h + 1], omlF[:, h:h + 1], ALU.mult, ALU.add)
                sgT = AR.alloc("sgT", [512])
                tT = AR.alloc("tT", [512])
                lfT = AR.alloc("lfT", [512])
                kTM = AR.alloc("kTM", [512])
                vTM = AR.alloc("vTM", [512], BF16)
                kend = AR.alloc("kend", [512], BF16)
                kendm = AR.alloc("kendm", [4, 512], BF16) if last else None
                eq = AR.alloc("eq", [128])
                ek = AR.alloc("ek", [128])
                dm = AR.alloc("dm", [8])
                qt = AR.alloc("qt", [128], BF16)
                kt_ = AR.alloc("kt_", [128], BF16)
                attm = AR.alloc("attm", [128], BF16)
                sbf = AR.alloc("sbf", [128], BF16)
                fac1 = AR.alloc("fac1", [1])
                tl = [(c0 + t * 128, t * 128, 128, "p") for t in range(4)] + ([(SEG, TB, NS, "s")] if last else [])
                for (a, ha, T, tag) in tl:
                    nb = 2 if tag == "p" else 4
                    blk = T // nb
                    hdext = C(f"hdext_{tag}", T)
                    hle = C(f"hle_{tag}", T)
                    hgt = C(f"hgt_{tag}", T)
                    pf = nxt("pa", psA)
                    pi = nxt("pa", psA)
                    for kt in range(8):
                        mm(pf[0:T, :], xbf[:, kt, a:a + T], wbf[:, kt, :], start=(kt == 0), stop=(kt == 7))
                    for kt in range(8):
                        mm(pi[0:T, :], xbf[:, kt, a:a + T], wbi[:, kt, :], start=(kt == 0), stop=(kt == 7))
                    act(sgT[0:T, :], pf[0:T, :], AF.Sigmoid)
                    tt("dve", tT[0:T, :], sgT[0:T, :], omlT[0:T, :], ALU.mult)
                    stt(sgT[0:T, :], tT[0:T, :], 1e-30, lbT[0:T, :], ALU.max, ALU.add)
                    act(lfT[0:T, :], sgT[0:T, :], AF.Ln)
                    tt("pool", kTM[0:T, :], omlT[0:T, :], tT[0:T, :], ALU.subtract)
                    cp("act", vTM[0:T, :], pi[0:T, :])
                    pr = nxt("pa", psA)
                    mm(pr[0:T, :], hgt, lfT[0:T, :])
                    act(tT[0:T, :], pr[0:T, :], AF.Exp)
                    tt("dve", kend[0:T, :], kTM[0:T, :], tT[0:T, :], ALU.mult)
                    if tag == "s":
                        bls = C("blk_s", T)
                        for bb in range(4):
                            ts("pool", kendm[0:T, bb, :], kend[0:T, :], bls[:, bb:bb + 1], None, ALU.mult)
                    for h in range(4):
                        hs = slice(h * 128, (h + 1) * 128)
                        pc = nxt("pb", psB)
                        mm(pc[:, 0:T + 2 * nb], lfT[0:T, hs], hdext)
                        act(eq[:, 0:T], pc[:, 0:T], AF.Exp)
                        tt("dve", qt[:, 0:T], qT[:, h, ha:ha + T], eq[:, 0:T], ALU.mult)
                        act(ek[:, 0:T], pc[:, 0:T], AF.Exp, scale=-1.0)
                        tt("pool", kt_[:, 0:T], kT[:, h, ha:ha + T], ek[:, 0:T], ALU.mult)
                        act(dm[:, 0:2 * nb], pc[:, T:T + 2 * nb], AF.Exp)
                        pat = nxt("pb", psB)
                        mm(pat[0:T, 0:T], kt_[:, 0:T], qt[:, 0:T])
                        stt(attm[0:T, 0:T], pat[0:T, 0:T], 1e30, hle, ALU.min, ALU.mult)
                        po = nxt("pb", psB)
                        mm(po[:, 0:T], vTM[0:T, hs], attm[0:T, 0:T], start=True, stop=False)
                        for bb in range(nb):
                            Sb = hS[:, h, :] if tag == "p" else hSs[:, bb, h, :]
                            bs = slice(bb * blk, (bb + 1) * blk)
                            act(sbf, Sb, AF.Copy, scale=dm[:, nb + bb:nb + bb + 1])
                            mm(po[:, bs], sbf, qt[:, bs], start=False, stop=(bb == nb - 1))
                            pu = nxt("pa", psA)
                            if tag == "p":
                                ts("dve", fac1, dm[:, nb + bb:nb + bb + 1], hpre[:, h:h + 1], None, ALU.mult)
                                ts("dve", qhat[:, h, ha + bb * blk:ha + (bb + 1) * blk], qt[:, bs], fac1, None, ALU.mult)
                                mm(pu[:, 0:128], kend[bs, hs], vTM[bs, hs])
                            else:
                                mm(pu[:, 0:128], kendm[0:T, bb, hs], vTM[0:T, hs])
                            stt(Sb, Sb, dm[:, bb:bb + 1], pu[:, 0:128], ALU.mult, ALU.add)
                            if tag == "p":
                                tt("dve", hpre[:, h:h + 1], hpre[:, h:h + 1], dm[:, bb:bb + 1], ALU.mult)
                        cp("act", obuf[:, h, ha:ha + T], po[:, 0:T])
                S.dma("sp", dr["ob_loc"][:, :, c0:c0 + TB], obuf[:, :, 0:TB])
                S.dma("sp", dr["qh"][:, :, c0:c0 + TB], qhat)
                if last:
                    S.dma("sp", dr["ob_loc"][:, :, SEG:SEG + NS], obuf[:, :, TB:TB + NS])
                S.fence()
            for sq in range(4):
                S.dma("sp", V(dr["s_hgrn"].ap[l][sq].rearrange("h k v -> k h v"), "out_s_hgrn"), hSs[:, sq, :, :])
            if stage >= 7:
                S.fence()
                AR.off = 0
                AR.gen += 1
                for tap in range(4):
                    st_ = nxt("stg", stg)
                    S.dma("sp", st_[0:12, :], V(dr["gdn_conv_w"].ap[l][tap].rearrange("(t p) -> t p", p=128), "RO"))
                    pb = nxt("pb", psB)
                    tr(pb[:, 0:12], st_[0:12, :], ident[0:12, 0:12])
                    cp("dve", gpar[:, tap * 12:(tap + 1) * 12], pb[:, 0:12])
                gtmp = AR.alloc("gtmp", [2, 4 * DEPTH])
                S.dma("sp", gtmp[:, 0, :], V(dr["gdn_dt_bias"].ap.rearrange("l h -> (l h)").partition_broadcast(128), "RO"))
                S.dma("sp", gtmp[:, 1, :], V(dr["gdn_a_log"].ap.rearrange("l h -> (l h)").partition_broadcast(128), "RO"))
                cp("dve", gpar[:, 48:52], gtmp[:, 0, 4 * l:4 * l + 4])
                act(gpar[:, 52:56], gtmp[:, 1, 4 * l:4 * l + 4], AF.Exp)
                ts("dve", gpar[:, 52:56], gpar[:, 52:56], -1.0, None, ALU.mult)
                gS = AR.alloc("gS", [4, 256])
                gSb = AR.alloc("gSb", [4, 256], BF16)
                memset("pool", gS, 0.0)
                for h in range(4):
                    cp("dve", gS[:, h, 128:256], ident)
                cp("act", gSb, gS)
                keepg = AR.off
                triLE = C("triLE_p")
                triGT = C("triGT_p")
                mbias = C("mbias_p")
                offd = C("offd_p")
                S.fence()
                import os
                for u_ in range(NBLK + (0 if os.environ.get('GDN_NOSAMPLE') else 1)):
                    AR.off = keepg
                    AR.gen += 1
                    smp = (u_ == NBLK)
                    n = NS if smp else TB
                    a0 = SEG if smp else u_ * TB
                    wcq = [load_w(wi[:, OFF["cqkv"] + jg * 512:OFF["cqkv"] + (jg + 1) * 512], 128, 8, 512) for jg in range(1)]
                    qn = AR.alloc("qn", [4, n], BF16)
                    kn = AR.alloc("kn", [4, n], BF16)
                    ntile = 4
                    TT = 16 if smp else 128
                    kTMa = AR.alloc("kTMa", [ntile, 4, 128], BF16)
                    vTMa = AR.alloc("vTMa", [ntile, 4, 128], BF16)
                    ocb = AR.alloc("ocb", [4, n], BF16)
                    rhb = AR.alloc("rhb", [4, n], BF16)
                    cpre = [AR.alloc(f"cpre{i}", [n + 12]) for i in range(2)]
                    u32 = [AR.alloc(f"u32{i}", [n]) for i in range(2)]
                    sqg = AR.alloc("sqg", [n])
                    rsg = AR.alloc("rsg", [n])
                    if smp:
                        gSs = AR.alloc("gSs", [4, 4, 128])
                        gSsb = AR.alloc("gSsb", [4, 4, 128], BF16)
                        for sq in range(4):
                            S.dma("sp", gSs[:, sq, :, :], V(dr["st_gdn"].ap[l][sq].rearrange("h k v -> k h v"), "RO"))
                        cp("act", gSsb, gSs)
                        sg12 = AR.alloc("sg12", [C_QKV])
                        so12 = AR.alloc("so12", [C_QKV])
                        S.dma("sp", sg12[0:12, :], V(dr["st_gconv"].ap[l].rearrange("s r c -> (s r) c"), "RO"))
                    for j in range(12):
                        if j % 4 == 0 and j > 0:
                            wcq = [load_w(wi[:, OFF["cqkv"] + (j // 4) * 512:OFF["cqkv"] + (j // 4 + 1) * 512], 128, 8, 512)]
                        pa = nxt("pa", psA)
                        for kt in range(8):
                            mm(pa[:, 0:n], wcq[0][:, kt, (j % 4) * 128:(j % 4 + 1) * 128], xbf[:, kt, a0:a0 + n], start=(kt == 0), stop=(kt == 7))
                        cpj = cpre[j % 2]
                        uj = u32[j % 2]
                        gw = lambda t, j=j: gpar[:, t * 12 + j:t * 12 + j + 1]
                        if not smp:
                            cp("pool", cpj[:, 0:3], gc_prev[:, j, :])
                            cp("act", cpj[:, 3:3 + n], pa[:, 0:n])
                            cp("pool", gc_prev[:, j, :], cpj[:, n:n + 3])
                            ts("pool", uj, cpj[:, 0:n], gw(0), None, ALU.mult)
                            for t in range(1, 4):
                                stt(uj, cpj[:, t:t + n], gw(t), uj, ALU.mult, ALU.add)
                        else:
                            c3 = cpj[:, 0:76].re("p (s t) -> p s t", s=4)
                            pb = nxt("pb", psB)
                            tr(pb[:, 0:12], sg12[0:12, j * 128:(j + 1) * 128], ident[0:12, 0:12])
                            cp("dve", c3[:, :, 0:3], pb[:, 0:12].re("p (s r) -> p s r", s=4))
                            cp("act", c3[:, :, 3:19], pa[:, 0:n].re("p (s t) -> p s t", s=4))
                            u3 = uj.re("p (s t) -> p s t", s=4)
                            ts("pool", u3, c3[:, :, 0:16], gw(0), None, ALU.mult)
                            for t in range(1, 4):
                                stt(u3, c3[:, :, t:t + 16], gw(t), u3, ALU.mult, ALU.add)
                            uc12 = sqg[:, 0:12]
                            cp("pool", uc12.re("p (s r) -> p s r", s=4), c3[:, :, 16:19])
                            pb = nxt("pb", psB)
                            tr(pb[0:12, 0:128], uc12, ident)
                            cp("dve", so12[0:12, j * 128:(j + 1) * 128], pb[0:12, 0:128])
                        act(uj, uj, AF.Silu)
                        h = j % 4
                        if j < 8:
                            act(sqg, uj, AF.Square)
                            ps_ = nxt("pb", psB)
                            mm(ps_[:, 0:n], ones32, sqg)
                            act(rsg, ps_[:, 0:n], AF.Sqrt, bias=eps1)
                            recip(rsg, rsg)
                            if j < 4:
                                stt(qn[:, h, :], uj, 128.0 ** -0.5, rsg, ALU.mult, ALU.mult)
                            else:
                                tt("dve", uj, uj, rsg, ALU.mult)
                                cp("pool", kn[:, h, :], uj)
                        if j >= 4:
                            dstT = kTMa if j < 8 else vTMa
                            for t in range(ntile):
                                pb = nxt("pb", psB)
                                tr(pb[0:TT, 0:128], uj[:, t * TT:(t + 1) * TT], ident)
                                cp("act" if t % 2 else "dve", dstT[0:TT, t, h, :], pb[0:TT, 0:128])
                    if smp:
                        S.dma("sp", V(dr["s_gconv"].ap[l].rearrange("s r c -> (s r) c"), "out_s_gconv"), so12[0:12, :])
                    wcb = load_w(wi[:, OFF["cb"] - 248:OFF["cb"] + 8], 128, 8, 256)[:, :, 248:256]
                    bg8 = AR.alloc("bg8", [8])
                    beta = AR.alloc("beta", [4])
                    gg = AR.alloc("gg", [4])
                    sm8 = AR.alloc("sm8", [16])
                    erow = AR.alloc("erow", [128])
                    decT = AR.alloc("decT", [128])
                    decS = AR.alloc("decS", [128])
                    U32 = AR.alloc("U32", [128])
                    QKd = AR.alloc("QKd", [128], BF16)
                    keT = AR.alloc("keT", [128], BF16)
                    qeT = AR.alloc("qeT", [128], BF16)
                    ktil = AR.alloc("ktil", [128], BF16)
                    vaug = AR.alloc("vaug", [256], BF16)
                    memset("pool", vaug, 0.0)
                    Aa = [AR.alloc(f"Aa{i}", [128]) for i in range(2)]
                    At = [AR.alloc(f"At{i}", [128]) for i in range(2)]
                    Wm = [AR.alloc(f"Wm{i}", [128]) for i in range(2)]
                    Wmb = AR.alloc("Wmb", [128], BF16)
                    P1s = AR.alloc("P1s", [256], BF16)
                    wbf_ = AR.alloc("wbf_", [256], BF16)
                    GC = int(os.environ.get('GDN_CUT', '9'))
                    for t in range(ntile if GC >= 2 else 0):
                        T = TT
                        a = a0 + t * T
                        cs = slice(t * T, (t + 1) * T)
                        NV = 128 if smp else 256
                        pg = nxt("pb", psB)
                        for kt in range(8):
                            mm(pg[0:T, 0:8], xbf[:, kt, a:a + T], wcb[:, kt, :], start=(kt == 0), stop=(kt == 7))
                        cp("dve", bg8[0:T, :], pg[0:T, 0:8])
                        act(beta[0:T, :], bg8[0:T, 0:4], AF.Sigmoid)
                        tt("dve", gg[0:T, :], bg8[0:T, 4:8], gpar[0:T, 48:52], ALU.add)
                        act(gg[0:T, :], gg[0:T, :], AF.Exp)
                        act(gg[0:T, :], gg[0:T, :], AF.Ln, bias=1.0)
                        tt("dve", gg[0:T, :], gg[0:T, :], gpar[0:T, 52:56], ALU.mult)
                        pcm = nxt("pb", psB)
                        mm(pcm[0:T, 0:4], triLE[0:T, 0:T], gg[0:T, :])
                        mm(pcm[0:T, 4:8], triGT[0:T, 0:T], gg[0:T, :])
                        mm(pcm[:, 8:12], ones32[0:T, :], gg[0:T, :])
                        act(sm8[0:T, 0:4], pcm[0:T, 0:4], AF.Exp)
                        act(sm8[0:T, 4:8], pcm[0:T, 0:4], AF.Copy, scale=-1.0)
                        act(sm8[0:T, 8:12], pcm[0:T, 4:8], AF.Exp)
                        act(sm8[:, 12:16], pcm[:, 8:12], AF.Exp)
                        for h in range(4):
                            St = gSs[:, t, h, :] if smp else gS[:, h, :]
                            Sb = gSsb[:, t, h, :] if smp else gSb[:, h, :]
                            pcr = nxt("pb", psB)
                            gbc = gg[0:T, h:h + 1].bc([T, 128])
                            mm(pcr[:, 0:T], gbc, triLE[0:T, 0:T])
                            mm(pcr[0:T, 128:128 + T], gg[0:T, h:h + 1].bc([T, T]), triLE[0:T, 0:T], start=True, stop=False)
                            mm(pcr[0:T, 128:128 + T], ident[0:T, 0:T], mbias[0:T, 0:T], start=False, stop=True)
                            act(erow[:, 0:T], pcr[:, 0:T], AF.Exp)
                            act(decT[0:T, 0:T], pcr[0:T, 128:128 + T], AF.Exp, bias=sm8[0:T, 4 + h:5 + h])
                            tt("pool", decS[0:T, 0:T], decT[0:T, 0:T], offd[0:T, 0:T], ALU.mult)
                            pk = nxt("pb", psB)
                            mm(pk[0:T, 0:T], kn[:, h, cs], kn[:, h, cs])
                            mm(pk[0:T, 128:128 + T], kn[:, h, cs], qn[:, h, cs])
                            stt(U32[0:T, 0:T], pk[0:T, 0:T], beta[0:T, h:h + 1], decS[0:T, 0:T], ALU.mult, ALU.mult)
                            tt("dve", QKd[0:T, 0:T], pk[0:T, 128:128 + T], decT[0:T, 0:T], ALU.mult)
                            tt("pool", keT[:, 0:T], kn[:, h, cs], erow[:, 0:T], ALU.mult)
                            tt("pool", qeT[:, 0:T], qn[:, h, cs], erow[:, 0:T], ALU.mult)
                            ts("pool", ktil[0:T, :], kTMa[0:T, t, h, :], sm8[0:T, 8 + h:9 + h], None, ALU.mult)
                            cp("pool", vaug[0:T, 0:128], vTMa[0:T, t, h, :])
                            if GC < 3:
                                continue
                            pl_ = nxt("pb", psB)
                            tr(pl_[0:T, 0:T], U32[0:T, 0:T], ident[0:T, 0:T])
                            A, ATr = Aa[0], At[0]
                            cp("dve", A[0:T, 0:T], U32[0:T, 0:T])
                            cp("act", ATr[0:T, 0:T], pl_[0:T, 0:T])
                            W = Wm[0]
                            tt("dve", W[0:T, 0:T], ident[0:T, 0:T], U32[0:T, 0:T], ALU.subtract)
                            nsq = 3 if smp else 6
                            nsq = min(nsq, int(os.environ.get('GDN_NSQ', '9')))
                            for js in range(nsq):
                                A2, AT2, W2 = Aa[(js + 1) % 2], At[(js + 1) % 2], Wm[(js + 1) % 2]
                                p1_ = nxt("pb", psB)
                                p1b_ = nxt("pb", psB)
                                mm(p1_[0:T, 0:T], ATr[0:T, 0:T], A[0:T, 0:T])
                                mm(p1b_[0:T, 0:T], A[0:T, 0:T], ATr[0:T, 0:T])
                                cp("dve", A2[0:T, 0:T], p1_[0:T, 0:T])
                                cp("act", AT2[0:T, 0:T], p1b_[0:T, 0:T])
                                if os.environ.get('GDN_NOW'):
                                    A, ATr = A2, AT2
                                    continue
                                p2_ = nxt("pb", psB)
                                mm(p2_[0:T, 0:T], AT2[0:T, 0:T], W[0:T, 0:T])
                                tt("dve", W2[0:T, 0:T], W[0:T, 0:T], p2_[0:T, 0:T], ALU.add)
                                A, ATr, W = A2, AT2, W2
                            cp("act", Wmb[0:T, 0:T], W[0:T, 0:T])
                            if GC < 4:
                                continue
                            pp = nxt("pa", psA)
                            mm(pp[0:T, 0:NV], keT[:, 0:T], Sb[:, 0:NV] if not smp else Sb)
                            tt("dve", P1s[0:T, 0:NV], vaug[0:T, 0:NV], pp[0:T, 0:NV], ALU.subtract)
                            pw = nxt("pa", psA)
                            mm(pw[0:T, 0:NV], Wmb[0:T, 0:T], P1s[0:T, 0:NV])
                            act(wbf_[0:T, 0:NV], pw[0:T, 0:NV], AF.Copy, scale=beta[0:T, h:h + 1])
                            pq = nxt("pa", psA)
                            mm(pq[:, 0:T], wbf_[0:T, 0:128], QKd[0:T, 0:T], start=True, stop=False)
                            mm(pq[:, 0:T], Sb[:, 0:128] if not smp else Sb, qeT[:, 0:T], start=False, stop=True)
                            cp("act", ocb[:, h, cs], pq[:, 0:T])
                            if not smp:
                                mm(pq[:, 128:128 + T], wbf_[0:T, 128:256], QKd[0:T, 0:T], start=True, stop=False)
                                mm(pq[:, 128:128 + T], Sb[:, 128:256], qeT[:, 0:T], start=False, stop=True)
                                cp("dve", rhb[:, h, cs], pq[:, 128:128 + T])
                            pu = nxt("pa", psA)
                            mm(pu[:, 0:NV], ktil[0:T, :], wbf_[0:T, 0:NV])
                            if smp:
                                stt(St, St, sm8[:, 12 + h:13 + h], pu[:, 0:NV], ALU.mult, ALU.add)
                            else:
                                stt(St, St, sm8[:, 12 + h:13 + h], pu[:, 0:NV], ALU.mult, ALU.add)
                            cp("act", Sb, St)
                    if smp:
                        S.dma("sp", dr["oc_loc"][:, :, SEG:SEG + NS], ocb)
                        for sq in range(4):
                            S.dma("sp", V(dr["s_gdn"].ap[l][sq].rearrange("h k v -> k h v"), "out_s_gdn"), gSs[:, sq, :, :])
                    else:
                        S.dma("sp", dr["oc_loc"][:, :, a0:a0 + TB], ocb)
                        S.dma("sp", dr["rh"][:, :, a0:a0 + TB], rhb)
                    S.fence()
                S.dma("sp", dr["ag2_in"][:, 516:516 + 1024].re("p (h c) -> p h c", h=4), gS)
                AR.off = keepg
                AR.gen += 1
            AR.off = keep1
            AR.gen += 1
            S.dma("sp", dr["ag2_in"][:, 0:512], hS.re("p h v -> p (h v)"))
            S.dma("sp", dr["ag2_in"][:, 512:516], hpre)
            S.fence()
            a2i, a2o = dr["ag2_in"].ap, dr["ag2_out"].ap
            S.cc(lambda e: e.collective_compute("AllGather", ALU.bypass, replica_groups=GROUPS,
                                                       ins=[a2i], outs=[a2o]), ["ag2_in"], ["ag2_out"])
            g2 = AR.alloc("g2", [4, AG2W])
            S.dma("sp", g2, dr["ag2_out"].re("(r p) c -> p r c", p=128))
            X = AR.alloc("X", [4, 128])
            Sin = AR.alloc("Sin", [4, 128])
            memset("pool", X, 0.0)
            memset("pool", Sin, 0.0)
            for j in range(3):
                for h in range(4):
                    stt(X[:, h, :], X[:, h, :], g2[:, j, 512 + h:513 + h], g2[:, j, h * 128:(h + 1) * 128], ALU.mult, ALU.add)
                stt(Sin.re("p h v -> p (h v)"), X.re("p h v -> p (h v)"), corev[:, 6 + j:7 + j], Sin.re("p h v -> p (h v)"), ALU.mult, ALU.add)
            cp("act", sinB, Sin)
            for h in range(4):
                stt(X[:, h, :], Sin[:, h, :], hpre[:, h:h + 1], hS[:, h, :], ALU.mult, ALU.add)
            S.dma("sp", V(dr["p_hgrn"].ap[l].rearrange("h k v -> k h v"), "out_p_hgrn"), X)
            if stage >= 7:
                Xc = AR.alloc("Xc", [4, 128])
                SinC32 = AR.alloc("SinC32", [4, 128])
                Xcb = AR.alloc("Xcb", [4, 128], BF16)
                PmT = AR.alloc("PmT", [128], BF16)
                memset("pool", Xc, 0.0)
                memset("pool", SinC32, 0.0)
                for j in range(3):
                    for h in range(4):
                        base = 516 + h * 256
                        pt = nxt("pb", psB)
                        tr(pt[:, 0:128], g2[:, j, base + 128:base + 256], ident)
                        cp("act", PmT, pt[:, 0:128])
                        cp("dve", Xcb[:, h, :], Xc[:, h, :])
                        px = nxt("pb", psB)
                        mm(px[:, 0:128], PmT, Xcb[:, h, :])
                        tt("dve", Xc[:, h, :], px[:, 0:128], g2[:, j, base:base + 128], ALU.add)
                    stt(SinC32.re("p h v -> p (h v)"), Xc.re("p h v -> p (h v)"), corev[:, 6 + j:7 + j], SinC32.re("p h v -> p (h v)"), ALU.mult, ALU.add)
                cp("act", sinC, SinC32)
                for h in range(4):
                    pt = nxt("pb", psB)
                    tr(pt[:, 0:128], gS[:, h, 128:256], ident)
                    cp("act", PmT, pt[:, 0:128])
                    px = nxt("pb", psB)
                    mm(px[:, 0:128], PmT, sinC[:, h, :])
                    tt("dve", Xc[:, h, :], px[:, 0:128], gS[:, h, 0:128], ALU.add)
                S.dma("sp", V(dr["p_gdn"].ap[l].rearrange("h k v -> k h v"), "out_p_gdn"), Xc)
            S.fence()

        for b in range(NBLK):
            AR.reset()
            c0 = b * TB
            last = (b == NBLK - 1)
            n = TB + (NS if last else 0)
            segs = [(c0, 0, TB)] + ([(SEG, TB, NS)] if last else [])
            o_a = AR.alloc("o_a", [8, n], BF16)
            o_bn = AR.alloc("o_bn", [4, n], BF16)
            o_cn = AR.alloc("o_cn", [4, n], BF16)
            mark = AR.off
            for (tagB, onrm, loc, qsrc, sinX, zoff, nwc, stg_min) in (("B", o_bn, "ob_loc", "qh", sinB, OFF["bg"], 0, 6), ("C", o_cn, "oc_loc", "rh", sinC, OFF["cz"], 1, 7)):
                if stage < stg_min:
                    continue
                obl = AR.alloc("obl", [4, n], BF16)
                qhl = AR.alloc("qhl", [4, TB], BF16)
                o32 = AR.alloc("o32", [512])
                sq32 = AR.alloc("sq32", [512])
                rs_ = AR.alloc("rs_", [512])
                sz = AR.alloc("sz", [512])
                S.dma("sp", obl[:, :, 0:TB], dr[loc][:, :, c0:c0 + TB])
                if last:
                    S.dma("sp", obl[:, :, TB:TB + NS], dr[loc][:, :, SEG:SEG + NS])
                S.dma("sp", qhl, dr[qsrc][:, :, c0:c0 + TB])
                wz = load_w(wi[:, zoff:zoff + 512], 128, 8, 512)
                for h in range(4):
                    for (a, ha, sn) in segs:
                        if ha == 0:
                            pd = nxt("pb", psB)
                            mm(pd[:, 0:TB], sinX[:, h, :], qhl[:, h, :])
                            tt("dve", o32[:, 0:sn], pd[:, 0:TB], obl[:, h, 0:TB], ALU.add)
                        else:
                            cp("dve", o32[:, 0:sn], obl[:, h, TB:TB + NS])
                        act(sq32[:, 0:sn], o32[:, 0:sn], AF.Square)
                        ps_ = nxt("pb", psB)
                        mm(ps_[:, 0:sn], ones32, sq32[:, 0:sn])
                        act(rs_[:, 0:sn], ps_[:, 0:sn], AF.Sqrt, scale=1.0 / 128, bias=eps1)
                        recip(rs_[:, 0:sn], rs_[:, 0:sn])
                        pz = nxt("pa", psA)
                        for kt in range(8):
                            mm(pz[:, 0:sn], wz[:, kt, h * 128:(h + 1) * 128], xbf[:, kt, a:a + sn], start=(kt == 0), stop=(kt == 7))
                        act(sz[:, 0:sn], pz[:, 0:sn], AF.Silu)
                        tt("dve", o32[:, 0:sn], o32[:, 0:sn], rs_[:, 0:sn], ALU.mult)
                        stt(onrm[:, h, ha:ha + sn], o32[:, 0:sn], nrmw[:, nwc:nwc + 1], sz[:, 0:sn], ALU.mult, ALU.mult)
                S.fence()
                AR.off = mark
                AR.gen += 1
            if stage >= 5:
                qF = AR.alloc("qF", [8, n], BF16)
                kF = AR.alloc("kF", [2, 128 + TB], BF16)
                vT = AR.alloc("vT", [5, 128], BF16)
                wq = load_w(wi[:, OFF["aq"]:OFF["aq"] + 512], 128, 8, 512)
                wkv = load_w(wi[:, OFF["ak"]:OFF["ak"] + 256], 128, 8, 256)
                cp("pool", kF[0:64, :, 0:128], kprev)
                cp("pool", vT[:, 0, :], vprev)
                if last:
                    kFs = AR.alloc("kFs", [2, NS], BF16)
                    vTs = AR.alloc("vTs", [128], BF16)
                for h in range(8):
                    for (a, ha, sn) in segs:
                        pa = nxt("pa", psA)
                        for kt in range(8):
                            mm(pa[0:64, 0:sn], wq[:, kt, h * 64:(h + 1) * 64], xbf[:, kt, a:a + sn], start=(kt == 0), stop=(kt == 7))
                        act(qF[0:64, h, ha:ha + sn], pa[0:64, 0:sn], AF.Copy, scale=0.125)
                for g in range(2):
                    for (a, ha, sn) in segs:
                        pa = nxt("pa", psA)
                        for kt in range(8):
                            mm(pa[0:64, 0:sn], wkv[:, kt, g * 64:(g + 1) * 64], xbf[:, kt, a:a + sn], start=(kt == 0), stop=(kt == 7))
                        if ha == 0:
                            cp("act", kF[0:64, g, 128:128 + TB], pa[0:64, 0:TB])
                        else:
                            cp("act", kFs[0:64, g, :], pa[0:64, 0:NS])
                for t in range(4):
                    pa = nxt("pa", psA)
                    for kt in range(8):
                        mm(pa[:, 0:128], xbf[:, kt, c0 + t * 128:c0 + (t + 1) * 128], wkv[:, kt, 128:256], start=(kt == 0), stop=(kt == 7))
                    cp("dve", vT[:, 1 + t, :], pa[:, 0:128])
                if last:
                    pa = nxt("pa", psA)
                    for kt in range(8):
                        mm(pa[0:NS, 0:128], xbf[:, kt, SEG:SEG + NS], wkv[:, kt, 128:256], start=(kt == 0), stop=(kt == 7))
                    cp("dve", vTs[0:NS, :], pa[0:NS, 0:128])
                E = [AR.alloc(f"E{i}", [512], BF16) for i in range(2)]
                dsm = [AR.alloc(f"dsm{i}", [256]) for i in range(2)]
                it = 0
                for cq in range(8):
                    i = cq // 2
                    for g in range(2):
                        rX = slice(0, 128) if cq % 2 == 0 else slice(64, 128)
                        rY = slice(0, 64) if cq % 2 == 0 else slice(0, 128)
                        ps_ = nxt("pa", psA)
                        q3 = qF[0:64, 4 * g:4 * g + 4, cq * 64:(cq + 1) * 64]
                        mm(ps_[:, 0:256].re("p (h t) -> p h t", h=4), kF[0:64, g, i * 128:(i + 1) * 128], q3)
                        mm(ps_[:, 256:512].re("p (h t) -> p h t", h=4), kF[0:64, g, (i + 1) * 128:(i + 2) * 128], q3)
                        e_ = E[it % 2]
                        bX = corev[:, 9:10] if (b == 0 and i == 0) else zero1
                        act(e_[:, 0:256], ps_[:, 0:256], AF.Exp, bias=bX)
                        act(e_[:, 256:512], ps_[:, 256:512], AF.Exp, bias=zero1)
                        po = nxt("pb", psB)
                        mm(po[0:64, 0:256], vT[rX, i, g * 64:(g + 1) * 64], e_[rX, 0:256], start=True, stop=False)
                        mm(po[0:64, 0:256], vT[rY, i + 1, g * 64:(g + 1) * 64], e_[rY, 256:512], start=False, stop=True)
                        mm(po[0:64, 256:512], onesb[rX, 0:64], e_[rX, 0:256], start=True, stop=False)
                        mm(po[0:64, 256:512], onesb[rY, 0:64], e_[rY, 256:512], start=False, stop=True)
                        d_ = dsm[it % 2]
                        tt("dve", d_[0:64, :], po[0:64, 256:512], esinkB[:, g * 256:(g + 1) * 256], ALU.add)
                        recip(d_[0:64, :], d_[0:64, :])
                        tt("dve", o_a[0:64, 4 * g:4 * g + 4, cq * 64:(cq + 1) * 64],
                           po[0:64, 0:256].re("p (h t) -> p h t", h=4), d_[0:64, :].re("p (h t) -> p h t", h=4), ALU.mult)
                        it += 1
                if last:
                    kct = [AR.alloc(f"kct{i}", [128]) for i in range(2)]
                    vct = [AR.alloc(f"vct{i}", [128]) for i in range(2)]
                    kc = [AR.alloc(f"kc{i}", [2, 128], BF16) for i in range(2)]
                    vc = [AR.alloc(f"vc{i}", [128], BF16) for i in range(2)]
                    for sq in range(4):
                        S.dma("sp", kct[sq % 2], dr["cache_k"][l][sq])
                        S.dma("sp", vct[sq % 2], dr["cache_v"][l][sq])
                        pb = nxt("pb", psB)
                        for g in range(2):
                            tr(pb[0:64, g * 128:(g + 1) * 128], kct[sq % 2][:, g * 64:(g + 1) * 64], ident)
                        cp("act", kc[sq % 2][0:64, :, :], pb[0:64, 0:256].re("p (g t) -> p g t", g=2))
                        cp("pool", vc[sq % 2], vct[sq % 2])
                        for g in range(2):
                            ps_ = nxt("pa", psA)
                            q3 = qF[0:64, 4 * g:4 * g + 4, TB + sq * 16:TB + (sq + 1) * 16]
                            mm(ps_[:, 0:64].re("p (h t) -> p h t", h=4), kc[sq % 2][0:64, g, :], q3)
                            mm(ps_[0:64, 64:128].re("p (h t) -> p h t", h=4), kFs[0:64, g, :], q3)
                            e_ = E[it % 2]
                            act(e_[:, 0:64], ps_[:, 0:64], AF.Exp, bias=zero1)
                            act(e_[0:64, 64:128], ps_[0:64, 64:128], AF.Exp, bias=seqmask[:, sq:sq + 1])
                            po = nxt("pb", psB)
                            mm(po[0:64, 0:64], vc[sq % 2][:, g * 64:(g + 1) * 64], e_[:, 0:64], start=True, stop=False)
                            mm(po[0:64, 0:64], vTs[0:64, g * 64:(g + 1) * 64], e_[0:64, 64:128], start=False, stop=True)
                            mm(po[0:64, 64:128], onesb[:, 0:64], e_[:, 0:64], start=True, stop=False)
                            mm(po[0:64, 64:128], onesb[0:64, 0:64], e_[0:64, 64:128], start=False, stop=True)
                            d_ = dsm[it % 2]
                            tt("dve", d_[0:64, 0:64].re("p (h t) -> p h t", h=4), po[0:64, 64:128].re("p (h t) -> p h t", h=4),
                               esinkB[:, g * 256:(g + 1) * 256].re("p (h t) -> p h t", h=4)[:, :, 0:16], ALU.add)
                            recip(d_[0:64, 0:64], d_[0:64, 0:64])
                            tt("dve", o_a[0:64, 4 * g:4 * g + 4, TB + sq * 16:TB + (sq + 1) * 16],
                               po[0:64, 0:64].re("p (h t) -> p h t", h=4), d_[0:64, 0:64].re("p (h t) -> p h t", h=4), ALU.mult)
                            it += 1
                cp("pool", kprev, kF[0:64, :, TB:TB + 128])
                cp("pool", vprev, vT[:, 4, :])
                S.fence()
            AR.off = mark
            AR.gen += 1
            mixacc = AR.alloc("mixacc", [8, n])
            mixbf = AR.alloc("mixbf", [8, n], BF16)
            gsb = [AR.alloc(f"gsb{i}", [512]) for i in range(2)]
            branches = (["A"] if stage >= 5 else []) + (["B"] if stage >= 6 else []) + (["C"] if stage >= 7 else [])
            if not branches:
                memset("pool", mixbf, 0.0)
            for bi_, br in enumerate(branches):
                brow = {"A": 0, "B": 512, "C": 1024}[br]
                for hf in range(2):
                    if br == "A":
                        wb = load_w(WB["w_branch"][l][brow:brow + 512, hf * 512:(hf + 1) * 512], 64, 8, 512)
                    else:
                        wb = load_w(WB["w_branch"][l][brow:brow + 512, hf * 512:(hf + 1) * 512], 128, 4, 512)
                    gcol = OFF["gt"] + bi_ * 0 + {"A": 0, "B": 1024, "C": 2048}[br] + hf * 512
                    wg = load_w(wi[:, gcol:gcol + 512], 128, 8, 512)
                    for q in range(4):
                        i = hf * 4 + q
                        for (a, ha, sn) in segs:
                            p1 = nxt("pa", psA)
                            p2 = nxt("pa", psA)
                            if br == "A":
                                for kt in range(8):
                                    mm(p1[:, 0:sn], wb[0:64, kt, q * 128:(q + 1) * 128], o_a[0:64, kt, ha:ha + sn], start=(kt == 0), stop=(kt == 7))
                            else:
                                osrc = o_bn if br == "B" else o_cn
                                for kt in range(4):
                                    mm(p1[:, 0:sn], wb[:, kt, q * 128:(q + 1) * 128], osrc[:, kt, ha:ha + sn], start=(kt == 0), stop=(kt == 3))
                            for kt in range(8):
                                mm(p2[:, 0:sn], wg[:, kt, q * 128:(q + 1) * 128], xbf[:, kt, a:a + sn], start=(kt == 0), stop=(kt == 7))
                            gs_ = gsb[(i + ha) % 2]
                            act(gs_[:, 0:sn], p2[:, 0:sn], AF.Sigmoid)
                            if bi_ == 0:
                                tt("dve", mixacc[:, i, ha:ha + sn], p1[:, 0:sn], gs_[:, 0:sn], ALU.mult)
                            else:
                                tt("dve", gs_[:, 0:sn], p1[:, 0:sn], gs_[:, 0:sn], ALU.mult)
                                tt("pool", mixacc[:, i, ha:ha + sn], mixacc[:, i, ha:ha + sn], gs_[:, 0:sn], ALU.add)
            if branches:
                cp("pool", mixbf, mixacc)
            for hf in range(2):
                wo = load_w(WB["w_out"][l][:, hf * 512:(hf + 1) * 512], 128, 8, 512)
                for q in range(4):
                    i = hf * 4 + q
                    for (a, ha, sn) in segs:
                        pa = nxt("pa", psA)
                        for kt in range(8):
                            mm(pa[:, 0:sn], wo[:, kt, q * 128:(q + 1) * 128], mixbf[:, kt, ha:ha + sn], start=(kt == 0), stop=(kt == 7))
                        stt(xres[:, i, a:a + sn], xres[:, i, a:a + sn], ALPHA, pa[:, 0:sn], ALU.mult, ALU.add)
            for (a, ha, sn) in segs:
                layer_norm(a, sn, PC["ln1g"], PC["ln1b"])
            S.fence()

        if stage < 3:
            continue
        AR.reset()
        tailx = AR.alloc("tailx", [8, 2])
        S.dma("sp", dr["ag3_in"].re("p (a b) -> p a b", b=2), xres[:, :, SEG - 2:SEG])
        S.fence()
        agi, ago = dr["ag3_in"].ap, dr["ag3_out"].ap
        ccop = S.cc(lambda e: e.collective_compute("AllGather", ALU.bypass, replica_groups=GROUPS,
                                                          ins=[agi], outs=[ago]), ["ag3_in"], ["ag3_out"])
        g4 = AR.alloc("g4", [4, 16])
        S.dma("sp", g4, dr["ag3_out"].re("(r p) c -> p r c", p=128))
        tx = tailx.re("p a b -> p (a b)")
        ts("dve", tx, g4[:, 0, :], corev[:, 1:2], None, ALU.mult)
        for j in range(1, 4):
            stt(tx, g4[:, j, :], corev[:, 1 + j:2 + j], tx, ALU.mult, ALU.add)
        tailb = AR.alloc("tailb", [8, 2], BF16)
        cp("dve", tailb, tailx)
        S.fence()
        keep = AR.off

        for b in range(NBLK if stage >= 4 else 0):
            AR.off = keep
            AR.gen += 1
            c0 = b * TB
            last = (b == NBLK - 1)
            h = AR.alloc("h", [22, TB + NS], BF16)
            ug = [AR.alloc(f"ug{i}", [2 + TB]) for i in range(2)]
            uv = [AR.alloc(f"uv{i}", [2 + TB]) for i in range(2)]
            cg = [AR.alloc(f"cg{i}", [TB]) for i in range(2)]
            cv = [AR.alloc(f"cv{i}", [TB]) for i in range(2)]
            usg = [AR.alloc(f"usg{i}", [4, 18]) for i in range(2)]
            usv = [AR.alloc(f"usv{i}", [4, 18]) for i in range(2)]
            csg = [AR.alloc(f"csg{i}", [4, 16]) for i in range(2)]
            csv = [AR.alloc(f"csv{i}", [4, 16]) for i in range(2)]
            sio = [(AR.alloc(f"sin{i}", [512]), AR.alloc(f"sout{i}", [512]), None) for i in range(2)] if last else None
            ucb = [AR.alloc(f"uc{i}", [16]) for i in range(2)]
            for jg in range(6):
                nj = 4 if jg < 5 else 2
                wg = load_w(WB["w_up"][l][:, jg * 512:jg * 512 + nj * 128], 128, 8, nj * 128)
                wv = load_w(WB["w_up"][l][:, D_FF + jg * 512:D_FF + jg * 512 + nj * 128], 128, 8, nj * 128)
                for jj in range(nj):
                    j = jg * 4 + jj
                    fw = PC["fcw"]
                    for half, (wt, ubuf, cbuf, usb, csb) in enumerate(((wg, ug, cg, usg, csg), (wv, uv, cv, usv, csv))):
                        jc = j + 22 * half
                        u = ubuf[j % 2]
                        cc_ = cbuf[j % 2]
                        pa = nxt("pa", psA)
                        for kt in range(8):
                            mm(pa[:, 0:TB], wt[:, kt, jj * 128:(jj + 1) * 128], xbf[:, kt, c0:c0 + TB],
                               start=(kt == 0), stop=(kt == 7))
                        if b == 0:
                            pb = nxt("pb", psB)
                            for kt in range(8):
                                mm(pb[:, 0:2], wt[:, kt, jj * 128:(jj + 1) * 128], tailb[:, kt, :],
                                   start=(kt == 0), stop=(kt == 7))
                            cp("dve", u[:, 0:2], pb[:, 0:2])
                        else:
                            cp("pool", u[:, 0:2], ffn_tail[:, jc, :])
                        cp("act", u[:, 2:2 + TB], pa[:, 0:TB])
                        cp("pool", ffn_tail[:, jc, :], u[:, TB:TB + 2])
                        wc = lambda t, jc=jc: par[:, fw + t * 44 + jc:fw + t * 44 + jc + 1]
                        ts("pool", cc_, u[:, 0:TB], wc(0), par[:, PC["fcb"] + jc:PC["fcb"] + jc + 1], ALU.mult, ALU.add)
                        stt(cc_, u[:, 1:TB + 1], wc(1), cc_, ALU.mult, ALU.add)
                        stt(cc_, u[:, 2:TB + 2], wc(2), cc_, ALU.mult, ALU.add)
                        if last:
                            us = usb[j % 2]
                            cs = csb[j % 2]
                            pb = nxt("pb", psB)
                            for kt in range(8):
                                mm(pb[:, 0:NS], wt[:, kt, jj * 128:(jj + 1) * 128], xbf[:, kt, SEG:SEG + NS],
                                   start=(kt == 0), stop=(kt == 7))
                            sin, sout, pout = sio[half]
                            if jj == 0:
                                S.dma("sp", sin[0:8, 0:nj * 128],
                                      V(dr["st_fconv"].ap[l].rearrange("s r c -> (s r) c")[:, jc * 128:(jc + nj) * 128], "RO"))
                            pb2 = nxt("pb", psB)
                            tr(pb2[:, 0:8], sin[0:8, jj * 128:(jj + 1) * 128], ident[0:8, 0:8])
                            cp("dve", us[:, :, 0:2], pb2[:, 0:8].re("p (s r) -> p s r", s=4))
                            cp("act", us[:, :, 2:18], pb[:, 0:NS].re("p (s t) -> p s t", s=4))
                            ts("pool", cs, us[:, :, 0:16], wc(0), par[:, PC["fcb"] + jc:PC["fcb"] + jc + 1], ALU.mult, ALU.add)
                            stt(cs, us[:, :, 1:17], wc(1), cs, ALU.mult, ALU.add)
                            stt(cs, us[:, :, 2:18], wc(2), cs, ALU.mult, ALU.add)
                            uc = ucb[(2 * j + half) % 2]
                            cp("pool", uc[:, 0:8].re("p (s r) -> p s r", s=4), us[:, :, 16:18])
                            cp("pool", uc[:, 8:10], u[:, TB:TB + 2])
                            pb3 = nxt("pb", psB)
                            tr(pb3[0:10, 0:128], uc[:, 0:10], ident)
                            cp("dve", sout[0:10, jj * 128:(jj + 1) * 128], pb3[0:10, 0:128])
                            if jj == nj - 1:
                                S.dma("sp", V(dr["s_fconv"].ap[l].rearrange("s r c -> (s r) c")[:, (jc - nj + 1) * 128:(jc + 1) * 128], "out_s_fconv"),
                                      sout[0:8, 0:nj * 128])
                                S.dma("sp", V(dr["p_fconv"].ap[l][:, (jc - nj + 1) * 128:(jc + 1) * 128], "out_p_fconv"),
                                      sout[8:10, 0:nj * 128])
                    act(cg[j % 2], cg[j % 2], AF.Silu)
                    tt("dve", h[:, j, 0:TB], cg[j % 2], cv[j % 2], ALU.mult)
                    if last:
                        act(csg[j % 2], csg[j % 2], AF.Silu)
                        tt("dve", h[:, j, TB:TB + NS].re("p (s t) -> p s t", s=4), csg[j % 2], csv[j % 2], ALU.mult)
            segs = [(c0, 0, TB)] + ([(SEG, TB, NS)] if last else [])
            for i2 in range(2):
                wd = [load_w(WB["w_down"][l][:, (i2 * 4 + q) * 128:(i2 * 4 + q + 1) * 128], 128, 22, 128) for q in range(1)]
                for q in range(4):
                    i = i2 * 4 + q
                    if q > 0:
                        wd = [load_w(WB["w_down"][l][:, i * 128:(i + 1) * 128], 128, 22, 128)]
                    for (a, ha, sn) in segs:
                        pa = nxt("pa", psA)
                        for kt in range(22):
                            mm(pa[:, 0:sn], wd[0][:, kt, :], h[:, kt, ha:ha + sn], start=(kt == 0), stop=(kt == 21))
                        stt(xres[:, i, a:a + sn], xres[:, i, a:a + sn], ALPHA, pa[:, 0:sn], ALU.mult, ALU.add)
            for (a, ha, sn) in segs:
                layer_norm(a, sn, PC["ln2g"], PC["ln2b"])
            S.fence()

    AR.reset()
    yo = [AR.alloc(f"yo{i}", [D]) for i in range(2)]
    for ti, (src, r0, nr, c0) in enumerate(tiles):
        y = yo[ti % 2]
        for dq in range(2):
            pb = nxt("pb", psB)
            for q in range(4):
                dt_ = dq * 4 + q
                tr(pb[0:nr, q * 128:(q + 1) * 128], xres[:, dt_, c0:c0 + nr], ident)
            cp("act" if dq % 2 else "dve", y[0:nr, dq * 512:(dq + 1) * 512], pb[0:nr, :])
        dst = dr["y_p"][r0:r0 + nr, :] if src == "xp" else dr["y_s"][0:nr, :]
        S.dma("sp", dst, y[0:nr, :])
    S.fence()
    S.emit(nc, es)
    es.close()
    return nc


_NC_CACHE = {}


def _core_inputs(inp, c):
    b, p = c // 4, c % 4
    m = {}
    m["xp"] = np.ascontiguousarray(inp["x_prompt"][b, p * SEG:(p + 1) * SEG])
    m["xs"] = np.ascontiguousarray(inp["x_sample"][4 * c:4 * c + 4]).reshape(NS, D)
    m["cache_k"] = np.ascontiguousarray(inp["cache_swa_k"][:, 4 * c:4 * c + 4]).reshape(DEPTH, 4, 128, 128)
    m["cache_v"] = np.ascontiguousarray(inp["cache_swa_v"][:, 4 * c:4 * c + 4]).reshape(DEPTH, 4, 128, 128)
    m["st_hgrn"] = np.ascontiguousarray(inp["state_hgrn"][:, 4 * c:4 * c + 4])
    m["st_gdn"] = np.ascontiguousarray(inp["state_gdn"][:, 4 * c:4 * c + 4])
    m["st_gconv"] = np.ascontiguousarray(inp["state_gdn_conv"][:, 4 * c:4 * c + 4])
    m["st_fconv"] = np.ascontiguousarray(inp["state_ffn_conv"][:, 4 * c:4 * c + 4])
    for k in ("ln_in_g", "ln_in_b", "w_in", "attn_sinks", "hgrn_lb_logits", "hgrn_norm_w", "gdn_conv_w",
              "gdn_a_log", "gdn_dt_bias", "gdn_norm_w", "w_branch", "w_out", "ln1_g", "ln1_b", "w_up",
              "ffn_conv_w", "ffn_conv_b", "w_down", "ln2_g", "ln2_b"):
        m[k] = np.ascontiguousarray(inp[k], dtype=np.float32)
    m["consts"] = CONSTS
    cv = np.zeros((128, 16), np.float32)
    cv[:, 0] = 1.0 if p > 0 else 0.0
    if p > 0:
        cv[:, 1 + (p - 1)] = 1.0
    cv[:, 5 + p] = 1.0
    cv[:, 9] = 0.0 if p > 0 else NEGM
    m["corev"] = cv
    return m


def run_cores(inp, nlayers=DEPTH, stage=99):
    key = (nlayers, stage)
    if key not in _NC_CACHE:
        _NC_CACHE[key] = build_nc(stage=stage, nlayers=nlayers)
    nc = _NC_CACHE[key]
    in_maps = [_core_inputs(inp, c) for c in range(8)]
    if nlayers < DEPTH:
        for m in in_maps:
            for n, shp in DRAM_IN:
                if shp[0] == DEPTH and n not in ('hgrn_lb_logits', 'gdn_dt_bias', 'gdn_a_log'):
                    m[n] = np.ascontiguousarray(m[n][:nlayers])
    res = run_bass_kernel_spmd(nc, in_maps, core_ids=list(range(8)))
    return res.results


def kernel(**inp):
    inp = {k: np.asarray(v) for k, v in inp.items()}
    r = run_cores(inp)
    y_p = np.stack([np.concatenate([r[b * 4 + p]["y_p"] for p in range(4)], 0) for b in range(2)], 0)
    y_s = np.concatenate([r[c]["y_s"].reshape(4, 16, D) for c in range(8)], 0)

    def pl(name, shape):
        return np.stack([r[3][name], r[7][name]], 1).reshape(shape)

    def sl(name, shape):
        return np.concatenate([r[c][name] for c in range(8)], 1).reshape(shape)

    outs = (y_p, y_s,
            pl("p_swa_k", (DEPTH, 2, 128, 2, 64)), pl("p_swa_v", (DEPTH, 2, 128, 2, 64)),
            pl("p_hgrn", (DEPTH, 2, 4, 128, 128)), pl("p_gdn", (DEPTH, 2, 4, 128, 128)),
            pl("p_gconv", (DEPTH, 2, 3, C_QKV)), pl("p_fconv", (DEPTH, 2, 2, 2 * D_FF)),
            sl("s_swa_k", (DEPTH, 32, 128, 2, 64)), sl("s_swa_v", (DEPTH, 32, 128, 2, 64)),
            sl("s_hgrn", (DEPTH, 32, 4, 128, 128)), sl("s_gdn", (DEPTH, 32, 4, 128, 128)),
            sl("s_gconv", (DEPTH, 32, 3, C_QKV)), sl("s_fconv", (DEPTH, 32, 2, 2 * D_FF)))
    return tuple(np.ascontiguousarray(o, dtype=np.float32) for o in outs)
```

```python
import numpy as np
from contextlib import ExitStack
import concourse.bass as bass
import concourse.mybir as mybir
from concourse.bass_utils import run_bass_kernel_spmd

F32 = mybir.dt.float32
BF16 = mybir.dt.bfloat16
AF = mybir.ActivationFunctionType
ALU = mybir.AluOpType

D = 1024
DEPTH = 4
SEG = 2048
NS = 64
NTOK = SEG + NS
TB = 512
NBLK = SEG // TB
A_HEADS, A_KV, A_HD = 8, 2, 64
D_FF = 2816
C_QKV = 1536
D_IN = 7944
OFF = {}
_o = 0
for _n, _w in (("aq", 512), ("ak", 128), ("av", 128), ("bq", 512), ("bf", 512), ("bi", 512), ("bg", 512),
               ("cqkv", 1536), ("cz", 512), ("cb", 4), ("ca", 4), ("gt", 3072)):
    OFF[_n] = _o
    _o += _w
ALPHA = (2 * DEPTH) ** 0.25
LN_EPS = 1e-5
RMS_EPS = 1e-6
NEGM = -30000.0


class V:
    __slots__ = ("ap", "key")

    def __init__(self, ap, key):
        self.ap = ap
        self.key = key

    def __getitem__(self, idx):
        return V(self.ap[idx], self.key)

    def k(self, key):
        return V(self.ap, key)

    def re(self, pat, **kw):
        return V(self.ap.rearrange(pat, **kw), self.key)

    def bc(self, shape):
        return V(self.ap.to_broadcast(shape), self.key)


class Op:
    __slots__ = ("eng", "kind", "fn", "deps", "idx", "inc", "sem", "val", "gid")


class Sched:
    ENGS = ("pe", "act", "dve", "pool", "sp")
    NDMA = 20

    def __init__(self):
        self.ops = {e: [] for e in self.ENGS}
        self.res = {}
        self.dma_slots = {e: [None] * self.NDMA for e in ("sp", "act", "pool")}
        self.dma_cnt = {e: [0] * self.NDMA for e in ("sp", "act", "pool")}
        self.dma_rr = {e: 0 for e in ("sp", "act", "pool")}
        self.all_dma = []
        self.gid = 0
        self.pending_dma = []
        self.ccs = []
        self.bg = {}

    def _rec(self, eng, kind, fn, reads, writes, extra_deps=()):
        op = Op()
        op.eng, op.kind, op.fn = eng, kind, fn
        op.inc = False
        op.sem = None
        op.val = 0
        op.gid = self.gid
        self.gid += 1
        deps = []
        for r in reads:
            if r in self.bg:
                deps.append(self.bg[r])
            st = self.res.get(r)
            if st and st[0] is not None:
                deps.append(st[0])
        for w in writes:
            st = self.res.get(w)
            if st:
                if st[0] is not None:
                    deps.append(st[0])
                deps.extend(st[1])
        deps.extend(extra_deps)
        for r in reads:
            st = self.res.setdefault(r, [None, []])
            st[1].append(op)
        for w in writes:
            self.res[w] = [op, []]
        seen = set()
        dd = []
        for d in deps:
            if d is op or id(d) in seen:
                continue
            seen.add(id(d))
            if d.kind == "c" and kind == "c" and d.eng == "pe" and eng == "pe":
                continue
            dd.append(d)
        op.deps = dd
        op.idx = len(self.ops[eng])
        self.ops[eng].append(op)
        return op

    def c(self, eng, fn, reads, writes):
        return self._rec(eng, "c", fn, reads, writes)

    def cc(self, fn, reads, writes):
        op = self._rec("pool", "x", fn, reads, writes)
        op.sem = ("cc", len(self.ccs))
        op.val = 1
        self.ccs.append(op)
        self.pending_dma.append(op)
        return op

    def dma(self, q, out, in_, bg=False, **kw):
        slot = self.dma_rr[q]
        self.dma_rr[q] = (slot + 1) % self.NDMA
        prev = self.dma_slots[q][slot]
        extra = [prev] if prev is not None else []
        o, i = out.ap, in_.ap
        op = self._rec(q, "d", lambda e: e.dma_start(out=o, in_=i, **kw), [in_.key], [out.key], extra)
        self.dma_cnt[q][slot] += 1
        op.sem = (q, slot)
        op.val = 16 * self.dma_cnt[q][slot]
        self.dma_slots[q][slot] = op
        self.all_dma.append(op)
        if bg:
            self.bg[out.key] = op
        else:
            self.pending_dma.append(op)
        return op

    def fence(self):
        lasts = [self.ops[e][-1] for e in self.ENGS if self.ops[e] and self.ops[e][-1].kind != "d"]
        for e in self.ENGS:
            pass
        lastc = []
        for e in self.ENGS:
            for o in reversed(self.ops[e]):
                if o.kind == "c" and o.fn is not None:
                    lastc.append(o)
                    break
                if o.fn is None:
                    break
        pend = list(self.pending_dma)
        self.pending_dma = []
        for e in self.ENGS:
            deps = list(lastc) + pend
            op = Op()
            op.eng, op.kind, op.fn = e, "c", None
            op.inc, op.sem, op.val, op.gid = False, None, 0, self.gid
            self.gid += 1
            op.deps = deps
            op.idx = len(self.ops[e])
            self.ops[e].append(op)
        self.res = {}

    def emit(self, nc, es):
        CH = 20000
        for e in self.ENGS:
            for op in self.ops[e]:
                for d in op.deps:
                    d.inc = True
        self.csems = {}
        for e in self.ENGS:
            n = 0
            for op in self.ops[e]:
                if op.kind == "c" and op.inc:
                    op.sem = (e, n // CH)
                    op.val = n % CH + 1
                    n += 1
            nsem = max(1, (n + CH - 1) // CH)
            self.csems[e] = [es.enter_context(nc.semaphore(f"c_{e}_{j}")) for j in range(nsem)]
        self.dsems = {q: [es.enter_context(nc.semaphore(f"d_{q}_{j}")) for j in range(self.NDMA)]
                      for q in ("sp", "act", "pool")}
        block = es.enter_context(nc.Block())
        engmap = {"pe": block.tensor, "act": block.scalar, "dve": block.vector, "pool": block.gpsimd, "sp": block.sync}

        self.xsems = [es.enter_context(nc.semaphore(f"x_{j}")) for j in range(len(self.ccs))]

        def semof(op):
            if op.kind == "d":
                return self.dsems[op.sem[0]][op.sem[1]]
            if op.kind == "x":
                return self.xsems[op.sem[1]]
            return self.csems[op.sem[0]][op.sem[1]]

        def make(e):
            oplist = self.ops[e]

            def body(eng):
                waited = {}
                for op in oplist:
                    for d in op.deps:
                        key = (d.kind, d.sem)
                        if waited.get(key, 0) >= d.val:
                            continue
                        eng.wait_ge(semof(d), d.val)
                        waited[key] = d.val
                    if op.fn is None:
                        continue
                    ins = op.fn(eng)
                    if op.kind == "d":
                        ins.then_inc(semof(op), 16)
                    elif op.kind == "x":
                        ins.then_inc(semof(op), 1)
                    elif op.inc:
                        ins.then_inc(semof(op), 1)
            return body

        for e in self.ENGS:
            engmap[e](make(e))


S = None


def kof(*vs):
    return [v.key for v in vs if isinstance(v, V)]


def apof(x):
    return x.ap if isinstance(x, V) else x


def mm(out, lhsT, rhs, start=True, stop=True):
    o, l, r = out.ap, lhsT.ap, rhs.ap
    return S.c("pe", lambda e: e.matmul(o, l, r, start=start, stop=stop), kof(lhsT, rhs), kof(out))


def tr(out, in_, ident):
    o, i, d = out.ap, in_.ap, ident.ap
    return S.c("pe", lambda e: e.transpose(o, i, d), kof(in_, ident), kof(out))


def act(out, in_, func, bias=None, scale=None):
    kw = {}
    if bias is not None:
        kw["bias"] = apof(bias)
    if scale is not None:
        kw["scale"] = apof(scale)
    o, i = out.ap, in_.ap
    return S.c("act", lambda e: e.activation(out=o, in_=i, func=func, **kw), kof(in_, bias, scale), kof(out))


def ts(eng, out, in0, s1, s2, op0, op1=None):
    o, i = out.ap, in0.ap
    a1, a2 = apof(s1), apof(s2)
    if op1 is None:
        return S.c(eng, lambda e: e.tensor_scalar(o, i, a1, None, op0), kof(in0, s1), kof(out))
    return S.c(eng, lambda e: e.tensor_scalar(o, i, a1, a2, op0, op1), kof(in0, s1, s2), kof(out))


def tt(eng, out, in0, in1, op):
    o, a, b = out.ap, in0.ap, in1.ap
    return S.c(eng, lambda e: e.tensor_tensor(out=o, in0=a, in1=b, op=op), kof(in0, in1), kof(out))


def stt(out, in0, scalar, in1, op0, op1):
    o, a, b = out.ap, in0.ap, in1.ap
    sc = apof(scalar)
    return S.c("dve", lambda e: e.scalar_tensor_tensor(out=o, in0=a, scalar=sc, in1=b, op0=op0, op1=op1),
               kof(in0, scalar, in1), kof(out))


def cp(eng, out, in_):
    o, i = out.ap, in_.ap
    if eng == "act":
        return S.c("act", lambda e: e.activation(out=o, in_=i, func=AF.Copy), kof(in_), kof(out))
    return S.c(eng, lambda e: e.tensor_copy(o, i), kof(in_), kof(out))


def recip(out, in_):
    o, i = out.ap, in_.ap
    return S.c("dve", lambda e: e.reciprocal(o, i), kof(in_), kof(out))


def memset(eng, out, val):
    o = out.ap
    return S.c(eng, lambda e: e.memset(o, val), [], kof(out))


def make_consts():
    cols = {}
    parts = []
    pos = [0]

    def add(name, a):
        a = np.asarray(a, np.float32)
        if a.shape[0] < 128:
            a = np.concatenate([a, np.zeros((128 - a.shape[0], a.shape[1]), np.float32)], 0)
        cols[name] = (pos[0], a.shape[1])
        parts.append(a)
        pos[0] += a.shape[1]

    i = np.arange(128)
    add("ident", np.eye(128))
    add("ones", np.ones((128, 128)))
    for tag, T, blk in (("p", 128, 128), ("s", 64, 16)):
        s = np.arange(T)[:, None]
        t = np.arange(T)[None, :]
        same = (s // blk) == (t // blk)
        add(f"triLE_{tag}", (same & (s <= t)))
        add(f"triGT_{tag}", (same & (s > t)))
        add(f"mbias_{tag}", np.where(same & (s <= t), 0.0, NEGM))
        add(f"offd_{tag}", (same & (s < t)))
    for tag, T, blk in (("p", 128, 64), ("s", 64, 16)):
        nb = T // blk
        sidx = np.arange(T)[:, None]
        tidx = np.arange(T)[None, :]
        same = (sidx // blk) == (tidx // blk)
        mid = (tidx // blk) * blk + blk // 2 - 1
        drel = (same & (sidx <= tidx)).astype(np.float32) - (same & (sidx <= mid)).astype(np.float32)
        tot = np.stack([(np.arange(T) // blk == b_) for b_ in range(nb)], 1).astype(np.float32)
        midc = np.stack([((np.arange(T) // blk == b_) & (np.arange(T) <= b_ * blk + blk // 2 - 1)) for b_ in range(nb)], 1).astype(np.float32)
        add(f"hdext_{tag}", np.concatenate([drel, tot, midc], 1))
        add(f"hle_{tag}", (same & (sidx <= tidx)))
        add(f"hgt_{tag}", (same & (sidx > tidx)))
        add(f"blk_{tag}", tot)
    sm = np.full((64, 4), NEGM, np.float32)
    for q in range(4):
        sm[16 * q:16 * q + 16, q] = 0.0
    add("seqmask", sm)
    return np.concatenate(parts, 1), cols


CONSTS, CCOL = make_consts()
NCON = CONSTS.shape[1]


class Arena:
    def __init__(self, ap, ncols32):
        self.ap = ap
        self.n = ncols32
        self.off = 0
        self.gen = 0

    def reset(self):
        self.off = 0
        self.gen += 1

    def alloc(self, name, shape, dtype=F32, parts=128):
        free = int(np.prod(shape))
        n32 = free if dtype == F32 else (free + 1) // 2
        self.off = (self.off + 31) // 32 * 32
        assert self.off + n32 <= self.n, (name, self.off, n32, self.n)
        a = self.ap[0:parts, self.off:self.off + n32]
        self.off += n32
        if dtype != F32:
            a = a.bitcast(dtype)[:, 0:free]
        if len(shape) == 2:
            a = a.rearrange("p (a b) -> p a b", a=shape[0])
        elif len(shape) == 3:
            a = a.rearrange("p (a b c) -> p a b c", a=shape[0], b=shape[1])
        return V(a, f"{name}#{self.gen}")


DRAM_IN = [
    ("xp", [SEG, D]), ("xs", [NS, D]),
    ("cache_k", [DEPTH, 4, 128, 128]), ("cache_v", [DEPTH, 4, 128, 128]),
    ("st_hgrn", [DEPTH, 4, 4, 128, 128]), ("st_gdn", [DEPTH, 4, 4, 128, 128]),
    ("st_gconv", [DEPTH, 4, 3, C_QKV]), ("st_fconv", [DEPTH, 4, 2, 2 * D_FF]),
    ("ln_in_g", [D]), ("ln_in_b", [D]), ("w_in", [DEPTH, D, D_IN]), ("attn_sinks", [DEPTH, 8]),
    ("hgrn_lb_logits", [DEPTH, 512]), ("hgrn_norm_w", [DEPTH, 128]), ("gdn_conv_w", [DEPTH, 4, C_QKV]),
    ("gdn_a_log", [DEPTH, 4]), ("gdn_dt_bias", [DEPTH, 4]), ("gdn_norm_w", [DEPTH, 128]),
    ("w_branch", [DEPTH, 1536, D]), ("w_out", [DEPTH, D, D]), ("ln1_g", [DEPTH, D]), ("ln1_b", [DEPTH, D]),
    ("w_up", [DEPTH, D, 2 * D_FF]), ("ffn_conv_w", [DEPTH, 3, 2 * D_FF]), ("ffn_conv_b", [DEPTH, 2 * D_FF]),
    ("w_down", [DEPTH, D_FF, D]), ("ln2_g", [DEPTH, D]), ("ln2_b", [DEPTH, D]),
    ("consts", [128, NCON]), ("corev", [128, 16]),
]
DRAM_OUT = [
    ("y_p", [SEG, D]), ("y_s", [NS, D]),
    ("p_swa_k", [DEPTH, 128, 128]), ("p_swa_v", [DEPTH, 128, 128]),
    ("p_hgrn", [DEPTH, 4, 128, 128]), ("p_gdn", [DEPTH, 4, 128, 128]),
    ("p_gconv", [DEPTH, 3, C_QKV]), ("p_fconv", [DEPTH, 2, 2 * D_FF]),
    ("s_swa_k", [DEPTH, 4, 128, 128]), ("s_swa_v", [DEPTH, 4, 128, 128]),
    ("s_hgrn", [DEPTH, 4, 4, 128, 128]), ("s_gdn", [DEPTH, 4, 4, 128, 128]),
    ("s_gconv", [DEPTH, 4, 3, C_QKV]), ("s_fconv", [DEPTH, 4, 2, 2 * D_FF]),
]
GROUPS = [[0, 1, 2, 3], [4, 5, 6, 7]]
NW = 3
WSLOT = 4096
ARENA32 = 13900


def build_nc(stage=99, nlayers=DEPTH):
    global S
    S = Sched()
    nc = bass.Bass("TRN2", target_bir_lowering=False)
    es = ExitStack()
    dr = {}
    for n, shp in DRAM_IN:
        if shp[0] == DEPTH and n not in ('hgrn_lb_logits', 'gdn_dt_bias', 'gdn_a_log'):
            shp = [nlayers] + list(shp[1:])
        dr[n] = V(nc.dram_tensor(n, shp, F32, kind="ExternalInput").ap(), "RO")
    for n, shp in DRAM_OUT:
        dr[n] = V(nc.dram_tensor(n, shp, F32, kind="ExternalOutput").ap(), "out_" + n)
    AG3W = D * 2
    AG1W = 256 + 36
    dr["ag1_in"] = V(nc.dram_tensor("ag1_in", [128, AG1W], F32).ap(), "ag1_in")
    dr["ag1_out"] = V(nc.dram_tensor("ag1_out", [512, AG1W], F32).ap(), "ag1_out")
    AG2W = 516 + 1024
    dr["ag2_in"] = V(nc.dram_tensor("ag2_in", [128, AG2W], F32).ap(), "ag2_in")
    dr["ag2_out"] = V(nc.dram_tensor("ag2_out", [512, AG2W], F32).ap(), "ag2_out")
    dr["ob_loc"] = V(nc.dram_tensor("ob_loc", [128, 4, NTOK], BF16).ap(), "ob_loc")
    dr["qh"] = V(nc.dram_tensor("qh", [128, 4, SEG], BF16).ap(), "qh")
    dr["oc_loc"] = V(nc.dram_tensor("oc_loc", [128, 4, NTOK], BF16).ap(), "oc_loc")
    dr["rh"] = V(nc.dram_tensor("rh", [128, 4, SEG], BF16).ap(), "rh")
    dr["ag3_in"] = V(nc.dram_tensor("ag3_in", [128, 16], F32).ap(), "ag3_in")
    dr["ag3_out"] = V(nc.dram_tensor("ag3_out", [512, 16], F32).ap(), "ag3_out")

    def sb(name, shape, dt=F32):
        return V(es.enter_context(nc.sbuf_tensor(name, shape, dt))[:], name)

    WSH = {"w_in": (D, D_IN), "w_branch": (1536, D), "w_out": (D, D), "w_up": (D, 2 * D_FF), "w_down": (D_FF, D)}
    WB = {}
    for wn, (r_, c_) in WSH.items():
        t_ = nc.dram_tensor("wb_" + wn, [nlayers, r_, c_], BF16).ap()
        WB[wn] = [V(t_[l_], f"wb_{wn}_{l_}") for l_ in range(nlayers)]

    def convert_weights(l_):
        for wn, (r_, c_) in WSH.items():
            step = 256 if r_ % 256 == 0 else 128
            step = r_ // max(1, min(4, r_ // step))
            for r0 in range(0, r_, step):
                r1 = min(r_, r0 + step)
                S.dma("pool", V(WB[wn][l_].ap[r0:r1, :], f"wb_{wn}_{l_}_{r0}"), V(dr[wn].ap[l_][r0:r1, :], "RO"), bg=True)

    xres = sb("xres", [128, 8, NTOK])
    xbf = sb("xbf", [128, 8, NTOK], BF16)
    con = sb("con", [128, NCON])
    conb = sb("conb", [128, 256], BF16)
    corev = sb("corev_sb", [128, 16])
    wsl = [sb(f"w{i}", [128, WSLOT], BF16) for i in range(NW)]
    par = sb("par", [128, 512])
    arena_t = sb("arena", [128, ARENA32])
    AR = Arena(arena_t.ap, ARENA32)
    ln_sq4 = sb("ln_sq4", [128, 4, 512])
    ln_mean = sb("ln_mean", [128, 512])
    ln_rstd = sb("ln_rstd", [128, 512])
    ffn_tail = sb("ffn_tail", [128, 44, 2])
    kprev = sb("kprev", [64, 2, 128], BF16)
    vprev = sb("vprev", [128, 128], BF16)
    gc_prev = sb("gc_prev", [128, 12, 3])
    esinkB = sb("esinkB", [64, 4 * 64 * 2])
    zero1 = sb("zero1", [128, 1])
    eps1 = sb("eps1", [128, 1])
    hS = sb("hS", [128, 4, 128])
    hpre = sb("hpre", [128, 4])
    lbF = sb("lbF", [128, 4, 4])
    sinB = sb("sinB", [128, 4, 128], BF16)
    sinC = sb("sinC", [128, 4, 128], BF16)
    gpar = sb("gpar", [128, 64])
    nrmw = sb("nrmw", [128, 2])
    psA = [V(es.enter_context(nc.psum_tensor(f"psA{i}", [128, 512], F32))[:], f"psA{i}") for i in range(4)]
    psB_t = [es.enter_context(nc.psum_tensor(f"psB{i}", [128, 512], F32)) for i in range(4)]
    psB = [V(psB_t[i][:], f"psB{i}") for i in range(4)]
    rr = {"w": 0, "pa": 0, "pb": 0, "stg": 0}

    def nxt(kind, lst):
        v = lst[rr[kind] % len(lst)]
        rr[kind] += 1
        return v

    def C(name, parts=128):
        c0, n = CCOL[name]
        return con[0:parts, c0:c0 + n]

    ident = C("ident")
    ones32 = C("ones")
    identb = conb[:, 0:128]
    onesb = conb[:, 128:256]

    S.dma("sp", con, dr["consts"])
    S.dma("sp", corev, dr["corev"])
    convert_weights(0)
    cp("dve", identb, ident)
    cp("dve", onesb, ones32)
    memset("pool", zero1, 0.0)
    memset("pool", eps1, RMS_EPS)
    seqmask = C("seqmask", 64)

    def load_w(src2d, kp, kt, ncols, q="pool"):
        slot = nxt("w", wsl)
        assert kt * ncols <= WSLOT
        v = V(slot.ap[0:kp, 0:kt * ncols].rearrange("p (k m) -> p k m", k=kt), slot.key)
        extra = [o for k_, o in S.bg.items() if k_.startswith(src2d.key + "_")]
        op = S.dma("sp", v, V(src2d.ap.rearrange("(k p) m -> p k m", p=kp), "RO"))
        have = set(id(d) for d in op.deps)
        op.deps.extend(o for o in extra if id(o) not in have)
        return v

    def layer_norm(c0, n, gcol, bcol):
        mean, rstd = ln_mean, ln_rstd
        sq4 = ln_sq4
        for s0 in range(0, n, 512):
            sn = min(512, n - s0)
            a, b = c0 + s0, c0 + s0 + sn
            p1 = nxt("pa", psA)
            p2 = nxt("pa", psA)
            for dt_ in range(8):
                mm(p1[:, 0:sn], ones32, xres[:, dt_, a:b], start=(dt_ == 0), stop=(dt_ == 7))
            for hf in range(2):
                act(sq4[:, :, 0:sn], xres[:, hf * 4:hf * 4 + 4, a:b], AF.Square)
                for q in range(4):
                    mm(p2[:, 0:sn], ones32, sq4[:, q, 0:sn], start=(hf == 0 and q == 0), stop=(hf == 1 and q == 3))
            act(mean[:, 0:sn], p1[:, 0:sn], AF.Copy, scale=1.0 / D)
            tt("pool", rstd[:, 0:sn], mean[:, 0:sn], mean[:, 0:sn], ALU.mult)
            stt(rstd[:, 0:sn], p2[:, 0:sn], 1.0 / D, rstd[:, 0:sn], ALU.mult, ALU.subtract)
            ts("dve", rstd[:, 0:sn], rstd[:, 0:sn], LN_EPS, None, ALU.add)
            act(rstd[:, 0:sn], rstd[:, 0:sn], AF.Sqrt)
            recip(rstd[:, 0:sn], rstd[:, 0:sn])
            xv = xres[:, :, a:b]
            mb = mean[:, 0:sn].re("p (o t) -> p o t", o=1).bc([128, 8, sn])
            rb = rstd[:, 0:sn].re("p (o t) -> p o t", o=1).bc([128, 8, sn])
            gb = par[:, gcol:gcol + 8].re("p (d o) -> p d o", o=1).bc([128, 8, sn])
            bb_ = par[:, bcol:bcol + 8].re("p (d o) -> p d o", o=1).bc([128, 8, sn])
            tt("dve", xv, xv, mb, ALU.subtract)
            tt("pool", xv, xv, rb, ALU.mult)
            tt("dve", xv, xv, gb, ALU.mult)
            tt("pool", xv, xv, bb_, ALU.add)
            cp("act", xbf[:, :, a:b], xv)

    stg = [sb(f"stg{i}", [128, 128]) for i in range(2)]

    def load_cols(col, src1d, t):
        st_ = nxt("stg", stg)
        S.dma("sp", st_[0:t, :], V(src1d.ap.rearrange("(t p) -> t p", p=128), "RO"))
        pb = nxt("pb", psB)
        tr(pb[:, 0:t], st_[0:t, :], ident[0:t, 0:t])
        cp("dve", par[:, col:col + t], pb[:, 0:t])

    def load_vec8(col, src1d):
        load_cols(col, src1d, 8)

    if stage == -1:
        AR.reset()
        t0 = AR.alloc("t0", [D])
        S.dma("sp", t0[0:NS, :], dr["xs"])
        S.dma("sp", dr["y_s"], t0[0:NS, :])
        S.fence()
        S.emit(nc, es)
        es.close()
        return nc
    load_vec8(0, dr["ln_in_g"])
    load_vec8(8, dr["ln_in_b"])
    AR.reset()
    xin = [AR.alloc(f"xin{i}", [D]) for i in range(2)]
    tiles = [("xp", i * 128, 128, i * 128) for i in range(SEG // 128)] + [("xs", 0, NS, SEG)]
    for ti, (src, r0, nr, c0) in enumerate(tiles):
        xi = xin[ti % 2]
        S.dma("sp", xi[0:nr, :], dr[src][r0:r0 + nr, :])
        for dq in range(2):
            pb = nxt("pb", psB)
            for q in range(4):
                dt_ = dq * 4 + q
                tr(pb[:, q * 128:q * 128 + nr], xi[0:nr, dt_ * 128:(dt_ + 1) * 128], ident[0:nr, 0:nr])
            cp("act" if dq % 2 else "dve", xres[:, dq * 4:dq * 4 + 4, c0:c0 + nr],
               pb.re("p (q t) -> p q t", q=4)[:, :, 0:nr])
    S.fence()
    AR.reset()
    if stage >= 1:
        layer_norm(0, NTOK, 0, 8)
    S.fence()

    PC = {"ln1g": 0, "ln1b": 8, "ln2g": 16, "ln2b": 24, "fcw": 32, "fcb": 32 + 132}

    for l in range(nlayers if stage >= 2 else 0):
        load_vec8(PC["ln1g"], dr["ln1_g"][l])
        load_vec8(PC["ln1b"], dr["ln1_b"][l])
        load_vec8(PC["ln2g"], dr["ln2_g"][l])
        load_vec8(PC["ln2b"], dr["ln2_b"][l])
        for tap in range(3):
            load_cols(PC["fcw"] + tap * 44, dr["ffn_conv_w"][l][tap], 44)
        load_cols(PC["fcb"], dr["ffn_conv_b"][l], 44)
        S.fence()

        wi = WB["w_in"][l]
        if l + 1 < nlayers:
            convert_weights(l + 1)
        if stage >= 5:
            AR.reset()
            wkv = load_w(wi[:, OFF["ak"]:OFF["ak"] + 256], 128, 8, 256)
            kvl = AR.alloc("kvl", [AG1W])
            kvs = AR.alloc("kvs", [256])
            pa = nxt("pa", psA)
            for kt in range(8):
                mm(pa[:, 0:256], xbf[:, kt, SEG - 128:SEG], wkv[:, kt, :], start=(kt == 0), stop=(kt == 7))
            cp("act", kvl[:, 0:256], pa[:, 0:256])
            pa = nxt("pa", psA)
            for kt in range(8):
                mm(pa[0:NS, 0:256], xbf[:, kt, SEG:SEG + NS], wkv[:, kt, :], start=(kt == 0), stop=(kt == 7))
            cp("act", kvs[0:NS, :], pa[0:NS, 0:256])
            S.dma("sp", dr["p_swa_k"][l], kvl[:, 0:128])
            S.dma("sp", dr["p_swa_v"][l], kvl[:, 128:256])
            for sq in range(4):
                S.dma("sp", dr["s_swa_k"][l][sq, 112:128, :], kvs[16 * sq:16 * sq + 16, 0:128])
                S.dma("sp", dr["s_swa_v"][l][sq, 112:128, :], kvs[16 * sq:16 * sq + 16, 128:256])
            S.dma("sp", dr["s_swa_k"][l][:, 0:112, :], dr["cache_k"][l][:, 16:128, :])
            S.dma("sp", dr["s_swa_v"][l][:, 0:112, :], dr["cache_v"][l][:, 16:128, :])
            gt3 = kvl[:, 256:AG1W].re("p (j t) -> p j t", t=3)
            for jg in range(3):
                wc_ = load_w(wi[:, OFF["cqkv"] + jg * 512:OFF["cqkv"] + (jg + 1) * 512], 128, 8, 512)
                pb = nxt("pb", psB)
                for jj in range(4):
                    for kt in range(8):
                        mm(pb[:, jj * 4:jj * 4 + 3], wc_[:, kt, jj * 128:(jj + 1) * 128], xbf[:, kt, SEG - 3:SEG],
                           start=(kt == 0), stop=(kt == 7))
                cp("dve", gt3[:, jg * 4:jg * 4 + 4, :], pb[:, 0:16].re("p (j t) -> p j t", t=4)[:, :, 0:3])
            gct = AR.alloc("gct", [C_QKV])
            for jq in range(3):
                pb = nxt("pb", psB)
                for jj in range(4):
                    tr(pb[0:3, jj * 128:(jj + 1) * 128], gt3[:, jq * 4 + jj, :], ident)
                cp("act", gct[0:3, jq * 512:(jq + 1) * 512], pb[0:3, :])
            S.dma("sp", dr["p_gconv"][l], gct[0:3, :])
            S.dma("sp", dr["ag1_in"], kvl)
            S.fence()
            a1i, a1o = dr["ag1_in"].ap, dr["ag1_out"].ap
            S.cc(lambda e: e.collective_compute("AllGather", ALU.bypass, replica_groups=GROUPS,
                                                       ins=[a1i], outs=[a1o]), ["ag1_in"], ["ag1_out"])
            g1 = AR.alloc("g1", [4, AG1W])
            S.dma("sp", g1, dr["ag1_out"].re("(r p) c -> p r c", p=128))
            hsel = AR.alloc("hsel", [AG1W])
            ts("dve", hsel, g1[:, 0, :], corev[:, 1:2], None, ALU.mult)
            for j in range(1, 4):
                stt(hsel, g1[:, j, :], corev[:, 1 + j:2 + j], hsel, ALU.mult, ALU.add)
            cp("pool", gc_prev, hsel[:, 256:AG1W].re("p (j t) -> p j t", t=3))
            cp("pool", vprev, hsel[:, 128:256])
            pb = nxt("pb", psB)
            for g in range(2):
                tr(pb[0:64, g * 128:(g + 1) * 128], hsel[:, g * 64:(g + 1) * 64], ident)
            cp("act", kprev, pb[0:64, 0:256].re("p (g t) -> p g t", g=2))
            es8 = AR.alloc("es8", [8])
            S.dma("sp", es8[0:64, :], V(dr["attn_sinks"].ap[l].partition_broadcast(64), "RO"))
            act(es8[0:64, :], es8[0:64, :], AF.Exp)
            cp("dve", esinkB.re("p (h t) -> p h t", h=8), es8[0:64, :].re("p (h o) -> p h o", o=1).bc([64, 8, 64]))
            S.fence()

        if stage >= 6:
            if l == 0:
                st_ = nxt("stg", stg)
                S.dma("sp", st_[0:16, :], V(dr["hgrn_lb_logits"].ap.rearrange("l (h p) -> (l h) p", p=128), "RO"))
                pb = nxt("pb", psB)
                tr(pb[:, 0:16], st_[0:16, :], ident[0:16, 0:16])
                AR.reset()
                e16 = AR.alloc("e16", [4, 4])
                act(e16.re("p l h -> p (l h)"), pb[:, 0:16], AF.Exp)
                ssum = AR.alloc("ssum", [4])
                tt("dve", ssum, e16[:, 0, :], e16[:, 1, :], ALU.add)
                tt("dve", ssum, ssum, e16[:, 2, :], ALU.add)
                tt("dve", ssum, ssum, e16[:, 3, :], ALU.add)
                recip(ssum, ssum)
                for ll in range(1, 4):
                    tt("dve", e16[:, ll, :], e16[:, ll, :], ssum, ALU.mult)
                memset("pool", lbF[:, 0, :], 0.0)
                cp("dve", lbF[:, 1, :], e16[:, 1, :])
                tt("dve", lbF[:, 2, :], lbF[:, 1, :], e16[:, 2, :], ALU.add)
                tt("dve", lbF[:, 3, :], lbF[:, 2, :], e16[:, 3, :], ALU.add)
                S.fence()
            AR.reset()
            lbT = AR.alloc("lbT", [512])
            omlT = AR.alloc("omlT", [512])
            omlF = AR.alloc("omlF", [4])
            nomlF = AR.alloc("nomlF", [4])
            hSs = AR.alloc("hSs", [4, 4, 128])
            keep1 = AR.off
            lg = AR.alloc("lg", [4, 512])
            S.dma("sp", lg, V(dr["hgrn_lb_logits"].ap.partition_broadcast(128), "RO"))
            act(lg, lg, AF.Exp)
            tt("dve", omlT, lg[:, 0, :], lg[:, 1, :], ALU.add)
            tt("dve", omlT, omlT, lg[:, 2, :], ALU.add)
            tt("dve", omlT, omlT, lg[:, 3, :], ALU.add)
            recip(omlT, omlT)
            memset("pool", lbT, 0.0)
            for ll in range(1, l + 1):
                tt("dve", lbT, lbT, lg[:, ll, :], ALU.add)
            tt("dve", lbT, lbT, omlT, ALU.mult)
            ts("dve", omlT, lbT, -1.0, 1.0, ALU.mult, ALU.add)
            ts("dve", omlF, lbF[:, l, :], -1.0, 1.0, ALU.mult, ALU.add)
            ts("dve", nomlF, omlF, -1.0, None, ALU.mult)
            S.dma("sp", nrmw[:, 0:1], V(dr["hgrn_norm_w"].ap[l].rearrange("(p o) -> p o", o=1), "RO"))
            S.dma("sp", nrmw[:, 1:2], V(dr["gdn_norm_w"].ap[l].rearrange("(p o) -> p o", o=1), "RO"))
            memset("pool", hS, 0.0)
            memset("pool", hpre, 1.0)
            for sq in range(4):
                S.dma("sp", hSs[:, sq, :, :], V(dr["st_hgrn"].ap[l][sq].rearrange("h k v -> k h v"), "RO"))
            S.fence()
            for b in range(NBLK):
                AR.off = keep1
                AR.gen += 1
                c0 = b * TB
                last = (b == NBLK - 1)
                n = TB + (NS if last else 0)
                segs = [(c0, 0, TB)] + ([(SEG, TB, NS)] if last else [])
                wbq = load_w(wi[:, OFF["bq"]:OFF["bq"] + 512], 128, 8, 512)
                wbf = load_w(wi[:, OFF["bf"]:OFF["bf"] + 512], 128, 8, 512)
                wbi = load_w(wi[:, OFF["bi"]:OFF["bi"] + 512], 128, 8, 512)
                qT = AR.alloc("qT", [4, n], BF16)
                kT = AR.alloc("kT", [4, n], BF16)
                obuf = AR.alloc("obuf", [4, n], BF16)
                qhat = AR.alloc("qhat", [4, TB], BF16)
                sgF = [AR.alloc(f"sgF{i}", [512]) for i in range(2)]
                for h in range(4):
                    for (a, ha, sn) in segs:
                        pa = nxt("pa", psA)
                        for kt in range(8):
                            mm(pa[:, 0:sn], wbq[:, kt, h * 128:(h + 1) * 128], xbf[:, kt, a:a + sn], start=(kt == 0), stop=(kt == 7))
                        act(qT[:, h, ha:ha + sn], pa[:, 0:sn], AF.Silu)
                        pa = nxt("pa", psA)
                        for kt in range(8):
                            mm(pa[:, 0:sn], wbf[:, kt, h * 128:(h + 1) * 128], xbf[:, kt, a:a + sn], start=(kt == 0), stop=(kt == 7))
                        sg = sgF[h % 2]
                        act(sg[:, 0:sn], pa[:, 0:sn], AF.Sigmoid)
                        ts("dve", kT[:, h, ha:ha + sn], sg[:, 0:sn], nomlF[:, h:h + 1], omlF[:, h:h + 1], ALU.mult, ALU.add)
                sgT = AR.alloc("sgT", [512])
                tT = AR.alloc("tT", [512])
                lfT = AR.alloc("lfT", [512])
                kTM = AR.alloc("kTM", [512])
                vTM = AR.alloc("vTM", [512], BF16)
                kend = AR.alloc("kend", [512], BF16)
                kendm = AR.alloc("kendm", [4, 512], BF16) if last else None
                eq = AR.alloc("eq", [128])
                ek = AR.alloc("ek", [128])
                dm = AR.alloc("dm", [8])
                qt = AR.alloc("qt", [128], BF16)
                kt_ = AR.alloc("kt_", [128], BF16)
                attm = AR.alloc("attm", [128], BF16)
                sbf = AR.alloc("sbf", [128], BF16)
                fac1 = AR.alloc("fac1", [1])
                tl = [(c0 + t * 128, t * 128, 128, "p") for t in range(4)] + ([(SEG, TB, NS, "s")] if last else [])
                for (a, ha, T, tag) in tl:
                    nb = 2 if tag == "p" else 4
                    blk = T // nb
                    hdext = C(f"hdext_{tag}", T)
                    hle = C(f"hle_{tag}", T)
                    hgt = C(f"hgt_{tag}", T)
                    pf = nxt("pa", psA)
                    pi = nxt("pa", psA)
                    for kt in range(8):
                        mm(pf[0:T, :], xbf[:, kt, a:a + T], wbf[:, kt, :], start=(kt == 0), stop=(kt == 7))
                    for kt in range(8):
                        mm(pi[0:T, :], xbf[:, kt, a:a + T], wbi[:, kt, :], start=(kt == 0), stop=(kt == 7))
                    act(sgT[0:T, :], pf[0:T, :], AF.Sigmoid)
                    tt("dve", tT[0:T, :], sgT[0:T, :], omlT[0:T, :], ALU.mult)
                    stt(sgT[0:T, :], tT[0:T, :], 1e-30, lbT[0:T, :], ALU.max, ALU.add)
                    act(lfT[0:T, :], sgT[0:T, :], AF.Ln)
                    tt("pool", kTM[0:T, :], omlT[0:T, :], tT[0:T, :], ALU.subtract)
                    cp("act", vTM[0:T, :], pi[0:T, :])
                    pr = nxt("pa", psA)
                    mm(pr[0:T, :], hgt, lfT[0:T, :])
                    act(tT[0:T, :], pr[0:T, :], AF.Exp)
                    tt("dve", kend[0:T, :], kTM[0:T, :], tT[0:T, :], ALU.mult)
                    if tag == "s":
                        bls = C("blk_s", T)
                        for bb in range(4):
                            ts("pool", kendm[0:T, bb, :], kend[0:T, :], bls[:, bb:bb + 1], None, ALU.mult)
                    for h in range(4):
                        hs = slice(h * 128, (h + 1) * 128)
                        pc = nxt("pb", psB)
                        mm(pc[:, 0:T + 2 * nb], lfT[0:T, hs], hdext)
                        act(eq[:, 0:T], pc[:, 0:T], AF.Exp)
                        tt("dve", qt[:, 0:T], qT[:, h, ha:ha + T], eq[:, 0:T], ALU.mult)
                        act(ek[:, 0:T], pc[:, 0:T], AF.Exp, scale=-1.0)
                        tt("pool", kt_[:, 0:T], kT[:, h, ha:ha + T], ek[:, 0:T], ALU.mult)
                        act(dm[:, 0:2 * nb], pc[:, T:T + 2 * nb], AF.Exp)
                        pat = nxt("pb", psB)
                        mm(pat[0:T, 0:T], kt_[:, 0:T], qt[:, 0:T])
                        stt(attm[0:T, 0:T], pat[0:T, 0:T], 1e30, hle, ALU.min, ALU.mult)
                        po = nxt("pb", psB)
                        mm(po[:, 0:T], vTM[0:T, hs], attm[0:T, 0:T], start=True, stop=False)
                        for bb in range(nb):
                            Sb = hS[:, h, :] if tag == "p" else hSs[:, bb, h, :]
                            bs = slice(bb * blk, (bb + 1) * blk)
                            act(sbf, Sb, AF.Copy, scale=dm[:, nb + bb:nb + bb + 1])
                            mm(po[:, bs], sbf, qt[:, bs], start=False, stop=(bb == nb - 1))
                            pu = nxt("pa", psA)
                            if tag == "p":
                                ts("dve", fac1, dm[:, nb + bb:nb + bb + 1], hpre[:, h:h + 1], None, ALU.mult)
                                ts("dve", qhat[:, h, ha + bb * blk:ha + (bb + 1) * blk], qt[:, bs], fac1, None, ALU.mult)
                                mm(pu[:, 0:128], kend[bs, hs], vTM[bs, hs])
                            else:
                                mm(pu[:, 0:128], kendm[0:T, bb, hs], vTM[0:T, hs])
                            stt(Sb, Sb, dm[:, bb:bb + 1], pu[:, 0:128], ALU.mult, ALU.add)
                            if tag == "p":
                                tt("dve", hpre[:, h:h + 1], hpre[:, h:h + 1], dm[:, bb:bb + 1], ALU.mult)
                        cp("act", obuf[:, h, ha:ha + T], po[:, 0:T])
                S.dma("sp", dr["ob_loc"][:, :, c0:c0 + TB], obuf[:, :, 0:TB])
                S.dma("sp", dr["qh"][:, :, c0:c0 + TB], qhat)
                if last:
                    S.dma("sp", dr["ob_loc"][:, :, SEG:SEG + NS], obuf[:, :, TB:TB + NS])
                S.fence()
            for sq in range(4):
                S.dma("sp", V(dr["s_hgrn"].ap[l][sq].rearrange("h k v -> k h v"), "out_s_hgrn"), hSs[:, sq, :, :])
            if stage >= 7:
                S.fence()
                AR.off = 0
                AR.gen += 1
                for tap in range(4):
                    st_ = nxt("stg", stg)
                    S.dma("sp", st_[0:12, :], V(dr["gdn_conv_w"].ap[l][tap].rearrange("(t p) -> t p", p=128), "RO"))
                    pb = nxt("pb", psB)
                    tr(pb[:, 0:12], st_[0:12, :], ident[0:12, 0:12])
                    cp("dve", gpar[:, tap * 12:(tap + 1) * 12], pb[:, 0:12])
                gtmp = AR.alloc("gtmp", [2, 4 * DEPTH])
                S.dma("sp", gtmp[:, 0, :], V(dr["gdn_dt_bias"].ap.rearrange("l h -> (l h)").partition_broadcast(128), "RO"))
                S.dma("sp", gtmp[:, 1, :], V(dr["gdn_a_log"].ap.rearrange("l h -> (l h)").partition_broadcast(128), "RO"))
                cp("dve", gpar[:, 48:52], gtmp[:, 0, 4 * l:4 * l + 4])
                act(gpar[:, 52:56], gtmp[:, 1, 4 * l:4 * l + 4], AF.Exp)
                ts("dve", gpar[:, 52:56], gpar[:, 52:56], -1.0, None, ALU.mult)
                gS = AR.alloc("gS", [4, 256])
                gSb = AR.alloc("gSb", [4, 256], BF16)
                memset("pool", gS, 0.0)
                for h in range(4):
                    cp("dve", gS[:, h, 128:256], ident)
                cp("act", gSb, gS)
                keepg = AR.off
                triLE = C("triLE_p")
                triGT = C("triGT_p")
                mbias = C("mbias_p")
                offd = C("offd_p")
                S.fence()
                import os
                for u_ in range(NBLK + (0 if os.environ.get('GDN_NOSAMPLE') else 1)):
                    AR.off = keepg
                    AR.gen += 1
                    smp = (u_ == NBLK)
                    n = NS if smp else TB
                    a0 = SEG if smp else u_ * TB
                    wcq = [load_w(wi[:, OFF["cqkv"] + jg * 512:OFF["cqkv"] + (jg + 1) * 512], 128, 8, 512) for jg in range(1)]
                    qn = AR.alloc("qn", [4, n], BF16)
                    kn = AR.alloc("kn", [4, n], BF16)
                    ntile = 4
                    TT = 16 if smp else 128
                    kTMa = AR.alloc("kTMa", [ntile, 4, 128], BF16)
                    vTMa = AR.alloc("vTMa", [ntile, 4, 128], BF16)
                    ocb = AR.alloc("ocb", [4, n], BF16)
                    rhb = AR.alloc("rhb", [4, n], BF16)
                    if smp:
                        gSs = AR.alloc("gSs", [4, 4, 128])
                        gSsb = AR.alloc("gSsb", [4, 4, 128], BF16)
                    mark_conv = AR.off
                    cpre = [AR.alloc(f"cpre{i}", [n + 12]) for i in range(2)]
                    u32 = [AR.alloc(f"u32{i}", [n]) for i in range(2)]
                    sqg = AR.alloc("sqg", [n])
                    rsg = AR.alloc("rsg", [n])
                    if smp:
                        for sq in range(4):
                            S.dma("sp", gSs[:, sq, :, :], V(dr["st_gdn"].ap[l][sq].rearrange("h k v -> k h v"), "RO"))
                        cp("act", gSsb, gSs)
                        sg12 = AR.alloc("sg12", [C_QKV])
                        so12 = AR.alloc("so12", [C_QKV])
                        S.dma("sp", sg12[0:12, :], V(dr["st_gconv"].ap[l].rearrange("s r c -> (s r) c"), "RO"))
                    for j in range(12):
                        if j % 4 == 0 and j > 0:
                            wcq = [load_w(wi[:, OFF["cqkv"] + (j // 4) * 512:OFF["cqkv"] + (j // 4 + 1) * 512], 128, 8, 512)]
                        pa = nxt("pa", psA)
                        for kt in range(8):
                            mm(pa[:, 0:n], wcq[0][:, kt, (j % 4) * 128:(j % 4 + 1) * 128], xbf[:, kt, a0:a0 + n], start=(kt == 0), stop=(kt == 7))
                        cpj = cpre[j % 2]
                        uj = u32[j % 2]
                        gw = lambda t, j=j: gpar[:, t * 12 + j:t * 12 + j + 1]
                        if not smp:
                            cp("pool", cpj[:, 0:3], gc_prev[:, j, :])
                            cp("act", cpj[:, 3:3 + n], pa[:, 0:n])
                            cp("pool", gc_prev[:, j, :], cpj[:, n:n + 3])
                            ts("pool", uj, cpj[:, 0:n], gw(0), None, ALU.mult)
                            for t in range(1, 4):
                                stt(uj, cpj[:, t:t + n], gw(t), uj, ALU.mult, ALU.add)
                        else:
                            c3 = cpj[:, 0:76].re("p (s t) -> p s t", s=4)
                            pb = nxt("pb", psB)
                            tr(pb[:, 0:12], sg12[0:12, j * 128:(j + 1) * 128], ident[0:12, 0:12])
                            cp("dve", c3[:, :, 0:3], pb[:, 0:12].re("p (s r) -> p s r", s=4))
                            cp("act", c3[:, :, 3:19], pa[:, 0:n].re("p (s t) -> p s t", s=4))
                            u3 = uj.re("p (s t) -> p s t", s=4)
                            ts("pool", u3, c3[:, :, 0:16], gw(0), None, ALU.mult)
                            for t in range(1, 4):
                                stt(u3, c3[:, :, t:t + 16], gw(t), u3, ALU.mult, ALU.add)
                            uc12 = sqg[:, 0:12]
                            cp("pool", uc12.re("p (s r) -> p s r", s=4), c3[:, :, 16:19])
                            pb = nxt("pb", psB)
                            tr(pb[0:12, 0:128], uc12, ident)
                            cp("dve", so12[0:12, j * 128:(j + 1) * 128], pb[0:12, 0:128])
                        act(uj, uj, AF.Silu)
                        h = j % 4
                        if j < 8:
                            act(sqg, uj, AF.Square)
                            ps_ = nxt("pb", psB)
                            mm(ps_[:, 0:n], ones32, sqg)
                            act(rsg, ps_[:, 0:n], AF.Sqrt, bias=eps1)
                            recip(rsg, rsg)
                            if j < 4:
                                stt(qn[:, h, :], uj, 128.0 ** -0.5, rsg, ALU.mult, ALU.mult)
                            else:
                                tt("dve", uj, uj, rsg, ALU.mult)
                                cp("pool", kn[:, h, :], uj)
                        if j >= 4:
                            dstT = kTMa if j < 8 else vTMa
                            for t in range(ntile):
                                pb = nxt("pb", psB)
                                tr(pb[0:TT, 0:128], uj[:, t * TT:(t + 1) * TT], ident)
                                cp("act" if t % 2 else "dve", dstT[0:TT, t, h, :], pb[0:TT, 0:128])
                    if smp:
                        S.dma("sp", V(dr["s_gconv"].ap[l].rearrange("s r c -> (s r) c"), "out_s_gconv"), so12[0:12, :])
                    S.fence()
                    AR.off = mark_conv
                    AR.gen += 1
                    wcb = load_w(wi[:, OFF["cb"] - 248:OFF["cb"] + 8], 128, 8, 256)[:, :, 248:256]
                    NV = 128 if smp else 256
                    bg8 = AR.alloc("bg8", [8])
                    beta = AR.alloc("beta", [4])
                    gg = AR.alloc("gg", [4])
                    sm8 = AR.alloc("sm8", [16])
                    er_ = [AR.alloc(f"erow{i}", [TT]) for i in range(2)]
                    dT_ = [AR.alloc(f"decT{i}", [TT]) for i in range(1)] * 2
                    dS_ = [AR.alloc(f"decS{i}", [TT]) for i in range(1)] * 2
                    AA = [[AR.alloc(f"A{h}_{i}", [TT]) for i in range(2)] for h in range(4)]
                    ATT = [[AR.alloc(f"AT{h}_{i}", [TT]) for i in range(2)] for h in range(4)]
                    WW = [[AR.alloc(f"W{h}_{i}", [TT]) for i in range(1)] * 2 for h in range(4)]
                    QKd = [AR.alloc(f"QKd{h}", [TT], BF16) for h in range(4)]
                    keT = [AR.alloc(f"keT{h}", [TT], BF16) for h in range(4)]
                    qeT = [AR.alloc(f"qeT{h}", [TT], BF16) for h in range(4)]
                    ktil = [AR.alloc(f"ktil{h}", [128], BF16) for h in range(4)]
                    vaug = [AR.alloc(f"vaug{h}", [NV], BF16) for h in range(4)]
                    Wmb = [AR.alloc(f"Wmb{h}", [TT], BF16) for h in range(4)]
                    P1s = [AR.alloc(f"P1s{h}", [NV], BF16) for h in range(4)]
                    wbf_ = [AR.alloc(f"wbf_{h}", [NV], BF16) for h in range(4)]
                    for h in range(4):
                        memset("pool", vaug[h], 0.0)
                    nsq = 3 if smp else 6
                    for t in range(ntile):
                        T = TT
                        a = a0 + t * T
                        cs = slice(t * T, (t + 1) * T)
                        pg = nxt("pb", psB)
                        for kt in range(8):
                            mm(pg[0:T, 0:8], xbf[:, kt, a:a + T], wcb[:, kt, :], start=(kt == 0), stop=(kt == 7))
                        cp("dve", bg8[0:T, :], pg[0:T, 0:8])
                        act(beta[0:T, :], bg8[0:T, 0:4], AF.Sigmoid)
                        tt("dve", gg[0:T, :], bg8[0:T, 4:8], gpar[0:T, 48:52], ALU.add)
                        act(gg[0:T, :], gg[0:T, :], AF.Exp)
                        act(gg[0:T, :], gg[0:T, :], AF.Ln, bias=1.0)
                        tt("dve", gg[0:T, :], gg[0:T, :], gpar[0:T, 52:56], ALU.mult)
                        pcm = nxt("pb", psB)
                        mm(pcm[0:T, 0:4], triLE[0:T, 0:T], gg[0:T, :])
                        mm(pcm[0:T, 4:8], triGT[0:T, 0:T], gg[0:T, :])
                        mm(pcm[:, 8:12], ones32[0:T, :], gg[0:T, :])
                        act(sm8[0:T, 0:4], pcm[0:T, 0:4], AF.Exp)
                        act(sm8[0:T, 4:8], pcm[0:T, 0:4], AF.Copy, scale=-1.0)
                        act(sm8[0:T, 8:12], pcm[0:T, 4:8], AF.Exp)
                        act(sm8[:, 12:16], pcm[:, 8:12], AF.Exp)
                        Sts = [(gSs[:, t, h, :] if smp else gS[:, h, :]) for h in range(4)]
                        Sbs = [(gSsb[:, t, h, :] if smp else gSb[:, h, :]) for h in range(4)]
                        for h in range(4):
                            erow, decT, decS = er_[h % 2], dT_[h % 2], dS_[h % 2]
                            pcr = nxt("pb", psB)
                            gbc = gg[0:T, h:h + 1].bc([T, 128])
                            mm(pcr[:, 0:T], gbc, triLE[0:T, 0:T])
                            mm(pcr[0:T, 128:128 + T], gg[0:T, h:h + 1].bc([T, T]), triLE[0:T, 0:T], start=True, stop=False)
                            mm(pcr[0:T, 128:128 + T], ident[0:T, 0:T], mbias[0:T, 0:T], start=False, stop=True)
                            act(erow[:, 0:T], pcr[:, 0:T], AF.Exp)
                            act(decT[0:T, 0:T], pcr[0:T, 128:128 + T], AF.Exp, bias=sm8[0:T, 4 + h:5 + h])
                            tt("pool", decS[0:T, 0:T], decT[0:T, 0:T], offd[0:T, 0:T], ALU.mult)
                            pk = nxt("pb", psB)
                            mm(pk[0:T, 0:T], kn[:, h, cs], kn[:, h, cs])
                            mm(pk[0:T, 128:128 + T], kn[:, h, cs], qn[:, h, cs])
                            U32 = AA[h][0]
                            stt(U32[0:T, 0:T], pk[0:T, 0:T], beta[0:T, h:h + 1], decS[0:T, 0:T], ALU.mult, ALU.mult)
                            tt("dve", QKd[h][0:T, 0:T], pk[0:T, 128:128 + T], decT[0:T, 0:T], ALU.mult)
                            tt("pool", keT[h][:, 0:T], kn[:, h, cs], erow[:, 0:T], ALU.mult)
                            tt("pool", qeT[h][:, 0:T], qn[:, h, cs], erow[:, 0:T], ALU.mult)
                            ts("pool", ktil[h][0:T, :], kTMa[0:T, t, h, :], sm8[0:T, 8 + h:9 + h], None, ALU.mult)
                            cp("pool", vaug[h][0:T, 0:128], vTMa[0:T, t, h, :])
                            pl_ = nxt("pb", psB)
                            tr(pl_[0:T, 0:T], U32[0:T, 0:T], ident[0:T, 0:T])
                            cp("act", ATT[h][0][0:T, 0:T], pl_[0:T, 0:T])
                            tt("dve", WW[h][0][0:T, 0:T], ident[0:T, 0:T], U32[0:T, 0:T], ALU.subtract)
                        for js in range(nsq):
                            i0_, i1_ = js % 2, (js + 1) % 2
                            for h in range(4):
                                p1_ = nxt("pb", psB)
                                p1b_ = nxt("pb", psB)
                                mm(p1_[0:T, 0:T], ATT[h][i0_][0:T, 0:T], AA[h][i0_][0:T, 0:T])
                                mm(p1b_[0:T, 0:T], AA[h][i0_][0:T, 0:T], ATT[h][i0_][0:T, 0:T])
                                cp("dve", AA[h][i1_][0:T, 0:T], p1_[0:T, 0:T])
                                cp("act", ATT[h][i1_][0:T, 0:T], p1b_[0:T, 0:T])
                            for h in range(4):
                                p2_ = nxt("pa", psA)
                                mm(p2_[0:T, 0:T], ATT[h][i1_][0:T, 0:T], WW[h][i0_][0:T, 0:T])
                                tt("dve", WW[h][i1_][0:T, 0:T], WW[h][i0_][0:T, 0:T], p2_[0:T, 0:T], ALU.add)
                        for h in range(4):
                            cp("act", Wmb[h][0:T, 0:T], WW[h][nsq % 2][0:T, 0:T])
                        for h in range(4):
                            pp = nxt("pa", psA)
                            mm(pp[0:T, 0:NV], keT[h][:, 0:T], Sbs[h])
                            tt("dve", P1s[h][0:T, 0:NV], vaug[h][0:T, 0:NV], pp[0:T, 0:NV], ALU.subtract)
                        for h in range(4):
                            pw = nxt("pa", psA)
                            mm(pw[0:T, 0:NV], Wmb[h][0:T, 0:T], P1s[h][0:T, 0:NV])
                            act(wbf_[h][0:T, 0:NV], pw[0:T, 0:NV], AF.Copy, scale=beta[0:T, h:h + 1])
                        for h in range(4):
                            pq = nxt("pa", psA)
                            mm(pq[:, 0:T], wbf_[h][0:T, 0:128], QKd[h][0:T, 0:T], start=True, stop=False)
                            mm(pq[:, 0:T], Sbs[h][:, 0:128], qeT[h][:, 0:T], start=False, stop=True)
                            cp("act", ocb[:, h, cs], pq[:, 0:T])
                            if not smp:
                                mm(pq[:, 128:128 + T], wbf_[h][0:T, 128:256], QKd[h][0:T, 0:T], start=True, stop=False)
                                mm(pq[:, 128:128 + T], Sbs[h][:, 128:256], qeT[h][:, 0:T], start=False, stop=True)
                                cp("dve", rhb[:, h, cs], pq[:, 128:128 + T])
                        for h in range(4):
                            pu = nxt("pa", psA)
                            mm(pu[:, 0:NV], ktil[h][0:T, :], wbf_[h][0:T, 0:NV])
                            stt(Sts[h], Sts[h], sm8[:, 12 + h:13 + h], pu[:, 0:NV], ALU.mult, ALU.add)
                            cp("act", Sbs[h], Sts[h])
                    if smp:
                        S.dma("sp", dr["oc_loc"][:, :, SEG:SEG + NS], ocb)
                        for sq in range(4):
                            S.dma("sp", V(dr["s_gdn"].ap[l][sq].rearrange("h k v -> k h v"), "out_s_gdn"), gSs[:, sq, :, :])
                    else:
                        S.dma("sp", dr["oc_loc"][:, :, a0:a0 + TB], ocb)
                        S.dma("sp", dr["rh"][:, :, a0:a0 + TB], rhb)
                    S.fence()
                S.dma("sp", dr["ag2_in"][:, 516:516 + 1024].re("p (h c) -> p h c", h=4), gS)
                AR.off = keepg
                AR.gen += 1
            AR.off = keep1
            AR.gen += 1
            S.dma("sp", dr["ag2_in"][:, 0:512], hS.re("p h v -> p (h v)"))
            S.dma("sp", dr["ag2_in"][:, 512:516], hpre)
            S.fence()
            a2i, a2o = dr["ag2_in"].ap, dr["ag2_out"].ap
            S.cc(lambda e: e.collective_compute("AllGather", ALU.bypass, replica_groups=GROUPS,
                                                       ins=[a2i], outs=[a2o]), ["ag2_in"], ["ag2_out"])
            g2 = AR.alloc("g2", [4, AG2W])
            S.dma("sp", g2, dr["ag2_out"].re("(r p) c -> p r c", p=128))
            X = AR.alloc("X", [4, 128])
            Sin = AR.alloc("Sin", [4, 128])
            memset("pool", X, 0.0)
            memset("pool", Sin, 0.0)
            for j in range(3):
                for h in range(4):
                    stt(X[:, h, :], X[:, h, :], g2[:, j, 512 + h:513 + h], g2[:, j, h * 128:(h + 1) * 128], ALU.mult, ALU.add)
                stt(Sin.re("p h v -> p (h v)"), X.re("p h v -> p (h v)"), corev[:, 6 + j:7 + j], Sin.re("p h v -> p (h v)"), ALU.mult, ALU.add)
            cp("act", sinB, Sin)
            for h in range(4):
                stt(X[:, h, :], Sin[:, h, :], hpre[:, h:h + 1], hS[:, h, :], ALU.mult, ALU.add)
            S.dma("sp", V(dr["p_hgrn"].ap[l].rearrange("h k v -> k h v"), "out_p_hgrn"), X)
            if stage >= 7:
                Xc = AR.alloc("Xc", [4, 128])
                SinC32 = AR.alloc("SinC32", [4, 128])
                Xcb = AR.alloc("Xcb", [4, 128], BF16)
                PmT = AR.alloc("PmT", [128], BF16)
                memset("pool", Xc, 0.0)
                memset("pool", SinC32, 0.0)
                for j in range(3):
                    for h in range(4):
                        base = 516 + h * 256
                        pt = nxt("pb", psB)
                        tr(pt[:, 0:128], g2[:, j, base + 128:base + 256], ident)
                        cp("act", PmT, pt[:, 0:128])
                        cp("dve", Xcb[:, h, :], Xc[:, h, :])
                        px = nxt("pb", psB)
                        mm(px[:, 0:128], PmT, Xcb[:, h, :])
                        tt("dve", Xc[:, h, :], px[:, 0:128], g2[:, j, base:base + 128], ALU.add)
                    stt(SinC32.re("p h v -> p (h v)"), Xc.re("p h v -> p (h v)"), corev[:, 6 + j:7 + j], SinC32.re("p h v -> p (h v)"), ALU.mult, ALU.add)
                cp("act", sinC, SinC32)
                for h in range(4):
                    pt = nxt("pb", psB)
                    tr(pt[:, 0:128], gS[:, h, 128:256], ident)
                    cp("act", PmT, pt[:, 0:128])
                    px = nxt("pb", psB)
                    mm(px[:, 0:128], PmT, sinC[:, h, :])
                    tt("dve", Xc[:, h, :], px[:, 0:128], gS[:, h, 0:128], ALU.add)
                S.dma("sp", V(dr["p_gdn"].ap[l].rearrange("h k v -> k h v"), "out_p_gdn"), Xc)
            S.fence()

        for b in range(NBLK):
            AR.reset()
            c0 = b * TB
            last = (b == NBLK - 1)
            n = TB + (NS if last else 0)
            segs = [(c0, 0, TB)] + ([(SEG, TB, NS)] if last else [])
            o_a = AR.alloc("o_a", [8, n], BF16)
            o_bn = AR.alloc("o_bn", [4, n], BF16)
            o_cn = AR.alloc("o_cn", [4, n], BF16)
            mark = AR.off
            for (tagB, onrm, loc, qsrc, sinX, zoff, nwc, stg_min) in (("B", o_bn, "ob_loc", "qh", sinB, OFF["bg"], 0, 6), ("C", o_cn, "oc_loc", "rh", sinC, OFF["cz"], 1, 7)):
                if stage < stg_min:
                    continue
                obl = AR.alloc("obl", [4, n], BF16)
                qhl = AR.alloc("qhl", [4, TB], BF16)
                o32 = AR.alloc("o32", [512])
                sq32 = AR.alloc("sq32", [512])
                rs_ = AR.alloc("rs_", [512])
                sz = AR.alloc("sz", [512])
                S.dma("sp", obl[:, :, 0:TB], dr[loc][:, :, c0:c0 + TB])
                if last:
                    S.dma("sp", obl[:, :, TB:TB + NS], dr[loc][:, :, SEG:SEG + NS])
                S.dma("sp", qhl, dr[qsrc][:, :, c0:c0 + TB])
                wz = load_w(wi[:, zoff:zoff + 512], 128, 8, 512)
                for h in range(4):
                    for (a, ha, sn) in segs:
                        if ha == 0:
                            pd = nxt("pb", psB)
                            mm(pd[:, 0:TB], sinX[:, h, :], qhl[:, h, :])
                            tt("dve", o32[:, 0:sn], pd[:, 0:TB], obl[:, h, 0:TB], ALU.add)
                        else:
                            cp("dve", o32[:, 0:sn], obl[:, h, TB:TB + NS])
                        act(sq32[:, 0:sn], o32[:, 0:sn], AF.Square)
                        ps_ = nxt("pb", psB)
                        mm(ps_[:, 0:sn], ones32, sq32[:, 0:sn])
                        act(rs_[:, 0:sn], ps_[:, 0:sn], AF.Sqrt, scale=1.0 / 128, bias=eps1)
                        recip(rs_[:, 0:sn], rs_[:, 0:sn])
                        pz = nxt("pa", psA)
                        for kt in range(8):
                            mm(pz[:, 0:sn], wz[:, kt, h * 128:(h + 1) * 128], xbf[:, kt, a:a + sn], start=(kt == 0), stop=(kt == 7))
                        act(sz[:, 0:sn], pz[:, 0:sn], AF.Silu)
                        tt("dve", o32[:, 0:sn], o32[:, 0:sn], rs_[:, 0:sn], ALU.mult)
                        stt(onrm[:, h, ha:ha + sn], o32[:, 0:sn], nrmw[:, nwc:nwc + 1], sz[:, 0:sn], ALU.mult, ALU.mult)
                S.fence()
                AR.off = mark
                AR.gen += 1
            if stage >= 5:
                qF = AR.alloc("qF", [8, n], BF16)
                kF = AR.alloc("kF", [2, 128 + TB], BF16)
                vT = AR.alloc("vT", [5, 128], BF16)
                wq = load_w(wi[:, OFF["aq"]:OFF["aq"] + 512], 128, 8, 512)
                wkv = load_w(wi[:, OFF["ak"]:OFF["ak"] + 256], 128, 8, 256)
                cp("pool", kF[0:64, :, 0:128], kprev)
                cp("pool", vT[:, 0, :], vprev)
                if last:
                    kFs = AR.alloc("kFs", [2, NS], BF16)
                    vTs = AR.alloc("vTs", [128], BF16)
                for h in range(8):
                    for (a, ha, sn) in segs:
                        pa = nxt("pa", psA)
                        for kt in range(8):
                            mm(pa[0:64, 0:sn], wq[:, kt, h * 64:(h + 1) * 64], xbf[:, kt, a:a + sn], start=(kt == 0), stop=(kt == 7))
                        act(qF[0:64, h, ha:ha + sn], pa[0:64, 0:sn], AF.Copy, scale=0.125)
                for g in range(2):
                    for (a, ha, sn) in segs:
                        pa = nxt("pa", psA)
                        for kt in range(8):
                            mm(pa[0:64, 0:sn], wkv[:, kt, g * 64:(g + 1) * 64], xbf[:, kt, a:a + sn], start=(kt == 0), stop=(kt == 7))
                        if ha == 0:
                            cp("act", kF[0:64, g, 128:128 + TB], pa[0:64, 0:TB])
                        else:
                            cp("act", kFs[0:64, g, :], pa[0:64, 0:NS])
                for t in range(4):
                    pa = nxt("pa", psA)
                    for kt in range(8):
                        mm(pa[:, 0:128], xbf[:, kt, c0 + t * 128:c0 + (t + 1) * 128], wkv[:, kt, 128:256], start=(kt == 0), stop=(kt == 7))
                    cp("dve", vT[:, 1 + t, :], pa[:, 0:128])
                if last:
                    pa = nxt("pa", psA)
                    for kt in range(8):
                        mm(pa[0:NS, 0:128], xbf[:, kt, SEG:SEG + NS], wkv[:, kt, 128:256], start=(kt == 0), stop=(kt == 7))
                    cp("dve", vTs[0:NS, :], pa[0:NS, 0:128])
                E = [AR.alloc(f"E{i}", [512], BF16) for i in range(2)]
                dsm = [AR.alloc(f"dsm{i}", [256]) for i in range(2)]
                it = 0
                for cq in range(8):
                    i = cq // 2
                    for g in range(2):
                        rX = slice(0, 128) if cq % 2 == 0 else slice(64, 128)
                        rY = slice(0, 64) if cq % 2 == 0 else slice(0, 128)
                        ps_ = nxt("pa", psA)
                        q3 = qF[0:64, 4 * g:4 * g + 4, cq * 64:(cq + 1) * 64]
                        mm(ps_[:, 0:256].re("p (h t) -> p h t", h=4), kF[0:64, g, i * 128:(i + 1) * 128], q3)
                        mm(ps_[:, 256:512].re("p (h t) -> p h t", h=4), kF[0:64, g, (i + 1) * 128:(i + 2) * 128], q3)
                        e_ = E[it % 2]
                        bX = corev[:, 9:10] if (b == 0 and i == 0) else zero1
                        act(e_[:, 0:256], ps_[:, 0:256], AF.Exp, bias=bX)
                        act(e_[:, 256:512], ps_[:, 256:512], AF.Exp, bias=zero1)
                        po = nxt("pb", psB)
                        mm(po[0:64, 0:256], vT[rX, i, g * 64:(g + 1) * 64], e_[rX, 0:256], start=True, stop=False)
                        mm(po[0:64, 0:256], vT[rY, i + 1, g * 64:(g + 1) * 64], e_[rY, 256:512], start=False, stop=True)
                        mm(po[0:64, 256:512], onesb[rX, 0:64], e_[rX, 0:256], start=True, stop=False)
                        mm(po[0:64, 256:512], onesb[rY, 0:64], e_[rY, 256:512], start=False, stop=True)
                        d_ = dsm[it % 2]
                        tt("dve", d_[0:64, :], po[0:64, 256:512], esinkB[:, g * 256:(g + 1) * 256], ALU.add)
                        recip(d_[0:64, :], d_[0:64, :])
                        tt("dve", o_a[0:64, 4 * g:4 * g + 4, cq * 64:(cq + 1) * 64],
                           po[0:64, 0:256].re("p (h t) -> p h t", h=4), d_[0:64, :].re("p (h t) -> p h t", h=4), ALU.mult)
                        it += 1
                if last:
                    kct = [AR.alloc(f"kct{i}", [128]) for i in range(2)]
                    vct = [AR.alloc(f"vct{i}", [128]) for i in range(2)]
                    kc = [AR.alloc(f"kc{i}", [2, 128], BF16) for i in range(2)]
                    vc = [AR.alloc(f"vc{i}", [128], BF16) for i in range(2)]
                    for sq in range(4):
                        S.dma("sp", kct[sq % 2], dr["cache_k"][l][sq])
                        S.dma("sp", vct[sq % 2], dr["cache_v"][l][sq])
                        pb = nxt("pb", psB)
                        for g in range(2):
                            tr(pb[0:64, g * 128:(g + 1) * 128], kct[sq % 2][:, g * 64:(g + 1) * 64], ident)
                        cp("act", kc[sq % 2][0:64, :, :], pb[0:64, 0:256].re("p (g t) -> p g t", g=2))
                        cp("pool", vc[sq % 2], vct[sq % 2])
                        for g in range(2):
                            ps_ = nxt("pa", psA)
                            q3 = qF[0:64, 4 * g:4 * g + 4, TB + sq * 16:TB + (sq + 1) * 16]
                            mm(ps_[:, 0:64].re("p (h t) -> p h t", h=4), kc[sq % 2][0:64, g, :], q3)
                            mm(ps_[0:64, 64:128].re("p (h t) -> p h t", h=4), kFs[0:64, g, :], q3)
                            e_ = E[it % 2]
                            act(e_[:, 0:64], ps_[:, 0:64], AF.Exp, bias=zero1)
                            act(e_[0:64, 64:128], ps_[0:64, 64:128], AF.Exp, bias=seqmask[:, sq:sq + 1])
                            po = nxt("pb", psB)
                            mm(po[0:64, 0:64], vc[sq % 2][:, g * 64:(g + 1) * 64], e_[:, 0:64], start=True, stop=False)
                            mm(po[0:64, 0:64], vTs[0:64, g * 64:(g + 1) * 64], e_[0:64, 64:128], start=False, stop=True)
                            mm(po[0:64, 64:128], onesb[:, 0:64], e_[:, 0:64], start=True, stop=False)
                            mm(po[0:64, 64:128], onesb[0:64, 0:64], e_[0:64, 64:128], start=False, stop=True)
                            d_ = dsm[it % 2]
                            tt("dve", d_[0:64, 0:64].re("p (h t) -> p h t", h=4), po[0:64, 64:128].re("p (h t) -> p h t", h=4),
                               esinkB[:, g * 256:(g + 1) * 256].re("p (h t) -> p h t", h=4)[:, :, 0:16], ALU.add)
                            recip(d_[0:64, 0:64], d_[0:64, 0:64])
                            tt("dve", o_a[0:64, 4 * g:4 * g + 4, TB + sq * 16:TB + (sq + 1) * 16],
                               po[0:64, 0:64].re("p (h t) -> p h t", h=4), d_[0:64, 0:64].re("p (h t) -> p h t", h=4), ALU.mult)
                            it += 1
                cp("pool", kprev, kF[0:64, :, TB:TB + 128])
                cp("pool", vprev, vT[:, 4, :])
                S.fence()
            AR.off = mark
            AR.gen += 1
            mixacc = AR.alloc("mixacc", [8, n])
            mixbf = AR.alloc("mixbf", [8, n], BF16)
            gsb = [AR.alloc(f"gsb{i}", [512]) for i in range(2)]
            branches = (["A"] if stage >= 5 else []) + (["B"] if stage >= 6 else []) + (["C"] if stage >= 7 else [])
            if not branches:
                memset("pool", mixbf, 0.0)
            for bi_, br in enumerate(branches):
                brow = {"A": 0, "B": 512, "C": 1024}[br]
                for hf in range(2):
                    if br == "A":
                        wb = load_w(WB["w_branch"][l][brow:brow + 512, hf * 512:(hf + 1) * 512], 64, 8, 512)
                    else:
                        wb = load_w(WB["w_branch"][l][brow:brow + 512, hf * 512:(hf + 1) * 512], 128, 4, 512)
                    gcol = OFF["gt"] + bi_ * 0 + {"A": 0, "B": 1024, "C": 2048}[br] + hf * 512
                    wg = load_w(wi[:, gcol:gcol + 512], 128, 8, 512)
                    for q in range(4):
                        i = hf * 4 + q
                        for (a, ha, sn) in segs:
                            p1 = nxt("pa", psA)
                            p2 = nxt("pa", psA)
                            if br == "A":
                                for kt in range(8):
                                    mm(p1[:, 0:sn], wb[0:64, kt, q * 128:(q + 1) * 128], o_a[0:64, kt, ha:ha + sn], start=(kt == 0), stop=(kt == 7))
                            else:
                                osrc = o_bn if br == "B" else o_cn
                                for kt in range(4):
                                    mm(p1[:, 0:sn], wb[:, kt, q * 128:(q + 1) * 128], osrc[:, kt, ha:ha + sn], start=(kt == 0), stop=(kt == 3))
                            for kt in range(8):
                                mm(p2[:, 0:sn], wg[:, kt, q * 128:(q + 1) * 128], xbf[:, kt, a:a + sn], start=(kt == 0), stop=(kt == 7))
                            gs_ = gsb[(i + ha) % 2]
                            act(gs_[:, 0:sn], p2[:, 0:sn], AF.Sigmoid)
                            if bi_ == 0:
                                tt("dve", mixacc[:, i, ha:ha + sn], p1[:, 0:sn], gs_[:, 0:sn], ALU.mult)
                            else:
                                tt("dve", gs_[:, 0:sn], p1[:, 0:sn], gs_[:, 0:sn], ALU.mult)
                                tt("pool", mixacc[:, i, ha:ha + sn], mixacc[:, i, ha:ha + sn], gs_[:, 0:sn], ALU.add)
            if branches:
                cp("pool", mixbf, mixacc)
            for hf in range(2):
                wo = load_w(WB["w_out"][l][:, hf * 512:(hf + 1) * 512], 128, 8, 512)
                for q in range(4):
                    i = hf * 4 + q
                    for (a, ha, sn) in segs:
                        pa = nxt("pa", psA)
                        for kt in range(8):
                            mm(pa[:, 0:sn], wo[:, kt, q * 128:(q + 1) * 128], mixbf[:, kt, ha:ha + sn], start=(kt == 0), stop=(kt == 7))
                        stt(xres[:, i, a:a + sn], xres[:, i, a:a + sn], ALPHA, pa[:, 0:sn], ALU.mult, ALU.add)
            for (a, ha, sn) in segs:
                layer_norm(a, sn, PC["ln1g"], PC["ln1b"])
            S.fence()

        if stage < 3:
            continue
        AR.reset()
        tailx = AR.alloc("tailx", [8, 2])
        S.dma("sp", dr["ag3_in"].re("p (a b) -> p a b", b=2), xres[:, :, SEG - 2:SEG])
        S.fence()
        agi, ago = dr["ag3_in"].ap, dr["ag3_out"].ap
        ccop = S.cc(lambda e: e.collective_compute("AllGather", ALU.bypass, replica_groups=GROUPS,
                                                          ins=[agi], outs=[ago]), ["ag3_in"], ["ag3_out"])
        g4 = AR.alloc("g4", [4, 16])
        S.dma("sp", g4, dr["ag3_out"].re("(r p) c -> p r c", p=128))
        tx = tailx.re("p a b -> p (a b)")
        ts("dve", tx, g4[:, 0, :], corev[:, 1:2], None, ALU.mult)
        for j in range(1, 4):
            stt(tx, g4[:, j, :], corev[:, 1 + j:2 + j], tx, ALU.mult, ALU.add)
        tailb = AR.alloc("tailb", [8, 2], BF16)
        cp("dve", tailb, tailx)
        S.fence()
        keep = AR.off

        for b in range(NBLK if stage >= 4 else 0):
            AR.off = keep
            AR.gen += 1
            c0 = b * TB
            last = (b == NBLK - 1)
            h = AR.alloc("h", [22, TB + NS], BF16)
            ug = [AR.alloc(f"ug{i}", [2 + TB]) for i in range(2)]
            uv = [AR.alloc(f"uv{i}", [2 + TB]) for i in range(2)]
            cg = [AR.alloc(f"cg{i}", [TB]) for i in range(2)]
            cv = [AR.alloc(f"cv{i}", [TB]) for i in range(2)]
            usg = [AR.alloc(f"usg{i}", [4, 18]) for i in range(2)]
            usv = [AR.alloc(f"usv{i}", [4, 18]) for i in range(2)]
            csg = [AR.alloc(f"csg{i}", [4, 16]) for i in range(2)]
            csv = [AR.alloc(f"csv{i}", [4, 16]) for i in range(2)]
            sio = [(AR.alloc(f"sin{i}", [512]), AR.alloc(f"sout{i}", [512]), None) for i in range(2)] if last else None
            ucb = [AR.alloc(f"uc{i}", [16]) for i in range(2)]
            for jg in range(6):
                nj = 4 if jg < 5 else 2
                wg = load_w(WB["w_up"][l][:, jg * 512:jg * 512 + nj * 128], 128, 8, nj * 128)
                wv = load_w(WB["w_up"][l][:, D_FF + jg * 512:D_FF + jg * 512 + nj * 128], 128, 8, nj * 128)
                for jj in range(nj):
                    j = jg * 4 + jj
                    fw = PC["fcw"]
                    for half, (wt, ubuf, cbuf, usb, csb) in enumerate(((wg, ug, cg, usg, csg), (wv, uv, cv, usv, csv))):
                        jc = j + 22 * half
                        u = ubuf[j % 2]
                        cc_ = cbuf[j % 2]
                        pa = nxt("pa", psA)
                        for kt in range(8):
                            mm(pa[:, 0:TB], wt[:, kt, jj * 128:(jj + 1) * 128], xbf[:, kt, c0:c0 + TB],
                               start=(kt == 0), stop=(kt == 7))
                        if b == 0:
                            pb = nxt("pb", psB)
                            for kt in range(8):
                                mm(pb[:, 0:2], wt[:, kt, jj * 128:(jj + 1) * 128], tailb[:, kt, :],
                                   start=(kt == 0), stop=(kt == 7))
                            cp("dve", u[:, 0:2], pb[:, 0:2])
                        else:
                            cp("pool", u[:, 0:2], ffn_tail[:, jc, :])
                        cp("act", u[:, 2:2 + TB], pa[:, 0:TB])
                        cp("pool", ffn_tail[:, jc, :], u[:, TB:TB + 2])
                        wc = lambda t, jc=jc: par[:, fw + t * 44 + jc:fw + t * 44 + jc + 1]
                        ts("pool", cc_, u[:, 0:TB], wc(0), par[:, PC["fcb"] + jc:PC["fcb"] + jc + 1], ALU.mult, ALU.add)
                        stt(cc_, u[:, 1:TB + 1], wc(1), cc_, ALU.mult, ALU.add)
                        stt(cc_, u[:, 2:TB + 2], wc(2), cc_, ALU.mult, ALU.add)
                        if last:
                            us = usb[j % 2]
                            cs = csb[j % 2]
                            pb = nxt("pb", psB)
                            for kt in range(8):
                                mm(pb[:, 0:NS], wt[:, kt, jj * 128:(jj + 1) * 128], xbf[:, kt, SEG:SEG + NS],
                                   start=(kt == 0), stop=(kt == 7))
                            sin, sout, pout = sio[half]
                            if jj == 0:
                                S.dma("sp", sin[0:8, 0:nj * 128],
                                      V(dr["st_fconv"].ap[l].rearrange("s r c -> (s r) c")[:, jc * 128:(jc + nj) * 128], "RO"))
                            pb2 = nxt("pb", psB)
                            tr(pb2[:, 0:8], sin[0:8, jj * 128:(jj + 1) * 128], ident[0:8, 0:8])
                            cp("dve", us[:, :, 0:2], pb2[:, 0:8].re("p (s r) -> p s r", s=4))
                            cp("act", us[:, :, 2:18], pb[:, 0:NS].re("p (s t) -> p s t", s=4))
                            ts("pool", cs, us[:, :, 0:16], wc(0), par[:, PC["fcb"] + jc:PC["fcb"] + jc + 1], ALU.mult, ALU.add)
                            stt(cs, us[:, :, 1:17], wc(1), cs, ALU.mult, ALU.add)
                            stt(cs, us[:, :, 2:18], wc(2), cs, ALU.mult, ALU.add)
                            uc = ucb[(2 * j + half) % 2]
                            cp("pool", uc[:, 0:8].re("p (s r) -> p s r", s=4), us[:, :, 16:18])
                            cp("pool", uc[:, 8:10], u[:, TB:TB + 2])
                            pb3 = nxt("pb", psB)
                            tr(pb3[0:10, 0:128], uc[:, 0:10], ident)
                            cp("dve", sout[0:10, jj * 128:(jj + 1) * 128], pb3[0:10, 0:128])
                            if jj == nj - 1:
                                S.dma("sp", V(dr["s_fconv"].ap[l].rearrange("s r c -> (s r) c")[:, (jc - nj + 1) * 128:(jc + 1) * 128], "out_s_fconv"),
                                      sout[0:8, 0:nj * 128])
                                S.dma("sp", V(dr["p_fconv"].ap[l][:, (jc - nj + 1) * 128:(jc + 1) * 128], "out_p_fconv"),
                                      sout[8:10, 0:nj * 128])
                    act(cg[j % 2], cg[j % 2], AF.Silu)
                    tt("dve", h[:, j, 0:TB], cg[j % 2], cv[j % 2], ALU.mult)
                    if last:
                        act(csg[j % 2], csg[j % 2], AF.Silu)
                        tt("dve", h[:, j, TB:TB + NS].re("p (s t) -> p s t", s=4), csg[j % 2], csv[j % 2], ALU.mult)
            segs = [(c0, 0, TB)] + ([(SEG, TB, NS)] if last else [])
            for i2 in range(2):
                wd = [load_w(WB["w_down"][l][:, (i2 * 4 + q) * 128:(i2 * 4 + q + 1) * 128], 128, 22, 128) for q in range(1)]
                for q in range(4):
                    i = i2 * 4 + q
                    if q > 0:
                        wd = [load_w(WB["w_down"][l][:, i * 128:(i + 1) * 128], 128, 22, 128)]
                    for (a, ha, sn) in segs:
                        pa = nxt("pa", psA)
                        for kt in range(22):
                            mm(pa[:, 0:sn], wd[0][:, kt, :], h[:, kt, ha:ha + sn], start=(kt == 0), stop=(kt == 21))
                        stt(xres[:, i, a:a + sn], xres[:, i, a:a + sn], ALPHA, pa[:, 0:sn], ALU.mult, ALU.add)
            for (a, ha, sn) in segs:
                layer_norm(a, sn, PC["ln2g"], PC["ln2b"])
            S.fence()

    AR.reset()
    yo = [AR.alloc(f"yo{i}", [D]) for i in range(2)]
    for ti, (src, r0, nr, c0) in enumerate(tiles):
        y = yo[ti % 2]
        for dq in range(2):
            pb = nxt("pb", psB)
            for q in range(4):
                dt_ = dq * 4 + q
                tr(pb[0:nr, q * 128:(q + 1) * 128], xres[:, dt_, c0:c0 + nr], ident)
            cp("act" if dq % 2 else "dve", y[0:nr, dq * 512:(dq + 1) * 512], pb[0:nr, :])
        dst = dr["y_p"][r0:r0 + nr, :] if src == "xp" else dr["y_s"][0:nr, :]
        S.dma("sp", dst, y[0:nr, :])
    S.fence()
    S.emit(nc, es)
    es.close()
    return nc


_NC_CACHE = {}


def _core_inputs(inp, c):
    b, p = c // 4, c % 4
    m = {}
    m["xp"] = np.ascontiguousarray(inp["x_prompt"][b, p * SEG:(p + 1) * SEG])
    m["xs"] = np.ascontiguousarray(inp["x_sample"][4 * c:4 * c + 4]).reshape(NS, D)
    m["cache_k"] = np.ascontiguousarray(inp["cache_swa_k"][:, 4 * c:4 * c + 4]).reshape(DEPTH, 4, 128, 128)
    m["cache_v"] = np.ascontiguousarray(inp["cache_swa_v"][:, 4 * c:4 * c + 4]).reshape(DEPTH, 4, 128, 128)
    m["st_hgrn"] = np.ascontiguousarray(inp["state_hgrn"][:, 4 * c:4 * c + 4])
    m["st_gdn"] = np.ascontiguousarray(inp["state_gdn"][:, 4 * c:4 * c + 4])
    m["st_gconv"] = np.ascontiguousarray(inp["state_gdn_conv"][:, 4 * c:4 * c + 4])
    m["st_fconv"] = np.ascontiguousarray(inp["state_ffn_conv"][:, 4 * c:4 * c + 4])
    for k in ("ln_in_g", "ln_in_b", "w_in", "attn_sinks", "hgrn_lb_logits", "hgrn_norm_w", "gdn_conv_w",
              "gdn_a_log", "gdn_dt_bias", "gdn_norm_w", "w_branch", "w_out", "ln1_g", "ln1_b", "w_up",
              "ffn_conv_w", "ffn_conv_b", "w_down", "ln2_g", "ln2_b"):
        m[k] = np.ascontiguousarray(inp[k], dtype=np.float32)
    m["consts"] = CONSTS
    cv = np.zeros((128, 16), np.float32)
    cv[:, 0] = 1.0 if p > 0 else 0.0
    if p > 0:
        cv[:, 1 + (p - 1)] = 1.0
    cv[:, 5 + p] = 1.0
    cv[:, 9] = 0.0 if p > 0 else NEGM
    m["corev"] = cv
    return m


def run_cores(inp, nlayers=DEPTH, stage=99):
    key = (nlayers, stage)
    if key not in _NC_CACHE:
        _NC_CACHE[key] = build_nc(stage=stage, nlayers=nlayers)
    nc = _NC_CACHE[key]
    in_maps = [_core_inputs(inp, c) for c in range(8)]
    if nlayers < DEPTH:
        for m in in_maps:
            for n, shp in DRAM_IN:
                if shp[0] == DEPTH and n not in ('hgrn_lb_logits', 'gdn_dt_bias', 'gdn_a_log'):
                    m[n] = np.ascontiguousarray(m[n][:nlayers])
    res = run_bass_kernel_spmd(nc, in_maps, core_ids=list(range(8)))
    return res.results


def kernel(**inp):
    inp = {k: np.asarray(v) for k, v in inp.items()}
    r = run_cores(inp)
    y_p = np.stack([np.concatenate([r[b * 4 + p]["y_p"] for p in range(4)], 0) for b in range(2)], 0)
    y_s = np.concatenate([r[c]["y_s"].reshape(4, 16, D) for c in range(8)], 0)

    def pl(name, shape):
        return np.stack([r[3][name], r[7][name]], 1).reshape(shape)

    def sl(name, shape):
        return np.concatenate([r[c][name] for c in range(8)], 1).reshape(shape)

    outs = (y_p, y_s,
            pl("p_swa_k", (DEPTH, 2, 128, 2, 64)), pl("p_swa_v", (DEPTH, 2, 128, 2, 64)),
            pl("p_hgrn", (DEPTH, 2, 4, 128, 128)), pl("p_gdn", (DEPTH, 2, 4, 128, 128)),
            pl("p_gconv", (DEPTH, 2, 3, C_QKV)), pl("p_fconv", (DEPTH, 2, 2, 2 * D_FF)),
            sl("s_swa_k", (DEPTH, 32, 128, 2, 64)), sl("s_swa_v", (DEPTH, 32, 128, 2, 64)),
            sl("s_hgrn", (DEPTH, 32, 4, 128, 128)), sl("s_gdn", (DEPTH, 32, 4, 128, 128)),
            sl("s_gconv", (DEPTH, 32, 3, C_QKV)), sl("s_fconv", (DEPTH, 32, 2, 2 * D_FF)))
    return tuple(np.ascontiguousarray(o, dtype=np.float32) for o in outs)
```

```python
import numpy as np
from contextlib import ExitStack
import concourse.bass as bass
import concourse.mybir as mybir
from concourse.bass_utils import run_bass_kernel_spmd

F32 = mybir.dt.float32
BF16 = mybir.dt.bfloat16
AF = mybir.ActivationFunctionType
ALU = mybir.AluOpType

D = 1024
DEPTH = 4
SEG = 2048
NS = 64
NTOK = SEG + NS
TB = 512
NBLK = SEG // TB
A_HEADS, A_KV, A_HD = 8, 2, 64
D_FF = 2816
C_QKV = 1536
D_IN = 7944
OFF = {}
_o = 0
for _n, _w in (("aq", 512), ("ak", 128), ("av", 128), ("bq", 512), ("bf", 512), ("bi", 512), ("bg", 512),
               ("cqkv", 1536), ("cz", 512), ("cb", 4), ("ca", 4), ("gt", 3072)):
    OFF[_n] = _o
    _o += _w
ALPHA = (2 * DEPTH) ** 0.25
LN_EPS = 1e-5
RMS_EPS = 1e-6
NEGM = -30000.0


class V:
    __slots__ = ("ap", "key")

    def __init__(self, ap, key):
        self.ap = ap
        self.key = key

    def __getitem__(self, idx):
        return V(self.ap[idx], self.key)

    def k(self, key):
        return V(self.ap, key)

    def re(self, pat, **kw):
        return V(self.ap.rearrange(pat, **kw), self.key)

    def bc(self, shape):
        return V(self.ap.to_broadcast(shape), self.key)


class Op:
    __slots__ = ("eng", "kind", "fn", "deps", "idx", "inc", "sem", "val", "gid")


class Sched:
    ENGS = ("pe", "act", "dve", "pool", "sp")
    NDMA = 20

    def __init__(self):
        self.ops = {e: [] for e in self.ENGS}
        self.res = {}
        self.dma_slots = {e: [None] * self.NDMA for e in ("sp", "act", "pool")}
        self.dma_cnt = {e: [0] * self.NDMA for e in ("sp", "act", "pool")}
        self.dma_rr = {e: 0 for e in ("sp", "act", "pool")}
        self.all_dma = []
        self.gid = 0
        self.pending_dma = []
        self.ccs = []
        self.bg = {}

    def _rec(self, eng, kind, fn, reads, writes, extra_deps=()):
        op = Op()
        op.eng, op.kind, op.fn = eng, kind, fn
        op.inc = False
        op.sem = None
        op.val = 0
        op.gid = self.gid
        self.gid += 1
        deps = []
        for r in reads:
            if r in self.bg:
                deps.append(self.bg[r])
            st = self.res.get(r)
            if st and st[0] is not None:
                deps.append(st[0])
        for w in writes:
            st = self.res.get(w)
            if st:
                if st[0] is not None:
                    deps.append(st[0])
                deps.extend(st[1])
        deps.extend(extra_deps)
        for r in reads:
            st = self.res.setdefault(r, [None, []])
            st[1].append(op)
        for w in writes:
            self.res[w] = [op, []]
        seen = set()
        dd = []
        for d in deps:
            if d is op or id(d) in seen:
                continue
            seen.add(id(d))
            if d.kind == "c" and kind == "c" and d.eng == "pe" and eng == "pe":
                continue
            dd.append(d)
        op.deps = dd
        op.idx = len(self.ops[eng])
        self.ops[eng].append(op)
        return op

    def c(self, eng, fn, reads, writes):
        return self._rec(eng, "c", fn, reads, writes)

    def cc(self, fn, reads, writes):
        op = self._rec("pool", "x", fn, reads, writes)
        op.sem = ("cc", len(self.ccs))
        op.val = 1
        self.ccs.append(op)
        self.pending_dma.append(op)
        return op

    def dma(self, q, out, in_, bg=False, **kw):
        slot = self.dma_rr[q]
        self.dma_rr[q] = (slot + 1) % self.NDMA
        prev = self.dma_slots[q][slot]
        extra = [prev] if prev is not None else []
        o, i = out.ap, in_.ap
        op = self._rec(q, "d", lambda e: e.dma_start(out=o, in_=i, **kw), [in_.key], [out.key], extra)
        self.dma_cnt[q][slot] += 1
        op.sem = (q, slot)
        op.val = 16 * self.dma_cnt[q][slot]
        self.dma_slots[q][slot] = op
        self.all_dma.append(op)
        if bg:
            self.bg[out.key] = op
        else:
            self.pending_dma.append(op)
        return op

    def fence(self):
        lasts = [self.ops[e][-1] for e in self.ENGS if self.ops[e] and self.ops[e][-1].kind != "d"]
        for e in self.ENGS:
            pass
        lastc = []
        for e in self.ENGS:
            for o in reversed(self.ops[e]):
                if o.kind == "c" and o.fn is not None:
                    lastc.append(o)
                    break
                if o.fn is None:
                    break
        pend = list(self.pending_dma)
        self.pending_dma = []
        for e in self.ENGS:
            deps = list(lastc) + pend
            op = Op()
            op.eng, op.kind, op.fn = e, "c", None
            op.inc, op.sem, op.val, op.gid = False, None, 0, self.gid
            self.gid += 1
            op.deps = deps
            op.idx = len(self.ops[e])
            self.ops[e].append(op)
        self.res = {}

    def emit(self, nc, es):
        CH = 20000
        for e in self.ENGS:
            for op in self.ops[e]:
                for d in op.deps:
                    d.inc = True
        self.csems = {}
        for e in self.ENGS:
            n = 0
            for op in self.ops[e]:
                if op.kind == "c" and op.inc:
                    op.sem = (e, n // CH)
                    op.val = n % CH + 1
                    n += 1
            nsem = max(1, (n + CH - 1) // CH)
            self.csems[e] = [es.enter_context(nc.semaphore(f"c_{e}_{j}")) for j in range(nsem)]
        self.dsems = {q: [es.enter_context(nc.semaphore(f"d_{q}_{j}")) for j in range(self.NDMA)]
                      for q in ("sp", "act", "pool")}
        block = es.enter_context(nc.Block())
        engmap = {"pe": block.tensor, "act": block.scalar, "dve": block.vector, "pool": block.gpsimd, "sp": block.sync}

        self.xsems = [es.enter_context(nc.semaphore(f"x_{j}")) for j in range(len(self.ccs))]

        def semof(op):
            if op.kind == "d":
                return self.dsems[op.sem[0]][op.sem[1]]
            if op.kind == "x":
                return self.xsems[op.sem[1]]
            return self.csems[op.sem[0]][op.sem[1]]

        def make(e):
            oplist = self.ops[e]

            def body(eng):
                waited = {}
                for op in oplist:
                    for d in op.deps:
                        key = (d.kind, d.sem)
                        if waited.get(key, 0) >= d.val:
                            continue
                        eng.wait_ge(semof(d), d.val)
                        waited[key] = d.val
                    if op.fn is None:
                        continue
                    ins = op.fn(eng)
                    if op.kind == "d":
                        ins.then_inc(semof(op), 16)
                    elif op.kind == "x":
                        ins.then_inc(semof(op), 1)
                    elif op.inc:
                        ins.then_inc(semof(op), 1)
            return body

        for e in self.ENGS:
            engmap[e](make(e))


S = None


def kof(*vs):
    return [v.key for v in vs if isinstance(v, V)]


def apof(x):
    return x.ap if isinstance(x, V) else x


def mm(out, lhsT, rhs, start=True, stop=True):
    o, l, r = out.ap, lhsT.ap, rhs.ap
    return S.c("pe", lambda e: e.matmul(o, l, r, start=start, stop=stop), kof(lhsT, rhs), kof(out))


def tr(out, in_, ident):
    o, i, d = out.ap, in_.ap, ident.ap
    return S.c("pe", lambda e: e.transpose(o, i, d), kof(in_, ident), kof(out))


def act(out, in_, func, bias=None, scale=None):
    kw = {}
    if bias is not None:
        kw["bias"] = apof(bias)
    if scale is not None:
        kw["scale"] = apof(scale)
    o, i = out.ap, in_.ap
    return S.c("act", lambda e: e.activation(out=o, in_=i, func=func, **kw), kof(in_, bias, scale), kof(out))


def ts(eng, out, in0, s1, s2, op0, op1=None):
    o, i = out.ap, in0.ap
    a1, a2 = apof(s1), apof(s2)
    if op1 is None:
        return S.c(eng, lambda e: e.tensor_scalar(o, i, a1, None, op0), kof(in0, s1), kof(out))
    return S.c(eng, lambda e: e.tensor_scalar(o, i, a1, a2, op0, op1), kof(in0, s1, s2), kof(out))


def tt(eng, out, in0, in1, op):
    o, a, b = out.ap, in0.ap, in1.ap
    return S.c(eng, lambda e: e.tensor_tensor(out=o, in0=a, in1=b, op=op), kof(in0, in1), kof(out))


def stt(out, in0, scalar, in1, op0, op1):
    o, a, b = out.ap, in0.ap, in1.ap
    sc = apof(scalar)
    return S.c("dve", lambda e: e.scalar_tensor_tensor(out=o, in0=a, scalar=sc, in1=b, op0=op0, op1=op1),
               kof(in0, scalar, in1), kof(out))


def cp(eng, out, in_):
    o, i = out.ap, in_.ap
    if eng == "act":
        return S.c("act", lambda e: e.activation(out=o, in_=i, func=AF.Copy), kof(in_), kof(out))
    return S.c(eng, lambda e: e.tensor_copy(o, i), kof(in_), kof(out))


def recip(out, in_):
    o, i = out.ap, in_.ap
    return S.c("dve", lambda e: e.reciprocal(o, i), kof(in_), kof(out))


def memset(eng, out, val):
    o = out.ap
    return S.c(eng, lambda e: e.memset(o, val), [], kof(out))


def make_consts():
    cols = {}
    parts = []
    pos = [0]

    def add(name, a):
        a = np.asarray(a, np.float32)
        if a.shape[0] < 128:
            a = np.concatenate([a, np.zeros((128 - a.shape[0], a.shape[1]), np.float32)], 0)
        cols[name] = (pos[0], a.shape[1])
        parts.append(a)
        pos[0] += a.shape[1]

    i = np.arange(128)
    add("ident", np.eye(128))
    add("ones", np.ones((128, 128)))
    for tag, T, blk in (("p", 128, 128), ("s", 64, 16)):
        s = np.arange(T)[:, None]
        t = np.arange(T)[None, :]
        same = (s // blk) == (t // blk)
        add(f"triLE_{tag}", (same & (s <= t)))
        add(f"triGT_{tag}", (same & (s > t)))
        add(f"mbias_{tag}", np.where(same & (s <= t), 0.0, NEGM))
        add(f"offd_{tag}", (same & (s < t)))
    for tag, T, blk in (("p", 128, 64), ("s", 64, 16)):
        nb = T // blk
        sidx = np.arange(T)[:, None]
        tidx = np.arange(T)[None, :]
        same = (sidx // blk) == (tidx // blk)
        mid = (tidx // blk) * blk + blk // 2 - 1
        drel = (same & (sidx <= tidx)).astype(np.float32) - (same & (sidx <= mid)).astype(np.float32)
        tot = np.stack([(np.arange(T) // blk == b_) for b_ in range(nb)], 1).astype(np.float32)
        midc = np.stack([((np.arange(T) // blk == b_) & (np.arange(T) <= b_ * blk + blk // 2 - 1)) for b_ in range(nb)], 1).astype(np.float32)
        add(f"hdext_{tag}", np.concatenate([drel, tot, midc], 1))
        add(f"hle_{tag}", (same & (sidx <= tidx)))
        add(f"hgt_{tag}", (same & (sidx > tidx)))
        add(f"blk_{tag}", tot)
    sm = np.full((64, 4), NEGM, np.float32)
    for q in range(4):
        sm[16 * q:16 * q + 16, q] = 0.0
    add("seqmask", sm)
    return np.concatenate(parts, 1), cols


CONSTS, CCOL = make_consts()
NCON = CONSTS.shape[1]


class Arena:
    def __init__(self, ap, ncols32):
        self.ap = ap
        self.n = ncols32
        self.off = 0
        self.gen = 0

    def reset(self):
        self.off = 0
        self.gen += 1

    def alloc(self, name, shape, dtype=F32, parts=128):
        free = int(np.prod(shape))
        n32 = free if dtype == F32 else (free + 1) // 2
        self.off = (self.off + 31) // 32 * 32
        assert self.off + n32 <= self.n, (name, self.off, n32, self.n)
        a = self.ap[0:parts, self.off:self.off + n32]
        self.off += n32
        if dtype != F32:
            a = a.bitcast(dtype)[:, 0:free]
        if len(shape) == 2:
            a = a.rearrange("p (a b) -> p a b", a=shape[0])
        elif len(shape) == 3:
            a = a.rearrange("p (a b c) -> p a b c", a=shape[0], b=shape[1])
        return V(a, f"{name}#{self.gen}")


DRAM_IN = [
    ("xp", [SEG, D]), ("xs", [NS, D]),
    ("cache_k", [DEPTH, 4, 128, 128]), ("cache_v", [DEPTH, 4, 128, 128]),
    ("st_hgrn", [DEPTH, 4, 4, 128, 128]), ("st_gdn", [DEPTH, 4, 4, 128, 128]),
    ("st_gconv", [DEPTH, 4, 3, C_QKV]), ("st_fconv", [DEPTH, 4, 2, 2 * D_FF]),
    ("ln_in_g", [D]), ("ln_in_b", [D]), ("w_in", [DEPTH, D, D_IN]), ("attn_sinks", [DEPTH, 8]),
    ("hgrn_lb_logits", [DEPTH, 512]), ("hgrn_norm_w", [DEPTH, 128]), ("gdn_conv_w", [DEPTH, 4, C_QKV]),
    ("gdn_a_log", [DEPTH, 4]), ("gdn_dt_bias", [DEPTH, 4]), ("gdn_norm_w", [DEPTH, 128]),
    ("w_branch", [DEPTH, 1536, D]), ("w_out", [DEPTH, D, D]), ("ln1_g", [DEPTH, D]), ("ln1_b", [DEPTH, D]),
    ("w_up", [DEPTH, D, 2 * D_FF]), ("ffn_conv_w", [DEPTH, 3, 2 * D_FF]), ("ffn_conv_b", [DEPTH, 2 * D_FF]),
    ("w_down", [DEPTH, D_FF, D]), ("ln2_g", [DEPTH, D]), ("ln2_b", [DEPTH, D]),
    ("consts", [128, NCON]), ("corev", [128, 16]),
]
DRAM_OUT = [
    ("y_p", [SEG, D]), ("y_s", [NS, D]),
    ("p_swa_k", [DEPTH, 128, 128]), ("p_swa_v", [DEPTH, 128, 128]),
    ("p_hgrn", [DEPTH, 4, 128, 128]), ("p_gdn", [DEPTH, 4, 128, 128]),
    ("p_gconv", [DEPTH, 3, C_QKV]), ("p_fconv", [DEPTH, 2, 2 * D_FF]),
    ("s_swa_k", [DEPTH, 4, 128, 128]), ("s_swa_v", [DEPTH, 4, 128, 128]),
    ("s_hgrn", [DEPTH, 4, 4, 128, 128]), ("s_gdn", [DEPTH, 4, 4, 128, 128]),
    ("s_gconv", [DEPTH, 4, 3, C_QKV]), ("s_fconv", [DEPTH, 4, 2, 2 * D_FF]),
]
GROUPS = [[0, 1, 2, 3], [4, 5, 6, 7]]
NW = 3
WSLOT = 4096
ARENA32 = 13900


def build_nc(stage=99, nlayers=DEPTH):
    global S
    S = Sched()
    nc = bass.Bass("TRN2", target_bir_lowering=False)
    es = ExitStack()
    dr = {}
    for n, shp in DRAM_IN:
        if shp[0] == DEPTH and n not in ('hgrn_lb_logits', 'gdn_dt_bias', 'gdn_a_log'):
            shp = [nlayers] + list(shp[1:])
        dr[n] = V(nc.dram_tensor(n, shp, F32, kind="ExternalInput").ap(), "RO")
    for n, shp in DRAM_OUT:
        dr[n] = V(nc.dram_tensor(n, shp, F32, kind="ExternalOutput").ap(), "out_" + n)
    AG3W = D * 2
    AG1W = 256 + 36
    dr["ag1_in"] = V(nc.dram_tensor("ag1_in", [128, AG1W], F32).ap(), "ag1_in")
    dr["ag1_out"] = V(nc.dram_tensor("ag1_out", [512, AG1W], F32).ap(), "ag1_out")
    AG2W = 516 + 1024
    dr["ag2_in"] = V(nc.dram_tensor("ag2_in", [128, AG2W], F32).ap(), "ag2_in")
    dr["ag2_out"] = V(nc.dram_tensor("ag2_out", [512, AG2W], F32).ap(), "ag2_out")
    dr["ob_loc"] = V(nc.dram_tensor("ob_loc", [128, 4, NTOK], BF16).ap(), "ob_loc")
    dr["qh"] = V(nc.dram_tensor("qh", [128, 4, SEG], BF16).ap(), "qh")
    dr["oc_loc"] = V(nc.dram_tensor("oc_loc", [128, 4, NTOK], BF16).ap(), "oc_loc")
    dr["rh"] = V(nc.dram_tensor("rh", [128, 4, SEG], BF16).ap(), "rh")
    dr["ag3_in"] = V(nc.dram_tensor("ag3_in", [128, 16], F32).ap(), "ag3_in")
    dr["ag3_out"] = V(nc.dram_tensor("ag3_out", [512, 16], F32).ap(), "ag3_out")

    def sb(name, shape, dt=F32):
        return V(es.enter_context(nc.sbuf_tensor(name, shape, dt))[:], name)

    WSH = {"w_in": (D, D_IN), "w_branch": (1536, D), "w_out": (D, D), "w_up": (D, 2 * D_FF), "w_down": (D_FF, D)}
    WB = {}
    for wn, (r_, c_) in WSH.items():
        t_ = nc.dram_tensor("wb_" + wn, [nlayers, r_, c_], BF16).ap()
        WB[wn] = [V(t_[l_], f"wb_{wn}_{l_}") for l_ in range(nlayers)]

    def convert_weights(l_):
        for wn, (r_, c_) in WSH.items():
            step = 256 if r_ % 256 == 0 else 128
            step = r_ // max(1, min(4, r_ // step))
            for r0 in range(0, r_, step):
                r1 = min(r_, r0 + step)
                S.dma("pool", V(WB[wn][l_].ap[r0:r1, :], f"wb_{wn}_{l_}_{r0}"), V(dr[wn].ap[l_][r0:r1, :], "RO"), bg=True)

    xres = sb("xres", [128, 8, NTOK])
    xbf = sb("xbf", [128, 8, NTOK], BF16)
    con = sb("con", [128, NCON])
    conb = sb("conb", [128, 256], BF16)
    corev = sb("corev_sb", [128, 16])
    wsl = [sb(f"w{i}", [128, WSLOT], BF16) for i in range(NW)]
    par = sb("par", [128, 512])
    arena_t = sb("arena", [128, ARENA32])
    AR = Arena(arena_t.ap, ARENA32)
    ln_sq4 = sb("ln_sq4", [128, 4, 512])
    ln_mean = sb("ln_mean", [128, 512])
    ln_rstd = sb("ln_rstd", [128, 512])
    ffn_tail = sb("ffn_tail", [128, 44, 2])
    kprev = sb("kprev", [64, 2, 128], BF16)
    vprev = sb("vprev", [128, 128], BF16)
    gc_prev = sb("gc_prev", [128, 12, 3])
    esinkB = sb("esinkB", [64, 4 * 64 * 2])
    zero1 = sb("zero1", [128, 1])
    eps1 = sb("eps1", [128, 1])
    hS = sb("hS", [128, 4, 128])
    hpre = sb("hpre", [128, 4])
    lbF = sb("lbF", [128, 4, 4])
    sinB = sb("sinB", [128, 4, 128], BF16)
    sinC = sb("sinC", [128, 4, 128], BF16)
    gpar = sb("gpar", [128, 64])
    nrmw = sb("nrmw", [128, 2])
    psA = [V(es.enter_context(nc.psum_tensor(f"psA{i}", [128, 512], F32))[:], f"psA{i}") for i in range(4)]
    psB_t = [es.enter_context(nc.psum_tensor(f"psB{i}", [128, 512], F32)) for i in range(4)]
    psB = [V(psB_t[i][:], f"psB{i}") for i in range(4)]
    rr = {"w": 0, "pa": 0, "pb": 0, "stg": 0}

    def nxt(kind, lst):
        v = lst[rr[kind] % len(lst)]
        rr[kind] += 1
        return v

    def C(name, parts=128):
        c0, n = CCOL[name]
        return con[0:parts, c0:c0 + n]

    ident = C("ident")
    ones32 = C("ones")
    identb = conb[:, 0:128]
    onesb = conb[:, 128:256]

    S.dma("sp", con, dr["consts"])
    S.dma("sp", corev, dr["corev"])
    convert_weights(0)
    cp("dve", identb, ident)
    cp("dve", onesb, ones32)
    memset("pool", zero1, 0.0)
    memset("pool", eps1, RMS_EPS)
    seqmask = C("seqmask", 64)

    def load_w(src2d, kp, kt, ncols, q="pool"):
        slot = nxt("w", wsl)
        assert kt * ncols <= WSLOT
        v = V(slot.ap[0:kp, 0:kt * ncols].rearrange("p (k m) -> p k m", k=kt), slot.key)
        extra = [o for k_, o in S.bg.items() if k_.startswith(src2d.key + "_")]
        op = S.dma("sp", v, V(src2d.ap.rearrange("(k p) m -> p k m", p=kp), "RO"))
        have = set(id(d) for d in op.deps)
        op.deps.extend(o for o in extra if id(o) not in have)
        return v

    def layer_norm(c0, n, gcol, bcol):
        mean, rstd = ln_mean, ln_rstd
        sq4 = ln_sq4
        for s0 in range(0, n, 512):
            sn = min(512, n - s0)
            a, b = c0 + s0, c0 + s0 + sn
            p1 = nxt("pa", psA)
            p2 = nxt("pa", psA)
            for dt_ in range(8):
                mm(p1[:, 0:sn], ones32, xres[:, dt_, a:b], start=(dt_ == 0), stop=(dt_ == 7))
            for hf in range(2):
                act(sq4[:, :, 0:sn], xres[:, hf * 4:hf * 4 + 4, a:b], AF.Square)
                for q in range(4):
                    mm(p2[:, 0:sn], ones32, sq4[:, q, 0:sn], start=(hf == 0 and q == 0), stop=(hf == 1 and q == 3))
            act(mean[:, 0:sn], p1[:, 0:sn], AF.Copy, scale=1.0 / D)
            tt("pool", rstd[:, 0:sn], mean[:, 0:sn], mean[:, 0:sn], ALU.mult)
            stt(rstd[:, 0:sn], p2[:, 0:sn], 1.0 / D, rstd[:, 0:sn], ALU.mult, ALU.subtract)
            ts("dve", rstd[:, 0:sn], rstd[:, 0:sn], LN_EPS, None, ALU.add)
            act(rstd[:, 0:sn], rstd[:, 0:sn], AF.Sqrt)
            recip(rstd[:, 0:sn], rstd[:, 0:sn])
            xv = xres[:, :, a:b]
            mb = mean[:, 0:sn].re("p (o t) -> p o t", o=1).bc([128, 8, sn])
            rb = rstd[:, 0:sn].re("p (o t) -> p o t", o=1).bc([128, 8, sn])
            gb = par[:, gcol:gcol + 8].re("p (d o) -> p d o", o=1).bc([128, 8, sn])
            bb_ = par[:, bcol:bcol + 8].re("p (d o) -> p d o", o=1).bc([128, 8, sn])
            tt("dve", xv, xv, mb, ALU.subtract)
            tt("pool", xv, xv, rb, ALU.mult)
            tt("dve", xv, xv, gb, ALU.mult)
            tt("pool", xv, xv, bb_, ALU.add)
            cp("act", xbf[:, :, a:b], xv)

    stg = [sb(f"stg{i}", [128, 128]) for i in range(2)]

    def load_cols(col, src1d, t):
        st_ = nxt("stg", stg)
        S.dma("sp", st_[0:t, :], V(src1d.ap.rearrange("(t p) -> t p", p=128), "RO"))
        pb = nxt("pb", psB)
        tr(pb[:, 0:t], st_[0:t, :], ident[0:t, 0:t])
        cp("dve", par[:, col:col + t], pb[:, 0:t])

    def load_vec8(col, src1d):
        load_cols(col, src1d, 8)

    if stage == -1:
        AR.reset()
        t0 = AR.alloc("t0", [D])
        S.dma("sp", t0[0:NS, :], dr["xs"])
        S.dma("sp", dr["y_s"], t0[0:NS, :])
        S.fence()
        S.emit(nc, es)
        es.close()
        return nc
    load_vec8(0, dr["ln_in_g"])
    load_vec8(8, dr["ln_in_b"])
    AR.reset()
    xin = [AR.alloc(f"xin{i}", [D]) for i in range(2)]
    tiles = [("xp", i * 128, 128, i * 128) for i in range(SEG // 128)] + [("xs", 0, NS, SEG)]
    for ti, (src, r0, nr, c0) in enumerate(tiles):
        xi = xin[ti % 2]
        S.dma("sp", xi[0:nr, :], dr[src][r0:r0 + nr, :])
        for dq in range(2):
            pb = nxt("pb", psB)
            for q in range(4):
                dt_ = dq * 4 + q
                tr(pb[:, q * 128:q * 128 + nr], xi[0:nr, dt_ * 128:(dt_ + 1) * 128], ident[0:nr, 0:nr])
            cp("act" if dq % 2 else "dve", xres[:, dq * 4:dq * 4 + 4, c0:c0 + nr],
               pb.re("p (q t) -> p q t", q=4)[:, :, 0:nr])
    S.fence()
    AR.reset()
    if stage >= 1:
        layer_norm(0, NTOK, 0, 8)
    S.fence()

    PC = {"ln1g": 0, "ln1b": 8, "ln2g": 16, "ln2b": 24, "fcw": 32, "fcb": 32 + 132}

    for l in range(nlayers if stage >= 2 else 0):
        load_vec8(PC["ln1g"], dr["ln1_g"][l])
        load_vec8(PC["ln1b"], dr["ln1_b"][l])
        load_vec8(PC["ln2g"], dr["ln2_g"][l])
        load_vec8(PC["ln2b"], dr["ln2_b"][l])
        for tap in range(3):
            load_cols(PC["fcw"] + tap * 44, dr["ffn_conv_w"][l][tap], 44)
        load_cols(PC["fcb"], dr["ffn_conv_b"][l], 44)
        S.fence()

        wi = WB["w_in"][l]
        if l + 1 < nlayers:
            convert_weights(l + 1)
        if stage >= 5:
            AR.reset()
            wkv = load_w(wi[:, OFF["ak"]:OFF["ak"] + 256], 128, 8, 256)
            kvl = AR.alloc("kvl", [AG1W])
            kvs = AR.alloc("kvs", [256])
            pa = nxt("pa", psA)
            for kt in range(8):
                mm(pa[:, 0:256], xbf[:, kt, SEG - 128:SEG], wkv[:, kt, :], start=(kt == 0), stop=(kt == 7))
            cp("act", kvl[:, 0:256], pa[:, 0:256])
            pa = nxt("pa", psA)
            for kt in range(8):
                mm(pa[0:NS, 0:256], xbf[:, kt, SEG:SEG + NS], wkv[:, kt, :], start=(kt == 0), stop=(kt == 7))
            cp("act", kvs[0:NS, :], pa[0:NS, 0:256])
            S.dma("sp", dr["p_swa_k"][l], kvl[:, 0:128])
            S.dma("sp", dr["p_swa_v"][l], kvl[:, 128:256])
            for sq in range(4):
                S.dma("sp", dr["s_swa_k"][l][sq, 112:128, :], kvs[16 * sq:16 * sq + 16, 0:128])
                S.dma("sp", dr["s_swa_v"][l][sq, 112:128, :], kvs[16 * sq:16 * sq + 16, 128:256])
            S.dma("sp", dr["s_swa_k"][l][:, 0:112, :], dr["cache_k"][l][:, 16:128, :])
            S.dma("sp", dr["s_swa_v"][l][:, 0:112, :], dr["cache_v"][l][:, 16:128, :])
            gt3 = kvl[:, 256:AG1W].re("p (j t) -> p j t", t=3)
            for jg in range(3):
                wc_ = load_w(wi[:, OFF["cqkv"] + jg * 512:OFF["cqkv"] + (jg + 1) * 512], 128, 8, 512)
                pb = nxt("pb", psB)
                for jj in range(4):
                    for kt in range(8):
                        mm(pb[:, jj * 4:jj * 4 + 3], wc_[:, kt, jj * 128:(jj + 1) * 128], xbf[:, kt, SEG - 3:SEG],
                           start=(kt == 0), stop=(kt == 7))
                cp("dve", gt3[:, jg * 4:jg * 4 + 4, :], pb[:, 0:16].re("p (j t) -> p j t", t=4)[:, :, 0:3])
            gct = AR.alloc("gct", [C_QKV])
            for jq in range(3):
                pb = nxt("pb", psB)
                for jj in range(4):
                    tr(pb[0:3, jj * 128:(jj + 1) * 128], gt3[:, jq * 4 + jj, :], ident)
                cp("act", gct[0:3, jq * 512:(jq + 1) * 512], pb[0:3, :])
            S.dma("sp", dr["p_gconv"][l], gct[0:3, :])
            S.dma("sp", dr["ag1_in"], kvl)
            S.fence()
            a1i, a1o = dr["ag1_in"].ap, dr["ag1_out"].ap
            S.cc(lambda e: e.collective_compute("AllGather", ALU.bypass, replica_groups=GROUPS,
                                                       ins=[a1i], outs=[a1o]), ["ag1_in"], ["ag1_out"])
            g1 = AR.alloc("g1", [4, AG1W])
            S.dma("sp", g1, dr["ag1_out"].re("(r p) c -> p r c", p=128))
            hsel = AR.alloc("hsel", [AG1W])
            ts("dve", hsel, g1[:, 0, :], corev[:, 1:2], None, ALU.mult)
            for j in range(1, 4):
                stt(hsel, g1[:, j, :], corev[:, 1 + j:2 + j], hsel, ALU.mult, ALU.add)
            cp("pool", gc_prev, hsel[:, 256:AG1W].re("p (j t) -> p j t", t=3))
            cp("pool", vprev, hsel[:, 128:256])
            pb = nxt("pb", psB)
            for g in range(2):
                tr(pb[0:64, g * 128:(g + 1) * 128], hsel[:, g * 64:(g + 1) * 64], ident)
            cp("act", kprev, pb[0:64, 0:256].re("p (g t) -> p g t", g=2))
            es8 = AR.alloc("es8", [8])
            S.dma("sp", es8[0:64, :], V(dr["attn_sinks"].ap[l].partition_broadcast(64), "RO"))
            act(es8[0:64, :], es8[0:64, :], AF.Exp)
            cp("dve", esinkB.re("p (h t) -> p h t", h=8), es8[0:64, :].re("p (h o) -> p h o", o=1).bc([64, 8, 64]))
            S.fence()

        if stage >= 6:
            if l == 0:
                st_ = nxt("stg", stg)
                S.dma("sp", st_[0:16, :], V(dr["hgrn_lb_logits"].ap.rearrange("l (h p) -> (l h) p", p=128), "RO"))
                pb = nxt("pb", psB)
                tr(pb[:, 0:16], st_[0:16, :], ident[0:16, 0:16])
                AR.reset()
                e16 = AR.alloc("e16", [4, 4])
                act(e16.re("p l h -> p (l h)"), pb[:, 0:16], AF.Exp)
                ssum = AR.alloc("ssum", [4])
                tt("dve", ssum, e16[:, 0, :], e16[:, 1, :], ALU.add)
                tt("dve", ssum, ssum, e16[:, 2, :], ALU.add)
                tt("dve", ssum, ssum, e16[:, 3, :], ALU.add)
                recip(ssum, ssum)
                for ll in range(1, 4):
                    tt("dve", e16[:, ll, :], e16[:, ll, :], ssum, ALU.mult)
                memset("pool", lbF[:, 0, :], 0.0)
                cp("dve", lbF[:, 1, :], e16[:, 1, :])
                tt("dve", lbF[:, 2, :], lbF[:, 1, :], e16[:, 2, :], ALU.add)
                tt("dve", lbF[:, 3, :], lbF[:, 2, :], e16[:, 3, :], ALU.add)
                S.fence()
            AR.reset()
            lbT = AR.alloc("lbT", [512])
            omlT = AR.alloc("omlT", [512])
            omlF = AR.alloc("omlF", [4])
            nomlF = AR.alloc("nomlF", [4])
            hSs = AR.alloc("hSs", [4, 4, 128])
            keep1 = AR.off
            lg = AR.alloc("lg", [4, 512])
            S.dma("sp", lg, V(dr["hgrn_lb_logits"].ap.partition_broadcast(128), "RO"))
            act(lg, lg, AF.Exp)
            tt("dve", omlT, lg[:, 0, :], lg[:, 1, :], ALU.add)
            tt("dve", omlT, omlT, lg[:, 2, :], ALU.add)
            tt("dve", omlT, omlT, lg[:, 3, :], ALU.add)
            recip(omlT, omlT)
            memset("pool", lbT, 0.0)
            for ll in range(1, l + 1):
                tt("dve", lbT, lbT, lg[:, ll, :], ALU.add)
            tt("dve", lbT, lbT, omlT, ALU.mult)
            ts("dve", omlT, lbT, -1.0, 1.0, ALU.mult, ALU.add)
            ts("dve", omlF, lbF[:, l, :], -1.0, 1.0, ALU.mult, ALU.add)
            ts("dve", nomlF, omlF, -1.0, None, ALU.mult)
            S.dma("sp", nrmw[:, 0:1], V(dr["hgrn_norm_w"].ap[l].rearrange("(p o) -> p o", o=1), "RO"))
            S.dma("sp", nrmw[:, 1:2], V(dr["gdn_norm_w"].ap[l].rearrange("(p o) -> p o", o=1), "RO"))
            memset("pool", hS, 0.0)
            memset("pool", hpre, 1.0)
            for sq in range(4):
                S.dma("sp", hSs[:, sq, :, :], V(dr["st_hgrn"].ap[l][sq].rearrange("h k v -> k h v"), "RO"))
            S.fence()
            for b in range(NBLK):
                AR.off = keep1
                AR.gen += 1
                c0 = b * TB
                last = (b == NBLK - 1)
                n = TB + (NS if last else 0)
                segs = [(c0, 0, TB)] + ([(SEG, TB, NS)] if last else [])
                wbq = load_w(wi[:, OFF["bq"]:OFF["bq"] + 512], 128, 8, 512)
                wbf = load_w(wi[:, OFF["bf"]:OFF["bf"] + 512], 128, 8, 512)
                wbi = load_w(wi[:, OFF["bi"]:OFF["bi"] + 512], 128, 8, 512)
                qT = AR.alloc("qT", [4, n], BF16)
                kT = AR.alloc("kT", [4, n], BF16)
                obuf = AR.alloc("obuf", [4, n], BF16)
                qhat = AR.alloc("qhat", [4, TB], BF16)
                sgF = [AR.alloc(f"sgF{i}", [512]) for i in range(1)] * 2
                for h in range(4):
                    for (a, ha, sn) in segs:
                        pa = nxt("pa", psA)
                        for kt in range(8):
                            mm(pa[:, 0:sn], wbq[:, kt, h * 128:(h + 1) * 128], xbf[:, kt, a:a + sn], start=(kt == 0), stop=(kt == 7))
                        act(qT[:, h, ha:ha + sn], pa[:, 0:sn], AF.Silu)
                        pa = nxt("pa", psA)
                        for kt in range(8):
                            mm(pa[:, 0:sn], wbf[:, kt, h * 128:(h + 1) * 128], xbf[:, kt, a:a + sn], start=(kt == 0), stop=(kt == 7))
                        sg = sgF[h % 2]
                        act(sg[:, 0:sn], pa[:, 0:sn], AF.Sigmoid)
                        ts("dve", kT[:, h, ha:ha + sn], sg[:, 0:sn], nomlF[:, h:h + 1], omlF[:, h:h + 1], ALU.mult, ALU.add)
                sgT = AR.alloc("sgT", [512])
                tT = AR.alloc("tT", [512])
                lfT = AR.alloc("lfT", [512])
                kTM = AR.alloc("kTM", [512])
                vTM = AR.alloc("vTM", [512], BF16)
                kend = AR.alloc("kend", [512], BF16)
                kendm = AR.alloc("kendm", [4, 512], BF16) if last else None
                eqs = [AR.alloc(f"eq{i}", [128]) for i in range(2)]
                eks = [AR.alloc(f"ek{i}", [128]) for i in range(2)]
                dms = [AR.alloc(f"dm{h}", [8]) for h in range(4)]
                qts = [AR.alloc(f"qt{h}", [128], BF16) for h in range(4)]
                kts = [AR.alloc(f"kt_{h}", [128], BF16) for h in range(4)]
                attms = [AR.alloc(f"attm{h}", [128], BF16) for h in range(4)]
                sbfs = [AR.alloc(f"sbf{h}", [128], BF16) for h in range(4)]
                facs = [AR.alloc(f"fac{h}", [1]) for h in range(4)]
                tl = [(c0 + t * 128, t * 128, 128, "p") for t in range(4)] + ([(SEG, TB, NS, "s")] if last else [])
                for (a, ha, T, tag) in tl:
                    nb = 2 if tag == "p" else 4
                    blk = T // nb
                    hdext = C(f"hdext_{tag}", T)
                    hle = C(f"hle_{tag}", T)
                    hgt = C(f"hgt_{tag}", T)
                    pf = nxt("pa", psA)
                    pi = nxt("pa", psA)
                    for kt in range(8):
                        mm(pf[0:T, :], xbf[:, kt, a:a + T], wbf[:, kt, :], start=(kt == 0), stop=(kt == 7))
                    for kt in range(8):
                        mm(pi[0:T, :], xbf[:, kt, a:a + T], wbi[:, kt, :], start=(kt == 0), stop=(kt == 7))
                    act(sgT[0:T, :], pf[0:T, :], AF.Sigmoid)
                    tt("dve", tT[0:T, :], sgT[0:T, :], omlT[0:T, :], ALU.mult)
                    stt(sgT[0:T, :], tT[0:T, :], 1e-30, lbT[0:T, :], ALU.max, ALU.add)
                    act(lfT[0:T, :], sgT[0:T, :], AF.Ln)
                    tt("pool", kTM[0:T, :], omlT[0:T, :], tT[0:T, :], ALU.subtract)
                    cp("act", vTM[0:T, :], pi[0:T, :])
                    pr = nxt("pa", psA)
                    mm(pr[0:T, :], hgt, lfT[0:T, :])
                    act(tT[0:T, :], pr[0:T, :], AF.Exp)
                    tt("dve", kend[0:T, :], kTM[0:T, :], tT[0:T, :], ALU.mult)
                    if tag == "s":
                        bls = C("blk_s", T)
                        for bb in range(4):
                            ts("pool", kendm[0:T, bb, :], kend[0:T, :], bls[:, bb:bb + 1], None, ALU.mult)
                    for h in range(4):
                        hs = slice(h * 128, (h + 1) * 128)
                        pc = nxt("pb", psB)
                        mm(pc[:, 0:T + 2 * nb], lfT[0:T, hs], hdext)
                        eq, ek = eqs[h % 2], eks[h % 2]
                        act(eq[:, 0:T], pc[:, 0:T], AF.Exp)
                        tt("dve", qts[h][:, 0:T], qT[:, h, ha:ha + T], eq[:, 0:T], ALU.mult)
                        act(ek[:, 0:T], pc[:, 0:T], AF.Exp, scale=-1.0)
                        tt("pool", kts[h][:, 0:T], kT[:, h, ha:ha + T], ek[:, 0:T], ALU.mult)
                        act(dms[h][:, 0:2 * nb], pc[:, T:T + 2 * nb], AF.Exp)
                    for h in range(4):
                        pat = nxt("pb", psB)
                        mm(pat[0:T, 0:T], kts[h][:, 0:T], qts[h][:, 0:T])
                        stt(attms[h][0:T, 0:T], pat[0:T, 0:T], 1e30, hle, ALU.min, ALU.mult)
                    pos = [nxt("pb", psB) for _ in range(4)]
                    for h in range(4):
                        hs = slice(h * 128, (h + 1) * 128)
                        mm(pos[h][:, 0:T], vTM[0:T, hs], attms[h][0:T, 0:T], start=True, stop=False)
                    for bb in range(nb):
                        bs = slice(bb * blk, (bb + 1) * blk)
                        for h in range(4):
                            hs = slice(h * 128, (h + 1) * 128)
                            dm = dms[h]
                            qt = qts[h]
                            Sb = hS[:, h, :] if tag == "p" else hSs[:, bb, h, :]
                            act(sbfs[h], Sb, AF.Copy, scale=dm[:, nb + bb:nb + bb + 1])
                            mm(pos[h][:, bs], sbfs[h], qt[:, bs], start=False, stop=(bb == nb - 1))
                            pu = nxt("pa", psA)
                            if tag == "p":
                                ts("dve", facs[h], dm[:, nb + bb:nb + bb + 1], hpre[:, h:h + 1], None, ALU.mult)
                                ts("dve", qhat[:, h, ha + bb * blk:ha + (bb + 1) * blk], qt[:, bs], facs[h], None, ALU.mult)
                                mm(pu[:, 0:128], kend[bs, hs], vTM[bs, hs])
                            else:
                                mm(pu[:, 0:128], kendm[0:T, bb, hs], vTM[0:T, hs])
                            stt(Sb, Sb, dm[:, bb:bb + 1], pu[:, 0:128], ALU.mult, ALU.add)
                            if tag == "p":
                                tt("dve", hpre[:, h:h + 1], hpre[:, h:h + 1], dm[:, bb:bb + 1], ALU.mult)
                    for h in range(4):
                        cp("act", obuf[:, h, ha:ha + T], pos[h][:, 0:T])
                S.dma("sp", dr["ob_loc"][:, :, c0:c0 + TB], obuf[:, :, 0:TB])
                S.dma("sp", dr["qh"][:, :, c0:c0 + TB], qhat)
                if last:
                    S.dma("sp", dr["ob_loc"][:, :, SEG:SEG + NS], obuf[:, :, TB:TB + NS])
                S.fence()
            for sq in range(4):
                S.dma("sp", V(dr["s_hgrn"].ap[l][sq].rearrange("h k v -> k h v"), "out_s_hgrn"), hSs[:, sq, :, :])
            if stage >= 7:
                S.fence()
                AR.off = 0
                AR.gen += 1
                for tap in range(4):
                    st_ = nxt("stg", stg)
                    S.dma("sp", st_[0:12, :], V(dr["gdn_conv_w"].ap[l][tap].rearrange("(t p) -> t p", p=128), "RO"))
                    pb = nxt("pb", psB)
                    tr(pb[:, 0:12], st_[0:12, :], ident[0:12, 0:12])
                    cp("dve", gpar[:, tap * 12:(tap + 1) * 12], pb[:, 0:12])
                gtmp = AR.alloc("gtmp", [2, 4 * DEPTH])
                S.dma("sp", gtmp[:, 0, :], V(dr["gdn_dt_bias"].ap.rearrange("l h -> (l h)").partition_broadcast(128), "RO"))
                S.dma("sp", gtmp[:, 1, :], V(dr["gdn_a_log"].ap.rearrange("l h -> (l h)").partition_broadcast(128), "RO"))
                cp("dve", gpar[:, 48:52], gtmp[:, 0, 4 * l:4 * l + 4])
                act(gpar[:, 52:56], gtmp[:, 1, 4 * l:4 * l + 4], AF.Exp)
                ts("dve", gpar[:, 52:56], gpar[:, 52:56], -1.0, None, ALU.mult)
                gS = AR.alloc("gS", [4, 256])
                gSb = AR.alloc("gSb", [4, 256], BF16)
                memset("pool", gS, 0.0)
                for h in range(4):
                    cp("dve", gS[:, h, 128:256], ident)
                cp("act", gSb, gS)
                keepg = AR.off
                triLE = C("triLE_p")
                triGT = C("triGT_p")
                mbias = C("mbias_p")
                offd = C("offd_p")
                S.fence()
                import os
                for u_ in range(NBLK + (0 if os.environ.get('GDN_NOSAMPLE') else 1)):
                    AR.off = keepg
                    AR.gen += 1
                    smp = (u_ == NBLK)
                    n = NS if smp else TB
                    a0 = SEG if smp else u_ * TB
                    wcq = [load_w(wi[:, OFF["cqkv"] + jg * 512:OFF["cqkv"] + (jg + 1) * 512], 128, 8, 512) for jg in range(1)]
                    qn = AR.alloc("qn", [4, n], BF16)
                    kn = AR.alloc("kn", [4, n], BF16)
                    ntile = 4
                    TT = 16 if smp else 128
                    kTMa = AR.alloc("kTMa", [ntile, 4, 128], BF16)
                    vTMa = AR.alloc("vTMa", [ntile, 4, 128], BF16)
                    ocb = AR.alloc("ocb", [4, n], BF16)
                    rhb = AR.alloc("rhb", [4, n], BF16)
                    if smp:
                        gSs = AR.alloc("gSs", [4, 4, 128])
                        gSsb = AR.alloc("gSsb", [4, 4, 128], BF16)
                    mark_conv = AR.off
                    cpre = [AR.alloc(f"cpre{i}", [n + 12]) for i in range(2)]
                    u32 = [AR.alloc(f"u32{i}", [n]) for i in range(2)]
                    sqg = AR.alloc("sqg", [n])
                    rsg = AR.alloc("rsg", [n])
                    if smp:
                        for sq in range(4):
                            S.dma("sp", gSs[:, sq, :, :], V(dr["st_gdn"].ap[l][sq].rearrange("h k v -> k h v"), "RO"))
                        cp("act", gSsb, gSs)
                        sg12 = AR.alloc("sg12", [C_QKV])
                        so12 = AR.alloc("so12", [C_QKV])
                        S.dma("sp", sg12[0:12, :], V(dr["st_gconv"].ap[l].rearrange("s r c -> (s r) c"), "RO"))
                    for j in range(12):
                        if j % 4 == 0 and j > 0:
                            wcq = [load_w(wi[:, OFF["cqkv"] + (j // 4) * 512:OFF["cqkv"] + (j // 4 + 1) * 512], 128, 8, 512)]
                        pa = nxt("pa", psA)
                        for kt in range(8):
                            mm(pa[:, 0:n], wcq[0][:, kt, (j % 4) * 128:(j % 4 + 1) * 128], xbf[:, kt, a0:a0 + n], start=(kt == 0), stop=(kt == 7))
                        cpj = cpre[j % 2]
                        uj = u32[j % 2]
                        gw = lambda t, j=j: gpar[:, t * 12 + j:t * 12 + j + 1]
                        if not smp:
                            cp("pool", cpj[:, 0:3], gc_prev[:, j, :])
                            cp("act", cpj[:, 3:3 + n], pa[:, 0:n])
                            cp("pool", gc_prev[:, j, :], cpj[:, n:n + 3])
                            ts("pool", uj, cpj[:, 0:n], gw(0), None, ALU.mult)
                            for t in range(1, 4):
                                stt(uj, cpj[:, t:t + n], gw(t), uj, ALU.mult, ALU.add)
                        else:
                            c3 = cpj[:, 0:76].re("p (s t) -> p s t", s=4)
                            pb = nxt("pb", psB)
                            tr(pb[:, 0:12], sg12[0:12, j * 128:(j + 1) * 128], ident[0:12, 0:12])
                            cp("dve", c3[:, :, 0:3], pb[:, 0:12].re("p (s r) -> p s r", s=4))
                            cp("act", c3[:, :, 3:19], pa[:, 0:n].re("p (s t) -> p s t", s=4))
                            u3 = uj.re("p (s t) -> p s t", s=4)
                            ts("pool", u3, c3[:, :, 0:16], gw(0), None, ALU.mult)
                            for t in range(1, 4):
                                stt(u3, c3[:, :, t:t + 16], gw(t), u3, ALU.mult, ALU.add)
                            uc12 = sqg[:, 0:12]
                            cp("pool", uc12.re("p (s r) -> p s r", s=4), c3[:, :, 16:19])
                            pb = nxt("pb", psB)
                            tr(pb[0:12, 0:128], uc12, ident)
                            cp("dve", so12[0:12, j * 128:(j + 1) * 128], pb[0:12, 0:128])
                        act(uj, uj, AF.Silu)
                        h = j % 4
                        if j < 8:
                            act(sqg, uj, AF.Square)
                            ps_ = nxt("pb", psB)
                            mm(ps_[:, 0:n], ones32, sqg)
                            act(rsg, ps_[:, 0:n], AF.Sqrt, bias=eps1)
                            recip(rsg, rsg)
                            if j < 4:
                                stt(qn[:, h, :], uj, 128.0 ** -0.5, rsg, ALU.mult, ALU.mult)
                            else:
                                tt("dve", uj, uj, rsg, ALU.mult)
                                cp("pool", kn[:, h, :], uj)
                        if j >= 4:
                            dstT = kTMa if j < 8 else vTMa
                            for t in range(ntile):
                                pb = nxt("pb", psB)
                                tr(pb[0:TT, 0:128], uj[:, t * TT:(t + 1) * TT], ident)
                                cp("act" if t % 2 else "dve", dstT[0:TT, t, h, :], pb[0:TT, 0:128])
                    if smp:
                        S.dma("sp", V(dr["s_gconv"].ap[l].rearrange("s r c -> (s r) c"), "out_s_gconv"), so12[0:12, :])
                    S.fence()
                    AR.off = mark_conv
                    AR.gen += 1
                    wcb = load_w(wi[:, OFF["cb"] - 248:OFF["cb"] + 8], 128, 8, 256)[:, :, 248:256]
                    NV = 128 if smp else 256
                    bg8 = AR.alloc("bg8", [8])
                    beta = AR.alloc("beta", [4])
                    gg = AR.alloc("gg", [4])
                    sm8 = AR.alloc("sm8", [16])
                    er_ = [AR.alloc(f"erow{i}", [TT]) for i in range(2)]
                    dT_ = [AR.alloc(f"decT{i}", [TT]) for i in range(1)] * 2
                    dS_ = [AR.alloc(f"decS{i}", [TT]) for i in range(1)] * 2
                    AA = [[AR.alloc(f"A{h}_{i}", [TT]) for i in range(2)] for h in range(4)]
                    ATT = [[AR.alloc(f"AT{h}_{i}", [TT]) for i in range(2)] for h in range(4)]
                    WW = [[AR.alloc(f"W{h}_{i}", [TT]) for i in range(1)] * 2 for h in range(4)]
                    QKd = [AR.alloc(f"QKd{h}", [TT], BF16) for h in range(4)]
                    keT = [AR.alloc(f"keT{h}", [TT], BF16) for h in range(4)]
                    qeT = [AR.alloc(f"qeT{h}", [TT], BF16) for h in range(4)]
                    ktil = [AR.alloc(f"ktil{h}", [128], BF16) for h in range(4)]
                    vaug = [AR.alloc(f"vaug{h}", [NV], BF16) for h in range(4)]
                    Wmb = [AR.alloc(f"Wmb{h}", [TT], BF16) for h in range(4)]
                    P1s = [AR.alloc(f"P1s{h}", [NV], BF16) for h in range(4)]
                    wbf_ = [AR.alloc(f"wbf_{h}", [NV], BF16) for h in range(4)]
                    for h in range(4):
                        memset("pool", vaug[h], 0.0)
                    nsq = 3 if smp else 6
                    for t in range(ntile):
                        T = TT
                        a = a0 + t * T
                        cs = slice(t * T, (t + 1) * T)
                        pg = nxt("pb", psB)
                        for kt in range(8):
                            mm(pg[0:T, 0:8], xbf[:, kt, a:a + T], wcb[:, kt, :], start=(kt == 0), stop=(kt == 7))
                        cp("dve", bg8[0:T, :], pg[0:T, 0:8])
                        act(beta[0:T, :], bg8[0:T, 0:4], AF.Sigmoid)
                        tt("dve", gg[0:T, :], bg8[0:T, 4:8], gpar[0:T, 48:52], ALU.add)
                        act(gg[0:T, :], gg[0:T, :], AF.Exp)
                        act(gg[0:T, :], gg[0:T, :], AF.Ln, bias=1.0)
                        tt("dve", gg[0:T, :], gg[0:T, :], gpar[0:T, 52:56], ALU.mult)
                        pcm = nxt("pb", psB)
                        mm(pcm[0:T, 0:4], triLE[0:T, 0:T], gg[0:T, :])
                        mm(pcm[0:T, 4:8], triGT[0:T, 0:T], gg[0:T, :])
                        mm(pcm[:, 8:12], ones32[0:T, :], gg[0:T, :])
                        act(sm8[0:T, 0:4], pcm[0:T, 0:4], AF.Exp)
                        act(sm8[0:T, 4:8], pcm[0:T, 0:4], AF.Copy, scale=-1.0)
                        act(sm8[0:T, 8:12], pcm[0:T, 4:8], AF.Exp)
                        act(sm8[:, 12:16], pcm[:, 8:12], AF.Exp)
                        Sts = [(gSs[:, t, h, :] if smp else gS[:, h, :]) for h in range(4)]
                        Sbs = [(gSsb[:, t, h, :] if smp else gSb[:, h, :]) for h in range(4)]
                        for h in range(4):
                            erow, decT, decS = er_[h % 2], dT_[h % 2], dS_[h % 2]
                            pcr = nxt("pb", psB)
                            gbc = gg[0:T, h:h + 1].bc([T, 128])
                            mm(pcr[:, 0:T], gbc, triLE[0:T, 0:T])
                            mm(pcr[0:T, 128:128 + T], gg[0:T, h:h + 1].bc([T, T]), triLE[0:T, 0:T], start=True, stop=False)
                            mm(pcr[0:T, 128:128 + T], ident[0:T, 0:T], mbias[0:T, 0:T], start=False, stop=True)
                            act(erow[:, 0:T], pcr[:, 0:T], AF.Exp)
                            act(decT[0:T, 0:T], pcr[0:T, 128:128 + T], AF.Exp, bias=sm8[0:T, 4 + h:5 + h])
                            tt("pool", decS[0:T, 0:T], decT[0:T, 0:T], offd[0:T, 0:T], ALU.mult)
                            pk = nxt("pb", psB)
                            mm(pk[0:T, 0:T], kn[:, h, cs], kn[:, h, cs])
                            mm(pk[0:T, 128:128 + T], kn[:, h, cs], qn[:, h, cs])
                            U32 = AA[h][0]
                            stt(U32[0:T, 0:T], pk[0:T, 0:T], beta[0:T, h:h + 1], decS[0:T, 0:T], ALU.mult, ALU.mult)
                            tt("dve", QKd[h][0:T, 0:T], pk[0:T, 128:128 + T], decT[0:T, 0:T], ALU.mult)
                            tt("pool", keT[h][:, 0:T], kn[:, h, cs], erow[:, 0:T], ALU.mult)
                            tt("pool", qeT[h][:, 0:T], qn[:, h, cs], erow[:, 0:T], ALU.mult)
                            ts("pool", ktil[h][0:T, :], kTMa[0:T, t, h, :], sm8[0:T, 8 + h:9 + h], None, ALU.mult)
                            cp("pool", vaug[h][0:T, 0:128], vTMa[0:T, t, h, :])
                            pl_ = nxt("pb", psB)
                            tr(pl_[0:T, 0:T], U32[0:T, 0:T], ident[0:T, 0:T])
                            cp("act", ATT[h][0][0:T, 0:T], pl_[0:T, 0:T])
                            tt("dve", WW[h][0][0:T, 0:T], ident[0:T, 0:T], U32[0:T, 0:T], ALU.subtract)
                        for js in range(nsq):
                            i0_, i1_ = js % 2, (js + 1) % 2
                            for h in range(4):
                                p1_ = nxt("pb", psB)
                                p1b_ = nxt("pb", psB)
                                mm(p1_[0:T, 0:T], ATT[h][i0_][0:T, 0:T], AA[h][i0_][0:T, 0:T])
                                mm(p1b_[0:T, 0:T], AA[h][i0_][0:T, 0:T], ATT[h][i0_][0:T, 0:T])
                                cp("dve", AA[h][i1_][0:T, 0:T], p1_[0:T, 0:T])
                                cp("act", ATT[h][i1_][0:T, 0:T], p1b_[0:T, 0:T])
                            for h in range(4):
                                p2_ = nxt("pa", psA)
                                mm(p2_[0:T, 0:T], ATT[h][i1_][0:T, 0:T], WW[h][i0_][0:T, 0:T])
                                tt("dve", WW[h][i1_][0:T, 0:T], WW[h][i0_][0:T, 0:T], p2_[0:T, 0:T], ALU.add)
                        for h in range(4):
                            cp("act", Wmb[h][0:T, 0:T], WW[h][nsq % 2][0:T, 0:T])
                        for h in range(4):
                            pp = nxt("pa", psA)
                            mm(pp[0:T, 0:NV], keT[h][:, 0:T], Sbs[h])
                            tt("dve", P1s[h][0:T, 0:NV], vaug[h][0:T, 0:NV], pp[0:T, 0:NV], ALU.subtract)
                        for h in range(4):
                            pw = nxt("pa", psA)
                            mm(pw[0:T, 0:NV], Wmb[h][0:T, 0:T], P1s[h][0:T, 0:NV])
                            act(wbf_[h][0:T, 0:NV], pw[0:T, 0:NV], AF.Copy, scale=beta[0:T, h:h + 1])
                        for h in range(4):
                            pq = nxt("pa", psA)
                            mm(pq[:, 0:T], wbf_[h][0:T, 0:128], QKd[h][0:T, 0:T], start=True, stop=False)
                            mm(pq[:, 0:T], Sbs[h][:, 0:128], qeT[h][:, 0:T], start=False, stop=True)
                            cp("act", ocb[:, h, cs], pq[:, 0:T])
                            if not smp:
                                mm(pq[:, 128:128 + T], wbf_[h][0:T, 128:256], QKd[h][0:T, 0:T], start=True, stop=False)
                                mm(pq[:, 128:128 + T], Sbs[h][:, 128:256], qeT[h][:, 0:T], start=False, stop=True)
                                cp("dve", rhb[:, h, cs], pq[:, 128:128 + T])
                        for h in range(4):
                            pu = nxt("pa", psA)
                            mm(pu[:, 0:NV], ktil[h][0:T, :], wbf_[h][0:T, 0:NV])
                            stt(Sts[h], Sts[h], sm8[:, 12 + h:13 + h], pu[:, 0:NV], ALU.mult, ALU.add)
                            cp("act", Sbs[h], Sts[h])
                    if smp:
                        S.dma("sp", dr["oc_loc"][:, :, SEG:SEG + NS], ocb)
                        for sq in range(4):
                            S.dma("sp", V(dr["s_gdn"].ap[l][sq].rearrange("h k v -> k h v"), "out_s_gdn"), gSs[:, sq, :, :])
                    else:
                        S.dma("sp", dr["oc_loc"][:, :, a0:a0 + TB], ocb)
                        S.dma("sp", dr["rh"][:, :, a0:a0 + TB], rhb)
                    S.fence()
                S.dma("sp", dr["ag2_in"][:, 516:516 + 1024].re("p (h c) -> p h c", h=4), gS)
                AR.off = keepg
                AR.gen += 1
            AR.off = keep1
            AR.gen += 1
            S.dma("sp", dr["ag2_in"][:, 0:512], hS.re("p h v -> p (h v)"))
            S.dma("sp", dr["ag2_in"][:, 512:516], hpre)
            S.fence()
            a2i, a2o = dr["ag2_in"].ap, dr["ag2_out"].ap
            S.cc(lambda e: e.collective_compute("AllGather", ALU.bypass, replica_groups=GROUPS,
                                                       ins=[a2i], outs=[a2o]), ["ag2_in"], ["ag2_out"])
            g2 = AR.alloc("g2", [4, AG2W])
            S.dma("sp", g2, dr["ag2_out"].re("(r p) c -> p r c", p=128))
            X = AR.alloc("X", [4, 128])
            Sin = AR.alloc("Sin", [4, 128])
            memset("pool", X, 0.0)
            memset("pool", Sin, 0.0)
            for j in range(3):
                for h in range(4):
                    stt(X[:, h, :], X[:, h, :], g2[:, j, 512 + h:513 + h], g2[:, j, h * 128:(h + 1) * 128], ALU.mult, ALU.add)
                stt(Sin.re("p h v -> p (h v)"), X.re("p h v -> p (h v)"), corev[:, 6 + j:7 + j], Sin.re("p h v -> p (h v)"), ALU.mult, ALU.add)
            cp("act", sinB, Sin)
            for h in range(4):
                stt(X[:, h, :], Sin[:, h, :], hpre[:, h:h + 1], hS[:, h, :], ALU.mult, ALU.add)
            S.dma("sp", V(dr["p_hgrn"].ap[l].rearrange("h k v -> k h v"), "out_p_hgrn"), X)
            if stage >= 7:
                Xc = AR.alloc("Xc", [4, 128])
                SinC32 = AR.alloc("SinC32", [4, 128])
                Xcb = AR.alloc("Xcb", [4, 128], BF16)
                PmT = AR.alloc("PmT", [128], BF16)
                memset("pool", Xc, 0.0)
                memset("pool", SinC32, 0.0)
                for j in range(3):
                    for h in range(4):
                        base = 516 + h * 256
                        pt = nxt("pb", psB)
                        tr(pt[:, 0:128], g2[:, j, base + 128:base + 256], ident)
                        cp("act", PmT, pt[:, 0:128])
                        cp("dve", Xcb[:, h, :], Xc[:, h, :])
                        px = nxt("pb", psB)
                        mm(px[:, 0:128], PmT, Xcb[:, h, :])
                        tt("dve", Xc[:, h, :], px[:, 0:128], g2[:, j, base:base + 128], ALU.add)
                    stt(SinC32.re("p h v -> p (h v)"), Xc.re("p h v -> p (h v)"), corev[:, 6 + j:7 + j], SinC32.re("p h v -> p (h v)"), ALU.mult, ALU.add)
                cp("act", sinC, SinC32)
                for h in range(4):
                    pt = nxt("pb", psB)
                    tr(pt[:, 0:128], gS[:, h, 128:256], ident)
                    cp("act", PmT, pt[:, 0:128])
                    px = nxt("pb", psB)
                    mm(px[:, 0:128], PmT, sinC[:, h, :])
                    tt("dve", Xc[:, h, :], px[:, 0:128], gS[:, h, 0:128], ALU.add)
                S.dma("sp", V(dr["p_gdn"].ap[l].rearrange("h k v -> k h v"), "out_p_gdn"), Xc)
            S.fence()

        for b in range(NBLK):
            AR.reset()
            c0 = b * TB
            last = (b == NBLK - 1)
            n = TB + (NS if last else 0)
            segs = [(c0, 0, TB)] + ([(SEG, TB, NS)] if last else [])
            o_a = AR.alloc("o_a", [8, n], BF16)
            o_bn = AR.alloc("o_bn", [4, n], BF16)
            o_cn = AR.alloc("o_cn", [4, n], BF16)
            mark = AR.off
            for (tagB, onrm, loc, qsrc, sinX, zoff, nwc, stg_min) in (("B", o_bn, "ob_loc", "qh", sinB, OFF["bg"], 0, 6), ("C", o_cn, "oc_loc", "rh", sinC, OFF["cz"], 1, 7)):
                if stage < stg_min:
                    continue
                obl = AR.alloc("obl", [4, n], BF16)
                qhl = AR.alloc("qhl", [4, TB], BF16)
                o32 = AR.alloc("o32", [512])
                sq32 = AR.alloc("sq32", [512])
                rs_ = AR.alloc("rs_", [512])
                sz = AR.alloc("sz", [512])
                S.dma("sp", obl[:, :, 0:TB], dr[loc][:, :, c0:c0 + TB])
                if last:
                    S.dma("sp", obl[:, :, TB:TB + NS], dr[loc][:, :, SEG:SEG + NS])
                S.dma("sp", qhl, dr[qsrc][:, :, c0:c0 + TB])
                wz = load_w(wi[:, zoff:zoff + 512], 128, 8, 512)
                for h in range(4):
                    for (a, ha, sn) in segs:
                        if ha == 0:
                            pd = nxt("pb", psB)
                            mm(pd[:, 0:TB], sinX[:, h, :], qhl[:, h, :])
                            tt("dve", o32[:, 0:sn], pd[:, 0:TB], obl[:, h, 0:TB], ALU.add)
                        else:
                            cp("dve", o32[:, 0:sn], obl[:, h, TB:TB + NS])
                        act(sq32[:, 0:sn], o32[:, 0:sn], AF.Square)
                        ps_ = nxt("pb", psB)
                        mm(ps_[:, 0:sn], ones32, sq32[:, 0:sn])
                        act(rs_[:, 0:sn], ps_[:, 0:sn], AF.Sqrt, scale=1.0 / 128, bias=eps1)
                        recip(rs_[:, 0:sn], rs_[:, 0:sn])
                        pz = nxt("pa", psA)
                        for kt in range(8):
                            mm(pz[:, 0:sn], wz[:, kt, h * 128:(h + 1) * 128], xbf[:, kt, a:a + sn], start=(kt == 0), stop=(kt == 7))
                        act(sz[:, 0:sn], pz[:, 0:sn], AF.Silu)
                        tt("dve", o32[:, 0:sn], o32[:, 0:sn], rs_[:, 0:sn], ALU.mult)
                        stt(onrm[:, h, ha:ha + sn], o32[:, 0:sn], nrmw[:, nwc:nwc + 1], sz[:, 0:sn], ALU.mult, ALU.mult)
                S.fence()
                AR.off = mark
                AR.gen += 1
            if stage >= 5:
                qF = AR.alloc("qF", [8, n], BF16)
                kF = AR.alloc("kF", [2, 128 + TB], BF16)
                vT = AR.alloc("vT", [5, 128], BF16)
                wq = load_w(wi[:, OFF["aq"]:OFF["aq"] + 512], 128, 8, 512)
                wkv = load_w(wi[:, OFF["ak"]:OFF["ak"] + 256], 128, 8, 256)
                cp("pool", kF[0:64, :, 0:128], kprev)
                cp("pool", vT[:, 0, :], vprev)
                if last:
                    kFs = AR.alloc("kFs", [2, NS], BF16)
                    vTs = AR.alloc("vTs", [128], BF16)
                for h in range(8):
                    for (a, ha, sn) in segs:
                        pa = nxt("pa", psA)
                        for kt in range(8):
                            mm(pa[0:64, 0:sn], wq[:, kt, h * 64:(h + 1) * 64], xbf[:, kt, a:a + sn], start=(kt == 0), stop=(kt == 7))
                        act(qF[0:64, h, ha:ha + sn], pa[0:64, 0:sn], AF.Copy, scale=0.125)
                for g in range(2):
                    for (a, ha, sn) in segs:
                        pa = nxt("pa", psA)
                        for kt in range(8):
                            mm(pa[0:64, 0:sn], wkv[:, kt, g * 64:(g + 1) * 64], xbf[:, kt, a:a + sn], start=(kt == 0), stop=(kt == 7))
                        if ha == 0:
                            cp("act", kF[0:64, g, 128:128 + TB], pa[0:64, 0:TB])
                        else:
                            cp("act", kFs[0:64, g, :], pa[0:64, 0:NS])
                for t in range(4):
                    pa = nxt("pa", psA)
                    for kt in range(8):
                        mm(pa[:, 0:128], xbf[:, kt, c0 + t * 128:c0 + (t + 1) * 128], wkv[:, kt, 128:256], start=(kt == 0), stop=(kt == 7))
                    cp("dve", vT[:, 1 + t, :], pa[:, 0:128])
                if last:
                    pa = nxt("pa", psA)
                    for kt in range(8):
                        mm(pa[0:NS, 0:128], xbf[:, kt, SEG:SEG + NS], wkv[:, kt, 128:256], start=(kt == 0), stop=(kt == 7))
                    cp("dve", vTs[0:NS, :], pa[0:NS, 0:128])
                E = [AR.alloc(f"E{i}", [512], BF16) for i in range(2)]
                dsm = [AR.alloc(f"dsm{i}", [256]) for i in range(2)]
                it = 0
                for cq in range(8):
                    i = cq // 2
                    for g in range(2):
                        rX = slice(0, 128) if cq % 2 == 0 else slice(64, 128)
                        rY = slice(0, 64) if cq % 2 == 0 else slice(0, 128)
                        ps_ = nxt("pa", psA)
                        q3 = qF[0:64, 4 * g:4 * g + 4, cq * 64:(cq + 1) * 64]
                        mm(ps_[:, 0:256].re("p (h t) -> p h t", h=4), kF[0:64, g, i * 128:(i + 1) * 128], q3)
                        mm(ps_[:, 256:512].re("p (h t) -> p h t", h=4), kF[0:64, g, (i + 1) * 128:(i + 2) * 128], q3)
                        e_ = E[it % 2]
                        bX = corev[:, 9:10] if (b == 0 and i == 0) else zero1
                        act(e_[:, 0:256], ps_[:, 0:256], AF.Exp, bias=bX)
                        act(e_[:, 256:512], ps_[:, 256:512], AF.Exp, bias=zero1)
                        po = nxt("pb", psB)
                        mm(po[0:64, 0:256], vT[rX, i, g * 64:(g + 1) * 64], e_[rX, 0:256], start=True, stop=False)
                        mm(po[0:64, 0:256], vT[rY, i + 1, g * 64:(g + 1) * 64], e_[rY, 256:512], start=False, stop=True)
                        mm(po[0:64, 256:512], onesb[rX, 0:64], e_[rX, 0:256], start=True, stop=False)
                        mm(po[0:64, 256:512], onesb[rY, 0:64], e_[rY, 256:512], start=False, stop=True)
                        d_ = dsm[it % 2]
                        tt("dve", d_[0:64, :], po[0:64, 256:512], esinkB[:, g * 256:(g + 1) * 256], ALU.add)
                        recip(d_[0:64, :], d_[0:64, :])
                        tt("dve", o_a[0:64, 4 * g:4 * g + 4, cq * 64:(cq + 1) * 64],
                           po[0:64, 0:256].re("p (h t) -> p h t", h=4), d_[0:64, :].re("p (h t) -> p h t", h=4), ALU.mult)
                        it += 1
                if last:
                    kct = [AR.alloc(f"kct{i}", [128]) for i in range(2)]
                    vct = [AR.alloc(f"vct{i}", [128]) for i in range(2)]
                    kc = [AR.alloc(f"kc{i}", [2, 128], BF16) for i in range(2)]
                    vc = [AR.alloc(f"vc{i}", [128], BF16) for i in range(2)]
                    for sq in range(4):
                        S.dma("sp", kct[sq % 2], dr["cache_k"][l][sq])
                        S.dma("sp", vct[sq % 2], dr["cache_v"][l][sq])
                        pb = nxt("pb", psB)
                        for g in range(2):
                            tr(pb[0:64, g * 128:(g + 1) * 128], kct[sq % 2][:, g * 64:(g + 1) * 64], ident)
                        cp("act", kc[sq % 2][0:64, :, :], pb[0:64, 0:256].re("p (g t) -> p g t", g=2))
                        cp("pool", vc[sq % 2], vct[sq % 2])
                        for g in range(2):
                            ps_ = nxt("pa", psA)
                            q3 = qF[0:64, 4 * g:4 * g + 4, TB + sq * 16:TB + (sq + 1) * 16]
                            mm(ps_[:, 0:64].re("p (h t) -> p h t", h=4), kc[sq % 2][0:64, g, :], q3)
                            mm(ps_[0:64, 64:128].re("p (h t) -> p h t", h=4), kFs[0:64, g, :], q3)
                            e_ = E[it % 2]
                            act(e_[:, 0:64], ps_[:, 0:64], AF.Exp, bias=zero1)
                            act(e_[0:64, 64:128], ps_[0:64, 64:128], AF.Exp, bias=seqmask[:, sq:sq + 1])
                            po = nxt("pb", psB)
                            mm(po[0:64, 0:64], vc[sq % 2][:, g * 64:(g + 1) * 64], e_[:, 0:64], start=True, stop=False)
                            mm(po[0:64, 0:64], vTs[0:64, g * 64:(g + 1) * 64], e_[0:64, 64:128], start=False, stop=True)
                            mm(po[0:64, 64:128], onesb[:, 0:64], e_[:, 0:64], start=True, stop=False)
                            mm(po[0:64, 64:128], onesb[0:64, 0:64], e_[0:64, 64:128], start=False, stop=True)
                            d_ = dsm[it % 2]
                            tt("dve", d_[0:64, 0:64].re("p (h t) -> p h t", h=4), po[0:64, 64:128].re("p (h t) -> p h t", h=4),
                               esinkB[:, g * 256:(g + 1) * 256].re("p (h t) -> p h t", h=4)[:, :, 0:16], ALU.add)
                            recip(d_[0:64, 0:64], d_[0:64, 0:64])
                            tt("dve", o_a[0:64, 4 * g:4 * g + 4, TB + sq * 16:TB + (sq + 1) * 16],
                               po[0:64, 0:64].re("p (h t) -> p h t", h=4), d_[0:64, 0:64].re("p (h t) -> p h t", h=4), ALU.mult)
                            it += 1
                cp("pool", kprev, kF[0:64, :, TB:TB + 128])
                cp("pool", vprev, vT[:, 4, :])
                S.fence()
            AR.off = mark
            AR.gen += 1
            mixacc = AR.alloc("mixacc", [8, n])
            mixbf = AR.alloc("mixbf", [8, n], BF16)
            gsb = [AR.alloc(f"gsb{i}", [512]) for i in range(2)]
            branches = (["A"] if stage >= 5 else []) + (["B"] if stage >= 6 else []) + (["C"] if stage >= 7 else [])
            if not branches:
                memset("pool", mixbf, 0.0)
            for bi_, br in enumerate(branches):
                brow = {"A": 0, "B": 512, "C": 1024}[br]
                for hf in range(2):
                    if br == "A":
                        wb = load_w(WB["w_branch"][l][brow:brow + 512, hf * 512:(hf + 1) * 512], 64, 8, 512)
                    else:
                        wb = load_w(WB["w_branch"][l][brow:brow + 512, hf * 512:(hf + 1) * 512], 128, 4, 512)
                    gcol = OFF["gt"] + bi_ * 0 + {"A": 0, "B": 1024, "C": 2048}[br] + hf * 512
                    wg = load_w(wi[:, gcol:gcol + 512], 128, 8, 512)
                    for q in range(4):
                        i = hf * 4 + q
                        for (a, ha, sn) in segs:
                            p1 = nxt("pa", psA)
                            p2 = nxt("pa", psA)
                            if br == "A":
                                for kt in range(8):
                                    mm(p1[:, 0:sn], wb[0:64, kt, q * 128:(q + 1) * 128], o_a[0:64, kt, ha:ha + sn], start=(kt == 0), stop=(kt == 7))
                            else:
                                osrc = o_bn if br == "B" else o_cn
                                for kt in range(4):
                                    mm(p1[:, 0:sn], wb[:, kt, q * 128:(q + 1) * 128], osrc[:, kt, ha:ha + sn], start=(kt == 0), stop=(kt == 3))
                            for kt in range(8):
                                mm(p2[:, 0:sn], wg[:, kt, q * 128:(q + 1) * 128], xbf[:, kt, a:a + sn], start=(kt == 0), stop=(kt == 7))
                            gs_ = gsb[(i + ha) % 2]
                            act(gs_[:, 0:sn], p2[:, 0:sn], AF.Sigmoid)
                            if bi_ == 0:
                                tt("dve", mixacc[:, i, ha:ha + sn], p1[:, 0:sn], gs_[:, 0:sn], ALU.mult)
                            else:
                                tt("dve", gs_[:, 0:sn], p1[:, 0:sn], gs_[:, 0:sn], ALU.mult)
                                tt("pool", mixacc[:, i, ha:ha + sn], mixacc[:, i, ha:ha + sn], gs_[:, 0:sn], ALU.add)
            if branches:
                cp("pool", mixbf, mixacc)
            for hf in range(2):
                wo = load_w(WB["w_out"][l][:, hf * 512:(hf + 1) * 512], 128, 8, 512)
                for q in range(4):
                    i = hf * 4 + q
                    for (a, ha, sn) in segs:
                        pa = nxt("pa", psA)
                        for kt in range(8):
                            mm(pa[:, 0:sn], wo[:, kt, q * 128:(q + 1) * 128], mixbf[:, kt, ha:ha + sn], start=(kt == 0), stop=(kt == 7))
                        stt(xres[:, i, a:a + sn], xres[:, i, a:a + sn], ALPHA, pa[:, 0:sn], ALU.mult, ALU.add)
            for (a, ha, sn) in segs:
                layer_norm(a, sn, PC["ln1g"], PC["ln1b"])
            S.fence()

        if stage < 3:
            continue
        AR.reset()
        tailx = AR.alloc("tailx", [8, 2])
        S.dma("sp", dr["ag3_in"].re("p (a b) -> p a b", b=2), xres[:, :, SEG - 2:SEG])
        S.fence()
        agi, ago = dr["ag3_in"].ap, dr["ag3_out"].ap
        ccop = S.cc(lambda e: e.collective_compute("AllGather", ALU.bypass, replica_groups=GROUPS,
                                                          ins=[agi], outs=[ago]), ["ag3_in"], ["ag3_out"])
        g4 = AR.alloc("g4", [4, 16])
        S.dma("sp", g4, dr["ag3_out"].re("(r p) c -> p r c", p=128))
        tx = tailx.re("p a b -> p (a b)")
        ts("dve", tx, g4[:, 0, :], corev[:, 1:2], None, ALU.mult)
        for j in range(1, 4):
            stt(tx, g4[:, j, :], corev[:, 1 + j:2 + j], tx, ALU.mult, ALU.add)
        tailb = AR.alloc("tailb", [8, 2], BF16)
        cp("dve", tailb, tailx)
        S.fence()
        keep = AR.off

        for b in range(NBLK if stage >= 4 else 0):
            AR.off = keep
            AR.gen += 1
            c0 = b * TB
            last = (b == NBLK - 1)
            h = AR.alloc("h", [22, TB + NS], BF16)
            ug = [AR.alloc(f"ug{i}", [2 + TB]) for i in range(2)]
            uv = [AR.alloc(f"uv{i}", [2 + TB]) for i in range(2)]
            cg = [AR.alloc(f"cg{i}", [TB]) for i in range(2)]
            cv = [AR.alloc(f"cv{i}", [TB]) for i in range(2)]
            usg = [AR.alloc(f"usg{i}", [4, 18]) for i in range(2)]
            usv = [AR.alloc(f"usv{i}", [4, 18]) for i in range(2)]
            csg = [AR.alloc(f"csg{i}", [4, 16]) for i in range(2)]
            csv = [AR.alloc(f"csv{i}", [4, 16]) for i in range(2)]
            sio = [(AR.alloc(f"sin{i}", [512]), AR.alloc(f"sout{i}", [512]), None) for i in range(2)] if last else None
            ucb = [AR.alloc(f"uc{i}", [16]) for i in range(2)]
            for jg in range(6):
                nj = 4 if jg < 5 else 2
                wg = load_w(WB["w_up"][l][:, jg * 512:jg * 512 + nj * 128], 128, 8, nj * 128)
                wv = load_w(WB["w_up"][l][:, D_FF + jg * 512:D_FF + jg * 512 + nj * 128], 128, 8, nj * 128)
                for jj in range(nj):
                    j = jg * 4 + jj
                    fw = PC["fcw"]
                    for half, (wt, ubuf, cbuf, usb, csb) in enumerate(((wg, ug, cg, usg, csg), (wv, uv, cv, usv, csv))):
                        jc = j + 22 * half
                        u = ubuf[j % 2]
                        cc_ = cbuf[j % 2]
                        pa = nxt("pa", psA)
                        for kt in range(8):
                            mm(pa[:, 0:TB], wt[:, kt, jj * 128:(jj + 1) * 128], xbf[:, kt, c0:c0 + TB],
                               start=(kt == 0), stop=(kt == 7))
                        if b == 0:
                            pb = nxt("pb", psB)
                            for kt in range(8):
                                mm(pb[:, 0:2], wt[:, kt, jj * 128:(jj + 1) * 128], tailb[:, kt, :],
                                   start=(kt == 0), stop=(kt == 7))
                            cp("dve", u[:, 0:2], pb[:, 0:2])
                        else:
                            cp("pool", u[:, 0:2], ffn_tail[:, jc, :])
                        cp("act", u[:, 2:2 + TB], pa[:, 0:TB])
                        cp("pool", ffn_tail[:, jc, :], u[:, TB:TB + 2])
                        wc = lambda t, jc=jc: par[:, fw + t * 44 + jc:fw + t * 44 + jc + 1]
                        ts("pool", cc_, u[:, 0:TB], wc(0), par[:, PC["fcb"] + jc:PC["fcb"] + jc + 1], ALU.mult, ALU.add)
                        stt(cc_, u[:, 1:TB + 1], wc(1), cc_, ALU.mult, ALU.add)
                        stt(cc_, u[:, 2:TB + 2], wc(2), cc_, ALU.mult, ALU.add)
                        if last:
                            us = usb[j % 2]
                            cs = csb[j % 2]
                            pb = nxt("pb", psB)
                            for kt in range(8):
                                mm(pb[:, 0:NS], wt[:, kt, jj * 128:(jj + 1) * 128], xbf[:, kt, SEG:SEG + NS],
                                   start=(kt == 0), stop=(kt == 7))
                            sin, sout, pout = sio[half]
                            if jj == 0:
                                S.dma("sp", sin[0:8, 0:nj * 128],
                                      V(dr["st_fconv"].ap[l].rearrange("s r c -> (s r) c")[:, jc * 128:(jc + nj) * 128], "RO"))
                            pb2 = nxt("pb", psB)
                            tr(pb2[:, 0:8], sin[0:8, jj * 128:(jj + 1) * 128], ident[0:8, 0:8])
                            cp("dve", us[:, :, 0:2], pb2[:, 0:8].re("p (s r) -> p s r", s=4))
                            cp("act", us[:, :, 2:18], pb[:, 0:NS].re("p (s t) -> p s t", s=4))
                            ts("pool", cs, us[:, :, 0:16], wc(0), par[:, PC["fcb"] + jc:PC["fcb"] + jc + 1], ALU.mult, ALU.add)
                            stt(cs, us[:, :, 1:17], wc(1), cs, ALU.mult, ALU.add)
                            stt(cs, us[:, :, 2:18], wc(2), cs, ALU.mult, ALU.add)
                            uc = ucb[(2 * j + half) % 2]
                            cp("pool", uc[:, 0:8].re("p (s r) -> p s r", s=4), us[:, :, 16:18])
                            cp("pool", uc[:, 8:10], u[:, TB:TB + 2])
                            pb3 = nxt("pb", psB)
                            tr(pb3[0:10, 0:128], uc[:, 0:10], ident)
                            cp("dve", sout[0:10, jj * 128:(jj + 1) * 128], pb3[0:10, 0:128])
                            if jj == nj - 1:
                                S.dma("sp", V(dr["s_fconv"].ap[l].rearrange("s r c -> (s r) c")[:, (jc - nj + 1) * 128:(jc + 1) * 128], "out_s_fconv"),
                                      sout[0:8, 0:nj * 128])
                                S.dma("sp", V(dr["p_fconv"].ap[l][:, (jc - nj + 1) * 128:(jc + 1) * 128], "out_p_fconv"),
                                      sout[8:10, 0:nj * 128])
                    act(cg[j % 2], cg[j % 2], AF.Silu)
                    tt("dve", h[:, j, 0:TB], cg[j % 2], cv[j % 2], ALU.mult)
                    if last:
                        act(csg[j % 2], csg[j % 2], AF.Silu)
                        tt("dve", h[:, j, TB:TB + NS].re("p (s t) -> p s t", s=4), csg[j % 2], csv[j % 2], ALU.mult)
            segs = [(c0, 0, TB)] + ([(SEG, TB, NS)] if last else [])
            for i2 in range(2):
                wd = [load_w(WB["w_down"][l][:, (i2 * 4 + q) * 128:(i2 * 4 + q + 1) * 128], 128, 22, 128) for q in range(1)]
                for q in range(4):
                    i = i2 * 4 + q
                    if q > 0:
                        wd = [load_w(WB["w_down"][l][:, i * 128:(i + 1) * 128], 128, 22, 128)]
                    for (a, ha, sn) in segs:
                        pa = nxt("pa", psA)
                        for kt in range(22):
                            mm(pa[:, 0:sn], wd[0][:, kt, :], h[:, kt, ha:ha + sn], start=(kt == 0), stop=(kt == 21))
                        stt(xres[:, i, a:a + sn], xres[:, i, a:a + sn], ALPHA, pa[:, 0:sn], ALU.mult, ALU.add)
            for (a, ha, sn) in segs:
                layer_norm(a, sn, PC["ln2g"], PC["ln2b"])
            S.fence()

    AR.reset()
    yo = [AR.alloc(f"yo{i}", [D]) for i in range(2)]
    for ti, (src, r0, nr, c0) in enumerate(tiles):
        y = yo[ti % 2]
        for dq in range(2):
            pb = nxt("pb", psB)
            for q in range(4):
                dt_ = dq * 4 + q
                tr(pb[0:nr, q * 128:(q + 1) * 128], xres[:, dt_, c0:c0 + nr], ident)
            cp("act" if dq % 2 else "dve", y[0:nr, dq * 512:(dq + 1) * 512], pb[0:nr, :])
        dst = dr["y_p"][r0:r0 + nr, :] if src == "xp" else dr["y_s"][0:nr, :]
        S.dma("sp", dst, y[0:nr, :])
    S.fence()
    S.emit(nc, es)
    es.close()
    return nc


_NC_CACHE = {}


def _core_inputs(inp, c):
    b, p = c // 4, c % 4
    m = {}
    m["xp"] = np.ascontiguousarray(inp["x_prompt"][b, p * SEG:(p + 1) * SEG])
    m["xs"] = np.ascontiguousarray(inp["x_sample"][4 * c:4 * c + 4]).reshape(NS, D)
    m["cache_k"] = np.ascontiguousarray(inp["cache_swa_k"][:, 4 * c:4 * c + 4]).reshape(DEPTH, 4, 128, 128)
    m["cache_v"] = np.ascontiguousarray(inp["cache_swa_v"][:, 4 * c:4 * c + 4]).reshape(DEPTH, 4, 128, 128)
    m["st_hgrn"] = np.ascontiguousarray(inp["state_hgrn"][:, 4 * c:4 * c + 4])
    m["st_gdn"] = np.ascontiguousarray(inp["state_gdn"][:, 4 * c:4 * c + 4])
    m["st_gconv"] = np.ascontiguousarray(inp["state_gdn_conv"][:, 4 * c:4 * c + 4])
    m["st_fconv"] = np.ascontiguousarray(inp["state_ffn_conv"][:, 4 * c:4 * c + 4])
    for k in ("ln_in_g", "ln_in_b", "w_in", "attn_sinks", "hgrn_lb_logits", "hgrn_norm_w", "gdn_conv_w",
              "gdn_a_log", "gdn_dt_bias", "gdn_norm_w", "w_branch", "w_out", "ln1_g", "ln1_b", "w_up",
              "ffn_conv_w", "ffn_conv_b", "w_down", "ln2_g", "ln2_b"):
        m[k] = np.ascontiguousarray(inp[k], dtype=np.float32)
    m["consts"] = CONSTS
    cv = np.zeros((128, 16), np.float32)
    cv[:, 0] = 1.0 if p > 0 else 0.0
    if p > 0:
        cv[:, 1 + (p - 1)] = 1.0
    cv[:, 5 + p] = 1.0
    cv[:, 9] = 0.0 if p > 0 else NEGM
    m["corev"] = cv
    return m


def run_cores(inp, nlayers=DEPTH, stage=99):
    key = (nlayers, stage)
    if key not in _NC_CACHE:
        _NC_CACHE[key] = build_nc(stage=stage, nlayers=nlayers)
    nc = _NC_CACHE[key]
    in_maps = [_core_inputs(inp, c) for c in range(8)]
    if nlayers < DEPTH:
        for m in in_maps:
            for n, shp in DRAM_IN:
                if shp[0] == DEPTH and n not in ('hgrn_lb_logits', 'gdn_dt_bias', 'gdn_a_log'):
                    m[n] = np.ascontiguousarray(m[n][:nlayers])
    res = run_bass_kernel_spmd(nc, in_maps, core_ids=list(range(8)))
    return res.results


def kernel(**inp):
    inp = {k: np.asarray(v) for k, v in inp.items()}
    r = run_cores(inp)
    y_p = np.stack([np.concatenate([r[b * 4 + p]["y_p"] for p in range(4)], 0) for b in range(2)], 0)
    y_s = np.concatenate([r[c]["y_s"].reshape(4, 16, D) for c in range(8)], 0)

    def pl(name, shape):
        return np.stack([r[3][name], r[7][name]], 1).reshape(shape)

    def sl(name, shape):
        return np.concatenate([r[c][name] for c in range(8)], 1).reshape(shape)

    outs = (y_p, y_s,
            pl("p_swa_k", (DEPTH, 2, 128, 2, 64)), pl("p_swa_v", (DEPTH, 2, 128, 2, 64)),
            pl("p_hgrn", (DEPTH, 2, 4, 128, 128)), pl("p_gdn", (DEPTH, 2, 4, 128, 128)),
            pl("p_gconv", (DEPTH, 2, 3, C_QKV)), pl("p_fconv", (DEPTH, 2, 2, 2 * D_FF)),
            sl("s_swa_k", (DEPTH, 32, 128, 2, 64)), sl("s_swa_v", (DEPTH, 32, 128, 2, 64)),
            sl("s_hgrn", (DEPTH, 32, 4, 128, 128)), sl("s_gdn", (DEPTH, 32, 4, 128, 128)),
            sl("s_gconv", (DEPTH, 32, 3, C_QKV)), sl("s_fconv", (DEPTH, 32, 2, 2 * D_FF)))
    return tuple(np.ascontiguousarray(o, dtype=np.float32) for o in outs)
```

```python
import numpy as np
from contextlib import ExitStack
import concourse.bass as bass
import concourse.mybir as mybir
from concourse.bass_utils import run_bass_kernel_spmd

F32 = mybir.dt.float32
BF16 = mybir.dt.bfloat16
AF = mybir.ActivationFunctionType
ALU = mybir.AluOpType

D = 1024
DEPTH = 4
SEG = 2048
NS = 64
NTOK = SEG + NS
TB = 512
NBLK = SEG // TB
A_HEADS, A_KV, A_HD = 8, 2, 64
D_FF = 2816
C_QKV = 1536
D_IN = 7944
OFF = {}
_o = 0
for _n, _w in (("aq", 512), ("ak", 128), ("av", 128), ("bq", 512), ("bf", 512), ("bi", 512), ("bg", 512),
               ("cqkv", 1536), ("cz", 512), ("cb", 4), ("ca", 4), ("gt", 3072)):
    OFF[_n] = _o
    _o += _w
ALPHA = (2 * DEPTH) ** 0.25
LN_EPS = 1e-5
RMS_EPS = 1e-6
NEGM = -30000.0


class V:
    __slots__ = ("ap", "key")

    def __init__(self, ap, key):
        self.ap = ap
        self.key = key

    def __getitem__(self, idx):
        return V(self.ap[idx], self.key)

    def k(self, key):
        return V(self.ap, key)

    def re(self, pat, **kw):
        return V(self.ap.rearrange(pat, **kw), self.key)

    def bc(self, shape):
        return V(self.ap.to_broadcast(shape), self.key)


class Op:
    __slots__ = ("eng", "kind", "fn", "deps", "idx", "inc", "sem", "val", "gid")


class Sched:
    ENGS = ("pe", "act", "dve", "pool", "sp")
    NDMA = 20

    def __init__(self):
        self.ops = {e: [] for e in self.ENGS}
        self.res = {}
        self.dma_slots = {e: [None] * self.NDMA for e in ("sp", "act", "pool")}
        self.dma_cnt = {e: [0] * self.NDMA for e in ("sp", "act", "pool")}
        self.dma_rr = {e: 0 for e in ("sp", "act", "pool")}
        self.all_dma = []
        self.gid = 0
        self.pending_dma = []
        self.ccs = []
        self.bg = {}

    def _rec(self, eng, kind, fn, reads, writes, extra_deps=()):
        op = Op()
        op.eng, op.kind, op.fn = eng, kind, fn
        op.inc = False
        op.sem = None
        op.val = 0
        op.gid = self.gid
        self.gid += 1
        deps = []
        for r in reads:
            if r in self.bg:
                deps.append(self.bg[r])
            st = self.res.get(r)
            if st and st[0] is not None:
                deps.append(st[0])
        for w in writes:
            st = self.res.get(w)
            if st:
                if st[0] is not None:
                    deps.append(st[0])
                deps.extend(st[1])
        deps.extend(extra_deps)
        for r in reads:
            st = self.res.setdefault(r, [None, []])
            st[1].append(op)
        for w in writes:
            self.res[w] = [op, []]
        seen = set()
        dd = []
        for d in deps:
            if d is op or id(d) in seen:
                continue
            seen.add(id(d))
            if d.kind == "c" and kind == "c" and d.eng == "pe" and eng == "pe":
                continue
            dd.append(d)
        op.deps = dd
        op.idx = len(self.ops[eng])
        self.ops[eng].append(op)
        return op

    def c(self, eng, fn, reads, writes):
        return self._rec(eng, "c", fn, reads, writes)

    def cc(self, fn, reads, writes):
        op = self._rec("pool", "x", fn, reads, writes)
        op.sem = ("cc", len(self.ccs))
        op.val = 1
        self.ccs.append(op)
        self.pending_dma.append(op)
        return op

    def dma(self, q, out, in_, bg=False, **kw):
        slot = self.dma_rr[q]
        self.dma_rr[q] = (slot + 1) % self.NDMA
        prev = self.dma_slots[q][slot]
        extra = [prev] if prev is not None else []
        o, i = out.ap, in_.ap
        op = self._rec(q, "d", lambda e: e.dma_start(out=o, in_=i, **kw), [in_.key], [out.key], extra)
        self.dma_cnt[q][slot] += 1
        op.sem = (q, slot)
        op.val = 16 * self.dma_cnt[q][slot]
        self.dma_slots[q][slot] = op
        self.all_dma.append(op)
        if bg:
            self.bg[out.key] = op
        else:
            self.pending_dma.append(op)
        return op

    def fence(self):
        lasts = [self.ops[e][-1] for e in self.ENGS if self.ops[e] and self.ops[e][-1].kind != "d"]
        for e in self.ENGS:
            pass
        lastc = []
        for e in self.ENGS:
            for o in reversed(self.ops[e]):
                if o.kind == "c" and o.fn is not None:
                    lastc.append(o)
                    break
                if o.fn is None:
                    break
        pend = list(self.pending_dma)
        self.pending_dma = []
        for e in self.ENGS:
            deps = list(lastc) + pend
            op = Op()
            op.eng, op.kind, op.fn = e, "c", None
            op.inc, op.sem, op.val, op.gid = False, None, 0, self.gid
            self.gid += 1
            op.deps = deps
            op.idx = len(self.ops[e])
            self.ops[e].append(op)
        self.res = {}

    def emit(self, nc, es):
        CH = 20000
        for e in self.ENGS:
            for op in self.ops[e]:
                for d in op.deps:
                    d.inc = True
        self.csems = {}
        for e in self.ENGS:
            n = 0
            for op in self.ops[e]:
                if op.kind == "c" and op.inc:
                    op.sem = (e, n // CH)
                    op.val = n % CH + 1
                    n += 1
            nsem = max(1, (n + CH - 1) // CH)
            self.csems[e] = [es.enter_context(nc.semaphore(f"c_{e}_{j}")) for j in range(nsem)]
        self.dsems = {q: [es.enter_context(nc.semaphore(f"d_{q}_{j}")) for j in range(self.NDMA)]
                      for q in ("sp", "act", "pool")}
        block = es.enter_context(nc.Block())
        engmap = {"pe": block.tensor, "act": block.scalar, "dve": block.vector, "pool": block.gpsimd, "sp": block.sync}

        self.xsems = [es.enter_context(nc.semaphore(f"x_{j}")) for j in range(len(self.ccs))]

        def semof(op):
            if op.kind == "d":
                return self.dsems[op.sem[0]][op.sem[1]]
            if op.kind == "x":
                return self.xsems[op.sem[1]]
            return self.csems[op.sem[0]][op.sem[1]]

        def make(e):
            oplist = self.ops[e]

            def body(eng):
                waited = {}
                for op in oplist:
                    for d in op.deps:
                        key = (d.kind, d.sem)
                        if waited.get(key, 0) >= d.val:
                            continue
                        eng.wait_ge(semof(d), d.val)
                        waited[key] = d.val
                    if op.fn is None:
                        continue
                    ins = op.fn(eng)
                    if op.kind == "d":
                        ins.then_inc(semof(op), 16)
                    elif op.kind == "x":
                        ins.then_inc(semof(op), 1)
                    elif op.inc:
                        ins.then_inc(semof(op), 1)
            return body

        for e in self.ENGS:
            engmap[e](make(e))


S = None


def kof(*vs):
    return [v.key for v in vs if isinstance(v, V)]


def apof(x):
    return x.ap if isinstance(x, V) else x


def mm(out, lhsT, rhs, start=True, stop=True):
    o, l, r = out.ap, lhsT.ap, rhs.ap
    return S.c("pe", lambda e: e.matmul(o, l, r, start=start, stop=stop), kof(lhsT, rhs), kof(out))


def tr(out, in_, ident):
    o, i, d = out.ap, in_.ap, ident.ap
    return S.c("pe", lambda e: e.transpose(o, i, d), kof(in_, ident), kof(out))


def act(out, in_, func, bias=None, scale=None):
    kw = {}
    if bias is not None:
        kw["bias"] = apof(bias)
    if scale is not None:
        kw["scale"] = apof(scale)
    o, i = out.ap, in_.ap
    return S.c("act", lambda e: e.activation(out=o, in_=i, func=func, **kw), kof(in_, bias, scale), kof(out))


def ts(eng, out, in0, s1, s2, op0, op1=None):
    o, i = out.ap, in0.ap
    a1, a2 = apof(s1), apof(s2)
    if op1 is None:
        return S.c(eng, lambda e: e.tensor_scalar(o, i, a1, None, op0), kof(in0, s1), kof(out))
    return S.c(eng, lambda e: e.tensor_scalar(o, i, a1, a2, op0, op1), kof(in0, s1, s2), kof(out))


def tt(eng, out, in0, in1, op):
    o, a, b = out.ap, in0.ap, in1.ap
    return S.c(eng, lambda e: e.tensor_tensor(out=o, in0=a, in1=b, op=op), kof(in0, in1), kof(out))


def stt(out, in0, scalar, in1, op0, op1):
    o, a, b = out.ap, in0.ap, in1.ap
    sc = apof(scalar)
    return S.c("dve", lambda e: e.scalar_tensor_tensor(out=o, in0=a, scalar=sc, in1=b, op0=op0, op1=op1),
               kof(in0, scalar, in1), kof(out))


def cp(eng, out, in_):
    o, i = out.ap, in_.ap
    if eng == "act":
        return S.c("act", lambda e: e.activation(out=o, in_=i, func=AF.Copy), kof(in_), kof(out))
    return S.c(eng, lambda e: e.tensor_copy(o, i), kof(in_), kof(out))


def recip(out, in_):
    o, i = out.ap, in_.ap
    return S.c("dve", lambda e: e.reciprocal(o, i), kof(in_), kof(out))


def memset(eng, out, val):
    o = out.ap
    return S.c(eng, lambda e: e.memset(o, val), [], kof(out))


def make_consts():
    cols = {}
    parts = []
    pos = [0]

    def add(name, a):
        a = np.asarray(a, np.float32)
        if a.shape[0] < 128:
            a = np.concatenate([a, np.zeros((128 - a.shape[0], a.shape[1]), np.float32)], 0)
        cols[name] = (pos[0], a.shape[1])
        parts.append(a)
        pos[0] += a.shape[1]

    i = np.arange(128)
    add("ident", np.eye(128))
    add("ones", np.ones((128, 128)))
    for tag, T, blk in (("p", 128, 128), ("s", 64, 16)):
        s = np.arange(T)[:, None]
        t = np.arange(T)[None, :]
        same = (s // blk) == (t // blk)
        add(f"triLE_{tag}", (same & (s <= t)))
        add(f"triGT_{tag}", (same & (s > t)))
        add(f"mbias_{tag}", np.where(same & (s <= t), 0.0, NEGM))
        add(f"offd_{tag}", (same & (s < t)))
    for tag, T, blk in (("p", 128, 64), ("s", 64, 16)):
        nb = T // blk
        sidx = np.arange(T)[:, None]
        tidx = np.arange(T)[None, :]
        same = (sidx // blk) == (tidx // blk)
        mid = (tidx // blk) * blk + blk // 2 - 1
        drel = (same & (sidx <= tidx)).astype(np.float32) - (same & (sidx <= mid)).astype(np.float32)
        tot = np.stack([(np.arange(T) // blk == b_) for b_ in range(nb)], 1).astype(np.float32)
        midc = np.stack([((np.arange(T) // blk == b_) & (np.arange(T) <= b_ * blk + blk // 2 - 1)) for b_ in range(nb)], 1).astype(np.float32)
        add(f"hdext_{tag}", np.concatenate([drel, tot, midc], 1))
        add(f"hle_{tag}", (same & (sidx <= tidx)))
        add(f"hgt_{tag}", (same & (sidx > tidx)))
        add(f"blk_{tag}", tot)
    sm = np.full((64, 4), NEGM, np.float32)
    for q in range(4):
        sm[16 * q:16 * q + 16, q] = 0.0
    add("seqmask", sm)
    return np.concatenate(parts, 1), cols


CONSTS, CCOL = make_consts()
NCON = CONSTS.shape[1]


class Arena:
    def __init__(self, ap, ncols32):
        self.ap = ap
        self.n = ncols32
        self.off = 0
        self.gen = 0

    def reset(self):
        self.off = 0
        self.gen += 1

    def alloc(self, name, shape, dtype=F32, parts=128):
        free = int(np.prod(shape))
        n32 = free if dtype == F32 else (free + 1) // 2
        self.off = (self.off + 31) // 32 * 32
        assert self.off + n32 <= self.n, (name, self.off, n32, self.n)
        a = self.ap[0:parts, self.off:self.off + n32]
        self.off += n32
        if dtype != F32:
            a = a.bitcast(dtype)[:, 0:free]
        if len(shape) == 2:
            a = a.rearrange("p (a b) -> p a b", a=shape[0])
        elif len(shape) == 3:
            a = a.rearrange("p (a b c) -> p a b c", a=shape[0], b=shape[1])
        return V(a, f"{name}#{self.gen}")


DRAM_IN = [
    ("xp", [SEG, D]), ("xs", [NS, D]),
    ("cache_k", [DEPTH, 4, 128, 128]), ("cache_v", [DEPTH, 4, 128, 128]),
    ("st_hgrn", [DEPTH, 4, 4, 128, 128]), ("st_gdn", [DEPTH, 4, 4, 128, 128]),
    ("st_gconv", [DEPTH, 4, 3, C_QKV]), ("st_fconv", [DEPTH, 4, 2, 2 * D_FF]),
    ("ln_in_g", [D]), ("ln_in_b", [D]), ("w_in", [DEPTH, D, D_IN]), ("attn_sinks", [DEPTH, 8]),
    ("hgrn_lb_logits", [DEPTH, 512]), ("hgrn_norm_w", [DEPTH, 128]), ("gdn_conv_w", [DEPTH, 4, C_QKV]),
    ("gdn_a_log", [DEPTH, 4]), ("gdn_dt_bias", [DEPTH, 4]), ("gdn_norm_w", [DEPTH, 128]),
    ("w_branch", [DEPTH, 1536, D]), ("w_out", [DEPTH, D, D]), ("ln1_g", [DEPTH, D]), ("ln1_b", [DEPTH, D]),
    ("w_up", [DEPTH, D, 2 * D_FF]), ("ffn_conv_w", [DEPTH, 3, 2 * D_FF]), ("ffn_conv_b", [DEPTH, 2 * D_FF]),
    ("w_down", [DEPTH, D_FF, D]), ("ln2_g", [DEPTH, D]), ("ln2_b", [DEPTH, D]),
    ("consts", [128, NCON]), ("corev", [128, 16]),
]
DRAM_OUT = [
    ("y_p", [SEG, D]), ("y_s", [NS, D]),
    ("p_swa_k", [DEPTH, 128, 128]), ("p_swa_v", [DEPTH, 128, 128]),
    ("p_hgrn", [DEPTH, 4, 128, 128]), ("p_gdn", [DEPTH, 4, 128, 128]),
    ("p_gconv", [DEPTH, 3, C_QKV]), ("p_fconv", [DEPTH, 2, 2 * D_FF]),
    ("s_swa_k", [DEPTH, 4, 128, 128]), ("s_swa_v", [DEPTH, 4, 128, 128]),
    ("s_hgrn", [DEPTH, 4, 4, 128, 128]), ("s_gdn", [DEPTH, 4, 4, 128, 128]),
    ("s_gconv", [DEPTH, 4, 3, C_QKV]), ("s_fconv", [DEPTH, 4, 2, 2 * D_FF]),
]
GROUPS = [[0, 1, 2, 3], [4, 5, 6, 7]]
NW = 3
WSLOT = 4096
ARENA32 = 13900


def build_nc(stage=99, nlayers=DEPTH):
    global S
    S = Sched()
    nc = bass.Bass("TRN2", target_bir_lowering=False)
    es = ExitStack()
    dr = {}
    for n, shp in DRAM_IN:
        if shp[0] == DEPTH and n not in ('hgrn_lb_logits', 'gdn_dt_bias', 'gdn_a_log'):
            shp = [nlayers] + list(shp[1:])
        dr[n] = V(nc.dram_tensor(n, shp, F32, kind="ExternalInput").ap(), "RO")
    for n, shp in DRAM_OUT:
        dr[n] = V(nc.dram_tensor(n, shp, F32, kind="ExternalOutput").ap(), "out_" + n)
    AG3W = D * 2
    AG1W = 256 + 36
    dr["ag1_in"] = V(nc.dram_tensor("ag1_in", [128, AG1W], F32).ap(), "ag1_in")
    dr["ag1_out"] = V(nc.dram_tensor("ag1_out", [512, AG1W], F32).ap(), "ag1_out")
    AG2W = 516 + 1024
    dr["ag2_in"] = V(nc.dram_tensor("ag2_in", [128, AG2W], F32).ap(), "ag2_in")
    dr["ag2_out"] = V(nc.dram_tensor("ag2_out", [512, AG2W], F32).ap(), "ag2_out")
    dr["ob_loc"] = V(nc.dram_tensor("ob_loc", [128, 4, NTOK], BF16).ap(), "ob_loc")
    dr["qh"] = V(nc.dram_tensor("qh", [128, 4, SEG], BF16).ap(), "qh")
    dr["oc_loc"] = V(nc.dram_tensor("oc_loc", [128, 4, NTOK], BF16).ap(), "oc_loc")
    dr["rh"] = V(nc.dram_tensor("rh", [128, 4, SEG], BF16).ap(), "rh")
    dr["ag3_in"] = V(nc.dram_tensor("ag3_in", [128, 16], F32).ap(), "ag3_in")
    dr["ag3_out"] = V(nc.dram_tensor("ag3_out", [512, 16], F32).ap(), "ag3_out")

    def sb(name, shape, dt=F32):
        return V(es.enter_context(nc.sbuf_tensor(name, shape, dt))[:], name)

    WSH = {"w_in": (D, D_IN), "w_branch": (1536, D), "w_out": (D, D), "w_up": (D, 2 * D_FF), "w_down": (D_FF, D)}
    WB = {}
    for wn, (r_, c_) in WSH.items():
        t_ = nc.dram_tensor("wb_" + wn, [nlayers, r_, c_], BF16).ap()
        WB[wn] = [V(t_[l_], f"wb_{wn}_{l_}") for l_ in range(nlayers)]

    def convert_weights(l_):
        for wn, (r_, c_) in WSH.items():
            step = 256 if r_ % 256 == 0 else 128
            step = r_ // max(1, min(4, r_ // step))
            for r0 in range(0, r_, step):
                r1 = min(r_, r0 + step)
                S.dma("pool", V(WB[wn][l_].ap[r0:r1, :], f"wb_{wn}_{l_}_{r0}"), V(dr[wn].ap[l_][r0:r1, :], "RO"), bg=True)

    xres = sb("xres", [128, 8, NTOK])
    xbf = sb("xbf", [128, 8, NTOK], BF16)
    con = sb("con", [128, NCON])
    conb = sb("conb", [128, 256], BF16)
    corev = sb("corev_sb", [128, 16])
    wsl = [sb(f"w{i}", [128, WSLOT], BF16) for i in range(NW)]
    par = sb("par", [128, 512])
    arena_t = sb("arena", [128, ARENA32])
    AR = Arena(arena_t.ap, ARENA32)
    ln_sq4 = sb("ln_sq4", [128, 4, 512])
    ln_mean = sb("ln_mean", [128, 512])
    ln_rstd = sb("ln_rstd", [128, 512])
    ffn_tail = sb("ffn_tail", [128, 44, 2])
    kprev = sb("kprev", [64, 2, 128], BF16)
    vprev = sb("vprev", [128, 128], BF16)
    gc_prev = sb("gc_prev", [128, 12, 3])
    esinkB = sb("esinkB", [64, 4 * 64 * 2])
    zero1 = sb("zero1", [128, 1])
    eps1 = sb("eps1", [128, 1])
    hS = sb("hS", [128, 4, 128])
    hpre = sb("hpre", [128, 4])
    lbF = sb("lbF", [128, 4, 4])
    sinB = sb("sinB", [128, 4, 128], BF16)
    sinC = sb("sinC", [128, 4, 128], BF16)
    gpar = sb("gpar", [128, 64])
    nrmw = sb("nrmw", [128, 2])
    psA = [V(es.enter_context(nc.psum_tensor(f"psA{i}", [128, 512], F32))[:], f"psA{i}") for i in range(4)]
    psB_t = [es.enter_context(nc.psum_tensor(f"psB{i}", [128, 512], F32)) for i in range(4)]
    psB = [V(psB_t[i][:], f"psB{i}") for i in range(4)]
    rr = {"w": 0, "pa": 0, "pb": 0, "stg": 0}

    def nxt(kind, lst):
        v = lst[rr[kind] % len(lst)]
        rr[kind] += 1
        return v

    def C(name, parts=128):
        c0, n = CCOL[name]
        return con[0:parts, c0:c0 + n]

    ident = C("ident")
    ones32 = C("ones")
    identb = conb[:, 0:128]
    onesb = conb[:, 128:256]

    S.dma("sp", con, dr["consts"])
    S.dma("sp", corev, dr["corev"])
    convert_weights(0)
    cp("dve", identb, ident)
    cp("dve", onesb, ones32)
    memset("pool", zero1, 0.0)
    memset("pool", eps1, RMS_EPS)
    seqmask = C("seqmask", 64)

    def load_w(src2d, kp, kt, ncols, q="pool"):
        slot = nxt("w", wsl)
        assert kt * ncols <= WSLOT
        v = V(slot.ap[0:kp, 0:kt * ncols].rearrange("p (k m) -> p k m", k=kt), slot.key)
        extra = [o for k_, o in S.bg.items() if k_.startswith(src2d.key + "_")]
        op = S.dma("sp", v, V(src2d.ap.rearrange("(k p) m -> p k m", p=kp), "RO"))
        have = set(id(d) for d in op.deps)
        op.deps.extend(o for o in extra if id(o) not in have)
        return v

    def layer_norm(c0, n, gcol, bcol):
        mean, rstd = ln_mean, ln_rstd
        sq4 = ln_sq4
        for s0 in range(0, n, 512):
            sn = min(512, n - s0)
            a, b = c0 + s0, c0 + s0 + sn
            p1 = nxt("pa", psA)
            p2 = nxt("pa", psA)
            for dt_ in range(8):
                mm(p1[:, 0:sn], ones32, xres[:, dt_, a:b], start=(dt_ == 0), stop=(dt_ == 7))
            for hf in range(2):
                act(sq4[:, :, 0:sn], xres[:, hf * 4:hf * 4 + 4, a:b], AF.Square)
                for q in range(4):
                    mm(p2[:, 0:sn], ones32, sq4[:, q, 0:sn], start=(hf == 0 and q == 0), stop=(hf == 1 and q == 3))
            act(mean[:, 0:sn], p1[:, 0:sn], AF.Copy, scale=1.0 / D)
            tt("pool", rstd[:, 0:sn], mean[:, 0:sn], mean[:, 0:sn], ALU.mult)
            stt(rstd[:, 0:sn], p2[:, 0:sn], 1.0 / D, rstd[:, 0:sn], ALU.mult, ALU.subtract)
            ts("dve", rstd[:, 0:sn], rstd[:, 0:sn], LN_EPS, None, ALU.add)
            act(rstd[:, 0:sn], rstd[:, 0:sn], AF.Sqrt)
            recip(rstd[:, 0:sn], rstd[:, 0:sn])
            xv = xres[:, :, a:b]
            mb = mean[:, 0:sn].re("p (o t) -> p o t", o=1).bc([128, 8, sn])
            rb = rstd[:, 0:sn].re("p (o t) -> p o t", o=1).bc([128, 8, sn])
            gb = par[:, gcol:gcol + 8].re("p (d o) -> p d o", o=1).bc([128, 8, sn])
            bb_ = par[:, bcol:bcol + 8].re("p (d o) -> p d o", o=1).bc([128, 8, sn])
            tt("dve", xv, xv, mb, ALU.subtract)
            tt("pool", xv, xv, rb, ALU.mult)
            tt("dve", xv, xv, gb, ALU.mult)
            tt("pool", xv, xv, bb_, ALU.add)
            cp("act", xbf[:, :, a:b], xv)

    stg = [sb(f"stg{i}", [128, 128]) for i in range(2)]

    def load_cols(col, src1d, t):
        st_ = nxt("stg", stg)
        S.dma("sp", st_[0:t, :], V(src1d.ap.rearrange("(t p) -> t p", p=128), "RO"))
        pb = nxt("pb", psB)
        tr(pb[:, 0:t], st_[0:t, :], ident[0:t, 0:t])
        cp("dve", par[:, col:col + t], pb[:, 0:t])

    def load_vec8(col, src1d):
        load_cols(col, src1d, 8)

    if stage == -1:
        AR.reset()
        t0 = AR.alloc("t0", [D])
        S.dma("sp", t0[0:NS, :], dr["xs"])
        S.dma("sp", dr["y_s"], t0[0:NS, :])
        S.fence()
        S.emit(nc, es)
        es.close()
        return nc
    load_vec8(0, dr["ln_in_g"])
    load_vec8(8, dr["ln_in_b"])
    AR.reset()
    xin = [AR.alloc(f"xin{i}", [D]) for i in range(2)]
    tiles = [("xp", i * 128, 128, i * 128) for i in range(SEG // 128)] + [("xs", 0, NS, SEG)]
    for ti, (src, r0, nr, c0) in enumerate(tiles):
        xi = xin[ti % 2]
        S.dma("sp", xi[0:nr, :], dr[src][r0:r0 + nr, :])
        for dq in range(2):
            pb = nxt("pb", psB)
            for q in range(4):
                dt_ = dq * 4 + q
                tr(pb[:, q * 128:q * 128 + nr], xi[0:nr, dt_ * 128:(dt_ + 1) * 128], ident[0:nr, 0:nr])
            cp("act" if dq % 2 else "dve", xres[:, dq * 4:dq * 4 + 4, c0:c0 + nr],
               pb.re("p (q t) -> p q t", q=4)[:, :, 0:nr])
    S.fence()
    AR.reset()
    if stage >= 1:
        layer_norm(0, NTOK, 0, 8)
    S.fence()

    PC = {"ln1g": 0, "ln1b": 8, "ln2g": 16, "ln2b": 24, "fcw": 32, "fcb": 32 + 132}

    for l in range(nlayers if stage >= 2 else 0):
        load_vec8(PC["ln1g"], dr["ln1_g"][l])
        load_vec8(PC["ln1b"], dr["ln1_b"][l])
        load_vec8(PC["ln2g"], dr["ln2_g"][l])
        load_vec8(PC["ln2b"], dr["ln2_b"][l])
        for tap in range(3):
            load_cols(PC["fcw"] + tap * 44, dr["ffn_conv_w"][l][tap], 44)
        load_cols(PC["fcb"], dr["ffn_conv_b"][l], 44)
        S.fence()

        wi = WB["w_in"][l]
        if l + 1 < nlayers:
            convert_weights(l + 1)
        if stage >= 5:
            AR.reset()
            wkv = load_w(wi[:, OFF["ak"]:OFF["ak"] + 256], 128, 8, 256)
            kvl = AR.alloc("kvl", [AG1W])
            kvs = AR.alloc("kvs", [256])
            pa = nxt("pa", psA)
            for kt in range(8):
                mm(pa[:, 0:256], xbf[:, kt, SEG - 128:SEG], wkv[:, kt, :], start=(kt == 0), stop=(kt == 7))
            cp("act", kvl[:, 0:256], pa[:, 0:256])
            pa = nxt("pa", psA)
            for kt in range(8):
                mm(pa[0:NS, 0:256], xbf[:, kt, SEG:SEG + NS], wkv[:, kt, :], start=(kt == 0), stop=(kt == 7))
            cp("act", kvs[0:NS, :], pa[0:NS, 0:256])
            S.dma("sp", dr["p_swa_k"][l], kvl[:, 0:128])
            S.dma("sp", dr["p_swa_v"][l], kvl[:, 128:256])
            for sq in range(4):
                S.dma("sp", dr["s_swa_k"][l][sq, 112:128, :], kvs[16 * sq:16 * sq + 16, 0:128])
                S.dma("sp", dr["s_swa_v"][l][sq, 112:128, :], kvs[16 * sq:16 * sq + 16, 128:256])
            S.dma("sp", dr["s_swa_k"][l][:, 0:112, :], dr["cache_k"][l][:, 16:128, :])
            S.dma("sp", dr["s_swa_v"][l][:, 0:112, :], dr["cache_v"][l][:, 16:128, :])
            gt3 = kvl[:, 256:AG1W].re("p (j t) -> p j t", t=3)
            for jg in range(3):
                wc_ = load_w(wi[:, OFF["cqkv"] + jg * 512:OFF["cqkv"] + (jg + 1) * 512], 128, 8, 512)
                pb = nxt("pb", psB)
                for jj in range(4):
                    for kt in range(8):
                        mm(pb[:, jj * 4:jj * 4 + 3], wc_[:, kt, jj * 128:(jj + 1) * 128], xbf[:, kt, SEG - 3:SEG],
                           start=(kt == 0), stop=(kt == 7))
                cp("dve", gt3[:, jg * 4:jg * 4 + 4, :], pb[:, 0:16].re("p (j t) -> p j t", t=4)[:, :, 0:3])
            gct = AR.alloc("gct", [C_QKV])
            for jq in range(3):
                pb = nxt("pb", psB)
                for jj in range(4):
                    tr(pb[0:3, jj * 128:(jj + 1) * 128], gt3[:, jq * 4 + jj, :], ident)
                cp("act", gct[0:3, jq * 512:(jq + 1) * 512], pb[0:3, :])
            S.dma("sp", dr["p_gconv"][l], gct[0:3, :])
            S.dma("sp", dr["ag1_in"], kvl)
            S.fence()
            a1i, a1o = dr["ag1_in"].ap, dr["ag1_out"].ap
            S.cc(lambda e: e.collective_compute("AllGather", ALU.bypass, replica_groups=GROUPS,
                                                       ins=[a1i], outs=[a1o]), ["ag1_in"], ["ag1_out"])
            g1 = AR.alloc("g1", [4, AG1W])
            S.dma("sp", g1, dr["ag1_out"].re("(r p) c -> p r c", p=128))
            hsel = AR.alloc("hsel", [AG1W])
            ts("dve", hsel, g1[:, 0, :], corev[:, 1:2], None, ALU.mult)
            for j in range(1, 4):
                stt(hsel, g1[:, j, :], corev[:, 1 + j:2 + j], hsel, ALU.mult, ALU.add)
            cp("pool", gc_prev, hsel[:, 256:AG1W].re("p (j t) -> p j t", t=3))
            cp("pool", vprev, hsel[:, 128:256])
            pb = nxt("pb", psB)
            for g in range(2):
                tr(pb[0:64, g * 128:(g + 1) * 128], hsel[:, g * 64:(g + 1) * 64], ident)
            cp("act", kprev, pb[0:64, 0:256].re("p (g t) -> p g t", g=2))
            es8 = AR.alloc("es8", [8])
            S.dma("sp", es8[0:64, :], V(dr["attn_sinks"].ap[l].partition_broadcast(64), "RO"))
            act(es8[0:64, :], es8[0:64, :], AF.Exp)
            cp("dve", esinkB.re("p (h t) -> p h t", h=8), es8[0:64, :].re("p (h o) -> p h o", o=1).bc([64, 8, 64]))
            S.fence()

        if stage >= 6:
            if l == 0:
                st_ = nxt("stg", stg)
                S.dma("sp", st_[0:16, :], V(dr["hgrn_lb_logits"].ap.rearrange("l (h p) -> (l h) p", p=128), "RO"))
                pb = nxt("pb", psB)
                tr(pb[:, 0:16], st_[0:16, :], ident[0:16, 0:16])
                AR.reset()
                e16 = AR.alloc("e16", [4, 4])
                act(e16.re("p l h -> p (l h)"), pb[:, 0:16], AF.Exp)
                ssum = AR.alloc("ssum", [4])
                tt("dve", ssum, e16[:, 0, :], e16[:, 1, :], ALU.add)
                tt("dve", ssum, ssum, e16[:, 2, :], ALU.add)
                tt("dve", ssum, ssum, e16[:, 3, :], ALU.add)
                recip(ssum, ssum)
                for ll in range(1, 4):
                    tt("dve", e16[:, ll, :], e16[:, ll, :], ssum, ALU.mult)
                memset("pool", lbF[:, 0, :], 0.0)
                cp("dve", lbF[:, 1, :], e16[:, 1, :])
                tt("dve", lbF[:, 2, :], lbF[:, 1, :], e16[:, 2, :], ALU.add)
                tt("dve", lbF[:, 3, :], lbF[:, 2, :], e16[:, 3, :], ALU.add)
                S.fence()
            AR.reset()
            lbT = AR.alloc("lbT", [512])
            omlT = AR.alloc("omlT", [512])
            omlF = AR.alloc("omlF", [4])
            nomlF = AR.alloc("nomlF", [4])
            hSs = AR.alloc("hSs", [4, 4, 128])
            keep1 = AR.off
            lg = AR.alloc("lg", [4, 512])
            S.dma("sp", lg, V(dr["hgrn_lb_logits"].ap.partition_broadcast(128), "RO"))
            act(lg, lg, AF.Exp)
            tt("dve", omlT, lg[:, 0, :], lg[:, 1, :], ALU.add)
            tt("dve", omlT, omlT, lg[:, 2, :], ALU.add)
            tt("dve", omlT, omlT, lg[:, 3, :], ALU.add)
            recip(omlT, omlT)
            memset("pool", lbT, 0.0)
            for ll in range(1, l + 1):
                tt("dve", lbT, lbT, lg[:, ll, :], ALU.add)
            tt("dve", lbT, lbT, omlT, ALU.mult)
            ts("dve", omlT, lbT, -1.0, 1.0, ALU.mult, ALU.add)
            ts("dve", omlF, lbF[:, l, :], -1.0, 1.0, ALU.mult, ALU.add)
            ts("dve", nomlF, omlF, -1.0, None, ALU.mult)
            S.dma("sp", nrmw[:, 0:1], V(dr["hgrn_norm_w"].ap[l].rearrange("(p o) -> p o", o=1), "RO"))
            S.dma("sp", nrmw[:, 1:2], V(dr["gdn_norm_w"].ap[l].rearrange("(p o) -> p o", o=1), "RO"))
            memset("pool", hS, 0.0)
            memset("pool", hpre, 1.0)
            for sq in range(4):
                S.dma("sp", hSs[:, sq, :, :], V(dr["st_hgrn"].ap[l][sq].rearrange("h k v -> k h v"), "RO"))
            S.fence()
            for b in range(NBLK):
                AR.off = keep1
                AR.gen += 1
                c0 = b * TB
                last = (b == NBLK - 1)
                n = TB + (NS if last else 0)
                segs = [(c0, 0, TB)] + ([(SEG, TB, NS)] if last else [])
                wbq = load_w(wi[:, OFF["bq"]:OFF["bq"] + 512], 128, 8, 512)
                wbf = load_w(wi[:, OFF["bf"]:OFF["bf"] + 512], 128, 8, 512)
                wbi = load_w(wi[:, OFF["bi"]:OFF["bi"] + 512], 128, 8, 512)
                qT = AR.alloc("qT", [4, n], BF16)
                kT = AR.alloc("kT", [4, n], BF16)
                obuf = AR.alloc("obuf", [4, n], BF16)
                qhat = AR.alloc("qhat", [4, TB], BF16)
                sgF = [AR.alloc(f"sgF{i}", [512]) for i in range(1)] * 2
                for h in range(4):
                    for (a, ha, sn) in segs:
                        pa = nxt("pa", psA)
                        for kt in range(8):
                            mm(pa[:, 0:sn], wbq[:, kt, h * 128:(h + 1) * 128], xbf[:, kt, a:a + sn], start=(kt == 0), stop=(kt == 7))
                        act(qT[:, h, ha:ha + sn], pa[:, 0:sn], AF.Silu)
                        pa = nxt("pa", psA)
                        for kt in range(8):
                            mm(pa[:, 0:sn], wbf[:, kt, h * 128:(h + 1) * 128], xbf[:, kt, a:a + sn], start=(kt == 0), stop=(kt == 7))
                        sg = sgF[h % 2]
                        act(sg[:, 0:sn], pa[:, 0:sn], AF.Sigmoid)
                        ts("dve", kT[:, h, ha:ha + sn], sg[:, 0:sn], nomlF[:, h:h + 1], omlF[:, h:h + 1], ALU.mult, ALU.add)
                sgT = AR.alloc("sgT", [512])
                tT = AR.alloc("tT", [512])
                lfT = AR.alloc("lfT", [512])
                kTM = AR.alloc("kTM", [512])
                vTM = AR.alloc("vTM", [512], BF16)
                kend = AR.alloc("kend", [512], BF16)
                kendm = AR.alloc("kendm", [4, 512], BF16) if last else None
                eqs = [AR.alloc(f"eq{i}", [128]) for i in range(2)]
                eks = [AR.alloc(f"ek{i}", [128]) for i in range(2)]
                dms = [AR.alloc(f"dm{h}", [8]) for h in range(4)]
                qts = [AR.alloc(f"qt{h}", [128], BF16) for h in range(4)]
                kts = [AR.alloc(f"kt_{h}", [128], BF16) for h in range(4)]
                attms = [AR.alloc(f"attm{h}", [128], BF16) for h in range(4)]
                sbfs = [AR.alloc(f"sbf{h}", [128], BF16) for h in range(4)]
                facs = [AR.alloc(f"fac{h}", [1]) for h in range(4)]
                tl = [(c0 + t * 128, t * 128, 128, "p") for t in range(4)] + ([(SEG, TB, NS, "s")] if last else [])
                for (a, ha, T, tag) in tl:
                    nb = 2 if tag == "p" else 4
                    blk = T // nb
                    hdext = C(f"hdext_{tag}", T)
                    hle = C(f"hle_{tag}", T)
                    hgt = C(f"hgt_{tag}", T)
                    pf = nxt("pa", psA)
                    pi = nxt("pa", psA)
                    for kt in range(8):
                        mm(pf[0:T, :], xbf[:, kt, a:a + T], wbf[:, kt, :], start=(kt == 0), stop=(kt == 7))
                    for kt in range(8):
                        mm(pi[0:T, :], xbf[:, kt, a:a + T], wbi[:, kt, :], start=(kt == 0), stop=(kt == 7))
                    act(sgT[0:T, :], pf[0:T, :], AF.Sigmoid)
                    tt("dve", tT[0:T, :], sgT[0:T, :], omlT[0:T, :], ALU.mult)
                    stt(sgT[0:T, :], tT[0:T, :], 1e-30, lbT[0:T, :], ALU.max, ALU.add)
                    act(lfT[0:T, :], sgT[0:T, :], AF.Ln)
                    tt("pool", kTM[0:T, :], omlT[0:T, :], tT[0:T, :], ALU.subtract)
                    cp("act", vTM[0:T, :], pi[0:T, :])
                    pr = nxt("pa", psA)
                    mm(pr[0:T, :], hgt, lfT[0:T, :])
                    act(tT[0:T, :], pr[0:T, :], AF.Exp)
                    tt("dve", kend[0:T, :], kTM[0:T, :], tT[0:T, :], ALU.mult)
                    if tag == "s":
                        bls = C("blk_s", T)
                        for bb in range(4):
                            ts("pool", kendm[0:T, bb, :], kend[0:T, :], bls[:, bb:bb + 1], None, ALU.mult)
                    for h in range(4):
                        hs = slice(h * 128, (h + 1) * 128)
                        pc = nxt("pb", psB)
                        mm(pc[:, 0:T + 2 * nb], lfT[0:T, hs], hdext)
                        eq, ek = eqs[h % 2], eks[h % 2]
                        act(eq[:, 0:T], pc[:, 0:T], AF.Exp)
                        tt("dve", qts[h][:, 0:T], qT[:, h, ha:ha + T], eq[:, 0:T], ALU.mult)
                        act(ek[:, 0:T], pc[:, 0:T], AF.Exp, scale=-1.0)
                        tt("pool", kts[h][:, 0:T], kT[:, h, ha:ha + T], ek[:, 0:T], ALU.mult)
                        act(dms[h][:, 0:2 * nb], pc[:, T:T + 2 * nb], AF.Exp)
                    for h in range(4):
                        pat = nxt("pb", psB)
                        mm(pat[0:T, 0:T], kts[h][:, 0:T], qts[h][:, 0:T])
                        stt(attms[h][0:T, 0:T], pat[0:T, 0:T], 1e30, hle, ALU.min, ALU.mult)
                    pos = [nxt("pb", psB) for _ in range(4)]
                    for h in range(4):
                        hs = slice(h * 128, (h + 1) * 128)
                        mm(pos[h][:, 0:T], vTM[0:T, hs], attms[h][0:T, 0:T], start=True, stop=False)
                    for bb in range(nb):
                        bs = slice(bb * blk, (bb + 1) * blk)
                        for h in range(4):
                            hs = slice(h * 128, (h + 1) * 128)
                            dm = dms[h]
                            qt = qts[h]
                            Sb = hS[:, h, :] if tag == "p" else hSs[:, bb, h, :]
                            act(sbfs[h], Sb, AF.Copy, scale=dm[:, nb + bb:nb + bb + 1])
                            mm(pos[h][:, bs], sbfs[h], qt[:, bs], start=False, stop=(bb == nb - 1))
                            pu = nxt("pa", psA)
                            if tag == "p":
                                ts("dve", facs[h], dm[:, nb + bb:nb + bb + 1], hpre[:, h:h + 1], None, ALU.mult)
                                ts("dve", qhat[:, h, ha + bb * blk:ha + (bb + 1) * blk], qt[:, bs], facs[h], None, ALU.mult)
                                mm(pu[:, 0:128], kend[bs, hs], vTM[bs, hs])
                            else:
                                mm(pu[:, 0:128], kendm[0:T, bb, hs], vTM[0:T, hs])
                            stt(Sb, Sb, dm[:, bb:bb + 1], pu[:, 0:128], ALU.mult, ALU.add)
                            if tag == "p":
                                tt("dve", hpre[:, h:h + 1], hpre[:, h:h + 1], dm[:, bb:bb + 1], ALU.mult)
                    for h in range(4):
                        cp("act", obuf[:, h, ha:ha + T], pos[h][:, 0:T])
                S.dma("sp", dr["ob_loc"][:, :, c0:c0 + TB], obuf[:, :, 0:TB])
                S.dma("sp", dr["qh"][:, :, c0:c0 + TB], qhat)
                if last:
                    S.dma("sp", dr["ob_loc"][:, :, SEG:SEG + NS], obuf[:, :, TB:TB + NS])
                S.fence()
            for sq in range(4):
                S.dma("sp", V(dr["s_hgrn"].ap[l][sq].rearrange("h k v -> k h v"), "out_s_hgrn"), hSs[:, sq, :, :])
            if stage >= 7:
                S.fence()
                AR.off = 0
                AR.gen += 1
                for tap in range(4):
                    st_ = nxt("stg", stg)
                    S.dma("sp", st_[0:12, :], V(dr["gdn_conv_w"].ap[l][tap].rearrange("(t p) -> t p", p=128), "RO"))
                    pb = nxt("pb", psB)
                    tr(pb[:, 0:12], st_[0:12, :], ident[0:12, 0:12])
                    cp("dve", gpar[:, tap * 12:(tap + 1) * 12], pb[:, 0:12])
                gtmp = AR.alloc("gtmp", [2, 4 * DEPTH])
                S.dma("sp", gtmp[:, 0, :], V(dr["gdn_dt_bias"].ap.rearrange("l h -> (l h)").partition_broadcast(128), "RO"))
                S.dma("sp", gtmp[:, 1, :], V(dr["gdn_a_log"].ap.rearrange("l h -> (l h)").partition_broadcast(128), "RO"))
                cp("dve", gpar[:, 48:52], gtmp[:, 0, 4 * l:4 * l + 4])
                act(gpar[:, 52:56], gtmp[:, 1, 4 * l:4 * l + 4], AF.Exp)
                ts("dve", gpar[:, 52:56], gpar[:, 52:56], -1.0, None, ALU.mult)
                gS = AR.alloc("gS", [4, 256])
                gSb = AR.alloc("gSb", [4, 256], BF16)
                memset("pool", gS, 0.0)
                for h in range(4):
                    cp("dve", gS[:, h, 128:256], ident)
                cp("act", gSb, gS)
                keepg = AR.off
                triLE = C("triLE_p")
                triGT = C("triGT_p")
                mbias = C("mbias_p")
                offd = C("offd_p")
                S.fence()
                import os
                for u_ in range(NBLK + (0 if os.environ.get('GDN_NOSAMPLE') else 1)):
                    AR.off = keepg
                    AR.gen += 1
                    smp = (u_ == NBLK)
                    n = NS if smp else TB
                    a0 = SEG if smp else u_ * TB
                    wcq = [load_w(wi[:, OFF["cqkv"] + jg * 512:OFF["cqkv"] + (jg + 1) * 512], 128, 8, 512) for jg in range(1)]
                    qn = AR.alloc("qn", [4, n], BF16)
                    kn = AR.alloc("kn", [4, n], BF16)
                    ntile = 4
                    TT = 16 if smp else 128
                    kTMa = AR.alloc("kTMa", [ntile, 4, 128], BF16)
                    vTMa = AR.alloc("vTMa", [ntile, 4, 128], BF16)
                    ocb = AR.alloc("ocb", [4, n], BF16)
                    rhb = AR.alloc("rhb", [4, n], BF16)
                    if smp:
                        gSs = AR.alloc("gSs", [4, 4, 128])
                        gSsb = AR.alloc("gSsb", [4, 4, 128], BF16)
                    mark_conv = AR.off
                    cpre = [AR.alloc(f"cpre{i}", [n + 12]) for i in range(3)]
                    u32 = [AR.alloc(f"u32{i}", [n]) for i in range(3)]
                    sqgs = [AR.alloc(f"sqg{i}", [n]) for i in range(2)]
                    rsgs = [AR.alloc(f"rsg{i}", [n]) for i in range(2)]
                    if smp:
                        for sq in range(4):
                            S.dma("sp", gSs[:, sq, :, :], V(dr["st_gdn"].ap[l][sq].rearrange("h k v -> k h v"), "RO"))
                        cp("act", gSsb, gSs)
                        sg12 = AR.alloc("sg12", [C_QKV])
                        so12 = AR.alloc("so12", [C_QKV])
                        S.dma("sp", sg12[0:12, :], V(dr["st_gconv"].ap[l].rearrange("s r c -> (s r) c"), "RO"))
                    for j in range(12):
                        if j % 4 == 0 and j > 0:
                            wcq = [load_w(wi[:, OFF["cqkv"] + (j // 4) * 512:OFF["cqkv"] + (j // 4 + 1) * 512], 128, 8, 512)]
                        pa = nxt("pa", psA)
                        for kt in range(8):
                            mm(pa[:, 0:n], wcq[0][:, kt, (j % 4) * 128:(j % 4 + 1) * 128], xbf[:, kt, a0:a0 + n], start=(kt == 0), stop=(kt == 7))
                        cpj = cpre[j % 3]
                        uj = u32[j % 3]
                        sqg, rsg = sqgs[j % 2], rsgs[j % 2]
                        gw = lambda t, j=j: gpar[:, t * 12 + j:t * 12 + j + 1]
                        if not smp:
                            cp("pool", cpj[:, 0:3], gc_prev[:, j, :])
                            cp("act", cpj[:, 3:3 + n], pa[:, 0:n])
                            cp("pool", gc_prev[:, j, :], cpj[:, n:n + 3])
                            ts("pool", uj, cpj[:, 0:n], gw(0), None, ALU.mult)
                            for t in range(1, 4):
                                stt(uj, cpj[:, t:t + n], gw(t), uj, ALU.mult, ALU.add)
                        else:
                            c3 = cpj[:, 0:76].re("p (s t) -> p s t", s=4)
                            pb = nxt("pb", psB)
                            tr(pb[:, 0:12], sg12[0:12, j * 128:(j + 1) * 128], ident[0:12, 0:12])
                            cp("dve", c3[:, :, 0:3], pb[:, 0:12].re("p (s r) -> p s r", s=4))
                            cp("act", c3[:, :, 3:19], pa[:, 0:n].re("p (s t) -> p s t", s=4))
                            u3 = uj.re("p (s t) -> p s t", s=4)
                            ts("pool", u3, c3[:, :, 0:16], gw(0), None, ALU.mult)
                            for t in range(1, 4):
                                stt(u3, c3[:, :, t:t + 16], gw(t), u3, ALU.mult, ALU.add)
                            uc12 = sqg[:, 0:12]
                            cp("pool", uc12.re("p (s r) -> p s r", s=4), c3[:, :, 16:19])
                            pb = nxt("pb", psB)
                            tr(pb[0:12, 0:128], uc12, ident)
                            cp("dve", so12[0:12, j * 128:(j + 1) * 128], pb[0:12, 0:128])
                        act(uj, uj, AF.Silu)
                        h = j % 4
                        if j < 8:
                            act(sqg, uj, AF.Square)
                            ps_ = nxt("pb", psB)
                            mm(ps_[:, 0:n], ones32, sqg)
                            act(rsg, ps_[:, 0:n], AF.Sqrt, bias=eps1)
                            recip(rsg, rsg)
                            if j < 4:
                                stt(qn[:, h, :], uj, 128.0 ** -0.5, rsg, ALU.mult, ALU.mult)
                            else:
                                tt("dve", uj, uj, rsg, ALU.mult)
                                cp("pool", kn[:, h, :], uj)
                        if j >= 4:
                            dstT = kTMa if j < 8 else vTMa
                            for t in range(ntile):
                                pb = nxt("pb", psB)
                                tr(pb[0:TT, 0:128], uj[:, t * TT:(t + 1) * TT], ident)
                                cp("act" if t % 2 else "dve", dstT[0:TT, t, h, :], pb[0:TT, 0:128])
                    if smp:
                        S.dma("sp", V(dr["s_gconv"].ap[l].rearrange("s r c -> (s r) c"), "out_s_gconv"), so12[0:12, :])
                    S.fence()
                    AR.off = mark_conv
                    AR.gen += 1
                    wcb = load_w(wi[:, OFF["cb"] - 248:OFF["cb"] + 8], 128, 8, 256)[:, :, 248:256]
                    NV = 128 if smp else 256
                    bg8 = AR.alloc("bg8", [8])
                    beta = AR.alloc("beta", [4])
                    gg = AR.alloc("gg", [4])
                    sm8 = AR.alloc("sm8", [16])
                    er_ = [AR.alloc(f"erow{i}", [TT]) for i in range(2)]
                    dT_ = [AR.alloc(f"decT{i}", [TT]) for i in range(1)] * 2
                    dS_ = [AR.alloc(f"decS{i}", [TT]) for i in range(1)] * 2
                    AA = [[AR.alloc(f"A{h}_{i}", [TT]) for i in range(2)] for h in range(4)]
                    ATT = [[AR.alloc(f"AT{h}_{i}", [TT]) for i in range(2)] for h in range(4)]
                    WW = [[AR.alloc(f"W{h}_{i}", [TT]) for i in range(1)] * 2 for h in range(4)]
                    QKd = [AR.alloc(f"QKd{h}", [TT], BF16) for h in range(4)]
                    keT = [AR.alloc(f"keT{h}", [TT], BF16) for h in range(4)]
                    qeT = [AR.alloc(f"qeT{h}", [TT], BF16) for h in range(4)]
                    ktil = [AR.alloc(f"ktil{h}", [128], BF16) for h in range(4)]
                    vaug = [AR.alloc(f"vaug{h}", [NV], BF16) for h in range(4)]
                    Wmb = [AR.alloc(f"Wmb{h}", [TT], BF16) for h in range(4)]
                    P1s = [AR.alloc(f"P1s{h}", [NV], BF16) for h in range(4)]
                    wbf_ = [AR.alloc(f"wbf_{h}", [NV], BF16) for h in range(4)]
                    for h in range(4):
                        memset("pool", vaug[h], 0.0)
                    nsq = 3 if smp else 6
                    for t in range(ntile):
                        T = TT
                        a = a0 + t * T
                        cs = slice(t * T, (t + 1) * T)
                        pg = nxt("pb", psB)
                        for kt in range(8):
                            mm(pg[0:T, 0:8], xbf[:, kt, a:a + T], wcb[:, kt, :], start=(kt == 0), stop=(kt == 7))
                        cp("dve", bg8[0:T, :], pg[0:T, 0:8])
                        act(beta[0:T, :], bg8[0:T, 0:4], AF.Sigmoid)
                        tt("dve", gg[0:T, :], bg8[0:T, 4:8], gpar[0:T, 48:52], ALU.add)
                        act(gg[0:T, :], gg[0:T, :], AF.Exp)
                        act(gg[0:T, :], gg[0:T, :], AF.Ln, bias=1.0)
                        tt("dve", gg[0:T, :], gg[0:T, :], gpar[0:T, 52:56], ALU.mult)
                        pcm = nxt("pb", psB)
                        mm(pcm[0:T, 0:4], triLE[0:T, 0:T], gg[0:T, :])
                        mm(pcm[0:T, 4:8], triGT[0:T, 0:T], gg[0:T, :])
                        mm(pcm[:, 8:12], ones32[0:T, :], gg[0:T, :])
                        act(sm8[0:T, 0:4], pcm[0:T, 0:4], AF.Exp)
                        act(sm8[0:T, 4:8], pcm[0:T, 0:4], AF.Copy, scale=-1.0)
                        act(sm8[0:T, 8:12], pcm[0:T, 4:8], AF.Exp)
                        act(sm8[:, 12:16], pcm[:, 8:12], AF.Exp)
                        Sts = [(gSs[:, t, h, :] if smp else gS[:, h, :]) for h in range(4)]
                        Sbs = [(gSsb[:, t, h, :] if smp else gSb[:, h, :]) for h in range(4)]
                        for h in range(4):
                            erow, decT, decS = er_[h % 2], dT_[h % 2], dS_[h % 2]
                            pcr = nxt("pb", psB)
                            gbc = gg[0:T, h:h + 1].bc([T, 128])
                            mm(pcr[:, 0:T], gbc, triLE[0:T, 0:T])
                            mm(pcr[0:T, 128:128 + T], gg[0:T, h:h + 1].bc([T, T]), triLE[0:T, 0:T], start=True, stop=False)
                            mm(pcr[0:T, 128:128 + T], ident[0:T, 0:T], mbias[0:T, 0:T], start=False, stop=True)
                            act(erow[:, 0:T], pcr[:, 0:T], AF.Exp)
                            act(decT[0:T, 0:T], pcr[0:T, 128:128 + T], AF.Exp, bias=sm8[0:T, 4 + h:5 + h])
                            tt("pool", decS[0:T, 0:T], decT[0:T, 0:T], offd[0:T, 0:T], ALU.mult)
                            pk = nxt("pb", psB)
                            mm(pk[0:T, 0:T], kn[:, h, cs], kn[:, h, cs])
                            mm(pk[0:T, 128:128 + T], kn[:, h, cs], qn[:, h, cs])
                            U32 = AA[h][0]
                            stt(U32[0:T, 0:T], pk[0:T, 0:T], beta[0:T, h:h + 1], decS[0:T, 0:T], ALU.mult, ALU.mult)
                            tt("dve", QKd[h][0:T, 0:T], pk[0:T, 128:128 + T], decT[0:T, 0:T], ALU.mult)
                            tt("pool", keT[h][:, 0:T], kn[:, h, cs], erow[:, 0:T], ALU.mult)
                            tt("pool", qeT[h][:, 0:T], qn[:, h, cs], erow[:, 0:T], ALU.mult)
                            ts("pool", ktil[h][0:T, :], kTMa[0:T, t, h, :], sm8[0:T, 8 + h:9 + h], None, ALU.mult)
                            cp("pool", vaug[h][0:T, 0:128], vTMa[0:T, t, h, :])
                            pl_ = nxt("pb", psB)
                            tr(pl_[0:T, 0:T], U32[0:T, 0:T], ident[0:T, 0:T])
                            cp("act", ATT[h][0][0:T, 0:T], pl_[0:T, 0:T])
                            tt("dve", WW[h][0][0:T, 0:T], ident[0:T, 0:T], U32[0:T, 0:T], ALU.subtract)
                        for js in range(nsq):
                            i0_, i1_ = js % 2, (js + 1) % 2
                            for h in range(4):
                                p1_ = nxt("pb", psB)
                                p1b_ = nxt("pb", psB)
                                mm(p1_[0:T, 0:T], ATT[h][i0_][0:T, 0:T], AA[h][i0_][0:T, 0:T])
                                mm(p1b_[0:T, 0:T], AA[h][i0_][0:T, 0:T], ATT[h][i0_][0:T, 0:T])
                                cp("dve", AA[h][i1_][0:T, 0:T], p1_[0:T, 0:T])
                                cp("act", ATT[h][i1_][0:T, 0:T], p1b_[0:T, 0:T])
                            for h in range(4):
                                p2_ = nxt("pa", psA)
                                mm(p2_[0:T, 0:T], ATT[h][i1_][0:T, 0:T], WW[h][i0_][0:T, 0:T])
                                tt("dve", WW[h][i1_][0:T, 0:T], WW[h][i0_][0:T, 0:T], p2_[0:T, 0:T], ALU.add)
                        for h in range(4):
                            cp("act", Wmb[h][0:T, 0:T], WW[h][nsq % 2][0:T, 0:T])
                        for h in range(4):
                            pp = nxt("pa", psA)
                            mm(pp[0:T, 0:NV], keT[h][:, 0:T], Sbs[h])
                            tt("dve", P1s[h][0:T, 0:NV], vaug[h][0:T, 0:NV], pp[0:T, 0:NV], ALU.subtract)
                        for h in range(4):
                            pw = nxt("pa", psA)
                            mm(pw[0:T, 0:NV], Wmb[h][0:T, 0:T], P1s[h][0:T, 0:NV])
                            act(wbf_[h][0:T, 0:NV], pw[0:T, 0:NV], AF.Copy, scale=beta[0:T, h:h + 1])
                        for h in range(4):
                            pq = nxt("pa", psA)
                            mm(pq[:, 0:T], wbf_[h][0:T, 0:128], QKd[h][0:T, 0:T], start=True, stop=False)
                            mm(pq[:, 0:T], Sbs[h][:, 0:128], qeT[h][:, 0:T], start=False, stop=True)
                            cp("act", ocb[:, h, cs], pq[:, 0:T])
                            if not smp:
                                mm(pq[:, 128:128 + T], wbf_[h][0:T, 128:256], QKd[h][0:T, 0:T], start=True, stop=False)
                                mm(pq[:, 128:128 + T], Sbs[h][:, 128:256], qeT[h][:, 0:T], start=False, stop=True)
                                cp("dve", rhb[:, h, cs], pq[:, 128:128 + T])
                        for h in range(4):
                            pu = nxt("pa", psA)
                            mm(pu[:, 0:NV], ktil[h][0:T, :], wbf_[h][0:T, 0:NV])
                            stt(Sts[h], Sts[h], sm8[:, 12 + h:13 + h], pu[:, 0:NV], ALU.mult, ALU.add)
                            cp("act", Sbs[h], Sts[h])
                    if smp:
                        S.dma("sp", dr["oc_loc"][:, :, SEG:SEG + NS], ocb)
                        for sq in range(4):
                            S.dma("sp", V(dr["s_gdn"].ap[l][sq].rearrange("h k v -> k h v"), "out_s_gdn"), gSs[:, sq, :, :])
                    else:
                        S.dma("sp", dr["oc_loc"][:, :, a0:a0 + TB], ocb)
                        S.dma("sp", dr["rh"][:, :, a0:a0 + TB], rhb)
                    S.fence()
                S.dma("sp", dr["ag2_in"][:, 516:516 + 1024].re("p (h c) -> p h c", h=4), gS)
                AR.off = keepg
                AR.gen += 1
            AR.off = keep1
            AR.gen += 1
            S.dma("sp", dr["ag2_in"][:, 0:512], hS.re("p h v -> p (h v)"))
            S.dma("sp", dr["ag2_in"][:, 512:516], hpre)
            S.fence()
            a2i, a2o = dr["ag2_in"].ap, dr["ag2_out"].ap
            S.cc(lambda e: e.collective_compute("AllGather", ALU.bypass, replica_groups=GROUPS,
                                                       ins=[a2i], outs=[a2o]), ["ag2_in"], ["ag2_out"])
            g2 = AR.alloc("g2", [4, AG2W])
            S.dma("sp", g2, dr["ag2_out"].re("(r p) c -> p r c", p=128))
            X = AR.alloc("X", [4, 128])
            Sin = AR.alloc("Sin", [4, 128])
            memset("pool", X, 0.0)
            memset("pool", Sin, 0.0)
            for j in range(3):
                for h in range(4):
                    stt(X[:, h, :], X[:, h, :], g2[:, j, 512 + h:513 + h], g2[:, j, h * 128:(h + 1) * 128], ALU.mult, ALU.add)
                stt(Sin.re("p h v -> p (h v)"), X.re("p h v -> p (h v)"), corev[:, 6 + j:7 + j], Sin.re("p h v -> p (h v)"), ALU.mult, ALU.add)
            cp("act", sinB, Sin)
            for h in range(4):
                stt(X[:, h, :], Sin[:, h, :], hpre[:, h:h + 1], hS[:, h, :], ALU.mult, ALU.add)
            S.dma("sp", V(dr["p_hgrn"].ap[l].rearrange("h k v -> k h v"), "out_p_hgrn"), X)
            if stage >= 7:
                Xc = AR.alloc("Xc", [4, 128])
                SinC32 = AR.alloc("SinC32", [4, 128])
                Xcb = AR.alloc("Xcb", [4, 128], BF16)
                PmT = AR.alloc("PmT", [128], BF16)
                memset("pool", Xc, 0.0)
                memset("pool", SinC32, 0.0)
                for j in range(3):
                    for h in range(4):
                        base = 516 + h * 256
                        pt = nxt("pb", psB)
                        tr(pt[:, 0:128], g2[:, j, base + 128:base + 256], ident)
                        cp("act", PmT, pt[:, 0:128])
                        cp("dve", Xcb[:, h, :], Xc[:, h, :])
                        px = nxt("pb", psB)
                        mm(px[:, 0:128], PmT, Xcb[:, h, :])
                        tt("dve", Xc[:, h, :], px[:, 0:128], g2[:, j, base:base + 128], ALU.add)
                    stt(SinC32.re("p h v -> p (h v)"), Xc.re("p h v -> p (h v)"), corev[:, 6 + j:7 + j], SinC32.re("p h v -> p (h v)"), ALU.mult, ALU.add)
                cp("act", sinC, SinC32)
                for h in range(4):
                    pt = nxt("pb", psB)
                    tr(pt[:, 0:128], gS[:, h, 128:256], ident)
                    cp("act", PmT, pt[:, 0:128])
                    px = nxt("pb", psB)
                    mm(px[:, 0:128], PmT, sinC[:, h, :])
                    tt("dve", Xc[:, h, :], px[:, 0:128], gS[:, h, 0:128], ALU.add)
                S.dma("sp", V(dr["p_gdn"].ap[l].rearrange("h k v -> k h v"), "out_p_gdn"), Xc)
            S.fence()

        for b in range(NBLK):
            AR.reset()
            c0 = b * TB
            last = (b == NBLK - 1)
            n = TB + (NS if last else 0)
            segs = [(c0, 0, TB)] + ([(SEG, TB, NS)] if last else [])
            o_a = AR.alloc("o_a", [8, n], BF16)
            o_bn = AR.alloc("o_bn", [4, n], BF16)
            o_cn = AR.alloc("o_cn", [4, n], BF16)
            mark = AR.off
            for (tagB, onrm, loc, qsrc, sinX, zoff, nwc, stg_min) in (("B", o_bn, "ob_loc", "qh", sinB, OFF["bg"], 0, 6), ("C", o_cn, "oc_loc", "rh", sinC, OFF["cz"], 1, 7)):
                if stage < stg_min:
                    continue
                obl = AR.alloc("obl", [4, n], BF16)
                qhl = AR.alloc("qhl", [4, TB], BF16)
                o32s = [AR.alloc(f"o32{i}", [512]) for i in range(2)]
                sq32s = [AR.alloc(f"sq32{i}", [512]) for i in range(2)]
                rss = [AR.alloc(f"rs_{i}", [512]) for i in range(2)]
                szs = [AR.alloc(f"sz{i}", [512]) for i in range(2)]
                S.dma("sp", obl[:, :, 0:TB], dr[loc][:, :, c0:c0 + TB])
                if last:
                    S.dma("sp", obl[:, :, TB:TB + NS], dr[loc][:, :, SEG:SEG + NS])
                S.dma("sp", qhl, dr[qsrc][:, :, c0:c0 + TB])
                wz = load_w(wi[:, zoff:zoff + 512], 128, 8, 512)
                for h in range(4):
                    for (a, ha, sn) in segs:
                        o32, sq32, rs_, sz = o32s[h % 2], sq32s[h % 2], rss[h % 2], szs[h % 2]
                        if ha == 0:
                            pd = nxt("pb", psB)
                            mm(pd[:, 0:TB], sinX[:, h, :], qhl[:, h, :])
                            tt("dve", o32[:, 0:sn], pd[:, 0:TB], obl[:, h, 0:TB], ALU.add)
                        else:
                            cp("dve", o32[:, 0:sn], obl[:, h, TB:TB + NS])
                        act(sq32[:, 0:sn], o32[:, 0:sn], AF.Square)
                        ps_ = nxt("pb", psB)
                        mm(ps_[:, 0:sn], ones32, sq32[:, 0:sn])
                        act(rs_[:, 0:sn], ps_[:, 0:sn], AF.Sqrt, scale=1.0 / 128, bias=eps1)
                        recip(rs_[:, 0:sn], rs_[:, 0:sn])
                        pz = nxt("pa", psA)
                        for kt in range(8):
                            mm(pz[:, 0:sn], wz[:, kt, h * 128:(h + 1) * 128], xbf[:, kt, a:a + sn], start=(kt == 0), stop=(kt == 7))
                        act(sz[:, 0:sn], pz[:, 0:sn], AF.Silu)
                        tt("dve", o32[:, 0:sn], o32[:, 0:sn], rs_[:, 0:sn], ALU.mult)
                        stt(onrm[:, h, ha:ha + sn], o32[:, 0:sn], nrmw[:, nwc:nwc + 1], sz[:, 0:sn], ALU.mult, ALU.mult)
                S.fence()
                AR.off = mark
                AR.gen += 1
            if stage >= 5:
                qF = AR.alloc("qF", [8, n], BF16)
                kF = AR.alloc("kF", [2, 128 + TB], BF16)
                vT = AR.alloc("vT", [5, 128], BF16)
                wq = load_w(wi[:, OFF["aq"]:OFF["aq"] + 512], 128, 8, 512)
                wkv = load_w(wi[:, OFF["ak"]:OFF["ak"] + 256], 128, 8, 256)
                cp("pool", kF[0:64, :, 0:128], kprev)
                cp("pool", vT[:, 0, :], vprev)
                if last:
                    kFs = AR.alloc("kFs", [2, NS], BF16)
                    vTs = AR.alloc("vTs", [128], BF16)
                for h in range(8):
                    for (a, ha, sn) in segs:
                        pa = nxt("pa", psA)
                        for kt in range(8):
                            mm(pa[0:64, 0:sn], wq[:, kt, h * 64:(h + 1) * 64], xbf[:, kt, a:a + sn], start=(kt == 0), stop=(kt == 7))
                        act(qF[0:64, h, ha:ha + sn], pa[0:64, 0:sn], AF.Copy, scale=0.125)
                for g in range(2):
                    for (a, ha, sn) in segs:
                        pa = nxt("pa", psA)
                        for kt in range(8):
                            mm(pa[0:64, 0:sn], wkv[:, kt, g * 64:(g + 1) * 64], xbf[:, kt, a:a + sn], start=(kt == 0), stop=(kt == 7))
                        if ha == 0:
                            cp("act", kF[0:64, g, 128:128 + TB], pa[0:64, 0:TB])
                        else:
                            cp("act", kFs[0:64, g, :], pa[0:64, 0:NS])
                for t in range(4):
                    pa = nxt("pa", psA)
                    for kt in range(8):
                        mm(pa[:, 0:128], xbf[:, kt, c0 + t * 128:c0 + (t + 1) * 128], wkv[:, kt, 128:256], start=(kt == 0), stop=(kt == 7))
                    cp("dve", vT[:, 1 + t, :], pa[:, 0:128])
                if last:
                    pa = nxt("pa", psA)
                    for kt in range(8):
                        mm(pa[0:NS, 0:128], xbf[:, kt, SEG:SEG + NS], wkv[:, kt, 128:256], start=(kt == 0), stop=(kt == 7))
                    cp("dve", vTs[0:NS, :], pa[0:NS, 0:128])
                E = [AR.alloc(f"E{i}", [512], BF16) for i in range(2)]
                dsm = [AR.alloc(f"dsm{i}", [256]) for i in range(2)]
                it = 0
                for cq in range(8):
                    i = cq // 2
                    for g in range(2):
                        rX = slice(0, 128) if cq % 2 == 0 else slice(64, 128)
                        rY = slice(0, 64) if cq % 2 == 0 else slice(0, 128)
                        ps_ = nxt("pa", psA)
                        q3 = qF[0:64, 4 * g:4 * g + 4, cq * 64:(cq + 1) * 64]
                        mm(ps_[:, 0:256].re("p (h t) -> p h t", h=4), kF[0:64, g, i * 128:(i + 1) * 128], q3)
                        mm(ps_[:, 256:512].re("p (h t) -> p h t", h=4), kF[0:64, g, (i + 1) * 128:(i + 2) * 128], q3)
                        e_ = E[it % 2]
                        bX = corev[:, 9:10] if (b == 0 and i == 0) else zero1
                        act(e_[:, 0:256], ps_[:, 0:256], AF.Exp, bias=bX)
                        act(e_[:, 256:512], ps_[:, 256:512], AF.Exp, bias=zero1)
                        po = nxt("pb", psB)
                        mm(po[0:64, 0:256], vT[rX, i, g * 64:(g + 1) * 64], e_[rX, 0:256], start=True, stop=False)
                        mm(po[0:64, 0:256], vT[rY, i + 1, g * 64:(g + 1) * 64], e_[rY, 256:512], start=False, stop=True)
                        mm(po[0:64, 256:512], onesb[rX, 0:64], e_[rX, 0:256], start=True, stop=False)
                        mm(po[0:64, 256:512], onesb[rY, 0:64], e_[rY, 256:512], start=False, stop=True)
                        d_ = dsm[it % 2]
                        tt("dve", d_[0:64, :], po[0:64, 256:512], esinkB[:, g * 256:(g + 1) * 256], ALU.add)
                        recip(d_[0:64, :], d_[0:64, :])
                        tt("dve", o_a[0:64, 4 * g:4 * g + 4, cq * 64:(cq + 1) * 64],
                           po[0:64, 0:256].re("p (h t) -> p h t", h=4), d_[0:64, :].re("p (h t) -> p h t", h=4), ALU.mult)
                        it += 1
                if last:
                    kct = [AR.alloc(f"kct{i}", [128]) for i in range(2)]
                    vct = [AR.alloc(f"vct{i}", [128]) for i in range(2)]
                    kc = [AR.alloc(f"kc{i}", [2, 128], BF16) for i in range(2)]
                    vc = [AR.alloc(f"vc{i}", [128], BF16) for i in range(2)]
                    for sq in range(4):
                        S.dma("sp", kct[sq % 2], dr["cache_k"][l][sq])
                        S.dma("sp", vct[sq % 2], dr["cache_v"][l][sq])
                        pb = nxt("pb", psB)
                        for g in range(2):
                            tr(pb[0:64, g * 128:(g + 1) * 128], kct[sq % 2][:, g * 64:(g + 1) * 64], ident)
                        cp("act", kc[sq % 2][0:64, :, :], pb[0:64, 0:256].re("p (g t) -> p g t", g=2))
                        cp("pool", vc[sq % 2], vct[sq % 2])
                        for g in range(2):
                            ps_ = nxt("pa", psA)
                            q3 = qF[0:64, 4 * g:4 * g + 4, TB + sq * 16:TB + (sq + 1) * 16]
                            mm(ps_[:, 0:64].re("p (h t) -> p h t", h=4), kc[sq % 2][0:64, g, :], q3)
                            mm(ps_[0:64, 64:128].re("p (h t) -> p h t", h=4), kFs[0:64, g, :], q3)
                            e_ = E[it % 2]
                            act(e_[:, 0:64], ps_[:, 0:64], AF.Exp, bias=zero1)
                            act(e_[0:64, 64:128], ps_[0:64, 64:128], AF.Exp, bias=seqmask[:, sq:sq + 1])
                            po = nxt("pb", psB)
                            mm(po[0:64, 0:64], vc[sq % 2][:, g * 64:(g + 1) * 64], e_[:, 0:64], start=True, stop=False)
                            mm(po[0:64, 0:64], vTs[0:64, g * 64:(g + 1) * 64], e_[0:64, 64:128], start=False, stop=True)
                            mm(po[0:64, 64:128], onesb[:, 0:64], e_[:, 0:64], start=True, stop=False)
                            mm(po[0:64, 64:128], onesb[0:64, 0:64], e_[0:64, 64:128], start=False, stop=True)
                            d_ = dsm[it % 2]
                            tt("dve", d_[0:64, 0:64].re("p (h t) -> p h t", h=4), po[0:64, 64:128].re("p (h t) -> p h t", h=4),
                               esinkB[:, g * 256:(g + 1) * 256].re("p (h t) -> p h t", h=4)[:, :, 0:16], ALU.add)
                            recip(d_[0:64, 0:64], d_[0:64, 0:64])
                            tt("dve", o_a[0:64, 4 * g:4 * g + 4, TB + sq * 16:TB + (sq + 1) * 16],
                               po[0:64, 0:64].re("p (h t) -> p h t", h=4), d_[0:64, 0:64].re("p (h t) -> p h t", h=4), ALU.mult)
                            it += 1
                cp("pool", kprev, kF[0:64, :, TB:TB + 128])
                cp("pool", vprev, vT[:, 4, :])
                S.fence()
            AR.off = mark
            AR.gen += 1
            mixacc = AR.alloc("mixacc", [8, n])
            mixbf = AR.alloc("mixbf", [8, n], BF16)
            gsb = [AR.alloc(f"gsb{i}", [512]) for i in range(2)]
            branches = (["A"] if stage >= 5 else []) + (["B"] if stage >= 6 else []) + (["C"] if stage >= 7 else [])
            if not branches:
                memset("pool", mixbf, 0.0)
            for bi_, br in enumerate(branches):
                brow = {"A": 0, "B": 512, "C": 1024}[br]
                for hf in range(2):
                    if br == "A":
                        wb = load_w(WB["w_branch"][l][brow:brow + 512, hf * 512:(hf + 1) * 512], 64, 8, 512)
                    else:
                        wb = load_w(WB["w_branch"][l][brow:brow + 512, hf * 512:(hf + 1) * 512], 128, 4, 512)
                    gcol = OFF["gt"] + bi_ * 0 + {"A": 0, "B": 1024, "C": 2048}[br] + hf * 512
                    wg = load_w(wi[:, gcol:gcol + 512], 128, 8, 512)
                    for q in range(4):
                        i = hf * 4 + q
                        for (a, ha, sn) in segs:
                            p1 = nxt("pa", psA)
                            p2 = nxt("pa", psA)
                            if br == "A":
                                for kt in range(8):
                                    mm(p1[:, 0:sn], wb[0:64, kt, q * 128:(q + 1) * 128], o_a[0:64, kt, ha:ha + sn], start=(kt == 0), stop=(kt == 7))
                            else:
                                osrc = o_bn if br == "B" else o_cn
                                for kt in range(4):
                                    mm(p1[:, 0:sn], wb[:, kt, q * 128:(q + 1) * 128], osrc[:, kt, ha:ha + sn], start=(kt == 0), stop=(kt == 3))
                            for kt in range(8):
                                mm(p2[:, 0:sn], wg[:, kt, q * 128:(q + 1) * 128], xbf[:, kt, a:a + sn], start=(kt == 0), stop=(kt == 7))
                            gs_ = gsb[(i + ha) % 2]
                            act(gs_[:, 0:sn], p2[:, 0:sn], AF.Sigmoid)
                            if bi_ == 0:
                                tt("dve", mixacc[:, i, ha:ha + sn], p1[:, 0:sn], gs_[:, 0:sn], ALU.mult)
                            else:
                                tt("dve", gs_[:, 0:sn], p1[:, 0:sn], gs_[:, 0:sn], ALU.mult)
                                tt("pool", mixacc[:, i, ha:ha + sn], mixacc[:, i, ha:ha + sn], gs_[:, 0:sn], ALU.add)
            if branches:
                cp("pool", mixbf, mixacc)
            for hf in range(2):
                wo = load_w(WB["w_out"][l][:, hf * 512:(hf + 1) * 512], 128, 8, 512)
                for q in range(4):
                    i = hf * 4 + q
                    for (a, ha, sn) in segs:
                        pa = nxt("pa", psA)
                        for kt in range(8):
                            mm(pa[:, 0:sn], wo[:, kt, q * 128:(q + 1) * 128], mixbf[:, kt, ha:ha + sn], start=(kt == 0), stop=(kt == 7))
                        stt(xres[:, i, a:a + sn], xres[:, i, a:a + sn], ALPHA, pa[:, 0:sn], ALU.mult, ALU.add)
            for (a, ha, sn) in segs:
                layer_norm(a, sn, PC["ln1g"], PC["ln1b"])
            S.fence()

        if stage < 3:
            continue
        AR.reset()
        tailx = AR.alloc("tailx", [8, 2])
        S.dma("sp", dr["ag3_in"].re("p (a b) -> p a b", b=2), xres[:, :, SEG - 2:SEG])
        S.fence()
        agi, ago = dr["ag3_in"].ap, dr["ag3_out"].ap
        ccop = S.cc(lambda e: e.collective_compute("AllGather", ALU.bypass, replica_groups=GROUPS,
                                                          ins=[agi], outs=[ago]), ["ag3_in"], ["ag3_out"])
        g4 = AR.alloc("g4", [4, 16])
        S.dma("sp", g4, dr["ag3_out"].re("(r p) c -> p r c", p=128))
        tx = tailx.re("p a b -> p (a b)")
        ts("dve", tx, g4[:, 0, :], corev[:, 1:2], None, ALU.mult)
        for j in range(1, 4):
            stt(tx, g4[:, j, :], corev[:, 1 + j:2 + j], tx, ALU.mult, ALU.add)
        tailb = AR.alloc("tailb", [8, 2], BF16)
        cp("dve", tailb, tailx)
        S.fence()
        keep = AR.off

        for b in range(NBLK if stage >= 4 else 0):
            AR.off = keep
            AR.gen += 1
            c0 = b * TB
            last = (b == NBLK - 1)
            h = AR.alloc("h", [22, TB + NS], BF16)
            ug = [AR.alloc(f"ug{i}", [2 + TB]) for i in range(2)]
            uv = [AR.alloc(f"uv{i}", [2 + TB]) for i in range(2)]
            cg = [AR.alloc(f"cg{i}", [TB]) for i in range(2)]
            cv = [AR.alloc(f"cv{i}", [TB]) for i in range(2)]
            usg = [AR.alloc(f"usg{i}", [4, 18]) for i in range(2)]
            usv = [AR.alloc(f"usv{i}", [4, 18]) for i in range(2)]
            csg = [AR.alloc(f"csg{i}", [4, 16]) for i in range(2)]
            csv = [AR.alloc(f"csv{i}", [4, 16]) for i in range(2)]
            sio = [(AR.alloc(f"sin{i}", [512]), AR.alloc(f"sout{i}", [512]), None) for i in range(2)] if last else None
            ucb = [AR.alloc(f"uc{i}", [16]) for i in range(2)]
            for jg in range(6):
                nj = 4 if jg < 5 else 2
                wg = load_w(WB["w_up"][l][:, jg * 512:jg * 512 + nj * 128], 128, 8, nj * 128)
                wv = load_w(WB["w_up"][l][:, D_FF + jg * 512:D_FF + jg * 512 + nj * 128], 128, 8, nj * 128)
                for jj in range(nj):
                    j = jg * 4 + jj
                    fw = PC["fcw"]
                    for half, (wt, ubuf, cbuf, usb, csb) in enumerate(((wg, ug, cg, usg, csg), (wv, uv, cv, usv, csv))):
                        jc = j + 22 * half
                        u = ubuf[j % 2]
                        cc_ = cbuf[j % 2]
                        pa = nxt("pa", psA)
                        for kt in range(8):
                            mm(pa[:, 0:TB], wt[:, kt, jj * 128:(jj + 1) * 128], xbf[:, kt, c0:c0 + TB],
                               start=(kt == 0), stop=(kt == 7))
                        if b == 0:
                            pb = nxt("pb", psB)
                            for kt in range(8):
                                mm(pb[:, 0:2], wt[:, kt, jj * 128:(jj + 1) * 128], tailb[:, kt, :],
                                   start=(kt == 0), stop=(kt == 7))
                            cp("dve", u[:, 0:2], pb[:, 0:2])
                        else:
                            cp("pool", u[:, 0:2], ffn_tail[:, jc, :])
                        cp("act", u[:, 2:2 + TB], pa[:, 0:TB])
                        cp("pool", ffn_tail[:, jc, :], u[:, TB:TB + 2])
                        wc = lambda t, jc=jc: par[:, fw + t * 44 + jc:fw + t * 44 + jc + 1]
                        ts("pool", cc_, u[:, 0:TB], wc(0), par[:, PC["fcb"] + jc:PC["fcb"] + jc + 1], ALU.mult, ALU.add)
                        stt(cc_, u[:, 1:TB + 1], wc(1), cc_, ALU.mult, ALU.add)
                        stt(cc_, u[:, 2:TB + 2], wc(2), cc_, ALU.mult, ALU.add)
                        if last:
                            us = usb[j % 2]
                            cs = csb[j % 2]
                            pb = nxt("pb", psB)
                            for kt in range(8):
                                mm(pb[:, 0:NS], wt[:, kt, jj * 128:(jj + 1) * 128], xbf[:, kt, SEG:SEG + NS],
                                   start=(kt == 0), stop=(kt == 7))
                            sin, sout, pout = sio[half]
                            if jj == 0:
                                S.dma("sp", sin[0:8, 0:nj * 128],
                                      V(dr["st_fconv"].ap[l].rearrange("s r c -> (s r) c")[:, jc * 128:(jc + nj) * 128], "RO"))
                            pb2 = nxt("pb", psB)
                            tr(pb2[:, 0:8], sin[0:8, jj * 128:(jj + 1) * 128], ident[0:8, 0:8])
                            cp("dve", us[:, :, 0:2], pb2[:, 0:8].re("p (s r) -> p s r", s=4))
                            cp("act", us[:, :, 2:18], pb[:, 0:NS].re("p (s t) -> p s t", s=4))
                            ts("pool", cs, us[:, :, 0:16], wc(0), par[:, PC["fcb"] + jc:PC["fcb"] + jc + 1], ALU.mult, ALU.add)
                            stt(cs, us[:, :, 1:17], wc(1), cs, ALU.mult, ALU.add)
                            stt(cs, us[:, :, 2:18], wc(2), cs, ALU.mult, ALU.add)
                            uc = ucb[(2 * j + half) % 2]
                            cp("pool", uc[:, 0:8].re("p (s r) -> p s r", s=4), us[:, :, 16:18])
                            cp("pool", uc[:, 8:10], u[:, TB:TB + 2])
                            pb3 = nxt("pb", psB)
                            tr(pb3[0:10, 0:128], uc[:, 0:10], ident)
                            cp("dve", sout[0:10, jj * 128:(jj + 1) * 128], pb3[0:10, 0:128])
                            if jj == nj - 1:
                                S.dma("sp", V(dr["s_fconv"].ap[l].rearrange("s r c -> (s r) c")[:, (jc - nj + 1) * 128:(jc + 1) * 128], "out_s_fconv"),
                                      sout[0:8, 0:nj * 128])
                                S.dma("sp", V(dr["p_fconv"].ap[l][:, (jc - nj + 1) * 128:(jc + 1) * 128], "out_p_fconv"),
                                      sout[8:10, 0:nj * 128])
                    act(cg[j % 2], cg[j % 2], AF.Silu)
                    tt("dve", h[:, j, 0:TB], cg[j % 2], cv[j % 2], ALU.mult)
                    if last:
                        act(csg[j % 2], csg[j % 2], AF.Silu)
                        tt("dve", h[:, j, TB:TB + NS].re("p (s t) -> p s t", s=4), csg[j % 2], csv[j % 2], ALU.mult)
            segs = [(c0, 0, TB)] + ([(SEG, TB, NS)] if last else [])
            for i2 in range(2):
                wd = [load_w(WB["w_down"][l][:, (i2 * 4 + q) * 128:(i2 * 4 + q + 1) * 128], 128, 22, 128) for q in range(1)]
                for q in range(4):
                    i = i2 * 4 + q
                    if q > 0:
                        wd = [load_w(WB["w_down"][l][:, i * 128:(i + 1) * 128], 128, 22, 128)]
                    for (a, ha, sn) in segs:
                        pa = nxt("pa", psA)
                        for kt in range(22):
                            mm(pa[:, 0:sn], wd[0][:, kt, :], h[:, kt, ha:ha + sn], start=(kt == 0), stop=(kt == 21))
                        stt(xres[:, i, a:a + sn], xres[:, i, a:a + sn], ALPHA, pa[:, 0:sn], ALU.mult, ALU.add)
            for (a, ha, sn) in segs:
                layer_norm(a, sn, PC["ln2g"], PC["ln2b"])
            S.fence()

    AR.reset()
    yo = [AR.alloc(f"yo{i}", [D]) for i in range(2)]
    for ti, (src, r0, nr, c0) in enumerate(tiles):
        y = yo[ti % 2]
        for dq in range(2):
            pb = nxt("pb", psB)
            for q in range(4):
                dt_ = dq * 4 + q
                tr(pb[0:nr, q * 128:(q + 1) * 128], xres[:, dt_, c0:c0 + nr], ident)
            cp("act" if dq % 2 else "dve", y[0:nr, dq * 512:(dq + 1) * 512], pb[0:nr, :])
        dst = dr["y_p"][r0:r0 + nr, :] if src == "xp" else dr["y_s"][0:nr, :]
        S.dma("sp", dst, y[0:nr, :])
    S.fence()
    S.emit(nc, es)
    es.close()
    return nc


_NC_CACHE = {}


def _core_inputs(inp, c):
    b, p = c // 4, c % 4
    m = {}
    m["xp"] = np.ascontiguousarray(inp["x_prompt"][b, p * SEG:(p + 1) * SEG])
    m["xs"] = np.ascontiguousarray(inp["x_sample"][4 * c:4 * c + 4]).reshape(NS, D)
    m["cache_k"] = np.ascontiguousarray(inp["cache_swa_k"][:, 4 * c:4 * c + 4]).reshape(DEPTH, 4, 128, 128)
    m["cache_v"] = np.ascontiguousarray(inp["cache_swa_v"][:, 4 * c:4 * c + 4]).reshape(DEPTH, 4, 128, 128)
    m["st_hgrn"] = np.ascontiguousarray(inp["state_hgrn"][:, 4 * c:4 * c + 4])
    m["st_gdn"] = np.ascontiguousarray(inp["state_gdn"][:, 4 * c:4 * c + 4])
    m["st_gconv"] = np.ascontiguousarray(inp["state_gdn_conv"][:, 4 * c:4 * c + 4])
    m["st_fconv"] = np.ascontiguousarray(inp["state_ffn_conv"][:, 4 * c:4 * c + 4])
    for k in ("ln_in_g", "ln_in_b", "w_in", "attn_sinks", "hgrn_lb_logits", "hgrn_norm_w", "gdn_conv_w",
              "gdn_a_log", "gdn_dt_bias", "gdn_norm_w", "w_branch", "w_out", "ln1_g", "ln1_b", "w_up",
              "ffn_conv_w", "ffn_conv_b", "w_down", "ln2_g", "ln2_b"):
        m[k] = np.ascontiguousarray(inp[k], dtype=np.float32)
    m["consts"] = CONSTS
    cv = np.zeros((128, 16), np.float32)
    cv[:, 0] = 1.0 if p > 0 else 0.0
    if p > 0:
        cv[:, 1 + (p - 1)] = 1.0
    cv[:, 5 + p] = 1.0
    cv[:, 9] = 0.0 if p > 0 else NEGM
    m["corev"] = cv
    return m


def run_cores(inp, nlayers=DEPTH, stage=99):
    key = (nlayers, stage)
    if key not in _NC_CACHE:
        _NC_CACHE[key] = build_nc(stage=stage, nlayers=nlayers)
    nc = _NC_CACHE[key]
    in_maps = [_core_inputs(inp, c) for c in range(8)]
    if nlayers < DEPTH:
        for m in in_maps:
            for n, shp in DRAM_IN:
                if shp[0] == DEPTH and n not in ('hgrn_lb_logits', 'gdn_dt_bias', 'gdn_a_log'):
                    m[n] = np.ascontiguousarray(m[n][:nlayers])
    res = run_bass_kernel_spmd(nc, in_maps, core_ids=list(range(8)))
    return res.results


def kernel(**inp):
    inp = {k: np.asarray(v) for k, v in inp.items()}
    r = run_cores(inp)
    y_p = np.stack([np.concatenate([r[b * 4 + p]["y_p"] for p in range(4)], 0) for b in range(2)], 0)
    y_s = np.concatenate([r[c]["y_s"].reshape(4, 16, D) for c in range(8)], 0)

    def pl(name, shape):
        return np.stack([r[3][name], r[7][name]], 1).reshape(shape)

    def sl(name, shape):
        return np.concatenate([r[c][name] for c in range(8)], 1).reshape(shape)

    outs = (y_p, y_s,
            pl("p_swa_k", (DEPTH, 2, 128, 2, 64)), pl("p_swa_v", (DEPTH, 2, 128, 2, 64)),
            pl("p_hgrn", (DEPTH, 2, 4, 128, 128)), pl("p_gdn", (DEPTH, 2, 4, 128, 128)),
            pl("p_gconv", (DEPTH, 2, 3, C_QKV)), pl("p_fconv", (DEPTH, 2, 2, 2 * D_FF)),
            sl("s_swa_k", (DEPTH, 32, 128, 2, 64)), sl("s_swa_v", (DEPTH, 32, 128, 2, 64)),
            sl("s_hgrn", (DEPTH, 32, 4, 128, 128)), sl("s_gdn", (DEPTH, 32, 4, 128, 128)),
            sl("s_gconv", (DEPTH, 32, 3, C_QKV)), sl("s_fconv", (DEPTH, 32, 2, 2 * D_FF)))
    return tuple(np.ascontiguousarray(o, dtype=np.float32) for o in outs)
```
